# Optimizing a Trainium2 kernel written in Bass

```python
import math
import jax, jax.numpy as jnp
from jax import lax
import numpy as np

D_MODEL = 1024
BATCH = 8
SEQ = 4096
DEPTH = 1

HEAD_DIM = 64
N_HEADS_DIL = 8
N_HEADS_SB = 8
D_DIL = N_HEADS_DIL * HEAD_DIM
D_SB = N_HEADS_SB * HEAD_DIM
D_MIX = D_DIL + D_SB
D_IN = 3 * D_DIL + 3 * D_SB
D_FF = -(-(8 * D_MODEL) // (3 * 256)) * 256
DILATED_PAIRS = ((128, 1), (512, 4), (2048, 16))
BLOCK = 128
ROPE_THETA = 10000.0
EPS = 1e-6

kernel_name = "hymba_dilated_stickbreaking_block"


def _rmsnorm(x, w):
    xf = x.astype(jnp.float32)
    y = xf * lax.rsqrt(jnp.mean(xf * xf, axis=-1, keepdims=True) + EPS)
    wb = w.astype(jnp.float32).reshape((1,) * (x.ndim - 1) + (w.shape[-1],))
    return (y * wb).astype(x.dtype)


def _rope_tables(seq_len):
    pos = jnp.arange(seq_len, dtype=jnp.float32)
    inv_freq = ROPE_THETA ** (-jnp.arange(0, HEAD_DIM, 2, dtype=jnp.float32) / HEAD_DIM)
    ang = pos[:, None] * inv_freq[None, :]
    return jnp.cos(ang)[None, None], jnp.sin(ang)[None, None]


def _apply_rope(x, cos, sin):
    xf = x.astype(jnp.float32)
    half = HEAD_DIM // 2
    x1, x2 = xf[..., :half], xf[..., half:]
    out = jnp.concatenate([x1 * cos - x2 * sin, x2 * cos + x1 * sin], axis=-1)
    return out.astype(x.dtype)


def _dilated_branch(q, k, v, window, dilation):
    B, H, S, Dh = q.shape
    r = dilation
    n_back = window // dilation
    L = S // r
    nb = -(-L // BLOCK)
    Lp = nb * BLOCK

    def strided(t):
        t = t.reshape(B, H, L, r, Dh).transpose(0, 1, 3, 2, 4)
        t = jnp.pad(t, ((0, 0), (0, 0), (0, 0), (0, Lp - L), (0, 0)))
        return t.reshape(B, H, r, nb, BLOCK, Dh)

    qb, kb, vb = strided(q), strided(k), strided(v)

    def with_prev(t):
        prev = jnp.pad(t, ((0, 0), (0, 0), (0, 0), (1, 0), (0, 0), (0, 0)))[:, :, :, :-1]
        return jnp.concatenate([prev, t], axis=-2)

    kk, vv = with_prev(kb), with_prev(vb)
    s = jnp.einsum('bhrnqd,bhrnkd->bhrnqk', qb, kk,
                   preferred_element_type=jnp.float32) * (Dh ** -0.5)
    i = jnp.arange(BLOCK)[:, None]
    j = jnp.arange(2 * BLOCK)[None, :]
    dist = i + BLOCK - j
    key_idx = (jnp.arange(nb)[:, None, None] - 1) * BLOCK + j[None]
    valid = (dist >= 0)[None] & (dist <= n_back)[None] & (key_idx >= 0)
    s = jnp.where(valid[None, None, None], s, -jnp.inf)
    m = jnp.max(s, axis=-1, keepdims=True)
    p = jnp.exp(s - m)
    den = jnp.sum(p, axis=-1, keepdims=True)
    o = jnp.einsum('bhrnqk,bhrnkd->bhrnqd', p.astype(v.dtype), vv,
                   preferred_element_type=jnp.float32) / den
    lse = (m + jnp.log(den))[..., 0]

    o = o.reshape(B, H, r, Lp, Dh)[:, :, :, :L].transpose(0, 1, 3, 2, 4).reshape(B, H, S, Dh)
    lse = lse.reshape(B, H, r, Lp)[:, :, :, :L].transpose(0, 1, 3, 2).reshape(B, H, S)
    return o, lse


def _dilated_attention(q, k, v):
    outs, lses = [], []
    for window, dilation in DILATED_PAIRS:
        o, lse = _dilated_branch(q, k, v, window, dilation)
        outs.append(o)
        lses.append(lse)
    w = jax.nn.softmax(jnp.stack(lses, axis=0), axis=0)
    o = jnp.sum(w[..., None] * jnp.stack(outs, axis=0), axis=0)
    return o.astype(q.dtype)


def _stick_breaking(q, k, v):
    B, H, S, Dh = q.shape
    nb = S // BLOCK
    scale = Dh ** -0.5
    qblocks = q.reshape(B, H, nb, BLOCK, Dh).transpose(2, 0, 1, 3, 4)
    kpos = jnp.arange(S)

    def one_block(args):
        qblk, bidx = args
        z = jnp.einsum('bhqd,bhkd->bhqk', qblk, k,
                       preferred_element_type=jnp.float32) * scale
        qpos = bidx * BLOCK + jnp.arange(BLOCK)
        causal = (kpos[None, :] < qpos[:, None])[None, None]
        log_beta = jax.nn.log_sigmoid(z)
        log_keep = jnp.where(causal, jax.nn.log_sigmoid(-z), 0.0)
        suffix = lax.cumsum(log_keep, axis=3, reverse=True) - log_keep
        a = jnp.where(causal, jnp.exp(log_beta + suffix), 0.0)
        return jnp.einsum('bhqk,bhkd->bhqd', a.astype(v.dtype), v,
                          preferred_element_type=jnp.float32).astype(v.dtype)

    out = lax.map(one_block, (qblocks, jnp.arange(nb)))
    return out.transpose(1, 2, 0, 3, 4).reshape(B, H, S, Dh)


def setup_inputs(seed: int = 0) -> dict:
    key = jax.random.key(seed)
    ks = jax.random.split(key, 12)
    f32 = jnp.float32

    def gain(k, n):
        return (1.0 + 0.02 * jax.random.normal(k, (DEPTH, n))).astype(f32)

    return {
        "x": jax.random.normal(ks[0], (BATCH, SEQ, D_MODEL), f32),
        "attn_norm_w": gain(ks[1], D_MODEL),
        "w_in": jax.random.normal(ks[2], (DEPTH, D_MODEL, D_IN), f32) * D_MODEL ** -0.5,
        "q_norm_w": gain(ks[3], HEAD_DIM),
        "k_norm_w": gain(ks[4], HEAD_DIM),
        "dil_out_norm_w": gain(ks[5], D_DIL),
        "sb_out_norm_w": gain(ks[6], D_SB),
        "w_out": jax.random.normal(ks[7], (DEPTH, D_MIX, D_MODEL), f32) * D_MIX ** -0.5,
        "ffn_norm_w": gain(ks[8], D_MODEL),
        "w_gate": jax.random.normal(ks[9], (DEPTH, D_MODEL, D_FF), f32) * D_MODEL ** -0.5,
        "w_up": jax.random.normal(ks[10], (DEPTH, D_MODEL, D_FF), f32) * D_MODEL ** -0.5,
        "w_down": jax.random.normal(ks[11], (DEPTH, D_FF, D_MODEL), f32) * D_FF ** -0.5,
    }


def reference(x, attn_norm_w, w_in, q_norm_w, k_norm_w, dil_out_norm_w, sb_out_norm_w,
              w_out, ffn_norm_w, w_gate, w_up, w_down):
    B, S, _ = x.shape
    cos, sin = _rope_tables(S)

    def heads(t, n):
        return t.reshape(B, S, n, HEAD_DIM).transpose(0, 2, 1, 3)

    def merge(t):
        return t.transpose(0, 2, 1, 3).reshape(B, S, t.shape[1] * t.shape[3])

    for l in range(DEPTH):
        h = _rmsnorm(x, attn_norm_w[l])
        proj = jnp.einsum('bsd,de->bse', h, w_in[l])
        qa = proj[..., 0:D_DIL]
        ka = proj[..., D_DIL:2 * D_DIL]
        va = proj[..., 2 * D_DIL:3 * D_DIL]
        o0 = 3 * D_DIL
        qs = proj[..., o0:o0 + D_SB]
        ksb = proj[..., o0 + D_SB:o0 + 2 * D_SB]
        vs = proj[..., o0 + 2 * D_SB:o0 + 3 * D_SB]

        qa = _apply_rope(_rmsnorm(heads(qa, N_HEADS_DIL), q_norm_w[l]), cos, sin)
        ka = _apply_rope(_rmsnorm(heads(ka, N_HEADS_DIL), k_norm_w[l]), cos, sin)
        o_dil = _dilated_attention(qa, ka, heads(va, N_HEADS_DIL))

        o_sb = _stick_breaking(heads(qs, N_HEADS_SB), heads(ksb, N_HEADS_SB),
                               heads(vs, N_HEADS_SB))

        mixed = jnp.concatenate([_rmsnorm(merge(o_dil), dil_out_norm_w[l]),
                                 _rmsnorm(merge(o_sb), sb_out_norm_w[l])], axis=-1)
        x = x + jnp.einsum('bse,ed->bsd', mixed, w_out[l])

        h = _rmsnorm(x, ffn_norm_w[l])
        g = jnp.einsum('bsd,df->bsf', h, w_gate[l])
        u = jnp.einsum('bsd,df->bsf', h, w_up[l])
        x = x + jnp.einsum('bsf,fd->bsd', jax.nn.silu(g) * u, w_down[l])
    return x
```

```python
from contextlib import ExitStack
import numpy as np
import concourse.bass as bass
import concourse.mybir as mybir
from concourse.bass_utils import run_bass_kernel_spmd

F32 = mybir.dt.float32
BF16 = mybir.dt.bfloat16
AF = mybir.ActivationFunctionType
ALU = mybir.AluOpType
AX = mybir.AxisListType

S = 4096
D = 1024
NB = S // 128
DFF = 2816
NFC = DFF // 128
EPS = 1e-6
ENGS = ("pe", "act", "dve", "pool", "sp")
NDMASEM = 8


class Res:
    __slots__ = ("w", "r", "pend")

    def __init__(self, init=None):
        self.w = dict(init) if init else {}
        self.r = {}
        self.pend = None


def _merge(dst, src):
    for k, v in src.items():
        if dst.get(k, 0) < v:
            dst[k] = v


class Prog:
    def __init__(self, nc):
        self.nc = nc
        self.es = ExitStack()
        self.ops = {e: [] for e in ENGS}
        self.sem = {}
        self.cnt = {}
        self.seen = {e: {} for e in ENGS}
        self.pending = {e: [] for e in ENGS}
        for e in ENGS:
            self.newsem("E_" + e)
        self.dq = {}
        for q in ("sp", "pool"):
            self.dq[q] = {"keys": [self.newsem(f"D_{q}{i}") for i in range(NDMASEM)], "i": 0}

    def newsem(self, key):
        self.sem[key] = self.es.enter_context(self.nc.semaphore(key))
        self.cnt[key] = 0
        return key

    def sb(self, name, shape, dt, es=None):
        return (es or self.es).enter_context(self.nc.sbuf_tensor(name, list(shape), dt))

    def ps(self, name, shape, dt):
        return self.es.enter_context(self.nc.psum_tensor(name, list(shape), dt))

    def fence(self):
        f = {}
        for k, v in self.cnt.items():
            if v > 0:
                f[k] = v
        for e in ENGS:
            assert not self.pending[e], "fence with pending un-tokened ops"
        return f

    def _collect(self, eng, reads, writes, deps):
        need = {}
        for r in reads:
            assert r.pend is None or r.pend == eng, "resource pending on another engine"
            _merge(need, r.w)
        for r in writes:
            assert r.pend is None or r.pend == eng, "resource pending on another engine"
            _merge(need, r.w)
            _merge(need, r.r)
        for d in deps:
            if d:
                _merge(need, d)
        ws = []
        seen = self.seen[eng]
        for k, v in need.items():
            if k == "E_pe" and eng == "pe":
                continue
            if seen.get(k, 0) >= v:
                continue
            seen[k] = v
            ws.append((k, v))
        return ws

    def _commit(self, eng, tok, reads, writes):
        allp = self.pending[eng] + [(reads, writes)]
        self.pending[eng] = []
        for rs, wsx in allp:
            for r in rs:
                _merge(r.r, tok)
                r.pend = None
            for r in wsx:
                r.w = dict(tok)
                r.r = {}
                r.pend = None

    def op(self, eng, fn, reads=(), writes=(), inc=True, deps=()):
        ws = self._collect(eng, reads, writes, deps)
        if inc:
            key = "E_" + eng
            self.cnt[key] += 1
            tok = {key: self.cnt[key]}
            self.ops[eng].append((ws, fn, (key, 1)))
            self._commit(eng, tok, reads, writes)
            return tok
        self.ops[eng].append((ws, fn, None))
        self.pending[eng].append((reads, writes))
        for r in list(reads) + list(writes):
            r.pend = eng
        return None

    def dma(self, q, fn, reads=(), writes=(), deps=()):
        assert not self.pending[q]
        dq = self.dq[q]
        key = dq["keys"][dq["i"] % NDMASEM]
        dq["i"] += 1
        prev = {key: self.cnt[key]} if self.cnt[key] else None
        ws = self._collect(q, reads, writes, list(deps) + [prev])
        self.cnt[key] += 16
        tok = {key: self.cnt[key]}
        self.ops[q].append((ws, fn, (key, 16)))
        for r in reads:
            _merge(r.r, tok)
        for r in writes:
            r.w = dict(tok)
            r.r = {}
        return tok

    def wait(self, eng, deps):
        ws = self._collect(eng, (), (), deps)
        if ws:
            self.ops[eng].append((ws, None, None))

    def emit(self):
        prog = self
        for e in ENGS:
            assert not self.pending[e], f"pending ops on {e}"

        def run(name, e):
            for ws, fn, inc in prog.ops[name]:
                for key, val in ws:
                    e.wait_ge(prog.sem[key], val)
                if fn is None:
                    continue
                ins = fn(e)
                if inc is not None:
                    ins.then_inc(prog.sem[inc[0]], inc[1])

        with self.nc.Block() as block:
            @block.tensor
            def _(e):
                run("pe", e)

            @block.scalar
            def _(e):
                run("act", e)

            @block.vector
            def _(e):
                run("dve", e)

            @block.gpsimd
            def _(e):
                run("pool", e)

            @block.sync
            def _(e):
                run("sp", e)

    def close(self):
        self.es.close()


def MM(out, lhsT, rhs, start, stop, **kw):
    return lambda e: e.matmul(out, lhsT=lhsT, rhs=rhs, start=start, stop=stop, **kw)


def TR(out, in_, idn):
    return lambda e: e.transpose(out, in_, idn)


def ACTV(out, in_, func, **kw):
    return lambda e: e.activation(out=out, in_=in_, func=func, **kw)


def TT(out, in0, in1, op):
    return lambda e: e.tensor_tensor(out=out, in0=in0, in1=in1, op=op)


def TS(out, in0, s1, op0):
    return lambda e: e.tensor_scalar(out=out, in0=in0, scalar1=s1, scalar2=None, op0=op0)


def STT(out, in0, scalar, in1, op0, op1):
    return lambda e: e.scalar_tensor_tensor(out=out, in0=in0, scalar=scalar, in1=in1, op0=op0, op1=op1)


def CP(out, in_):
    return lambda e: e.tensor_copy(out=out, in_=in_)


def RCP(out, in_):
    return lambda e: e.reciprocal(out=out, in_=in_)


def RSUM(out, in_):
    return lambda e: e.reduce_sum(out=out, in_=in_, axis=AX.X)


def DMA(out, in_):
    return lambda e: e.dma_start(out=out, in_=in_)


def MSET(ap, v):
    return lambda e: e.memset(ap, v)

PA_BLOCKS = NB
KNOB = {}


def build(pairs=tuple(range(8)), do_ffn=True, dbg=None):
    nc = bass.Bass("TRN2", target_bir_lowering=False)

    def din(name, shape, dt=F32):
        return nc.dram_tensor(name, list(shape), dt, kind="ExternalInput").ap()

    x_d = din("x", [S, D])
    anw_d = din("anw_bc", [128, D])
    fnw_d = din("fnw_bc", [128, D])
    qkw_d = din("qkw_bc", [128, 256])
    wcol_d = din("wcol", [128, 8])
    rope_d = din("rope", [S, 96])
    wp_d = din("w_pairs", [8, D, 384])
    wout_d = din("w_out", [D, D])
    wgu_d = din("wgu", [NFC, 128, 2048])
    wd_d = din("w_down", [DFF, D])
    out_d = nc.dram_tensor("out", [S, D], F32, kind="ExternalOutput").ap()
    dbg_d = None
    if dbg in ("hT", "oT"):
        dbg_d = nc.dram_tensor("dbg", [128, 8, S], BF16, kind="ExternalOutput").ap()

    P = Prog(nc)
    out_tokens = []

    pb = [P.ps(f"pb{i}", [128, 1024], F32) for i in range(4)]
    banks = []
    for i in range(4):
        banks += [pb[i][:, 0:512], pb[i][:, 512:1024]]
    bres = [Res() for _ in range(8)]
    bT = [banks[i].bitcast(BF16) for i in range(8)]

    ident = P.sb("ident", [128, 128], BF16)
    tri = P.sb("tri", [128, 128], BF16)
    sl = P.sb("sl", [128, 128], BF16)
    ones = P.sb("ones", [128, 128], BF16)
    sbm = [P.sb(f"sbm{j}", [128, 512], BF16) for j in range(4)]
    dm1 = P.sb("dm1", [128, 512], BF16)
    dmz = P.sb("dmz", [128, 512], BF16)
    dmz2 = P.sb("dmz2", [128, 512], BF16)
    wcol = P.sb("wcol_sb", [128, 8], F32)
    oT = P.sb("oT", [128, 8, S], BF16)
    r_const = Res()
    r_oT = [[Res() for _ in range(8)] for _ in range(8)]

    def SEL(t_ap, pattern, base, cm, cmp):
        return lambda e: e.affine_select(out=t_ap, in_=t_ap, pattern=pattern, base=base,
                                         channel_multiplier=cm, compare_op=cmp, fill=0.0)

    for t in (ident, tri, sl, ones, dm1, dmz, dmz2, *sbm):
        P.op("pool", MSET(t[:], 1.0), writes=[r_const])
    P.op("pool", SEL(ident[:], [[-1, 128]], 0, 1, ALU.is_equal), writes=[r_const])
    P.op("pool", SEL(tri[:], [[-1, 128]], 0, 1, ALU.is_ge), writes=[r_const])
    P.op("pool", SEL(sl[:], [[1, 128]], 0, -1, ALU.is_gt), writes=[r_const])
    for j in range(4):
        P.op("pool", SEL(sbm[j][:], [[1, 512]], -128 * j, -1, ALU.is_gt), writes=[r_const])
    for m in (dm1, dmz, dmz2):
        for slot in range(4):
            ap = m[:, slot * 128:(slot + 1) * 128]
            if slot % 2 == 0:
                P.op("pool", SEL(ap, [[-1, 128]], 0, 1, ALU.is_ge), writes=[r_const])
            else:
                P.op("pool", SEL(ap, [[1, 128]], 0, -1, ALU.is_ge), writes=[r_const])
    P.op("pool", MSET(dmz[:, 0:128], 0.0), writes=[r_const])
    P.op("pool", MSET(dmz2[:, 0:128], 0.0), writes=[r_const])
    P.op("pool", MSET(dmz2[:, 256:384], 0.0), writes=[r_const])
    P.dma("sp", DMA(wcol[:], wcol_d), writes=[r_const])
    wgu_bf = nc.dram_tensor("wgu_bf", [NFC, 128, 2048], BF16, kind="Internal").ap()
    r_wgubf = [Res() for _ in range(NFC)]

    es_att = ExitStack()
    hT = P.sb("hT", [128, 8, S], BF16, es_att)
    r_hT = [Res() for _ in range(NB)]
    Wp = P.sb("Wp", [128, 8, 384], BF16, es_att)
    r_Wp = Res()
    qkvT = P.sb("qkvT", [128, 3, S], BF16, es_att)
    r_qk = [Res() for _ in range(NB)]
    r_v = [Res() for _ in range(8)]
    sq = [P.sb(f"sq{i}", [128, 512], BF16, es_att) for i in range(4)]
    r_sq = [Res() for _ in range(4)]
    rs = P.sb("rs", [128, 512], F32, es_att)
    r_rs = Res()

    es_pa = ExitStack()
    xs = [P.sb(f"xs{i}", [128, D], F32, es_pa) for i in range(4)]
    r_xs = [Res() for _ in range(4)]
    hn = [P.sb(f"hn{i}", [128, D], BF16, es_pa) for i in range(4)]
    r_hn = [Res() for _ in range(4)]
    junk = P.sb("junkA", [128, D], BF16, es_pa)
    r_junk = Res()
    anw = P.sb("anw", [128, D], F32, es_pa)
    r_anw = Res()
    ssA = P.sb("ssA", [128, NB], F32, es_pa)
    sdA = P.sb("sdA", [128, NB], F32, es_pa)
    rsA = P.sb("rsA", [128, NB], F32, es_pa)
    r_ssA = [Res() for _ in range(NB)]
    P.dma("sp", DMA(anw[:], anw_d), writes=[r_anw])
    BS = 2

    def paA(sb):
        for b in range(BS):
            tb = sb * BS + b
            xi = (sb % 2) * BS + b
            tsl = slice(tb * 128, (tb + 1) * 128)
            P.dma("sp", DMA(xs[xi][:], x_d[tsl, :]), writes=[r_xs[xi]])
            P.op("act", ACTV(junk[:], xs[xi][:], AF.Square, accum_out=ssA[:, tb:tb + 1]),
                 reads=[r_xs[xi]], writes=[r_junk, r_ssA[sb]])
        c4 = slice(sb * BS, sb * BS + BS)
        P.op("act", ACTV(sdA[:, c4], ssA[:, c4], AF.Sqrt, scale=1.0 / D, bias=EPS), reads=[r_ssA[sb]], writes=[r_ssA[sb]])
        P.op("dve", RCP(rsA[:, c4], sdA[:, c4]), reads=[r_ssA[sb]], writes=[r_ssA[sb]])

    def paB(sb):
        for b in range(BS):
            tb = sb * BS + b
            xi = (sb % 2) * BS + b
            hi = tb % 4
            tsl = slice(tb * 128, (tb + 1) * 128)
            bk = tb % 4
            P.op("dve", STT(hn[hi][:], xs[xi][:], rsA[:, tb:tb + 1], anw[:], ALU.mult, ALU.mult),
                 reads=[r_xs[xi], r_ssA[sb], r_anw], writes=[r_hn[hi]])
            for c in range(8):
                csl = slice(c * 128, (c + 1) * 128)
                P.op("pe", TR(bT[bk][:, csl], hn[hi][:, csl], ident[:]),
                     reads=[r_hn[hi], r_const], writes=[bres[bk]], inc=(c == 7))
            if b % 2 == 0:
                P.op("act", ACTV(hT[:, :, tsl], bT[bk].rearrange("p (c t) -> p c t", t=128), AF.Copy),
                     reads=[bres[bk]], writes=[r_hT[tb]])
            else:
                P.op("dve", CP(hT[:, :, tsl], bT[bk].rearrange("p (c t) -> p c t", t=128)),
                     reads=[bres[bk]], writes=[r_hT[tb]])

    nsb = PA_BLOCKS // BS
    for sb in range(nsb + 1):
        if sb < nsb:
            paA(sb)
        if sb >= 1:
            paB(sb - 1)
    fenceA = P.fence()
    es_pa.close()
    casts_pending = [do_ffn]

    if dbg == "hT":
        for c in range(8):
            out_tokens.append(P.dma("sp", DMA(dbg_d[:, c, :], hT[:, c, :]), reads=r_hT))

    def proj_fm(col0, dst_idx, scale, res_for_slice):
        for ts in range(8):
            bk = ts % 2
            tsl = slice(ts * 512, (ts + 1) * 512)
            for c in range(8):
                P.op("pe", MM(banks[bk], Wp[:, c, col0:col0 + 128], hT[:, c, tsl], c == 0, c == 7),
                     reads=[r_Wp] + r_hT[ts * 4:(ts + 1) * 4], writes=[bres[bk]], inc=(c == 7))
            wr = res_for_slice(ts)
            if ts % 2 == 0:
                P.op("act", ACTV(qkvT[:, dst_idx, tsl], banks[bk], AF.Copy, scale=scale),
                     reads=[bres[bk]], writes=wr)
            else:
                P.op("dve", TS(qkvT[:, dst_idx, tsl], banks[bk], scale, ALU.mult),
                     reads=[bres[bk]], writes=wr)

    def load_pair_weights(pi):
        P.dma("pool", DMA(Wp[:], wp_d[pi].rearrange("(c p) e -> p c e", p=128)), writes=[r_Wp])

    def normalize_group(gc):
        for ts in range(8):
            tsl = slice(ts * 512, (ts + 1) * 512)
            for c in range(4):
                if c < 3:
                    P.op("act", ACTV(sq[c][:], oT[:, gc + c, tsl], AF.Square),
                         reads=[r_oT[gc + c][ts]], writes=[r_sq[c]])
                else:
                    P.op("pool", TT(sq[c][:], oT[:, gc + c, tsl], oT[:, gc + c, tsl], ALU.mult),
                         reads=[r_oT[gc + c][ts]], writes=[r_sq[c]])
            bk = 2 + ts % 2
            for c in range(4):
                P.op("pe", MM(banks[bk], ones[:], sq[c][:], c == 0, c == 3),
                     reads=[r_sq[c], r_const], writes=[bres[bk]], inc=(c == 3))
            P.op("act", ACTV(rs[:], banks[bk], AF.Sqrt, scale=1.0 / 512, bias=EPS),
                 reads=[bres[bk]], writes=[r_rs])
            P.op("dve", RCP(rs[:], rs[:]), reads=[r_rs], writes=[r_rs])
            for c in range(4):
                P.op("dve", STT(oT[:, gc + c, tsl], oT[:, gc + c, tsl], wcol[:, gc + c:gc + c + 1], rs[:],
                                ALU.mult, ALU.mult),
                     reads=[r_rs, r_const], writes=[r_oT[gc + c][ts]])

    LAG = 3
    NSL = 4
    pairsA = [p for p in pairs if p < 4]
    bank_rr = [0]

    def next_bank():
        b = bank_rr[0] % 4
        bank_rr[0] += 1
        return b

    if pairsA:
        es_ga = ExitStack()
        qkw = P.sb("qkw", [128, 256], F32, es_ga)
        r_qkw = Res(fenceA)
        P.dma("sp", DMA(qkw[:], qkw_d), writes=[r_qkw])
        P.op("dve", TS(qkw[:, 0:128], qkw[:, 0:128], 0.125, ALU.mult), writes=[r_qkw])
        fprev = fenceA
        for pi in pairsA:
            load_pair_weights(pi)
            es_p = ExitStack()
            NS = 4
            sa = [P.sb(f"sa{pi}_{i}", [128, 512], F32, es_p) for i in range(NS)]
            qn = [P.sb(f"qn{pi}_{i}", [128, 512], F32, es_p) for i in range(NS)]
            t2 = [P.sb(f"t2{pi}_{i}", [128, 512], F32, es_p) for i in range(NS)]
            qr = [P.sb(f"qr{pi}_{i}", [128, 512], BF16, es_p) for i in range(NS + 1)]
            st = [P.sb(f"st{pi}_{i}", [128, 24], F32, es_p) for i in range(NS)]
            ropeb = [P.sb(f"ropeb{pi}_{i}", [128, 2, 96], F32, es_p) for i in range(NS)]
            r_sa = [Res(fprev) for _ in range(NS)]
            r_qn = [Res(fprev) for _ in range(NS)]
            r_t2 = [Res(fprev) for _ in range(NS)]
            r_qr = [Res(fprev) for _ in range(NS + 1)]
            r_st = [Res(fprev) for _ in range(NS)]
            r_rope = [Res(fprev) for _ in range(NS)]
            w3 = lambda ap: ap.rearrange("p (s d) -> p s d", d=64)
            w4 = lambda ap: ap.rearrange("p (b s d) -> p b s d", b=2, d=32)
            w5 = lambda ap: ap.rearrange("p (b s h d) -> p b s h d", b=2, h=2, d=32)
            wb = lambda ap: ap.rearrange("p (b e) -> p b e", e=256)
            NBB = NB // 2

            def st0(bb):
                k = bb % NS
                tb0 = bb * 2
                bk = bb % 2
                P.dma("sp", DMA(ropeb[k][:], rope_d[tb0 * 128:(tb0 + 2) * 128, :].rearrange("(b p) d -> p b d", p=128)),
                      writes=[r_rope[k]])
                for b in range(2):
                    tsl = slice((tb0 + b) * 128, (tb0 + b + 1) * 128)
                    for c in range(8):
                        P.op("pe", MM(banks[bk][:, b * 256:(b + 1) * 256], hT[:, c, tsl], Wp[:, c, 0:256], c == 0, c == 7),
                             reads=[r_Wp, r_hT[tb0 + b]], writes=[bres[bk]], inc=(b == 1 and c == 7))

            def st1a(bb):
                k = bb % NS
                bk = bb % 2
                P.op("act", ACTV(sa[k][:], banks[bk], AF.Square), reads=[bres[bk]], writes=[r_sa[k]])
                P.op("dve", RSUM(st[k][:, 0:8], w3(sa[k][:])), reads=[r_sa[k]], writes=[r_st[k]])
                P.op("act", ACTV(st[k][:, 8:16], st[k][:, 0:8], AF.Sqrt, scale=1.0 / 64, bias=EPS),
                     reads=[r_st[k]], writes=[r_st[k]])

            def st1b(bb):
                k = bb % NS
                bk = bb % 2
                P.op("dve", RCP(st[k][:, 16:24], st[k][:, 8:16]), reads=[r_st[k]], writes=[r_st[k]])
                P.op("dve", TT(w3(qn[k][:]), w3(banks[bk]), st[k][:, 16:24].unsqueeze(2).to_broadcast([128, 8, 64]),
                               ALU.mult), reads=[bres[bk], r_st[k]], writes=[r_qn[k]])
                P.op("pool", TT(wb(qn[k][:]), wb(qn[k][:]), qkw[:].unsqueeze(1).to_broadcast([128, 2, 256]), ALU.mult),
                     reads=[r_qkw], writes=[r_qn[k]])

            def st2(bb):
                k = bb % NS
                cosb = ropeb[k][:, :, 0:32].unsqueeze(2).to_broadcast([128, 2, 8, 32])
                sinb = ropeb[k][:, :, 32:64].unsqueeze(2).to_broadcast([128, 2, 4, 32])
                nsinb = ropeb[k][:, :, 64:96].unsqueeze(2).to_broadcast([128, 2, 4, 32])
                P.op("dve", TT(w4(sa[k][:]), w4(qn[k][:]), cosb, ALU.mult),
                     reads=[r_qn[k], r_rope[k]], writes=[r_sa[k]])
                P.op("pool", TT(w5(t2[k][:])[:, :, :, 0, :], w5(qn[k][:])[:, :, :, 1, :], nsinb, ALU.mult),
                     reads=[r_qn[k], r_rope[k]], writes=[r_t2[k]])
                P.op("pool", TT(w5(t2[k][:])[:, :, :, 1, :], w5(qn[k][:])[:, :, :, 0, :], sinb, ALU.mult),
                     reads=[r_qn[k], r_rope[k]], writes=[r_t2[k]])

            def st3a(bb):
                k = bb % NS
                kq = bb % (NS + 1)
                P.op("dve", TT(qr[kq][:], sa[k][:], t2[k][:], ALU.add), reads=[r_sa[k], r_t2[k]], writes=[r_qr[kq]])

            def st3b(bb):
                kq = bb % (NS + 1)
                tb0 = bb * 2
                t2sl = slice(tb0 * 128, (tb0 + 2) * 128)
                bk2 = 2 + bb % 2
                for j in range(4):
                    jsl = slice(j * 128, (j + 1) * 128)
                    P.op("pe", TR(bT[bk2][:, jsl], qr[kq][:, jsl], ident[:]),
                         reads=[r_qr[kq], r_const], writes=[bres[bk2]], inc=(j == 3))
                P.op("act", ACTV(qkvT[:, 0:2, t2sl].rearrange("p j (b t) -> p j b t", t=128),
                                 bT[bk2][:, 0:512].rearrange("p (b j t) -> p j b t", j=2, t=128), AF.Copy),
                     reads=[bres[bk2]], writes=[r_qk[tb0], r_qk[tb0 + 1]])

            for it in range(NBB + 4):
                if it < NBB:
                    st0(it)
                if 0 <= it - 1 < NBB:
                    st1a(it - 1)
                if 0 <= it - 2 < NBB:
                    st2(it - 2)
                if 0 <= it - 1 < NBB:
                    st1b(it - 1)
                if 0 <= it - 3 < NBB:
                    st3a(it - 3)
                if 0 <= it - 4 < NBB:
                    st3b(it - 4)
            proj_fm(256, 2, 1.0, lambda ts: [r_v[ts]])
            fmid = P.fence()
            es_p.close()

            es_a = ExitStack()
            vaug = [P.sb(f"vaug{pi}_{i}", [128, NB, 128], BF16, es_a) for i in range(3)]
            r_vaug = [Res(fmid) for _ in range(3)]
            expS = [P.sb(f"expS{pi}_{i}", [128, 512], BF16, es_a) for i in range(NSL)]
            r_expS = [Res(fmid) for _ in range(NSL)]
            ptm = [P.sb(f"ptm{pi}_{i}", [128, 512], BF16, es_a) for i in range(NSL)]
            r_ptm = [Res(fmid) for _ in range(NSL)]
            rden = [P.sb(f"rden{pi}_{i}", [128, 512], F32, es_a) for i in range(1)]
            r_rden = [Res(fmid)]
            for i in range(3):
                P.op("pool", MSET(vaug[i][:, :, 64:128], 1.0), writes=[r_vaug[i]])
            if casts_pending[0]:
                casts_pending[0] = False
                for fc in range(NFC):
                    P.dma("pool", DMA(wgu_bf[fc], wgu_d[fc]), writes=[r_wgubf[fc]], deps=[fmid])
            VIDX = {1: 0, 4: 1, 16: 2}

            vi = 0
            gi = 0
            for hp in range(2):
                p0, p1 = hp * 64, hp * 64 + 64
                for R in range(2):
                    started = [False] * 4
                    groups = []
                    for r in (1, 4, 16):
                        nb = NB // r
                        if r == 16:
                            for c in range(0, 16, 2):
                                groups.append((r, [(c, R - 1, R), (c, R, R), (c + 1, R - 1, R), (c + 1, R, R)]))
                        else:
                            per = nb // 2
                            for c in range(r):
                                for n in range(R * per, (R + 1) * per, 2):
                                    groups.append((r, [(c, n - 1, n), (c, n, n), (c, n, n + 1), (c, n + 1, n + 1)]))

                    def build_vaug(r, vi_):
                        nb = NB // r
                        for g8 in range(4):
                            bk = next_bank()
                            for j in range(8):
                                blk = g8 * 8 + j
                                c, n = blk // nb, blk % nb
                                st_ = n * 128 * r + c
                                P.op("pe", TR(bT[bk][:, j * 64:(j + 1) * 64], qkvT[p0:p1, 2, st_:st_ + 127 * r + 1:r],
                                              ident[p0:p1, p0:p1]),
                                     reads=r_v + [r_const], writes=[bres[bk]], inc=(j == 7))
                            P.op("dve", CP(vaug[vi_][:, g8 * 8:(g8 + 1) * 8, 0:64],
                                           bT[bk][:, 0:512].rearrange("p (j d) -> p j d", d=64)),
                                 reads=[bres[bk]], writes=[r_vaug[vi_]])

                    def emit_pv(item):
                        r, slots, pslot, vi_ = item
                        nb = NB // r
                        mms = []
                        for si, (c, kb, qb) in enumerate(slots):
                            if kb < 0:
                                continue
                            vblk = c * nb + kb
                            if r == 16:
                                for u in range(4):
                                    stt = not started[u]
                                    started[u] = True
                                    q0 = si * 128 + u * 32
                                    mms.append((MM(banks[4 + u][:, c:512:16], vaug[vi_][:, vblk, :],
                                                   ptm[pslot][:, q0:q0 + 32], stt, True, skip_group_check=True), u))
                            else:
                                tok0 = qb * 128 * r + c - R * 2048
                                u = tok0 // 512
                                lo = tok0 - u * 512
                                stt = not started[u]
                                started[u] = True
                                mms.append((MM(banks[4 + u][:, lo:lo + 127 * r + 1:r], vaug[vi_][:, vblk, :],
                                               ptm[pslot][:, si * 128:(si + 1) * 128], stt, True,
                                               skip_group_check=True), u))
                        for kk, (fn, u) in enumerate(mms):
                            P.op("pe", fn, reads=[r_vaug[vi_], r_ptm[pslot]], writes=[bres[4 + u]],
                                 inc=(kk == len(mms) - 1))

                    pend = []
                    if R == 0:
                        for r_ in (1, 4, 16):
                            build_vaug(r_, VIDX[r_])
                    for (r, slots) in groups:
                        vi = VIDX[r]
                        bk = next_bank()
                        es_ = gi % NSL
                        for si, (c, kb, qb) in enumerate(slots):
                            ks = max(kb, 0) * 128 * r + c
                            qs = qb * 128 * r + c
                            P.op("pe", MM(banks[bk][:, si * 128:(si + 1) * 128],
                                          qkvT[p0:p1, 1, ks:ks + 127 * r + 1:r], qkvT[p0:p1, 0, qs:qs + 127 * r + 1:r],
                                          True, True),
                                 reads=r_qk, writes=[bres[bk]], inc=(si == 3))
                        P.op("act", ACTV(expS[es_][:], banks[bk], AF.Exp), reads=[bres[bk]], writes=[r_expS[es_]])
                        inv = [kb < 0 for (_, kb, _) in slots]
                        mk = dmz2 if (inv[0] and inv[2]) else (dmz if inv[0] else dm1)
                        P.op("dve", TT(ptm[es_][:], expS[es_][:], mk[:], ALU.mult),
                             reads=[r_expS[es_], r_const], writes=[r_ptm[es_]])
                        pend.append((r, slots, es_, vi))
                        if len(pend) > LAG:
                            emit_pv(pend.pop(0))
                        gi += 1
                    while pend:
                        emit_pv(pend.pop(0))
                    for u in range(4):
                        rd = 0
                        ts_ = R * 4 + u
                        P.op("act", ACTV(rden[rd][p0:p1, :], banks[4 + u][64:128, :], AF.Ln),
                             reads=[bres[4 + u]], writes=[r_rden[rd]])
                        P.op("act", ACTV(rden[rd][p0:p1, :], rden[rd][p0:p1, :], AF.Exp, scale=-1.0),
                             reads=[], writes=[r_rden[rd]])
                        P.op("dve", TT(oT[p0:p1, pi, ts_ * 512:(ts_ + 1) * 512], banks[4 + u][0:64, :],
                                       rden[rd][p0:p1, :], ALU.mult),
                             reads=[bres[4 + u], r_rden[rd]], writes=[r_oT[pi][ts_]])
            fprev = P.fence()
            es_a.close()
        if len(pairsA) == 4:
            normalize_group(0)
        fenceGA = P.fence()
        es_ga.close()
    else:
        fenceGA = fenceA

    pairsB = [p for p in pairs if p >= 4]
    if pairsB:
        es_gb = ExitStack()
        fB = fenceGA
        V2 = P.sb("V2", [128, NB, 128], BF16, es_gb)
        r_V2 = Res(fB)
        E2 = [P.sb(f"E2_{i}", [128, 1024], BF16, es_gb) for i in range(3)]
        r_E = [Res(fB) for _ in range(3)]
        sp2 = [P.sb(f"sp2_{i}", [128, 1024], BF16, es_gb) for i in range(3)]
        r_sp = [Res(fB) for _ in range(3)]
        X2 = [P.sb(f"X2_{i}", [128, 1024], BF16, es_gb) for i in range(2)]
        r_X = [Res(fB) for _ in range(2)]
        A2 = [P.sb(f"A2_{i}", [128, 1024], BF16, es_gb) for i in range(3)]
        r_A = [Res(fB) for _ in range(3)]

        for pi in pairsB:
            load_pair_weights(pi)
            proj_fm(0, 0, 0.125, lambda ts: r_qk[ts * 4:(ts + 1) * 4])
            proj_fm(128, 1, 1.0, lambda ts: r_qk[ts * 4:(ts + 1) * 4])
            proj_fm(256, 2, 1.0, lambda ts: [r_v[ts]])
            for g8 in range(4):
                bk = 2 + g8 % 2
                for j in range(8):
                    tb = g8 * 8 + j
                    P.op("pe", TR(bT[bk][:, j * 128:(j + 1) * 128], qkvT[:, 2, tb * 128:(tb + 1) * 128], ident[:]),
                         reads=r_v + [r_const], writes=[bres[bk]], inc=(j == 7))
                P.op("dve", CP(V2[:, g8 * 8:(g8 + 1) * 8, :], bT[bk].rearrange("p (j d) -> p j d", d=128)),
                     reads=[bres[bk]], writes=[r_V2])

            steps = [(G, J) for G in range(8) for J in range(4 * G + 3, -1, -1)]
            n = len(steps)

            def lo_of(i):
                G, J = steps[i]
                return max(0, J - 4 * G) * 128

            def hv(ap, lo):
                v = ap.rearrange("p (h t) -> p h t", h=2)
                return v if lo == 0 else v[:, :, lo:512]

            def pe_z(i):
                G, J = steps[i]
                k = i % 2
                lo = lo_of(i)
                for h in range(2):
                    hs = slice(h * 64, (h + 1) * 64)
                    P.op("pe", MM(banks[2 * k + h][:, lo:512], qkvT[hs, 1, J * 128:(J + 1) * 128],
                                  qkvT[hs, 0, G * 512 + lo:(G + 1) * 512], True, True),
                         reads=r_qk, writes=[bres[2 * k + h]], inc=(h == 1))

            def act_EL(i):
                G, J = steps[i]
                k = i % 2
                e_ = i % 3
                s_ = i % 3
                lo = lo_of(i)
                P.op("act", ACTV(hv(E2[e_][:], lo), hv(pb[k][:], lo), AF.Exp),
                     reads=[bres[2 * k], bres[2 * k + 1]], writes=[r_E[e_]])
                if J >= 4 * G:
                    ev = hv(E2[e_][:], lo)
                    P.op("dve", TT(ev, ev, sbm[J - 4 * G][:, lo:512].unsqueeze(1).to_broadcast([128, 2, 512 - lo]),
                                   ALU.mult), reads=[r_const], writes=[r_E[e_]])
                P.op("act", ACTV(hv(sp2[s_][:], lo), hv(E2[e_][:], lo), AF.Ln, bias=1.0, scale=1.0),
                     reads=[r_E[e_]], writes=[r_sp[s_]])

            def pe_C(i):
                G, J = steps[i]
                first = (J == 4 * G + 3)
                s_ = i % 3
                lo = lo_of(i)
                for h in range(2):
                    P.op("pe", MM(banks[4 + h][:, lo:512], tri[:], sp2[s_][:, h * 512 + lo:(h + 1) * 512], first, first,
                                  skip_group_check=True),
                         reads=[r_sp[s_], r_const], writes=[bres[4 + h]], inc=(first and h == 1))
                    if not first:
                        sp_prev = (i - 1) % 3
                        lop = lo_of(i - 1)
                        P.op("pe", MM(banks[4 + h][:, lop:512], sl[:], sp2[sp_prev][:, h * 512 + lop:(h + 1) * 512],
                                      False, True, skip_group_check=True),
                             reads=[r_sp[sp_prev], r_const], writes=[bres[4 + h]], inc=(h == 1))

            def act_X(i):
                e_ = i % 3
                x_ = i % 2
                a_ = i % 3
                lo = lo_of(i)
                P.op("act", ACTV(hv(X2[x_][:], lo), hv(pb[2][:], lo), AF.Exp, scale=-1.0),
                     reads=[bres[4], bres[5]], writes=[r_X[x_]])
                P.op("dve", TT(hv(A2[a_][:], lo), hv(E2[e_][:], lo), hv(X2[x_][:], lo), ALU.mult),
                     reads=[r_E[e_], r_X[x_]], writes=[r_A[a_]])

            def pe_pv(i):
                G, J = steps[i]
                first = (J == 4 * G + 3)
                last = (J == 0)
                ob = 6 + G % 2
                a_ = i % 3
                lo = lo_of(i)
                for h in range(2):
                    hs = slice(h * 64, (h + 1) * 64)
                    P.op("pe", MM(banks[ob][hs, lo:512], V2[:, J, hs], A2[a_][:, h * 512 + lo:(h + 1) * 512], first, last,
                                  skip_group_check=True),
                         reads=[r_V2, r_A[a_]], writes=[bres[ob]], inc=(h == 1))
                if last:
                    P.op("dve", CP(oT[:, pi, G * 512:(G + 1) * 512], banks[ob]),
                         reads=[bres[ob]], writes=[r_oT[pi][G]])

            for i in range(n + 3):
                if i < n:
                    pe_z(i)
                if 0 <= i - 3 < n:
                    pe_pv(i - 3)
                if 0 <= i - 1 < n:
                    act_EL(i - 1)
                if 0 <= i - 2 < n:
                    act_X(i - 2)
                if 0 <= i - 1 < n:
                    pe_C(i - 1)
        if len(pairsB) == 4:
            normalize_group(4)
        fenceGB = P.fence()
        es_gb.close()

    if dbg == "oT":
        for c in range(8):
            out_tokens.append(P.dma("sp", DMA(dbg_d[:, c, :], oT[:, c, :]), reads=r_oT[c]))
    fenceATT = P.fence()
    es_att.close()

    if do_ffn:
        fF = fenceATT
        es_f = ExitStack()
        Wd = P.sb("Wd", [128, NFC, D], BF16, es_f)
        r_Wd = Res(fF)
        Wo = P.sb("Wo", [128, 8, D], BF16, es_f)
        r_Wo = Res(fF)
        x1 = P.sb("x1", [128, 4, D], F32, es_f)
        r_x1 = [Res(fF) for _ in range(4)]
        h2T = P.sb("h2T", [128, 8, 512], BF16, es_f)
        r_h2T = [Res(fF) for _ in range(4)]
        actT = P.sb("actT", [128, NFC, 512], BF16, es_f)
        r_act = [Res(fF) for _ in range(NFC)]
        ring = [P.sb(f"ring{i}", [128, 8, 256], BF16, es_f) for i in range(3)]
        r_ring = [Res(fF) for _ in range(3)]
        fnw = P.sb("fnw", [128, D], F32, es_f)
        r_fnw = Res(fF)
        sg = [P.sb(f"sg{i}", [128, 512], BF16, es_f) for i in range(2)]
        r_sg = [Res(fF), Res(fF)]
        h2 = [P.sb(f"h2{i}", [128, D], BF16, es_f) for i in range(4)]
        r_h2 = [Res(fF) for _ in range(4)]
        r_ss2b = [Res(fF) for _ in range(4)]
        junk2 = P.sb("junk2", [128, D], BF16, es_f)
        r_junk2 = Res(fF)
        ss2 = P.sb("ss2", [128, 12], F32, es_f)
        r_ss2 = Res(fF)

        P.dma("pool", DMA(Wo[:], wout_d.rearrange("(c p) d -> p c d", p=128)), writes=[r_Wo])
        for q4_ in range(2):
            P.dma("pool", DMA(Wd[:, q4_ * 11:(q4_ + 1) * 11, :],
                              wd_d[q4_ * 11 * 128:(q4_ + 1) * 11 * 128, :].rearrange("(f p) d -> p f d", p=128)),
                  writes=[r_Wd])
        P.dma("sp", DMA(fnw[:], fnw_d), writes=[r_fnw])

        ydr = [0]

        def yd_bank():
            b = ydr[0] % 4
            ydr[0] += 1
            return b

        def step1_block(tt, b4):
            tbk = tt * 4 + b4
            tsl = slice(tbk * 128, (tbk + 1) * 128)
            P.dma("sp", DMA(x1[:, b4, :], x_d[tsl, :]), writes=[r_x1[b4]])
            for half in range(2):
                bk = yd_bank()
                hsl = slice(half * 512, (half + 1) * 512)
                for c in range(8):
                    P.op("pe", MM(banks[bk], oT[:, c, tsl], Wo[:, c, hsl], c == 0, c == 7),
                         reads=[r_Wo] + [r_oT[cc][tt] for cc in range(8)], writes=[bres[bk]], inc=(c == 7))
                P.op("dve", TT(x1[:, b4, hsl], banks[bk], x1[:, b4, hsl], ALU.add),
                     reads=[bres[bk]], writes=[r_x1[b4]])
            P.op("act", ACTV(junk2[:], x1[:, b4, :], AF.Square, accum_out=ss2[:, b4:b4 + 1]),
                 reads=[r_x1[b4]], writes=[r_junk2, r_ss2b[b4]])
            P.op("act", ACTV(ss2[:, 4 + b4:5 + b4], ss2[:, b4:b4 + 1], AF.Sqrt, scale=1.0 / D, bias=EPS),
                 reads=[r_ss2b[b4]], writes=[r_ss2b[b4]])
            P.op("dve", RCP(ss2[:, 8 + b4:9 + b4], ss2[:, 4 + b4:5 + b4]), reads=[r_ss2b[b4]], writes=[r_ss2b[b4]])
            P.op("dve", STT(h2[b4][:], x1[:, b4, :], ss2[:, 8 + b4:9 + b4], fnw[:], ALU.mult, ALU.mult),
                 reads=[r_x1[b4], r_ss2b[b4], r_fnw], writes=[r_h2[b4]])

        def step1_tr(b4):
            bk = yd_bank()
            for c in range(8):
                csl = slice(c * 128, (c + 1) * 128)
                P.op("pe", TR(bT[bk][:, csl], h2[b4][:, csl], ident[:]),
                     reads=[r_h2[b4], r_const], writes=[bres[bk]], inc=(c == 7))
            P.op("act", ACTV(h2T[:, :, b4 * 128:(b4 + 1) * 128], bT[bk].rearrange("p (c t) -> p c t", t=128), AF.Copy),
                 reads=[bres[bk]], writes=[r_h2T[b4]])

        def finish_norm(tt):
            pass

        gur = 0
        for b4 in range(4):
            step1_block(0, b4)
            if b4 >= 1:
                step1_tr(b4 - 1)
        step1_tr(3)
        for tt in range(8):
            finish_norm(tt)
            for fc in range(NFC):
                rg = fc % 3
                P.dma("pool", DMA(ring[rg][:], wgu_bf[fc].rearrange("p (c j) -> p c j", j=256)),
                      reads=[r_wgubf[fc]], writes=[r_ring[rg]])
                bg = 4 + (gur % 2) * 2
                bu = bg + 1
                gur += 1
                for (bk, off) in ((bg, 0), (bu, 128)):
                    for c in range(8):
                        P.op("pe", MM(banks[bk], ring[rg][:, c, off:off + 128], h2T[:, c, :], c == 0, c == 7),
                             reads=[r_ring[rg]] + r_h2T, writes=[bres[bk]], inc=(c == 7))
                s_ = fc % 2
                P.op("act", ACTV(sg[s_][:], banks[bg], AF.Silu), reads=[bres[bg]], writes=[r_sg[s_]])
                P.op("dve", TT(actT[:, fc, :], banks[bu], sg[s_][:], ALU.mult),
                     reads=[bres[bu], r_sg[s_]], writes=[r_act[fc]])
            for b4 in range(4):
                tbk = tt * 4 + b4
                for half in range(2):
                    bk = yd_bank()
                    hsl = slice(half * 512, (half + 1) * 512)
                    for fc in range(NFC):
                        P.op("pe", MM(banks[bk], actT[:, fc, b4 * 128:(b4 + 1) * 128], Wd[:, fc, hsl],
                                      fc == 0, fc == NFC - 1),
                             reads=[r_act[fc], r_Wd], writes=[bres[bk]], inc=(fc == NFC - 1))
                    P.op("dve", TT(x1[:, b4, hsl], banks[bk], x1[:, b4, hsl], ALU.add),
                         reads=[bres[bk]], writes=[r_x1[b4]])
                out_tokens.append(P.dma("sp", DMA(out_d[tbk * 128:(tbk + 1) * 128, :], x1[:, b4, :]),
                                        reads=[r_x1[b4]]))
                if tt + 1 < 8:
                    step1_block(tt + 1, b4)
                    if b4 >= 1:
                        step1_tr(b4 - 1)
            if tt + 1 < 8:
                step1_tr(3)
        es_f.close()

    P.wait("sp", out_tokens)
    P.emit()
    P.close()
    return nc


def make_inputs(x, attn_norm_w, w_in, q_norm_w, k_norm_w, dil_out_norm_w, sb_out_norm_w,
                w_out, ffn_norm_w, w_gate, w_up, w_down):
    f = np.float32
    w_in = np.asarray(w_in, f)[0]
    pairs = []
    for g in range(2):
        o0 = g * 1536
        for p in range(4):
            cols = [w_in[:, o0 + j * 512 + p * 128: o0 + j * 512 + (p + 1) * 128] for j in range(3)]
            pairs.append(np.concatenate(cols, axis=1))
    w_pairs = np.ascontiguousarray(np.stack(pairs, 0))
    wg = np.asarray(w_gate, f)[0].reshape(8, 128, NFC, 128)
    wu = np.asarray(w_up, f)[0].reshape(8, 128, NFC, 128)
    wgu = np.concatenate([wg, wu], axis=3)
    wgu = np.ascontiguousarray(wgu.transpose(2, 1, 0, 3)).reshape(NFC, 128, 2048)
    qw = np.asarray(q_norm_w, f)[0]
    kw = np.asarray(k_norm_w, f)[0]
    qkw = np.concatenate([qw, qw, kw, kw])[None, :]
    wcat = np.concatenate([np.asarray(dil_out_norm_w, f)[0], np.asarray(sb_out_norm_w, f)[0]])
    pos = np.arange(S, dtype=f)
    inv = (f(10000.0) ** (-np.arange(0, 64, 2, dtype=f) / f(64))).astype(f)
    ang = (pos[:, None] * inv[None, :]).astype(f)
    rope = np.concatenate([np.cos(ang), np.sin(ang), -np.sin(ang)], axis=1).astype(f)
    shared = {
        "anw_bc": np.ascontiguousarray(np.broadcast_to(np.asarray(attn_norm_w, f)[0][None, :], (128, D))),
        "fnw_bc": np.ascontiguousarray(np.broadcast_to(np.asarray(ffn_norm_w, f)[0][None, :], (128, D))),
        "qkw_bc": np.ascontiguousarray(np.broadcast_to(qkw, (128, 256))),
        "wcol": np.ascontiguousarray(wcat.reshape(8, 128).T),
        "rope": rope,
        "w_pairs": w_pairs,
        "w_out": np.ascontiguousarray(np.asarray(w_out, f)[0]),
        "wgu": wgu,
        "w_down": np.ascontiguousarray(np.asarray(w_down, f)[0]),
    }
    xs = np.asarray(x, f)
    return [dict(shared, x=np.ascontiguousarray(xs[b])) for b in range(xs.shape[0])]


_NC_CACHE = {}


def kernel(**inputs):
    in_maps = make_inputs(**inputs)
    if "nc" not in _NC_CACHE:
        _NC_CACHE["nc"] = build()
    nc = _NC_CACHE["nc"]
    res = run_bass_kernel_spmd(nc, in_maps, core_ids=list(range(8)))
    return np.stack([np.asarray(r["out"], np.float32) for r in res.results], axis=0)
```

```python
from contextlib import ExitStack
import numpy as np
import concourse.bass as bass
import concourse.mybir as mybir
from concourse.bass_utils import run_bass_kernel_spmd

F32 = mybir.dt.float32
BF16 = mybir.dt.bfloat16
AF = mybir.ActivationFunctionType
ALU = mybir.AluOpType
AX = mybir.AxisListType

S = 4096
D = 1024
NB = S // 128
DFF = 2816
NFC = DFF // 128
EPS = 1e-6
ENGS = ("pe", "act", "dve", "pool", "sp")
NDMASEM = 8


class Res:
    __slots__ = ("w", "r", "pend")

    def __init__(self, init=None):
        self.w = dict(init) if init else {}
        self.r = {}
        self.pend = None


def _merge(dst, src):
    for k, v in src.items():
        if dst.get(k, 0) < v:
            dst[k] = v


class Prog:
    def __init__(self, nc):
        self.nc = nc
        self.es = ExitStack()
        self.ops = {e: [] for e in ENGS}
        self.sem = {}
        self.cnt = {}
        self.seen = {e: {} for e in ENGS}
        self.pending = {e: [] for e in ENGS}
        for e in ENGS:
            self.newsem("E_" + e)
        self.dq = {}
        for q in ("sp", "pool"):
            self.dq[q] = {"keys": [self.newsem(f"D_{q}{i}") for i in range(NDMASEM)], "i": 0}

    def newsem(self, key):
        self.sem[key] = self.es.enter_context(self.nc.semaphore(key))
        self.cnt[key] = 0
        return key

    def sb(self, name, shape, dt, es=None):
        return (es or self.es).enter_context(self.nc.sbuf_tensor(name, list(shape), dt))

    def ps(self, name, shape, dt):
        return self.es.enter_context(self.nc.psum_tensor(name, list(shape), dt))

    def fence(self):
        f = {}
        for k, v in self.cnt.items():
            if v > 0:
                f[k] = v
        for e in ENGS:
            assert not self.pending[e], "fence with pending un-tokened ops"
        return f

    def _collect(self, eng, reads, writes, deps):
        need = {}
        for r in reads:
            assert r.pend is None or r.pend == eng, "resource pending on another engine"
            _merge(need, r.w)
        for r in writes:
            assert r.pend is None or r.pend == eng, "resource pending on another engine"
            _merge(need, r.w)
            _merge(need, r.r)
        for d in deps:
            if d:
                _merge(need, d)
        ws = []
        seen = self.seen[eng]
        for k, v in need.items():
            if k == "E_pe" and eng == "pe":
                continue
            if seen.get(k, 0) >= v:
                continue
            seen[k] = v
            ws.append((k, v))
        return ws

    def _commit(self, eng, tok, reads, writes):
        allp = self.pending[eng] + [(reads, writes)]
        self.pending[eng] = []
        for rs, wsx in allp:
            for r in rs:
                _merge(r.r, tok)
                r.pend = None
            for r in wsx:
                r.w = dict(tok)
                r.r = {}
                r.pend = None

    def op(self, eng, fn, reads=(), writes=(), inc=True, deps=()):
        ws = self._collect(eng, reads, writes, deps)
        if inc:
            key = "E_" + eng
            self.cnt[key] += 1
            tok = {key: self.cnt[key]}
            self.ops[eng].append((ws, fn, (key, 1)))
            self._commit(eng, tok, reads, writes)
            return tok
        self.ops[eng].append((ws, fn, None))
        self.pending[eng].append((reads, writes))
        for r in list(reads) + list(writes):
            r.pend = eng
        return None

    def dma(self, q, fn, reads=(), writes=(), deps=()):
        assert not self.pending[q]
        dq = self.dq[q]
        key = dq["keys"][dq["i"] % NDMASEM]
        dq["i"] += 1
        prev = {key: self.cnt[key]} if self.cnt[key] else None
        ws = self._collect(q, reads, writes, list(deps) + [prev])
        self.cnt[key] += 16
        tok = {key: self.cnt[key]}
        self.ops[q].append((ws, fn, (key, 16)))
        for r in reads:
            _merge(r.r, tok)
        for r in writes:
            r.w = dict(tok)
            r.r = {}
        return tok

    def wait(self, eng, deps):
        ws = self._collect(eng, (), (), deps)
        if ws:
            self.ops[eng].append((ws, None, None))

    def emit(self):
        prog = self
        for e in ENGS:
            assert not self.pending[e], f"pending ops on {e}"

        def run(name, e):
            for ws, fn, inc in prog.ops[name]:
                for key, val in ws:
                    e.wait_ge(prog.sem[key], val)
                if fn is None:
                    continue
                ins = fn(e)
                if inc is not None:
                    ins.then_inc(prog.sem[inc[0]], inc[1])

        with self.nc.Block() as block:
            @block.tensor
            def _(e):
                run("pe", e)

            @block.scalar
            def _(e):
                run("act", e)

            @block.vector
            def _(e):
                run("dve", e)

            @block.gpsimd
            def _(e):
                run("pool", e)

            @block.sync
            def _(e):
                run("sp", e)

    def close(self):
        self.es.close()


def MM(out, lhsT, rhs, start, stop, **kw):
    return lambda e: e.matmul(out, lhsT=lhsT, rhs=rhs, start=start, stop=stop, **kw)


def TR(out, in_, idn):
    return lambda e: e.transpose(out, in_, idn)


def ACTV(out, in_, func, **kw):
    return lambda e: e.activation(out=out, in_=in_, func=func, **kw)


def TT(out, in0, in1, op):
    return lambda e: e.tensor_tensor(out=out, in0=in0, in1=in1, op=op)


def TS(out, in0, s1, op0):
    return lambda e: e.tensor_scalar(out=out, in0=in0, scalar1=s1, scalar2=None, op0=op0)


def STT(out, in0, scalar, in1, op0, op1):
    return lambda e: e.scalar_tensor_tensor(out=out, in0=in0, scalar=scalar, in1=in1, op0=op0, op1=op1)


def CP(out, in_):
    return lambda e: e.tensor_copy(out=out, in_=in_)


def RCP(out, in_):
    return lambda e: e.reciprocal(out=out, in_=in_)


def RSUM(out, in_):
    return lambda e: e.reduce_sum(out=out, in_=in_, axis=AX.X)


def DMA(out, in_):
    return lambda e: e.dma_start(out=out, in_=in_)


def MSET(ap, v):
    return lambda e: e.memset(ap, v)

PA_BLOCKS = NB
KNOB = {}


def build(pairs=tuple(range(8)), do_ffn=True, dbg=None):
    nc = bass.Bass("TRN2", target_bir_lowering=False)

    def din(name, shape, dt=F32):
        return nc.dram_tensor(name, list(shape), dt, kind="ExternalInput").ap()

    x_d = din("x", [S, D])
    anw_d = din("anw_bc", [128, D])
    fnw_d = din("fnw_bc", [128, D])
    qkw_d = din("qkw_bc", [128, 256])
    wcol_d = din("wcol", [128, 8])
    rope_d = din("rope", [S, 96])
    wp_d = din("w_pairs", [8, D, 384])
    wout_d = din("w_out", [D, D])
    wgu_d = din("wgu", [NFC, 128, 2048])
    wd_d = din("w_down", [DFF, D])
    out_d = nc.dram_tensor("out", [S, D], F32, kind="ExternalOutput").ap()
    dbg_d = None
    if dbg in ("hT", "oT"):
        dbg_d = nc.dram_tensor("dbg", [128, 8, S], BF16, kind="ExternalOutput").ap()

    P = Prog(nc)
    out_tokens = []

    pb = [P.ps(f"pb{i}", [128, 1024], F32) for i in range(4)]
    banks = []
    for i in range(4):
        banks += [pb[i][:, 0:512], pb[i][:, 512:1024]]
    bres = [Res() for _ in range(8)]
    bT = [banks[i].bitcast(BF16) for i in range(8)]

    ident = P.sb("ident", [128, 128], BF16)
    tri = P.sb("tri", [128, 128], BF16)
    sl = P.sb("sl", [128, 128], BF16)
    ones = P.sb("ones", [128, 128], BF16)
    sbm = [P.sb(f"sbm{j}", [128, 512], BF16) for j in range(4)]
    dm1 = P.sb("dm1", [128, 512], BF16)
    dmz = P.sb("dmz", [128, 512], BF16)
    dmz2 = P.sb("dmz2", [128, 512], BF16)
    wcol = P.sb("wcol_sb", [128, 8], F32)
    oT = P.sb("oT", [128, 8, S], BF16)
    r_const = Res()
    r_oT = [[Res() for _ in range(8)] for _ in range(8)]

    def SEL(t_ap, pattern, base, cm, cmp):
        return lambda e: e.affine_select(out=t_ap, in_=t_ap, pattern=pattern, base=base,
                                         channel_multiplier=cm, compare_op=cmp, fill=0.0)

    for t in (ident, tri, sl, ones, dm1, dmz, dmz2, *sbm):
        P.op("pool", MSET(t[:], 1.0), writes=[r_const])
    P.op("pool", SEL(ident[:], [[-1, 128]], 0, 1, ALU.is_equal), writes=[r_const])
    P.op("pool", SEL(tri[:], [[-1, 128]], 0, 1, ALU.is_ge), writes=[r_const])
    P.op("pool", SEL(sl[:], [[1, 128]], 0, -1, ALU.is_gt), writes=[r_const])
    for j in range(4):
        P.op("pool", SEL(sbm[j][:], [[1, 512]], -128 * j, -1, ALU.is_gt), writes=[r_const])
    for m in (dm1, dmz, dmz2):
        for slot in range(4):
            ap = m[:, slot * 128:(slot + 1) * 128]
            if slot % 2 == 0:
                P.op("pool", SEL(ap, [[-1, 128]], 0, 1, ALU.is_ge), writes=[r_const])
            else:
                P.op("pool", SEL(ap, [[1, 128]], 0, -1, ALU.is_ge), writes=[r_const])
    P.op("pool", MSET(dmz[:, 0:128], 0.0), writes=[r_const])
    P.op("pool", MSET(dmz2[:, 0:128], 0.0), writes=[r_const])
    P.op("pool", MSET(dmz2[:, 256:384], 0.0), writes=[r_const])
    P.dma("sp", DMA(wcol[:], wcol_d), writes=[r_const])
    wgu_bf = nc.dram_tensor("wgu_bf", [NFC, 128, 2048], BF16, kind="Internal").ap()
    r_wgubf = [Res() for _ in range(NFC)]

    es_att = ExitStack()
    hT = P.sb("hT", [128, 8, S], BF16, es_att)
    r_hT = [Res() for _ in range(NB)]
    Wp = P.sb("Wp", [128, 8, 384], BF16, es_att)
    r_Wp = Res()
    qkvT = P.sb("qkvT", [128, 3, S], BF16, es_att)
    r_qk = [Res() for _ in range(NB)]
    r_v = [Res() for _ in range(8)]
    sq = [P.sb(f"sq{i}", [128, 512], BF16, es_att) for i in range(4)]
    r_sq = [Res() for _ in range(4)]
    rs = P.sb("rs", [128, 512], F32, es_att)
    r_rs = Res()

    es_pa = ExitStack()
    xs = [P.sb(f"xs{i}", [128, D], F32, es_pa)[:] for i in range(4)]
    for j in range(3):
        v32 = qkvT[:, j, :].bitcast(F32)
        xs += [v32[:, 0:D], v32[:, D:2 * D]]
    NX = len(xs)
    r_xs = [Res() for _ in range(NX)]
    hn = [P.sb(f"hn{i}", [128, D], BF16, es_pa) for i in range(4)]
    r_hn = [Res() for _ in range(4)]
    junk = P.sb("junkA", [128, D], BF16, es_pa)
    r_junk = Res()
    anw = P.sb("anw", [128, D], F32, es_pa)
    r_anw = Res()
    ssA = P.sb("ssA", [128, NB], F32, es_pa)
    sdA = P.sb("sdA", [128, NB], F32, es_pa)
    rsA = P.sb("rsA", [128, NB], F32, es_pa)
    r_ssA = [Res() for _ in range(NB)]
    P.dma("sp", DMA(anw[:], anw_d), writes=[r_anw])
    BS = 2

    def paA(sb):
        for b in range(BS):
            tb = sb * BS + b
            xi = tb % NX
            tsl = slice(tb * 128, (tb + 1) * 128)
            P.dma("sp", DMA(xs[xi], x_d[tsl, :]), writes=[r_xs[xi]])
            P.op("act", ACTV(junk[:], xs[xi], AF.Square, accum_out=ssA[:, tb:tb + 1]),
                 reads=[r_xs[xi]], writes=[r_junk, r_ssA[sb]])
        c4 = slice(sb * BS, sb * BS + BS)
        P.op("act", ACTV(sdA[:, c4], ssA[:, c4], AF.Sqrt, scale=1.0 / D, bias=EPS), reads=[r_ssA[sb]], writes=[r_ssA[sb]])
        P.op("dve", RCP(rsA[:, c4], sdA[:, c4]), reads=[r_ssA[sb]], writes=[r_ssA[sb]])

    def paB(sb):
        for b in range(BS):
            tb = sb * BS + b
            xi = tb % NX
            hi = tb % 4
            tsl = slice(tb * 128, (tb + 1) * 128)
            bk = tb % 4
            P.op("dve", STT(hn[hi][:], xs[xi], rsA[:, tb:tb + 1], anw[:], ALU.mult, ALU.mult),
                 reads=[r_xs[xi], r_ssA[sb], r_anw], writes=[r_hn[hi]])
            for c in range(8):
                csl = slice(c * 128, (c + 1) * 128)
                P.op("pe", TR(bT[bk][:, csl], hn[hi][:, csl], ident[:]),
                     reads=[r_hn[hi], r_const], writes=[bres[bk]], inc=(c == 7))
            if b % 2 == 0:
                P.op("act", ACTV(hT[:, :, tsl], bT[bk].rearrange("p (c t) -> p c t", t=128), AF.Copy),
                     reads=[bres[bk]], writes=[r_hT[tb]])
            else:
                P.op("dve", CP(hT[:, :, tsl], bT[bk].rearrange("p (c t) -> p c t", t=128)),
                     reads=[bres[bk]], writes=[r_hT[tb]])

    nsb = PA_BLOCKS // BS
    for sb in range(nsb + 1):
        if sb < nsb:
            paA(sb)
        if sb >= 1:
            paB(sb - 1)
    fenceA = P.fence()
    es_pa.close()
    for r_ in r_qk + r_v:
        _merge(r_.w, fenceA)
    casts_pending = [do_ffn]

    if dbg == "hT":
        for c in range(8):
            out_tokens.append(P.dma("sp", DMA(dbg_d[:, c, :], hT[:, c, :]), reads=r_hT))

    def proj_fm(col0, dst_idx, scale, res_for_slice):
        for ts in range(8):
            bk = ts % 2
            tsl = slice(ts * 512, (ts + 1) * 512)
            for c in range(8):
                P.op("pe", MM(banks[bk], Wp[:, c, col0:col0 + 128], hT[:, c, tsl], c == 0, c == 7),
                     reads=[r_Wp] + r_hT[ts * 4:(ts + 1) * 4], writes=[bres[bk]], inc=(c == 7))
            wr = res_for_slice(ts)
            if ts % 2 == 0:
                P.op("act", ACTV(qkvT[:, dst_idx, tsl], banks[bk], AF.Copy, scale=scale),
                     reads=[bres[bk]], writes=wr)
            else:
                P.op("dve", TS(qkvT[:, dst_idx, tsl], banks[bk], scale, ALU.mult),
                     reads=[bres[bk]], writes=wr)

    def load_pair_weights(pi):
        P.dma("pool", DMA(Wp[:], wp_d[pi].rearrange("(c p) e -> p c e", p=128)), writes=[r_Wp])

    def normalize_group(gc):
        for ts in range(8):
            tsl = slice(ts * 512, (ts + 1) * 512)
            for c in range(4):
                if c < 3:
                    P.op("act", ACTV(sq[c][:], oT[:, gc + c, tsl], AF.Square),
                         reads=[r_oT[gc + c][ts]], writes=[r_sq[c]])
                else:
                    P.op("pool", TT(sq[c][:], oT[:, gc + c, tsl], oT[:, gc + c, tsl], ALU.mult),
                         reads=[r_oT[gc + c][ts]], writes=[r_sq[c]])
            bk = 2 + ts % 2
            for c in range(4):
                P.op("pe", MM(banks[bk], ones[:], sq[c][:], c == 0, c == 3),
                     reads=[r_sq[c], r_const], writes=[bres[bk]], inc=(c == 3))
            P.op("act", ACTV(rs[:], banks[bk], AF.Sqrt, scale=1.0 / 512, bias=EPS),
                 reads=[bres[bk]], writes=[r_rs])
            P.op("dve", RCP(rs[:], rs[:]), reads=[r_rs], writes=[r_rs])
            for c in range(4):
                P.op("dve", STT(oT[:, gc + c, tsl], oT[:, gc + c, tsl], wcol[:, gc + c:gc + c + 1], rs[:],
                                ALU.mult, ALU.mult),
                     reads=[r_rs, r_const], writes=[r_oT[gc + c][ts]])

    LAG = 3
    NSL = 4
    pairsA = [p for p in pairs if p < 4]
    bank_rr = [0]

    def next_bank():
        b = bank_rr[0] % 4
        bank_rr[0] += 1
        return b

    if pairsA:
        es_ga = ExitStack()
        qkw = P.sb("qkw", [128, 256], F32, es_ga)
        r_qkw = Res(fenceA)
        P.dma("sp", DMA(qkw[:], qkw_d), writes=[r_qkw])
        P.op("dve", TS(qkw[:, 0:128], qkw[:, 0:128], 0.125, ALU.mult), writes=[r_qkw])
        fprev = fenceA
        for pi in pairsA:
            load_pair_weights(pi)
            es_p = ExitStack()
            NS = 4
            sa = [P.sb(f"sa{pi}_{i}", [128, 512], F32, es_p) for i in range(NS)]
            qn = [P.sb(f"qn{pi}_{i}", [128, 512], F32, es_p) for i in range(NS)]
            t2 = [P.sb(f"t2{pi}_{i}", [128, 512], F32, es_p) for i in range(NS)]
            qr = [P.sb(f"qr{pi}_{i}", [128, 512], BF16, es_p) for i in range(NS + 1)]
            st = [P.sb(f"st{pi}_{i}", [128, 24], F32, es_p) for i in range(NS)]
            ropeb = [P.sb(f"ropeb{pi}_{i}", [128, 2, 96], F32, es_p) for i in range(NS)]
            r_sa = [Res(fprev) for _ in range(NS)]
            r_qn = [Res(fprev) for _ in range(NS)]
            r_t2 = [Res(fprev) for _ in range(NS)]
            r_qr = [Res(fprev) for _ in range(NS + 1)]
            r_st = [Res(fprev) for _ in range(NS)]
            r_rope = [Res(fprev) for _ in range(NS)]
            w3 = lambda ap: ap.rearrange("p (s d) -> p s d", d=64)
            w4 = lambda ap: ap.rearrange("p (b s d) -> p b s d", b=2, d=32)
            w5 = lambda ap: ap.rearrange("p (b s h d) -> p b s h d", b=2, h=2, d=32)
            wb = lambda ap: ap.rearrange("p (b e) -> p b e", e=256)
            NBB = NB // 2

            def st0(bb):
                k = bb % NS
                tb0 = bb * 2
                bk = bb % 2
                P.dma("sp", DMA(ropeb[k][:], rope_d[tb0 * 128:(tb0 + 2) * 128, :].rearrange("(b p) d -> p b d", p=128)),
                      writes=[r_rope[k]])
                for b in range(2):
                    tsl = slice((tb0 + b) * 128, (tb0 + b + 1) * 128)
                    for c in range(8):
                        P.op("pe", MM(banks[bk][:, b * 256:(b + 1) * 256], hT[:, c, tsl], Wp[:, c, 0:256], c == 0, c == 7),
                             reads=[r_Wp, r_hT[tb0 + b]], writes=[bres[bk]], inc=(b == 1 and c == 7))

            def st1a(bb):
                k = bb % NS
                bk = bb % 2
                P.op("act", ACTV(sa[k][:], banks[bk], AF.Square), reads=[bres[bk]], writes=[r_sa[k]])
                P.op("dve", RSUM(st[k][:, 0:8], w3(sa[k][:])), reads=[r_sa[k]], writes=[r_st[k]])
                P.op("act", ACTV(st[k][:, 8:16], st[k][:, 0:8], AF.Sqrt, scale=1.0 / 64, bias=EPS),
                     reads=[r_st[k]], writes=[r_st[k]])

            def st1b(bb):
                k = bb % NS
                bk = bb % 2
                P.op("dve", RCP(st[k][:, 16:24], st[k][:, 8:16]), reads=[r_st[k]], writes=[r_st[k]])
                P.op("dve", TT(w3(qn[k][:]), w3(banks[bk]), st[k][:, 16:24].unsqueeze(2).to_broadcast([128, 8, 64]),
                               ALU.mult), reads=[bres[bk], r_st[k]], writes=[r_qn[k]])
                P.op("pool", TT(wb(qn[k][:]), wb(qn[k][:]), qkw[:].unsqueeze(1).to_broadcast([128, 2, 256]), ALU.mult),
                     reads=[r_qkw], writes=[r_qn[k]])

            def st2(bb):
                k = bb % NS
                cosb = ropeb[k][:, :, 0:32].unsqueeze(2).to_broadcast([128, 2, 8, 32])
                sinb = ropeb[k][:, :, 32:64].unsqueeze(2).to_broadcast([128, 2, 4, 32])
                nsinb = ropeb[k][:, :, 64:96].unsqueeze(2).to_broadcast([128, 2, 4, 32])
                P.op("dve", TT(w4(sa[k][:]), w4(qn[k][:]), cosb, ALU.mult),
                     reads=[r_qn[k], r_rope[k]], writes=[r_sa[k]])
                P.op("pool", TT(w5(t2[k][:])[:, :, :, 0, :], w5(qn[k][:])[:, :, :, 1, :], nsinb, ALU.mult),
                     reads=[r_qn[k], r_rope[k]], writes=[r_t2[k]])
                P.op("pool", TT(w5(t2[k][:])[:, :, :, 1, :], w5(qn[k][:])[:, :, :, 0, :], sinb, ALU.mult),
                     reads=[r_qn[k], r_rope[k]], writes=[r_t2[k]])

            def st3a(bb):
                k = bb % NS
                kq = bb % (NS + 1)
                P.op("dve", TT(qr[kq][:], sa[k][:], t2[k][:], ALU.add), reads=[r_sa[k], r_t2[k]], writes=[r_qr[kq]])

            def st3b(bb):
                kq = bb % (NS + 1)
                tb0 = bb * 2
                t2sl = slice(tb0 * 128, (tb0 + 2) * 128)
                bk2 = 2 + bb % 2
                for j in range(4):
                    jsl = slice(j * 128, (j + 1) * 128)
                    P.op("pe", TR(bT[bk2][:, jsl], qr[kq][:, jsl], ident[:]),
                         reads=[r_qr[kq], r_const], writes=[bres[bk2]], inc=(j == 3))
                P.op("act", ACTV(qkvT[:, 0:2, t2sl].rearrange("p j (b t) -> p j b t", t=128),
                                 bT[bk2][:, 0:512].rearrange("p (b j t) -> p j b t", j=2, t=128), AF.Copy),
                     reads=[bres[bk2]], writes=[r_qk[tb0], r_qk[tb0 + 1]])

            for it in range(NBB + 4):
                if it < NBB:
                    st0(it)
                if 0 <= it - 1 < NBB:
                    st1a(it - 1)
                if 0 <= it - 2 < NBB:
                    st2(it - 2)
                if 0 <= it - 1 < NBB:
                    st1b(it - 1)
                if 0 <= it - 3 < NBB:
                    st3a(it - 3)
                if 0 <= it - 4 < NBB:
                    st3b(it - 4)
            proj_fm(256, 2, 1.0, lambda ts: [r_v[ts]])
            fmid = P.fence()
            es_p.close()

            es_a = ExitStack()
            vaug = [P.sb(f"vaug{pi}_{i}", [128, NB, 128], BF16, es_a) for i in range(3)]
            r_vaug = [Res(fmid) for _ in range(3)]
            expS = [P.sb(f"expS{pi}_{i}", [128, 512], BF16, es_a) for i in range(NSL)]
            r_expS = [Res(fmid) for _ in range(NSL)]
            ptm = [P.sb(f"ptm{pi}_{i}", [128, 512], BF16, es_a) for i in range(NSL)]
            r_ptm = [Res(fmid) for _ in range(NSL)]
            rden = [P.sb(f"rden{pi}_{i}", [128, 512], F32, es_a) for i in range(1)]
            r_rden = [Res(fmid)]
            for i in range(3):
                P.op("pool", MSET(vaug[i][:, :, 64:128], 1.0), writes=[r_vaug[i]])
            if casts_pending[0]:
                casts_pending[0] = False
                for fc in range(NFC):
                    P.dma("pool", DMA(wgu_bf[fc], wgu_d[fc]), writes=[r_wgubf[fc]], deps=[fmid])
            VIDX = {1: 0, 4: 1, 16: 2}

            vi = 0
            gi = 0
            for hp in range(2):
                p0, p1 = hp * 64, hp * 64 + 64
                for R in range(2):
                    started = [False] * 4
                    groups = []
                    for r in (1, 4, 16):
                        nb = NB // r
                        if r == 16:
                            for c in range(0, 16, 2):
                                groups.append((r, [(c, R - 1, R), (c, R, R), (c + 1, R - 1, R), (c + 1, R, R)]))
                        else:
                            per = nb // 2
                            for c in range(r):
                                for n in range(R * per, (R + 1) * per, 2):
                                    groups.append((r, [(c, n - 1, n), (c, n, n), (c, n, n + 1), (c, n + 1, n + 1)]))

                    def build_vaug(r, vi_):
                        nb = NB // r
                        for g8 in range(4):
                            bk = next_bank()
                            for j in range(8):
                                blk = g8 * 8 + j
                                c, n = blk // nb, blk % nb
                                st_ = n * 128 * r + c
                                P.op("pe", TR(bT[bk][:, j * 64:(j + 1) * 64], qkvT[p0:p1, 2, st_:st_ + 127 * r + 1:r],
                                              ident[p0:p1, p0:p1]),
                                     reads=r_v + [r_const], writes=[bres[bk]], inc=(j == 7))
                            P.op("dve", CP(vaug[vi_][:, g8 * 8:(g8 + 1) * 8, 0:64],
                                           bT[bk][:, 0:512].rearrange("p (j d) -> p j d", d=64)),
                                 reads=[bres[bk]], writes=[r_vaug[vi_]])

                    def emit_pv(item):
                        r, slots, pslot, vi_ = item
                        nb = NB // r
                        mms = []
                        for si, (c, kb, qb) in enumerate(slots):
                            if kb < 0:
                                continue
                            vblk = c * nb + kb
                            if r == 16:
                                for u in range(4):
                                    stt = not started[u]
                                    started[u] = True
                                    q0 = si * 128 + u * 32
                                    mms.append((MM(banks[4 + u][:, c:512:16], vaug[vi_][:, vblk, :],
                                                   ptm[pslot][:, q0:q0 + 32], stt, True, skip_group_check=True), u))
                            else:
                                tok0 = qb * 128 * r + c - R * 2048
                                u = tok0 // 512
                                lo = tok0 - u * 512
                                stt = not started[u]
                                started[u] = True
                                mms.append((MM(banks[4 + u][:, lo:lo + 127 * r + 1:r], vaug[vi_][:, vblk, :],
                                               ptm[pslot][:, si * 128:(si + 1) * 128], stt, True,
                                               skip_group_check=True), u))
                        for kk, (fn, u) in enumerate(mms):
                            P.op("pe", fn, reads=[r_vaug[vi_], r_ptm[pslot]], writes=[bres[4 + u]],
                                 inc=(kk == len(mms) - 1))

                    pend = []
                    if R == 0:
                        for r_ in (1, 4, 16):
                            build_vaug(r_, VIDX[r_])
                    for (r, slots) in groups:
                        vi = VIDX[r]
                        bk = next_bank()
                        es_ = gi % NSL
                        for si, (c, kb, qb) in enumerate(slots):
                            ks = max(kb, 0) * 128 * r + c
                            qs = qb * 128 * r + c
                            P.op("pe", MM(banks[bk][:, si * 128:(si + 1) * 128],
                                          qkvT[p0:p1, 1, ks:ks + 127 * r + 1:r], qkvT[p0:p1, 0, qs:qs + 127 * r + 1:r],
                                          True, True),
                                 reads=r_qk, writes=[bres[bk]], inc=(si == 3))
                        P.op("act", ACTV(expS[es_][:], banks[bk], AF.Exp), reads=[bres[bk]], writes=[r_expS[es_]])
                        inv = [kb < 0 for (_, kb, _) in slots]
                        mk = dmz2 if (inv[0] and inv[2]) else (dmz if inv[0] else dm1)
                        P.op("dve", TT(ptm[es_][:], expS[es_][:], mk[:], ALU.mult),
                             reads=[r_expS[es_], r_const], writes=[r_ptm[es_]])
                        pend.append((r, slots, es_, vi))
                        if len(pend) > LAG:
                            emit_pv(pend.pop(0))
                        gi += 1
                    while pend:
                        emit_pv(pend.pop(0))
                    for u in range(4):
                        rd = 0
                        ts_ = R * 4 + u
                        P.op("act", ACTV(rden[rd][p0:p1, :], banks[4 + u][64:128, :], AF.Ln),
                             reads=[bres[4 + u]], writes=[r_rden[rd]])
                        P.op("act", ACTV(rden[rd][p0:p1, :], rden[rd][p0:p1, :], AF.Exp, scale=-1.0),
                             reads=[], writes=[r_rden[rd]])
                        P.op("dve", TT(oT[p0:p1, pi, ts_ * 512:(ts_ + 1) * 512], banks[4 + u][0:64, :],
                                       rden[rd][p0:p1, :], ALU.mult),
                             reads=[bres[4 + u], r_rden[rd]], writes=[r_oT[pi][ts_]])
            fprev = P.fence()
            es_a.close()
        if len(pairsA) == 4:
            normalize_group(0)
        fenceGA = P.fence()
        es_ga.close()
    else:
        fenceGA = fenceA

    pairsB = [p for p in pairs if p >= 4]
    if pairsB:
        es_gb = ExitStack()
        fB = fenceGA
        V2 = P.sb("V2", [128, NB, 128], BF16, es_gb)
        r_V2 = Res(fB)
        E2 = [P.sb(f"E2_{i}", [128, 1024], BF16, es_gb) for i in range(3)]
        r_E = [Res(fB) for _ in range(3)]
        sp2 = [P.sb(f"sp2_{i}", [128, 1024], BF16, es_gb) for i in range(3)]
        r_sp = [Res(fB) for _ in range(3)]
        X2 = [P.sb(f"X2_{i}", [128, 1024], BF16, es_gb) for i in range(2)]
        r_X = [Res(fB) for _ in range(2)]
        A2 = [P.sb(f"A2_{i}", [128, 1024], BF16, es_gb) for i in range(3)]
        r_A = [Res(fB) for _ in range(3)]

        for pi in pairsB:
            load_pair_weights(pi)
            proj_fm(0, 0, 0.125, lambda ts: r_qk[ts * 4:(ts + 1) * 4])
            proj_fm(128, 1, 1.0, lambda ts: r_qk[ts * 4:(ts + 1) * 4])
            proj_fm(256, 2, 1.0, lambda ts: [r_v[ts]])
            for g8 in range(4):
                bk = 2 + g8 % 2
                for j in range(8):
                    tb = g8 * 8 + j
                    P.op("pe", TR(bT[bk][:, j * 128:(j + 1) * 128], qkvT[:, 2, tb * 128:(tb + 1) * 128], ident[:]),
                         reads=r_v + [r_const], writes=[bres[bk]], inc=(j == 7))
                P.op("dve", CP(V2[:, g8 * 8:(g8 + 1) * 8, :], bT[bk].rearrange("p (j d) -> p j d", d=128)),
                     reads=[bres[bk]], writes=[r_V2])

            steps = [(G, J) for G in range(8) for J in range(4 * G + 3, -1, -1)]
            n = len(steps)

            def lo_of(i):
                G, J = steps[i]
                return max(0, J - 4 * G) * 128

            def hv(ap, lo):
                v = ap.rearrange("p (h t) -> p h t", h=2)
                return v if lo == 0 else v[:, :, lo:512]

            def pe_z(i):
                G, J = steps[i]
                k = i % 2
                lo = lo_of(i)
                for h in range(2):
                    hs = slice(h * 64, (h + 1) * 64)
                    P.op("pe", MM(banks[2 * k + h][:, lo:512], qkvT[hs, 1, J * 128:(J + 1) * 128],
                                  qkvT[hs, 0, G * 512 + lo:(G + 1) * 512], True, True),
                         reads=r_qk, writes=[bres[2 * k + h]], inc=(h == 1))

            def act_EL(i):
                G, J = steps[i]
                k = i % 2
                e_ = i % 3
                s_ = i % 3
                lo = lo_of(i)
                P.op("act", ACTV(hv(E2[e_][:], lo), hv(pb[k][:], lo), AF.Exp),
                     reads=[bres[2 * k], bres[2 * k + 1]], writes=[r_E[e_]])
                if J >= 4 * G:
                    ev = hv(E2[e_][:], lo)
                    P.op("dve", TT(ev, ev, sbm[J - 4 * G][:, lo:512].unsqueeze(1).to_broadcast([128, 2, 512 - lo]),
                                   ALU.mult), reads=[r_const], writes=[r_E[e_]])
                P.op("act", ACTV(hv(sp2[s_][:], lo), hv(E2[e_][:], lo), AF.Ln, bias=1.0, scale=1.0),
                     reads=[r_E[e_]], writes=[r_sp[s_]])

            def pe_C(i):
                G, J = steps[i]
                first = (J == 4 * G + 3)
                s_ = i % 3
                lo = lo_of(i)
                for h in range(2):
                    P.op("pe", MM(banks[4 + h][:, lo:512], tri[:], sp2[s_][:, h * 512 + lo:(h + 1) * 512], first, first,
                                  skip_group_check=True),
                         reads=[r_sp[s_], r_const], writes=[bres[4 + h]], inc=(first and h == 1))
                    if not first:
                        sp_prev = (i - 1) % 3
                        lop = lo_of(i - 1)
                        P.op("pe", MM(banks[4 + h][:, lop:512], sl[:], sp2[sp_prev][:, h * 512 + lop:(h + 1) * 512],
                                      False, True, skip_group_check=True),
                             reads=[r_sp[sp_prev], r_const], writes=[bres[4 + h]], inc=(h == 1))

            def act_X(i):
                e_ = i % 3
                x_ = i % 2
                a_ = i % 3
                lo = lo_of(i)
                P.op("act", ACTV(hv(X2[x_][:], lo), hv(pb[2][:], lo), AF.Exp, scale=-1.0),
                     reads=[bres[4], bres[5]], writes=[r_X[x_]])
                P.op("dve", TT(hv(A2[a_][:], lo), hv(E2[e_][:], lo), hv(X2[x_][:], lo), ALU.mult),
                     reads=[r_E[e_], r_X[x_]], writes=[r_A[a_]])

            def pe_pv(i):
                G, J = steps[i]
                first = (J == 4 * G + 3)
                last = (J == 0)
                ob = 6 + G % 2
                a_ = i % 3
                lo = lo_of(i)
                for h in range(2):
                    hs = slice(h * 64, (h + 1) * 64)
                    P.op("pe", MM(banks[ob][hs, lo:512], V2[:, J, hs], A2[a_][:, h * 512 + lo:(h + 1) * 512], first, last,
                                  skip_group_check=True),
                         reads=[r_V2, r_A[a_]], writes=[bres[ob]], inc=(h == 1))
                if last:
                    P.op("dve", CP(oT[:, pi, G * 512:(G + 1) * 512], banks[ob]),
                         reads=[bres[ob]], writes=[r_oT[pi][G]])

            for i in range(n + 3):
                if i < n:
                    pe_z(i)
                if 0 <= i - 3 < n:
                    pe_pv(i - 3)
                if 0 <= i - 1 < n:
                    act_EL(i - 1)
                if 0 <= i - 2 < n:
                    act_X(i - 2)
                if 0 <= i - 1 < n:
                    pe_C(i - 1)
        if len(pairsB) == 4:
            normalize_group(4)
        fenceGB = P.fence()
        es_gb.close()

    if dbg == "oT":
        for c in range(8):
            out_tokens.append(P.dma("sp", DMA(dbg_d[:, c, :], oT[:, c, :]), reads=r_oT[c]))
    fenceATT = P.fence()
    es_att.close()

    if do_ffn:
        fF = fenceATT
        es_f = ExitStack()
        Wd = P.sb("Wd", [128, NFC, D], BF16, es_f)
        r_Wd = Res(fF)
        Wo = P.sb("Wo", [128, 8, D], BF16, es_f)
        r_Wo = Res(fF)
        x1 = P.sb("x1", [128, 4, D], F32, es_f)
        r_x1 = [Res(fF) for _ in range(4)]
        h2T = P.sb("h2T", [128, 8, 512], BF16, es_f)
        r_h2T = [Res(fF) for _ in range(4)]
        actT = P.sb("actT", [128, NFC, 512], BF16, es_f)
        r_act = [Res(fF) for _ in range(NFC)]
        ring = [P.sb(f"ring{i}", [128, 8, 256], BF16, es_f) for i in range(3)]
        r_ring = [Res(fF) for _ in range(3)]
        fnw = P.sb("fnw", [128, D], F32, es_f)
        r_fnw = Res(fF)
        sg = [P.sb(f"sg{i}", [128, 512], BF16, es_f) for i in range(2)]
        r_sg = [Res(fF), Res(fF)]
        h2 = [P.sb(f"h2{i}", [128, D], BF16, es_f) for i in range(4)]
        r_h2 = [Res(fF) for _ in range(4)]
        r_ss2b = [Res(fF) for _ in range(4)]
        junk2 = P.sb("junk2", [128, D], BF16, es_f)
        r_junk2 = Res(fF)
        ss2 = P.sb("ss2", [128, 12], F32, es_f)
        r_ss2 = Res(fF)

        P.dma("pool", DMA(Wo[:], wout_d.rearrange("(c p) d -> p c d", p=128)), writes=[r_Wo])
        for q4_ in range(2):
            P.dma("pool", DMA(Wd[:, q4_ * 11:(q4_ + 1) * 11, :],
                              wd_d[q4_ * 11 * 128:(q4_ + 1) * 11 * 128, :].rearrange("(f p) d -> p f d", p=128)),
                  writes=[r_Wd])
        P.dma("sp", DMA(fnw[:], fnw_d), writes=[r_fnw])

        ydr = [0]

        def yd_bank():
            b = ydr[0] % 4
            ydr[0] += 1
            return b

        def step1_block(tt, b4):
            tbk = tt * 4 + b4
            tsl = slice(tbk * 128, (tbk + 1) * 128)
            P.dma("sp", DMA(x1[:, b4, :], x_d[tsl, :]), writes=[r_x1[b4]])
            for half in range(2):
                bk = yd_bank()
                hsl = slice(half * 512, (half + 1) * 512)
                for c in range(8):
                    P.op("pe", MM(banks[bk], oT[:, c, tsl], Wo[:, c, hsl], c == 0, c == 7),
                         reads=[r_Wo] + [r_oT[cc][tt] for cc in range(8)], writes=[bres[bk]], inc=(c == 7))
                P.op("dve", TT(x1[:, b4, hsl], banks[bk], x1[:, b4, hsl], ALU.add),
                     reads=[bres[bk]], writes=[r_x1[b4]])
            P.op("act", ACTV(junk2[:], x1[:, b4, :], AF.Square, accum_out=ss2[:, b4:b4 + 1]),
                 reads=[r_x1[b4]], writes=[r_junk2, r_ss2b[b4]])
            P.op("act", ACTV(ss2[:, 4 + b4:5 + b4], ss2[:, b4:b4 + 1], AF.Sqrt, scale=1.0 / D, bias=EPS),
                 reads=[r_ss2b[b4]], writes=[r_ss2b[b4]])
            P.op("dve", RCP(ss2[:, 8 + b4:9 + b4], ss2[:, 4 + b4:5 + b4]), reads=[r_ss2b[b4]], writes=[r_ss2b[b4]])
            P.op("dve", STT(h2[b4][:], x1[:, b4, :], ss2[:, 8 + b4:9 + b4], fnw[:], ALU.mult, ALU.mult),
                 reads=[r_x1[b4], r_ss2b[b4], r_fnw], writes=[r_h2[b4]])

        def step1_tr(b4):
            bk = yd_bank()
            for c in range(8):
                csl = slice(c * 128, (c + 1) * 128)
                P.op("pe", TR(bT[bk][:, csl], h2[b4][:, csl], ident[:]),
                     reads=[r_h2[b4], r_const], writes=[bres[bk]], inc=(c == 7))
            P.op("act", ACTV(h2T[:, :, b4 * 128:(b4 + 1) * 128], bT[bk].rearrange("p (c t) -> p c t", t=128), AF.Copy),
                 reads=[bres[bk]], writes=[r_h2T[b4]])

        def finish_norm(tt):
            pass

        gur = 0
        for b4 in range(4):
            step1_block(0, b4)
            if b4 >= 1:
                step1_tr(b4 - 1)
        step1_tr(3)
        for tt in range(8):
            finish_norm(tt)
            for fc in range(NFC):
                rg = fc % 3
                P.dma("pool", DMA(ring[rg][:], wgu_bf[fc].rearrange("p (c j) -> p c j", j=256)),
                      reads=[r_wgubf[fc]], writes=[r_ring[rg]])
                bg = 4 + (gur % 2) * 2
                bu = bg + 1
                gur += 1
                for (bk, off) in ((bg, 0), (bu, 128)):
                    for c in range(8):
                        P.op("pe", MM(banks[bk], ring[rg][:, c, off:off + 128], h2T[:, c, :], c == 0, c == 7),
                             reads=[r_ring[rg]] + r_h2T, writes=[bres[bk]], inc=(c == 7))
                s_ = fc % 2
                P.op("act", ACTV(sg[s_][:], banks[bg], AF.Silu), reads=[bres[bg]], writes=[r_sg[s_]])
                P.op("dve", TT(actT[:, fc, :], banks[bu], sg[s_][:], ALU.mult),
                     reads=[bres[bu], r_sg[s_]], writes=[r_act[fc]])
            for b4 in range(4):
                tbk = tt * 4 + b4
                for half in range(2):
                    bk = yd_bank()
                    hsl = slice(half * 512, (half + 1) * 512)
                    for fc in range(NFC):
                        P.op("pe", MM(banks[bk], actT[:, fc, b4 * 128:(b4 + 1) * 128], Wd[:, fc, hsl],
                                      fc == 0, fc == NFC - 1),
                             reads=[r_act[fc], r_Wd], writes=[bres[bk]], inc=(fc == NFC - 1))
                    P.op("dve", TT(x1[:, b4, hsl], banks[bk], x1[:, b4, hsl], ALU.add),
                         reads=[bres[bk]], writes=[r_x1[b4]])
                out_tokens.append(P.dma("sp", DMA(out_d[tbk * 128:(tbk + 1) * 128, :], x1[:, b4, :]),
                                        reads=[r_x1[b4]]))
                if tt + 1 < 8:
                    step1_block(tt + 1, b4)
                    if b4 >= 1:
                        step1_tr(b4 - 1)
            if tt + 1 < 8:
                step1_tr(3)
        es_f.close()

    P.wait("sp", out_tokens)
    P.emit()
    P.close()
    return nc


def make_inputs(x, attn_norm_w, w_in, q_norm_w, k_norm_w, dil_out_norm_w, sb_out_norm_w,
                w_out, ffn_norm_w, w_gate, w_up, w_down):
    f = np.float32
    w_in = np.asarray(w_in, f)[0]
    pairs = []
    for g in range(2):
        o0 = g * 1536
        for p in range(4):
            cols = [w_in[:, o0 + j * 512 + p * 128: o0 + j * 512 + (p + 1) * 128] for j in range(3)]
            pairs.append(np.concatenate(cols, axis=1))
    w_pairs = np.ascontiguousarray(np.stack(pairs, 0))
    wg = np.asarray(w_gate, f)[0].reshape(8, 128, NFC, 128)
    wu = np.asarray(w_up, f)[0].reshape(8, 128, NFC, 128)
    wgu = np.concatenate([wg, wu], axis=3)
    wgu = np.ascontiguousarray(wgu.transpose(2, 1, 0, 3)).reshape(NFC, 128, 2048)
    qw = np.asarray(q_norm_w, f)[0]
    kw = np.asarray(k_norm_w, f)[0]
    qkw = np.concatenate([qw, qw, kw, kw])[None, :]
    wcat = np.concatenate([np.asarray(dil_out_norm_w, f)[0], np.asarray(sb_out_norm_w, f)[0]])
    pos = np.arange(S, dtype=f)
    inv = (f(10000.0) ** (-np.arange(0, 64, 2, dtype=f) / f(64))).astype(f)
    ang = (pos[:, None] * inv[None, :]).astype(f)
    rope = np.concatenate([np.cos(ang), np.sin(ang), -np.sin(ang)], axis=1).astype(f)
    shared = {
        "anw_bc": np.ascontiguousarray(np.broadcast_to(np.asarray(attn_norm_w, f)[0][None, :], (128, D))),
        "fnw_bc": np.ascontiguousarray(np.broadcast_to(np.asarray(ffn_norm_w, f)[0][None, :], (128, D))),
        "qkw_bc": np.ascontiguousarray(np.broadcast_to(qkw, (128, 256))),
        "wcol": np.ascontiguousarray(wcat.reshape(8, 128).T),
        "rope": rope,
        "w_pairs": w_pairs,
        "w_out": np.ascontiguousarray(np.asarray(w_out, f)[0]),
        "wgu": wgu,
        "w_down": np.ascontiguousarray(np.asarray(w_down, f)[0]),
    }
    xs = np.asarray(x, f)
    return [dict(shared, x=np.ascontiguousarray(xs[b])) for b in range(xs.shape[0])]


_NC_CACHE = {}


def kernel(**inputs):
    in_maps = make_inputs(**inputs)
    if "nc" not in _NC_CACHE:
        _NC_CACHE["nc"] = build()
    nc = _NC_CACHE["nc"]
    res = run_bass_kernel_spmd(nc, in_maps, core_ids=list(range(8)))
    return np.stack([np.asarray(r["out"], np.float32) for r in res.results], axis=0)
```

```python
from contextlib import ExitStack
import numpy as np
import concourse.bass as bass
import concourse.mybir as mybir
from concourse.bass_utils import run_bass_kernel_spmd

F32 = mybir.dt.float32
BF16 = mybir.dt.bfloat16
AF = mybir.ActivationFunctionType
ALU = mybir.AluOpType
AX = mybir.AxisListType

S = 4096
D = 1024
NB = S // 128
DFF = 2816
NFC = DFF // 128
EPS = 1e-6
ENGS = ("pe", "act", "dve", "pool", "sp")
NDMASEM = 8


class Res:
    __slots__ = ("w", "r", "pend")

    def __init__(self, init=None):
        self.w = dict(init) if init else {}
        self.r = {}
        self.pend = None


def _merge(dst, src):
    for k, v in src.items():
        if dst.get(k, 0) < v:
            dst[k] = v


class Prog:
    def __init__(self, nc):
        self.nc = nc
        self.es = ExitStack()
        self.ops = {e: [] for e in ENGS}
        self.sem = {}
        self.cnt = {}
        self.seen = {e: {} for e in ENGS}
        self.pending = {e: [] for e in ENGS}
        for e in ENGS:
            self.newsem("E_" + e)
        self.dq = {}
        for q in ("sp", "pool"):
            self.dq[q] = {"keys": [self.newsem(f"D_{q}{i}") for i in range(NDMASEM)], "i": 0}

    def newsem(self, key):
        self.sem[key] = self.es.enter_context(self.nc.semaphore(key))
        self.cnt[key] = 0
        return key

    def sb(self, name, shape, dt, es=None):
        return (es or self.es).enter_context(self.nc.sbuf_tensor(name, list(shape), dt))

    def ps(self, name, shape, dt):
        return self.es.enter_context(self.nc.psum_tensor(name, list(shape), dt))

    def fence(self):
        f = {}
        for k, v in self.cnt.items():
            if v > 0:
                f[k] = v
        for e in ENGS:
            assert not self.pending[e], "fence with pending un-tokened ops"
        return f

    def _collect(self, eng, reads, writes, deps):
        need = {}
        for r in reads:
            assert r.pend is None or r.pend == eng, "resource pending on another engine"
            _merge(need, r.w)
        for r in writes:
            assert r.pend is None or r.pend == eng, "resource pending on another engine"
            _merge(need, r.w)
            _merge(need, r.r)
        for d in deps:
            if d:
                _merge(need, d)
        ws = []
        seen = self.seen[eng]
        for k, v in need.items():
            if k == "E_pe" and eng == "pe":
                continue
            if seen.get(k, 0) >= v:
                continue
            seen[k] = v
            ws.append((k, v))
        return ws

    def _commit(self, eng, tok, reads, writes):
        allp = self.pending[eng] + [(reads, writes)]
        self.pending[eng] = []
        for rs, wsx in allp:
            for r in rs:
                _merge(r.r, tok)
                r.pend = None
            for r in wsx:
                r.w = dict(tok)
                r.r = {}
                r.pend = None

    def op(self, eng, fn, reads=(), writes=(), inc=True, deps=()):
        ws = self._collect(eng, reads, writes, deps)
        if inc:
            key = "E_" + eng
            self.cnt[key] += 1
            tok = {key: self.cnt[key]}
            self.ops[eng].append((ws, fn, (key, 1)))
            self._commit(eng, tok, reads, writes)
            return tok
        self.ops[eng].append((ws, fn, None))
        self.pending[eng].append((reads, writes))
        for r in list(reads) + list(writes):
            r.pend = eng
        return None

    def dma(self, q, fn, reads=(), writes=(), deps=()):
        assert not self.pending[q]
        dq = self.dq[q]
        key = dq["keys"][dq["i"] % NDMASEM]
        dq["i"] += 1
        prev = {key: self.cnt[key]} if self.cnt[key] else None
        ws = self._collect(q, reads, writes, list(deps) + [prev])
        self.cnt[key] += 16
        tok = {key: self.cnt[key]}
        self.ops[q].append((ws, fn, (key, 16)))
        for r in reads:
            _merge(r.r, tok)
        for r in writes:
            r.w = dict(tok)
            r.r = {}
        return tok

    def wait(self, eng, deps):
        ws = self._collect(eng, (), (), deps)
        if ws:
            self.ops[eng].append((ws, None, None))

    def emit(self):
        prog = self
        for e in ENGS:
            assert not self.pending[e], f"pending ops on {e}"

        def run(name, e):
            for ws, fn, inc in prog.ops[name]:
                for key, val in ws:
                    e.wait_ge(prog.sem[key], val)
                if fn is None:
                    continue
                ins = fn(e)
                if inc is not None:
                    ins.then_inc(prog.sem[inc[0]], inc[1])

        with self.nc.Block() as block:
            @block.tensor
            def _(e):
                run("pe", e)

            @block.scalar
            def _(e):
                run("act", e)

            @block.vector
            def _(e):
                run("dve", e)

            @block.gpsimd
            def _(e):
                run("pool", e)

            @block.sync
            def _(e):
                run("sp", e)

    def close(self):
        self.es.close()


def MM(out, lhsT, rhs, start, stop, **kw):
    return lambda e: e.matmul(out, lhsT=lhsT, rhs=rhs, start=start, stop=stop, **kw)


def TR(out, in_, idn):
    return lambda e: e.transpose(out, in_, idn)


def ACTV(out, in_, func, **kw):
    return lambda e: e.activation(out=out, in_=in_, func=func, **kw)


def TT(out, in0, in1, op):
    return lambda e: e.tensor_tensor(out=out, in0=in0, in1=in1, op=op)


def TS(out, in0, s1, op0):
    return lambda e: e.tensor_scalar(out=out, in0=in0, scalar1=s1, scalar2=None, op0=op0)


def STT(out, in0, scalar, in1, op0, op1):
    return lambda e: e.scalar_tensor_tensor(out=out, in0=in0, scalar=scalar, in1=in1, op0=op0, op1=op1)


def CP(out, in_):
    return lambda e: e.tensor_copy(out=out, in_=in_)


def RCP(out, in_):
    return lambda e: e.reciprocal(out=out, in_=in_)


def RSUM(out, in_):
    return lambda e: e.reduce_sum(out=out, in_=in_, axis=AX.X)


def DMA(out, in_):
    return lambda e: e.dma_start(out=out, in_=in_)


def MSET(ap, v):
    return lambda e: e.memset(ap, v)

PA_BLOCKS = NB
KNOB = {}


def build(pairs=tuple(range(8)), do_ffn=True, dbg=None):
    nc = bass.Bass("TRN2", target_bir_lowering=False)

    def din(name, shape, dt=F32):
        return nc.dram_tensor(name, list(shape), dt, kind="ExternalInput").ap()

    x_d = din("x", [S, D])
    anw_d = din("anw_bc", [128, D])
    fnw_d = din("fnw_bc", [128, D])
    qkw_d = din("qkw_bc", [128, 256])
    wcol_d = din("wcol", [128, 8])
    rope_d = din("rope", [S, 96])
    wp_d = din("w_pairs", [8, D, 384])
    wout_d = din("w_out", [D, D])
    wgu_d = din("wgu", [NFC, 128, 2048])
    wd_d = din("w_down", [DFF, D])
    out_d = nc.dram_tensor("out", [S, D], F32, kind="ExternalOutput").ap()
    dbg_d = None
    if dbg in ("hT", "oT"):
        dbg_d = nc.dram_tensor("dbg", [128, 8, S], BF16, kind="ExternalOutput").ap()

    P = Prog(nc)
    out_tokens = []

    pb = [P.ps(f"pb{i}", [128, 1024], F32) for i in range(4)]
    banks = []
    for i in range(4):
        banks += [pb[i][:, 0:512], pb[i][:, 512:1024]]
    bres = [Res() for _ in range(8)]
    bT = [banks[i].bitcast(BF16) for i in range(8)]

    ident = P.sb("ident", [128, 128], BF16)
    tri = P.sb("tri", [128, 128], BF16)
    sl = P.sb("sl", [128, 128], BF16)
    ones = P.sb("ones", [128, 128], BF16)
    sbm = [P.sb(f"sbm{j}", [128, 512], BF16) for j in range(4)]
    dm1 = P.sb("dm1", [128, 512], BF16)
    dmz = P.sb("dmz", [128, 512], BF16)
    dmz2 = P.sb("dmz2", [128, 512], BF16)
    wcol = P.sb("wcol_sb", [128, 8], F32)
    oT = P.sb("oT", [128, 8, S], BF16)
    r_const = Res()
    r_oT = [[Res() for _ in range(8)] for _ in range(8)]

    def SEL(t_ap, pattern, base, cm, cmp):
        return lambda e: e.affine_select(out=t_ap, in_=t_ap, pattern=pattern, base=base,
                                         channel_multiplier=cm, compare_op=cmp, fill=0.0)

    for t in (ident, tri, sl, ones, dm1, dmz, dmz2, *sbm):
        P.op("pool", MSET(t[:], 1.0), writes=[r_const])
    P.op("pool", SEL(ident[:], [[-1, 128]], 0, 1, ALU.is_equal), writes=[r_const])
    P.op("pool", SEL(tri[:], [[-1, 128]], 0, 1, ALU.is_ge), writes=[r_const])
    P.op("pool", SEL(sl[:], [[1, 128]], 0, -1, ALU.is_gt), writes=[r_const])
    for j in range(4):
        P.op("pool", SEL(sbm[j][:], [[1, 512]], -128 * j, -1, ALU.is_gt), writes=[r_const])
    for m in (dm1, dmz, dmz2):
        for slot in range(4):
            ap = m[:, slot * 128:(slot + 1) * 128]
            if slot % 2 == 0:
                P.op("pool", SEL(ap, [[-1, 128]], 0, 1, ALU.is_ge), writes=[r_const])
            else:
                P.op("pool", SEL(ap, [[1, 128]], 0, -1, ALU.is_ge), writes=[r_const])
    P.op("pool", MSET(dmz[:, 0:128], 0.0), writes=[r_const])
    P.op("pool", MSET(dmz2[:, 0:128], 0.0), writes=[r_const])
    P.op("pool", MSET(dmz2[:, 256:384], 0.0), writes=[r_const])
    P.dma("sp", DMA(wcol[:], wcol_d), writes=[r_const])
    wgu_bf = nc.dram_tensor("wgu_bf", [NFC, 128, 2048], BF16, kind="Internal").ap()
    r_wgubf = [Res() for _ in range(NFC)]

    es_att = ExitStack()
    hT = P.sb("hT", [128, 8, S], BF16, es_att)
    r_hT = [Res() for _ in range(NB)]
    Wp = P.sb("Wp", [128, 8, 384], BF16, es_att)
    r_Wp = Res()
    qkvT = P.sb("qkvT", [128, 3, S], BF16, es_att)
    r_qk = [Res() for _ in range(NB)]
    r_v = [Res() for _ in range(8)]
    sq = [P.sb(f"sq{i}", [128, 512], BF16, es_att) for i in range(4)]
    r_sq = [Res() for _ in range(4)]
    rs = P.sb("rs", [128, 512], F32, es_att)
    r_rs = Res()

    es_pa = ExitStack()
    xs = [P.sb(f"xs{i}", [128, D], F32, es_pa)[:] for i in range(4)]
    for j in range(3):
        v32 = qkvT[:, j, :].bitcast(F32)
        xs += [v32[:, 0:D], v32[:, D:2 * D]]
    NX = len(xs)
    r_xs = [Res() for _ in range(NX)]
    hn = [P.sb(f"hn{i}", [128, D], BF16, es_pa) for i in range(4)]
    r_hn = [Res() for _ in range(4)]
    junk = P.sb("junkA", [128, D], BF16, es_pa)
    r_junk = Res()
    anw = P.sb("anw", [128, D], F32, es_pa)
    r_anw = Res()
    ssA = P.sb("ssA", [128, NB], F32, es_pa)
    sdA = P.sb("sdA", [128, NB], F32, es_pa)
    rsA = P.sb("rsA", [128, NB], F32, es_pa)
    r_ssA = [Res() for _ in range(NB)]
    P.dma("sp", DMA(anw[:], anw_d), writes=[r_anw])
    BS = 2

    def paA(sb):
        for b in range(BS):
            tb = sb * BS + b
            xi = tb % NX
            tsl = slice(tb * 128, (tb + 1) * 128)
            P.dma("sp", DMA(xs[xi], x_d[tsl, :]), writes=[r_xs[xi]])
            P.op("act", ACTV(junk[:], xs[xi], AF.Square, accum_out=ssA[:, tb:tb + 1]),
                 reads=[r_xs[xi]], writes=[r_junk, r_ssA[sb]])
        c4 = slice(sb * BS, sb * BS + BS)
        P.op("act", ACTV(sdA[:, c4], ssA[:, c4], AF.Sqrt, scale=1.0 / D, bias=EPS), reads=[r_ssA[sb]], writes=[r_ssA[sb]])
        P.op("dve", RCP(rsA[:, c4], sdA[:, c4]), reads=[r_ssA[sb]], writes=[r_ssA[sb]])

    def paB(sb):
        for b in range(BS):
            tb = sb * BS + b
            xi = tb % NX
            hi = tb % 4
            tsl = slice(tb * 128, (tb + 1) * 128)
            bk = tb % 4
            P.op("dve", STT(hn[hi][:], xs[xi], rsA[:, tb:tb + 1], anw[:], ALU.mult, ALU.mult),
                 reads=[r_xs[xi], r_ssA[sb], r_anw], writes=[r_hn[hi]])
            for c in range(8):
                csl = slice(c * 128, (c + 1) * 128)
                P.op("pe", TR(bT[bk][:, csl], hn[hi][:, csl], ident[:]),
                     reads=[r_hn[hi], r_const], writes=[bres[bk]], inc=(c == 7))
            if b % 2 == 0:
                P.op("act", ACTV(hT[:, :, tsl], bT[bk].rearrange("p (c t) -> p c t", t=128), AF.Copy),
                     reads=[bres[bk]], writes=[r_hT[tb]])
            else:
                P.op("dve", CP(hT[:, :, tsl], bT[bk].rearrange("p (c t) -> p c t", t=128)),
                     reads=[bres[bk]], writes=[r_hT[tb]])

    nsb = PA_BLOCKS // BS
    for sb in range(nsb + 1):
        if sb < nsb:
            paA(sb)
        if sb >= 1:
            paB(sb - 1)
    fenceA = P.fence()
    es_pa.close()
    for r_ in r_qk + r_v:
        _merge(r_.w, fenceA)
    casts_pending = [do_ffn]

    if dbg == "hT":
        for c in range(8):
            out_tokens.append(P.dma("sp", DMA(dbg_d[:, c, :], hT[:, c, :]), reads=r_hT))

    def proj_fm(col0, dst_idx, scale, res_for_slice):
        for ts in range(8):
            bk = ts % 2
            tsl = slice(ts * 512, (ts + 1) * 512)
            for c in range(8):
                P.op("pe", MM(banks[bk], Wp[:, c, col0:col0 + 128], hT[:, c, tsl], c == 0, c == 7),
                     reads=[r_Wp] + r_hT[ts * 4:(ts + 1) * 4], writes=[bres[bk]], inc=(c == 7))
            wr = res_for_slice(ts)
            if ts % 2 == 0:
                P.op("act", ACTV(qkvT[:, dst_idx, tsl], banks[bk], AF.Copy, scale=scale),
                     reads=[bres[bk]], writes=wr)
            else:
                P.op("dve", TS(qkvT[:, dst_idx, tsl], banks[bk], scale, ALU.mult),
                     reads=[bres[bk]], writes=wr)

    def load_pair_weights(pi):
        P.dma("pool", DMA(Wp[:], wp_d[pi].rearrange("(c p) e -> p c e", p=128)), writes=[r_Wp])

    def normalize_group(gc):
        for ts in range(8):
            tsl = slice(ts * 512, (ts + 1) * 512)
            for c in range(4):
                if c < 3:
                    P.op("act", ACTV(sq[c][:], oT[:, gc + c, tsl], AF.Square),
                         reads=[r_oT[gc + c][ts]], writes=[r_sq[c]])
                else:
                    P.op("pool", TT(sq[c][:], oT[:, gc + c, tsl], oT[:, gc + c, tsl], ALU.mult),
                         reads=[r_oT[gc + c][ts]], writes=[r_sq[c]])
            bk = 2 + ts % 2
            for c in range(4):
                P.op("pe", MM(banks[bk], ones[:], sq[c][:], c == 0, c == 3),
                     reads=[r_sq[c], r_const], writes=[bres[bk]], inc=(c == 3))
            P.op("act", ACTV(rs[:], banks[bk], AF.Sqrt, scale=1.0 / 512, bias=EPS),
                 reads=[bres[bk]], writes=[r_rs])
            P.op("dve", RCP(rs[:], rs[:]), reads=[r_rs], writes=[r_rs])
            for c in range(4):
                P.op("dve", STT(oT[:, gc + c, tsl], oT[:, gc + c, tsl], wcol[:, gc + c:gc + c + 1], rs[:],
                                ALU.mult, ALU.mult),
                     reads=[r_rs, r_const], writes=[r_oT[gc + c][ts]])

    LAG = 3
    NSL = 4
    pairsA = [p for p in pairs if p < 4]
    bank_rr = [0]

    def next_bank():
        b = bank_rr[0] % 4
        bank_rr[0] += 1
        return b

    if pairsA:
        es_ga = ExitStack()
        qkw = P.sb("qkw", [128, 256], F32, es_ga)
        r_qkw = Res(fenceA)
        P.dma("sp", DMA(qkw[:], qkw_d), writes=[r_qkw])
        P.op("dve", TS(qkw[:, 0:128], qkw[:, 0:128], 0.125, ALU.mult), writes=[r_qkw])
        fprev = fenceA
        for pi in pairsA:
            load_pair_weights(pi)
            es_p = ExitStack()
            NS = 4
            sa = [P.sb(f"sa{pi}_{i}", [128, 512], F32, es_p) for i in range(NS)]
            qn = [P.sb(f"qn{pi}_{i}", [128, 512], F32, es_p) for i in range(NS)]
            t2 = [P.sb(f"t2{pi}_{i}", [128, 512], F32, es_p) for i in range(NS)]
            qr = [P.sb(f"qr{pi}_{i}", [128, 512], BF16, es_p) for i in range(NS + 1)]
            st = [P.sb(f"st{pi}_{i}", [128, 24], F32, es_p) for i in range(NS)]
            ropeb = [P.sb(f"ropeb{pi}_{i}", [128, 2, 96], F32, es_p) for i in range(NS)]
            r_sa = [Res(fprev) for _ in range(NS)]
            r_qn = [Res(fprev) for _ in range(NS)]
            r_t2 = [Res(fprev) for _ in range(NS)]
            r_qr = [Res(fprev) for _ in range(NS + 1)]
            r_st = [Res(fprev) for _ in range(NS)]
            r_rope = [Res(fprev) for _ in range(NS)]
            w3 = lambda ap: ap.rearrange("p (s d) -> p s d", d=64)
            w4 = lambda ap: ap.rearrange("p (b s d) -> p b s d", b=2, d=32)
            w5 = lambda ap: ap.rearrange("p (b s h d) -> p b s h d", b=2, h=2, d=32)
            wb = lambda ap: ap.rearrange("p (b e) -> p b e", e=256)
            NBB = NB // 2

            def st0(bb):
                k = bb % NS
                tb0 = bb * 2
                bk = bb % 2
                P.dma("sp", DMA(ropeb[k][:], rope_d[tb0 * 128:(tb0 + 2) * 128, :].rearrange("(b p) d -> p b d", p=128)),
                      writes=[r_rope[k]])
                for b in range(2):
                    tsl = slice((tb0 + b) * 128, (tb0 + b + 1) * 128)
                    for c in range(8):
                        P.op("pe", MM(banks[bk][:, b * 256:(b + 1) * 256], hT[:, c, tsl], Wp[:, c, 0:256], c == 0, c == 7),
                             reads=[r_Wp, r_hT[tb0 + b]], writes=[bres[bk]], inc=(b == 1 and c == 7))

            def st1a(bb):
                k = bb % NS
                bk = bb % 2
                P.op("act", ACTV(sa[k][:], banks[bk], AF.Square), reads=[bres[bk]], writes=[r_sa[k]])
                P.op("dve", RSUM(st[k][:, 0:8], w3(sa[k][:])), reads=[r_sa[k]], writes=[r_st[k]])
                P.op("act", ACTV(st[k][:, 8:16], st[k][:, 0:8], AF.Sqrt, scale=1.0 / 64, bias=EPS),
                     reads=[r_st[k]], writes=[r_st[k]])

            def st1b(bb):
                k = bb % NS
                bk = bb % 2
                P.op("dve", RCP(st[k][:, 16:24], st[k][:, 8:16]), reads=[r_st[k]], writes=[r_st[k]])
                P.op("dve", TT(w3(qn[k][:]), w3(banks[bk]), st[k][:, 16:24].unsqueeze(2).to_broadcast([128, 8, 64]),
                               ALU.mult), reads=[bres[bk], r_st[k]], writes=[r_qn[k]])
                P.op("pool", TT(wb(qn[k][:]), wb(qn[k][:]), qkw[:].unsqueeze(1).to_broadcast([128, 2, 256]), ALU.mult),
                     reads=[r_qkw], writes=[r_qn[k]])

            def st2(bb):
                k = bb % NS
                cosb = ropeb[k][:, :, 0:32].unsqueeze(2).to_broadcast([128, 2, 8, 32])
                sinb = ropeb[k][:, :, 32:64].unsqueeze(2).to_broadcast([128, 2, 4, 32])
                nsinb = ropeb[k][:, :, 64:96].unsqueeze(2).to_broadcast([128, 2, 4, 32])
                P.op("dve", TT(w4(sa[k][:]), w4(qn[k][:]), cosb, ALU.mult),
                     reads=[r_qn[k], r_rope[k]], writes=[r_sa[k]])
                P.op("pool", TT(w5(t2[k][:])[:, :, :, 0, :], w5(qn[k][:])[:, :, :, 1, :], nsinb, ALU.mult),
                     reads=[r_qn[k], r_rope[k]], writes=[r_t2[k]])
                P.op("pool", TT(w5(t2[k][:])[:, :, :, 1, :], w5(qn[k][:])[:, :, :, 0, :], sinb, ALU.mult),
                     reads=[r_qn[k], r_rope[k]], writes=[r_t2[k]])

            def st3a(bb):
                k = bb % NS
                kq = bb % (NS + 1)
                P.op("dve", TT(qr[kq][:], sa[k][:], t2[k][:], ALU.add), reads=[r_sa[k], r_t2[k]], writes=[r_qr[kq]])

            def st3b(bb):
                kq = bb % (NS + 1)
                tb0 = bb * 2
                t2sl = slice(tb0 * 128, (tb0 + 2) * 128)
                bk2 = 2 + bb % 2
                for j in range(4):
                    jsl = slice(j * 128, (j + 1) * 128)
                    P.op("pe", TR(bT[bk2][:, jsl], qr[kq][:, jsl], ident[:]),
                         reads=[r_qr[kq], r_const], writes=[bres[bk2]], inc=(j == 3))
                P.op("act", ACTV(qkvT[:, 0:2, t2sl].rearrange("p j (b t) -> p j b t", t=128),
                                 bT[bk2][:, 0:512].rearrange("p (b j t) -> p j b t", j=2, t=128), AF.Copy),
                     reads=[bres[bk2]], writes=[r_qk[tb0], r_qk[tb0 + 1]])

            for it in range(NBB + 4):
                if it < NBB:
                    st0(it)
                if 0 <= it - 1 < NBB:
                    st1a(it - 1)
                if 0 <= it - 2 < NBB:
                    st2(it - 2)
                if 0 <= it - 1 < NBB:
                    st1b(it - 1)
                if 0 <= it - 3 < NBB:
                    st3a(it - 3)
                if 0 <= it - 4 < NBB:
                    st3b(it - 4)
            proj_fm(256, 2, 1.0, lambda ts: [r_v[ts]])
            fmid = P.fence()
            es_p.close()

            es_a = ExitStack()
            vaug = [P.sb(f"vaug{pi}_{i}", [128, NB, 128], BF16, es_a) for i in range(3)]
            r_vaug = [Res(fmid) for _ in range(3)]
            expS = [P.sb(f"expS{pi}_{i}", [128, 512], BF16, es_a) for i in range(NSL)]
            r_expS = [Res(fmid) for _ in range(NSL)]
            ptm = [P.sb(f"ptm{pi}_{i}", [128, 512], BF16, es_a) for i in range(NSL)]
            r_ptm = [Res(fmid) for _ in range(NSL)]
            rden = [P.sb(f"rden{pi}_{i}", [128, 512], F32, es_a) for i in range(1)]
            r_rden = [Res(fmid)]
            for i in range(3):
                P.op("pool", MSET(vaug[i][:, :, 64:128], 1.0), writes=[r_vaug[i]])
            if casts_pending[0]:
                casts_pending[0] = False
                for fc in range(NFC):
                    P.dma("pool", DMA(wgu_bf[fc], wgu_d[fc]), writes=[r_wgubf[fc]], deps=[fmid])
            VIDX = {1: 0, 4: 1, 16: 2}

            vi = 0
            gi = 0
            for hp in range(2):
                p0, p1 = hp * 64, hp * 64 + 64
                for R in range(2):
                    started = [False] * 4
                    groups = []
                    for r in (1, 4, 16):
                        nb = NB // r
                        if r == 16:
                            for c in range(0, 16, 2):
                                groups.append((r, [(c, R - 1, R), (c, R, R), (c + 1, R - 1, R), (c + 1, R, R)]))
                        else:
                            per = nb // 2
                            for c in range(r):
                                for n in range(R * per, (R + 1) * per, 2):
                                    groups.append((r, [(c, n - 1, n), (c, n, n), (c, n, n + 1), (c, n + 1, n + 1)]))

                    def build_vaug(r, vi_):
                        nb = NB // r
                        for g8 in range(4):
                            bk = next_bank()
                            for j in range(8):
                                blk = g8 * 8 + j
                                c, n = blk // nb, blk % nb
                                st_ = n * 128 * r + c
                                P.op("pe", TR(bT[bk][:, j * 64:(j + 1) * 64], qkvT[p0:p1, 2, st_:st_ + 127 * r + 1:r],
                                              ident[p0:p1, p0:p1]),
                                     reads=r_v + [r_const], writes=[bres[bk]], inc=(j == 7))
                            P.op("dve", CP(vaug[vi_][:, g8 * 8:(g8 + 1) * 8, 0:64],
                                           bT[bk][:, 0:512].rearrange("p (j d) -> p j d", d=64)),
                                 reads=[bres[bk]], writes=[r_vaug[vi_]])

                    def emit_pv(item):
                        r, slots, pslot, vi_ = item
                        nb = NB // r
                        mms = []
                        for si, (c, kb, qb) in enumerate(slots):
                            if kb < 0:
                                continue
                            vblk = c * nb + kb
                            if r == 16:
                                for u in range(4):
                                    stt = not started[u]
                                    started[u] = True
                                    q0 = si * 128 + u * 32
                                    mms.append((MM(banks[4 + u][:, c:512:16], vaug[vi_][:, vblk, :],
                                                   ptm[pslot][:, q0:q0 + 32], stt, True, skip_group_check=True), u))
                            else:
                                tok0 = qb * 128 * r + c - R * 2048
                                u = tok0 // 512
                                lo = tok0 - u * 512
                                stt = not started[u]
                                started[u] = True
                                mms.append((MM(banks[4 + u][:, lo:lo + 127 * r + 1:r], vaug[vi_][:, vblk, :],
                                               ptm[pslot][:, si * 128:(si + 1) * 128], stt, True,
                                               skip_group_check=True), u))
                        for kk, (fn, u) in enumerate(mms):
                            P.op("pe", fn, reads=[r_vaug[vi_], r_ptm[pslot]], writes=[bres[4 + u]],
                                 inc=(kk == len(mms) - 1))

                    pend = []
                    if R == 0:
                        for r_ in (1, 4, 16):
                            build_vaug(r_, VIDX[r_])
                    for (r, slots) in groups:
                        vi = VIDX[r]
                        bk = next_bank()
                        es_ = gi % NSL
                        for si, (c, kb, qb) in enumerate(slots):
                            ks = max(kb, 0) * 128 * r + c
                            qs = qb * 128 * r + c
                            P.op("pe", MM(banks[bk][:, si * 128:(si + 1) * 128],
                                          qkvT[p0:p1, 1, ks:ks + 127 * r + 1:r], qkvT[p0:p1, 0, qs:qs + 127 * r + 1:r],
                                          True, True),
                                 reads=r_qk, writes=[bres[bk]], inc=(si == 3))
                        P.op("act", ACTV(expS[es_][:], banks[bk], AF.Exp), reads=[bres[bk]], writes=[r_expS[es_]])
                        inv = [kb < 0 for (_, kb, _) in slots]
                        mk = dmz2 if (inv[0] and inv[2]) else (dmz if inv[0] else dm1)
                        P.op("dve", TT(ptm[es_][:], expS[es_][:], mk[:], ALU.mult),
                             reads=[r_expS[es_], r_const], writes=[r_ptm[es_]])
                        pend.append((r, slots, es_, vi))
                        if len(pend) > LAG:
                            emit_pv(pend.pop(0))
                        gi += 1
                    while pend:
                        emit_pv(pend.pop(0))
                    for u in range(4):
                        rd = 0
                        ts_ = R * 4 + u
                        P.op("act", ACTV(rden[rd][p0:p1, :], banks[4 + u][64:128, :], AF.Ln),
                             reads=[bres[4 + u]], writes=[r_rden[rd]])
                        P.op("act", ACTV(rden[rd][p0:p1, :], rden[rd][p0:p1, :], AF.Exp, scale=-1.0),
                             reads=[], writes=[r_rden[rd]])
                        P.op("dve", TT(oT[p0:p1, pi, ts_ * 512:(ts_ + 1) * 512], banks[4 + u][0:64, :],
                                       rden[rd][p0:p1, :], ALU.mult),
                             reads=[bres[4 + u], r_rden[rd]], writes=[r_oT[pi][ts_]])
            fprev = P.fence()
            es_a.close()
        if len(pairsA) == 4:
            normalize_group(0)
        fenceGA = P.fence()
        es_ga.close()
    else:
        fenceGA = fenceA

    pairsB = [p for p in pairs if p >= 4]
    if pairsB:
        es_gb = ExitStack()
        fB = fenceGA
        V2 = P.sb("V2", [128, NB, 128], BF16, es_gb)
        r_V2 = Res(fB)
        E2 = [P.sb(f"E2_{i}", [128, 1024], BF16, es_gb) for i in range(3)]
        r_E = [Res(fB) for _ in range(3)]
        sp2 = [P.sb(f"sp2_{i}", [128, 1024], BF16, es_gb) for i in range(3)]
        r_sp = [Res(fB) for _ in range(3)]
        X2 = [P.sb(f"X2_{i}", [128, 1024], BF16, es_gb) for i in range(2)]
        r_X = [Res(fB) for _ in range(2)]
        A2 = [P.sb(f"A2_{i}", [128, 1024], BF16, es_gb) for i in range(3)]
        r_A = [Res(fB) for _ in range(3)]

        for pi in pairsB:
            load_pair_weights(pi)
            proj_fm(0, 0, 0.125, lambda ts: r_qk[ts * 4:(ts + 1) * 4])
            proj_fm(128, 1, 1.0, lambda ts: r_qk[ts * 4:(ts + 1) * 4])
            proj_fm(256, 2, 1.0, lambda ts: [r_v[ts]])
            for g8 in range(4):
                bk = 2 + g8 % 2
                for j in range(8):
                    tb = g8 * 8 + j
                    P.op("pe", TR(bT[bk][:, j * 128:(j + 1) * 128], qkvT[:, 2, tb * 128:(tb + 1) * 128], ident[:]),
                         reads=r_v + [r_const], writes=[bres[bk]], inc=(j == 7))
                P.op("dve", CP(V2[:, g8 * 8:(g8 + 1) * 8, :], bT[bk].rearrange("p (j d) -> p j d", d=128)),
                     reads=[bres[bk]], writes=[r_V2])

            steps = [(G, J) for G in range(8) for J in range(4 * G + 3, -1, -1)]
            n = len(steps)

            def lo_of(i):
                G, J = steps[i]
                return max(0, J - 4 * G) * 128

            def hv(ap, lo):
                v = ap.rearrange("p (h t) -> p h t", h=2)
                return v if lo == 0 else v[:, :, lo:512]

            def pe_z(i):
                G, J = steps[i]
                k = i % 2
                lo = lo_of(i)
                for h in range(2):
                    hs = slice(h * 64, (h + 1) * 64)
                    P.op("pe", MM(banks[2 * k + h][:, lo:512], qkvT[hs, 1, J * 128:(J + 1) * 128],
                                  qkvT[hs, 0, G * 512 + lo:(G + 1) * 512], True, True),
                         reads=r_qk, writes=[bres[2 * k + h]], inc=(h == 1))

            def act_EL(i):
                G, J = steps[i]
                k = i % 2
                e_ = i % 3
                s_ = i % 3
                lo = lo_of(i)
                P.op("act", ACTV(hv(E2[e_][:], lo), hv(pb[k][:], lo), AF.Exp),
                     reads=[bres[2 * k], bres[2 * k + 1]], writes=[r_E[e_]])
                if J >= 4 * G:
                    ev = hv(E2[e_][:], lo)
                    P.op("dve", TT(ev, ev, sbm[J - 4 * G][:, lo:512].unsqueeze(1).to_broadcast([128, 2, 512 - lo]),
                                   ALU.mult), reads=[r_const], writes=[r_E[e_]])
                P.op("act", ACTV(hv(sp2[s_][:], lo), hv(E2[e_][:], lo), AF.Ln, bias=1.0, scale=1.0),
                     reads=[r_E[e_]], writes=[r_sp[s_]])

            def pe_C(i):
                G, J = steps[i]
                first = (J == 4 * G + 3)
                s_ = i % 3
                lo = lo_of(i)
                for h in range(2):
                    P.op("pe", MM(banks[4 + h][:, lo:512], tri[:], sp2[s_][:, h * 512 + lo:(h + 1) * 512], first, first,
                                  skip_group_check=True),
                         reads=[r_sp[s_], r_const], writes=[bres[4 + h]], inc=(first and h == 1))
                    if not first:
                        sp_prev = (i - 1) % 3
                        lop = lo_of(i - 1)
                        P.op("pe", MM(banks[4 + h][:, lop:512], sl[:], sp2[sp_prev][:, h * 512 + lop:(h + 1) * 512],
                                      False, True, skip_group_check=True),
                             reads=[r_sp[sp_prev], r_const], writes=[bres[4 + h]], inc=(h == 1))

            def act_X(i):
                e_ = i % 3
                x_ = i % 2
                a_ = i % 3
                lo = lo_of(i)
                P.op("act", ACTV(hv(X2[x_][:], lo), hv(pb[2][:], lo), AF.Exp, scale=-1.0),
                     reads=[bres[4], bres[5]], writes=[r_X[x_]])
                P.op("dve", TT(hv(A2[a_][:], lo), hv(E2[e_][:], lo), hv(X2[x_][:], lo), ALU.mult),
                     reads=[r_E[e_], r_X[x_]], writes=[r_A[a_]])

            def pe_pv(i):
                G, J = steps[i]
                first = (J == 4 * G + 3)
                last = (J == 0)
                ob = 6 + G % 2
                a_ = i % 3
                lo = lo_of(i)
                for h in range(2):
                    hs = slice(h * 64, (h + 1) * 64)
                    P.op("pe", MM(banks[ob][hs, lo:512], V2[:, J, hs], A2[a_][:, h * 512 + lo:(h + 1) * 512], first, last,
                                  skip_group_check=True),
                         reads=[r_V2, r_A[a_]], writes=[bres[ob]], inc=(h == 1))
                if last:
                    P.op("dve", CP(oT[:, pi, G * 512:(G + 1) * 512], banks[ob]),
                         reads=[bres[ob]], writes=[r_oT[pi][G]])

            for i in range(n + 3):
                if i < n:
                    pe_z(i)
                if 0 <= i - 3 < n:
                    pe_pv(i - 3)
                if 0 <= i - 1 < n:
                    act_EL(i - 1)
                if 0 <= i - 2 < n:
                    act_X(i - 2)
                if 0 <= i - 1 < n:
                    pe_C(i - 1)
        if len(pairsB) == 4:
            normalize_group(4)
        fenceGB = P.fence()
        es_gb.close()

    if dbg == "oT":
        for c in range(8):
            out_tokens.append(P.dma("sp", DMA(dbg_d[:, c, :], oT[:, c, :]), reads=r_oT[c]))
    fenceATT = P.fence()
    es_att.close()

    if do_ffn:
        fF = fenceATT
        es_f = ExitStack()
        Wd = P.sb("Wd", [128, NFC, D], BF16, es_f)
        r_Wd = Res(fF)
        Wo = P.sb("Wo", [128, 8, D], BF16, es_f)
        r_Wo = Res(fF)
        x1 = P.sb("x1", [128, 5, D], F32, es_f)
        r_x1 = [Res(fF) for _ in range(5)]
        h2T = P.sb("h2T", [128, 8, 512], BF16, es_f)
        r_h2T = [Res(fF) for _ in range(4)]
        actT = P.sb("actT", [128, NFC, 512], BF16, es_f)
        r_act = [Res(fF) for _ in range(NFC)]
        ring = [P.sb(f"ring{i}", [128, 8, 256], BF16, es_f) for i in range(3)]
        r_ring = [Res(fF) for _ in range(3)]
        fnw = P.sb("fnw", [128, D], F32, es_f)
        r_fnw = Res(fF)
        sg = [P.sb(f"sg{i}", [128, 512], BF16, es_f) for i in range(1)]
        r_sg = [Res(fF)]
        h2 = [P.sb(f"h2{i}", [128, D], BF16, es_f) for i in range(4)]
        r_h2 = [Res(fF) for _ in range(4)]
        r_ss2b = [Res(fF) for _ in range(4)]
        ss2 = P.sb("ss2", [128, 12], F32, es_f)
        r_ss2 = Res(fF)

        P.dma("pool", DMA(Wo[:], wout_d.rearrange("(c p) d -> p c d", p=128)), writes=[r_Wo])
        for q4_ in range(2):
            P.dma("pool", DMA(Wd[:, q4_ * 11:(q4_ + 1) * 11, :],
                              wd_d[q4_ * 11 * 128:(q4_ + 1) * 11 * 128, :].rearrange("(f p) d -> p f d", p=128)),
                  writes=[r_Wd])
        P.dma("sp", DMA(fnw[:], fnw_d), writes=[r_fnw])

        ydr = [0]

        def yd_bank():
            b = ydr[0] % 4
            ydr[0] += 1
            return b

        def step1_block(tt, b4):
            tbk = tt * 4 + b4
            xsl = tbk % 5
            tsl = slice(tbk * 128, (tbk + 1) * 128)
            P.dma("sp", DMA(x1[:, xsl, :], x_d[tsl, :]), writes=[r_x1[xsl]])
            for half in range(2):
                bk = yd_bank()
                hsl = slice(half * 512, (half + 1) * 512)
                for c in range(8):
                    P.op("pe", MM(banks[bk], oT[:, c, tsl], Wo[:, c, hsl], c == 0, c == 7),
                         reads=[r_Wo] + [r_oT[cc][tt] for cc in range(8)], writes=[bres[bk]], inc=(c == 7))
                P.op("dve", TT(x1[:, xsl, hsl], banks[bk], x1[:, xsl, hsl], ALU.add),
                     reads=[bres[bk]], writes=[r_x1[xsl]])
            P.op("act", ACTV(h2[b4][:], x1[:, xsl, :], AF.Square, accum_out=ss2[:, b4:b4 + 1]),
                 reads=[r_x1[xsl]], writes=[r_h2[b4], r_ss2b[b4]])
            P.op("act", ACTV(ss2[:, 4 + b4:5 + b4], ss2[:, b4:b4 + 1], AF.Sqrt, scale=1.0 / D, bias=EPS),
                 reads=[r_ss2b[b4]], writes=[r_ss2b[b4]])
            P.op("dve", RCP(ss2[:, 8 + b4:9 + b4], ss2[:, 4 + b4:5 + b4]), reads=[r_ss2b[b4]], writes=[r_ss2b[b4]])
            P.op("dve", STT(h2[b4][:], x1[:, xsl, :], ss2[:, 8 + b4:9 + b4], fnw[:], ALU.mult, ALU.mult),
                 reads=[r_x1[xsl], r_ss2b[b4], r_fnw], writes=[r_h2[b4]])

        def step1_tr(b4):
            bk = yd_bank()
            for c in range(8):
                csl = slice(c * 128, (c + 1) * 128)
                P.op("pe", TR(bT[bk][:, csl], h2[b4][:, csl], ident[:]),
                     reads=[r_h2[b4], r_const], writes=[bres[bk]], inc=(c == 7))
            P.op("act", ACTV(h2T[:, :, b4 * 128:(b4 + 1) * 128], bT[bk].rearrange("p (c t) -> p c t", t=128), AF.Copy),
                 reads=[bres[bk]], writes=[r_h2T[b4]])

        def finish_norm(tt):
            pass

        gur = 0
        for b4 in range(4):
            step1_block(0, b4)
            if b4 >= 1:
                step1_tr(b4 - 1)
        step1_tr(3)
        for tt in range(8):
            finish_norm(tt)
            for fc in range(NFC):
                rg = fc % 3
                P.dma("pool", DMA(ring[rg][:], wgu_bf[fc].rearrange("p (c j) -> p c j", j=256)),
                      reads=[r_wgubf[fc]], writes=[r_ring[rg]])
                bg = 4 + (gur % 2) * 2
                bu = bg + 1
                gur += 1
                for (bk, off) in ((bg, 0), (bu, 128)):
                    for c in range(8):
                        P.op("pe", MM(banks[bk], ring[rg][:, c, off:off + 128], h2T[:, c, :], c == 0, c == 7),
                             reads=[r_ring[rg]] + r_h2T, writes=[bres[bk]], inc=(c == 7))
                s_ = 0
                P.op("act", ACTV(sg[s_][:], banks[bg], AF.Silu), reads=[bres[bg]], writes=[r_sg[s_]])
                P.op("dve", TT(actT[:, fc, :], banks[bu], sg[s_][:], ALU.mult),
                     reads=[bres[bu], r_sg[s_]], writes=[r_act[fc]])
            nxt = tt + 1 < 8
            for b4 in range(4):
                tbk = tt * 4 + b4
                xsl = tbk % 5
                if nxt:
                    step1_block(tt + 1, b4)
                for half in range(2):
                    bk = yd_bank()
                    hsl = slice(half * 512, (half + 1) * 512)
                    for fc in range(NFC):
                        P.op("pe", MM(banks[bk], actT[:, fc, b4 * 128:(b4 + 1) * 128], Wd[:, fc, hsl],
                                      fc == 0, fc == NFC - 1),
                             reads=[r_act[fc], r_Wd], writes=[bres[bk]], inc=(fc == NFC - 1))
                    P.op("dve", TT(x1[:, xsl, hsl], banks[bk], x1[:, xsl, hsl], ALU.add),
                         reads=[bres[bk]], writes=[r_x1[xsl]])
                out_tokens.append(P.dma("sp", DMA(out_d[tbk * 128:(tbk + 1) * 128, :], x1[:, xsl, :]),
                                        reads=[r_x1[xsl]]))
                if nxt and b4 >= 1:
                    step1_tr(b4 - 1)
            if nxt:
                step1_tr(3)
        es_f.close()

    P.wait("sp", out_tokens)
    P.emit()
    P.close()
    return nc


def make_inputs(x, attn_norm_w, w_in, q_norm_w, k_norm_w, dil_out_norm_w, sb_out_norm_w,
                w_out, ffn_norm_w, w_gate, w_up, w_down):
    f = np.float32
    w_in = np.asarray(w_in, f)[0]
    pairs = []
    for g in range(2):
        o0 = g * 1536
        for p in range(4):
            cols = [w_in[:, o0 + j * 512 + p * 128: o0 + j * 512 + (p + 1) * 128] for j in range(3)]
            pairs.append(np.concatenate(cols, axis=1))
    w_pairs = np.ascontiguousarray(np.stack(pairs, 0))
    wg = np.asarray(w_gate, f)[0].reshape(8, 128, NFC, 128)
    wu = np.asarray(w_up, f)[0].reshape(8, 128, NFC, 128)
    wgu = np.concatenate([wg, wu], axis=3)
    wgu = np.ascontiguousarray(wgu.transpose(2, 1, 0, 3)).reshape(NFC, 128, 2048)
    qw = np.asarray(q_norm_w, f)[0]
    kw = np.asarray(k_norm_w, f)[0]
    qkw = np.concatenate([qw, qw, kw, kw])[None, :]
    wcat = np.concatenate([np.asarray(dil_out_norm_w, f)[0], np.asarray(sb_out_norm_w, f)[0]])
    pos = np.arange(S, dtype=f)
    inv = (f(10000.0) ** (-np.arange(0, 64, 2, dtype=f) / f(64))).astype(f)
    ang = (pos[:, None] * inv[None, :]).astype(f)
    rope = np.concatenate([np.cos(ang), np.sin(ang), -np.sin(ang)], axis=1).astype(f)
    shared = {
        "anw_bc": np.ascontiguousarray(np.broadcast_to(np.asarray(attn_norm_w, f)[0][None, :], (128, D))),
        "fnw_bc": np.ascontiguousarray(np.broadcast_to(np.asarray(ffn_norm_w, f)[0][None, :], (128, D))),
        "qkw_bc": np.ascontiguousarray(np.broadcast_to(qkw, (128, 256))),
        "wcol": np.ascontiguousarray(wcat.reshape(8, 128).T),
        "rope": rope,
        "w_pairs": w_pairs,
        "w_out": np.ascontiguousarray(np.asarray(w_out, f)[0]),
        "wgu": wgu,
        "w_down": np.ascontiguousarray(np.asarray(w_down, f)[0]),
    }
    xs = np.asarray(x, f)
    return [dict(shared, x=np.ascontiguousarray(xs[b])) for b in range(xs.shape[0])]


_NC_CACHE = {}


def kernel(**inputs):
    in_maps = make_inputs(**inputs)
    if "nc" not in _NC_CACHE:
        _NC_CACHE["nc"] = build()
    nc = _NC_CACHE["nc"]
    res = run_bass_kernel_spmd(nc, in_maps, core_ids=list(range(8)))
    return np.stack([np.asarray(r["out"], np.float32) for r in res.results], axis=0)
```

```python
from contextlib import ExitStack
import numpy as np
import concourse.bass as bass
import concourse.mybir as mybir
from concourse.bass_utils import run_bass_kernel_spmd

F32 = mybir.dt.float32
BF16 = mybir.dt.bfloat16
AF = mybir.ActivationFunctionType
ALU = mybir.AluOpType
AX = mybir.AxisListType

S = 4096
D = 1024
NB = S // 128
DFF = 2816
NFC = DFF // 128
EPS = 1e-6
ENGS = ("pe", "act", "dve", "pool", "sp")
NDMASEM = 8


class Res:
    __slots__ = ("w", "r", "pend")

    def __init__(self, init=None):
        self.w = dict(init) if init else {}
        self.r = {}
        self.pend = None


def _merge(dst, src):
    for k, v in src.items():
        if dst.get(k, 0) < v:
            dst[k] = v


class Prog:
    def __init__(self, nc):
        self.nc = nc
        self.es = ExitStack()
        self.ops = {e: [] for e in ENGS}
        self.sem = {}
        self.cnt = {}
        self.seen = {e: {} for e in ENGS}
        self.pending = {e: [] for e in ENGS}
        for e in ENGS:
            self.newsem("E_" + e)
        self.dq = {}
        for q in ("sp", "pool"):
            self.dq[q] = {"keys": [self.newsem(f"D_{q}{i}") for i in range(NDMASEM)], "i": 0}

    def newsem(self, key):
        self.sem[key] = self.es.enter_context(self.nc.semaphore(key))
        self.cnt[key] = 0
        return key

    def sb(self, name, shape, dt, es=None):
        return (es or self.es).enter_context(self.nc.sbuf_tensor(name, list(shape), dt))

    def ps(self, name, shape, dt):
        return self.es.enter_context(self.nc.psum_tensor(name, list(shape), dt))

    def fence(self):
        f = {}
        for k, v in self.cnt.items():
            if v > 0:
                f[k] = v
        for e in ENGS:
            assert not self.pending[e], "fence with pending un-tokened ops"
        return f

    def _collect(self, eng, reads, writes, deps):
        need = {}
        for r in reads:
            assert r.pend is None or r.pend == eng, "resource pending on another engine"
            _merge(need, r.w)
        for r in writes:
            assert r.pend is None or r.pend == eng, "resource pending on another engine"
            _merge(need, r.w)
            _merge(need, r.r)
        for d in deps:
            if d:
                _merge(need, d)
        ws = []
        seen = self.seen[eng]
        for k, v in need.items():
            if k == "E_pe" and eng == "pe":
                continue
            if seen.get(k, 0) >= v:
                continue
            seen[k] = v
            ws.append((k, v))
        return ws

    def _commit(self, eng, tok, reads, writes):
        allp = self.pending[eng] + [(reads, writes)]
        self.pending[eng] = []
        for rs, wsx in allp:
            for r in rs:
                _merge(r.r, tok)
                r.pend = None
            for r in wsx:
                r.w = dict(tok)
                r.r = {}
                r.pend = None

    def op(self, eng, fn, reads=(), writes=(), inc=True, deps=()):
        ws = self._collect(eng, reads, writes, deps)
        if inc:
            key = "E_" + eng
            self.cnt[key] += 1
            tok = {key: self.cnt[key]}
            self.ops[eng].append((ws, fn, (key, 1)))
            self._commit(eng, tok, reads, writes)
            return tok
        self.ops[eng].append((ws, fn, None))
        self.pending[eng].append((reads, writes))
        for r in list(reads) + list(writes):
            r.pend = eng
        return None

    def dma(self, q, fn, reads=(), writes=(), deps=()):
        assert not self.pending[q]
        dq = self.dq[q]
        key = dq["keys"][dq["i"] % NDMASEM]
        dq["i"] += 1
        prev = {key: self.cnt[key]} if self.cnt[key] else None
        ws = self._collect(q, reads, writes, list(deps) + [prev])
        self.cnt[key] += 16
        tok = {key: self.cnt[key]}
        self.ops[q].append((ws, fn, (key, 16)))
        for r in reads:
            _merge(r.r, tok)
        for r in writes:
            r.w = dict(tok)
            r.r = {}
        return tok

    def wait(self, eng, deps):
        ws = self._collect(eng, (), (), deps)
        if ws:
            self.ops[eng].append((ws, None, None))

    def emit(self):
        prog = self
        for e in ENGS:
            assert not self.pending[e], f"pending ops on {e}"

        def run(name, e):
            for ws, fn, inc in prog.ops[name]:
                for key, val in ws:
                    e.wait_ge(prog.sem[key], val)
                if fn is None:
                    continue
                ins = fn(e)
                if inc is not None:
                    ins.then_inc(prog.sem[inc[0]], inc[1])

        with self.nc.Block() as block:
            @block.tensor
            def _(e):
                run("pe", e)

            @block.scalar
            def _(e):
                run("act", e)

            @block.vector
            def _(e):
                run("dve", e)

            @block.gpsimd
            def _(e):
                run("pool", e)

            @block.sync
            def _(e):
                run("sp", e)

    def close(self):
        self.es.close()


def MM(out, lhsT, rhs, start, stop, **kw):
    return lambda e: e.matmul(out, lhsT=lhsT, rhs=rhs, start=start, stop=stop, **kw)


def TR(out, in_, idn):
    return lambda e: e.transpose(out, in_, idn)


def ACTV(out, in_, func, **kw):
    return lambda e: e.activation(out=out, in_=in_, func=func, **kw)


def TT(out, in0, in1, op):
    return lambda e: e.tensor_tensor(out=out, in0=in0, in1=in1, op=op)


def TS(out, in0, s1, op0):
    return lambda e: e.tensor_scalar(out=out, in0=in0, scalar1=s1, scalar2=None, op0=op0)


def STT(out, in0, scalar, in1, op0, op1):
    return lambda e: e.scalar_tensor_tensor(out=out, in0=in0, scalar=scalar, in1=in1, op0=op0, op1=op1)


def CP(out, in_):
    return lambda e: e.tensor_copy(out=out, in_=in_)


def RCP(out, in_):
    return lambda e: e.reciprocal(out=out, in_=in_)


def RSUM(out, in_):
    return lambda e: e.reduce_sum(out=out, in_=in_, axis=AX.X)


def DMA(out, in_):
    return lambda e: e.dma_start(out=out, in_=in_)


def MSET(ap, v):
    return lambda e: e.memset(ap, v)

PA_BLOCKS = NB
KNOB = {}


def build(pairs=tuple(range(8)), do_ffn=True, dbg=None):
    nc = bass.Bass("TRN2", target_bir_lowering=False)

    def din(name, shape, dt=F32):
        return nc.dram_tensor(name, list(shape), dt, kind="ExternalInput").ap()

    x_d = din("x", [S, D])
    anw_d = din("anw_bc", [128, D])
    fnw_d = din("fnw_bc", [128, D])
    qkw_d = din("qkw_bc", [128, 256])
    wcol_d = din("wcol", [128, 8])
    rope_d = din("rope", [S, 96])
    wp_d = din("w_pairs", [8, D, 384])
    wout_d = din("w_out", [D, D])
    wgu_d = din("wgu", [NFC, 128, 2048])
    wd_d = din("w_down", [DFF, D])
    out_d = nc.dram_tensor("out", [S, D], F32, kind="ExternalOutput").ap()
    dbg_d = None
    if dbg in ("hT", "oT"):
        dbg_d = nc.dram_tensor("dbg", [128, 8, S], BF16, kind="ExternalOutput").ap()

    P = Prog(nc)
    out_tokens = []

    pb = [P.ps(f"pb{i}", [128, 1024], F32) for i in range(4)]
    banks = []
    for i in range(4):
        banks += [pb[i][:, 0:512], pb[i][:, 512:1024]]
    bres = [Res() for _ in range(8)]
    bT = [banks[i].bitcast(BF16) for i in range(8)]

    ident = P.sb("ident", [128, 128], BF16)
    tri = P.sb("tri", [128, 128], BF16)
    sl = P.sb("sl", [128, 128], BF16)
    ones = P.sb("ones", [128, 128], BF16)
    sbm = [P.sb(f"sbm{j}", [128, 512], BF16) for j in range(4)]
    dm1 = P.sb("dm1", [128, 512], BF16)
    dmz = P.sb("dmz", [128, 512], BF16)
    dmz2 = P.sb("dmz2", [128, 512], BF16)
    wcol = P.sb("wcol_sb", [128, 8], F32)
    oT = P.sb("oT", [128, 8, S], BF16)
    r_const = Res()
    r_oT = [[Res() for _ in range(8)] for _ in range(8)]

    def SEL(t_ap, pattern, base, cm, cmp):
        return lambda e: e.affine_select(out=t_ap, in_=t_ap, pattern=pattern, base=base,
                                         channel_multiplier=cm, compare_op=cmp, fill=0.0)

    for t in (ident, tri, sl, ones, dm1, dmz, dmz2, *sbm):
        P.op("pool", MSET(t[:], 1.0), writes=[r_const])
    P.op("pool", SEL(ident[:], [[-1, 128]], 0, 1, ALU.is_equal), writes=[r_const])
    P.op("pool", SEL(tri[:], [[-1, 128]], 0, 1, ALU.is_ge), writes=[r_const])
    P.op("pool", SEL(sl[:], [[1, 128]], 0, -1, ALU.is_gt), writes=[r_const])
    for j in range(4):
        P.op("pool", SEL(sbm[j][:], [[1, 512]], -128 * j, -1, ALU.is_gt), writes=[r_const])
    for m in (dm1, dmz, dmz2):
        for slot in range(4):
            ap = m[:, slot * 128:(slot + 1) * 128]
            if slot % 2 == 0:
                P.op("pool", SEL(ap, [[-1, 128]], 0, 1, ALU.is_ge), writes=[r_const])
            else:
                P.op("pool", SEL(ap, [[1, 128]], 0, -1, ALU.is_ge), writes=[r_const])
    P.op("pool", MSET(dmz[:, 0:128], 0.0), writes=[r_const])
    P.op("pool", MSET(dmz2[:, 0:128], 0.0), writes=[r_const])
    P.op("pool", MSET(dmz2[:, 256:384], 0.0), writes=[r_const])
    P.dma("sp", DMA(wcol[:], wcol_d), writes=[r_const])
    wgu_bf = nc.dram_tensor("wgu_bf", [NFC, 128, 2048], BF16, kind="Internal").ap()
    r_wgubf = [Res() for _ in range(NFC)]

    es_att = ExitStack()
    hT = P.sb("hT", [128, 8, S], BF16, es_att)
    r_hT = [Res() for _ in range(NB)]
    Wp = P.sb("Wp", [128, 8, 384], BF16, es_att)
    r_Wp = Res()
    qkvT = P.sb("qkvT", [128, 3, S], BF16, es_att)
    r_qk = [Res() for _ in range(NB)]
    r_v = [Res() for _ in range(8)]
    sq = [P.sb(f"sq{i}", [128, 512], BF16, es_att) for i in range(4)]
    r_sq = [Res() for _ in range(4)]
    rs = P.sb("rs", [128, 512], F32, es_att)
    r_rs = Res()

    es_pa = ExitStack()
    xs = [P.sb(f"xs{i}", [128, D], F32, es_pa)[:] for i in range(4)]
    for j in range(3):
        v32 = qkvT[:, j, :].bitcast(F32)
        xs += [v32[:, 0:D], v32[:, D:2 * D]]
    NX = len(xs)
    r_xs = [Res() for _ in range(NX)]
    hn = [P.sb(f"hn{i}", [128, D], BF16, es_pa) for i in range(4)]
    r_hn = [Res() for _ in range(4)]
    junk = P.sb("junkA", [128, D], BF16, es_pa)
    r_junk = Res()
    anw = P.sb("anw", [128, D], F32, es_pa)
    r_anw = Res()
    ssA = P.sb("ssA", [128, NB], F32, es_pa)
    sdA = P.sb("sdA", [128, NB], F32, es_pa)
    rsA = P.sb("rsA", [128, NB], F32, es_pa)
    r_ssA = [Res() for _ in range(NB)]
    P.dma("sp", DMA(anw[:], anw_d), writes=[r_anw])
    BS = 2

    def paA(sb):
        for b in range(BS):
            tb = sb * BS + b
            xi = tb % NX
            tsl = slice(tb * 128, (tb + 1) * 128)
            P.dma("sp", DMA(xs[xi], x_d[tsl, :]), writes=[r_xs[xi]])
            P.op("act", ACTV(junk[:], xs[xi], AF.Square, accum_out=ssA[:, tb:tb + 1]),
                 reads=[r_xs[xi]], writes=[r_junk, r_ssA[sb]])
        c4 = slice(sb * BS, sb * BS + BS)
        P.op("act", ACTV(sdA[:, c4], ssA[:, c4], AF.Sqrt, scale=1.0 / D, bias=EPS), reads=[r_ssA[sb]], writes=[r_ssA[sb]])
        P.op("dve", RCP(rsA[:, c4], sdA[:, c4]), reads=[r_ssA[sb]], writes=[r_ssA[sb]])

    def paB(sb):
        for b in range(BS):
            tb = sb * BS + b
            xi = tb % NX
            hi = tb % 4
            tsl = slice(tb * 128, (tb + 1) * 128)
            bk = tb % 4
            P.op("dve", STT(hn[hi][:], xs[xi], rsA[:, tb:tb + 1], anw[:], ALU.mult, ALU.mult),
                 reads=[r_xs[xi], r_ssA[sb], r_anw], writes=[r_hn[hi]])
            for c in range(8):
                csl = slice(c * 128, (c + 1) * 128)
                P.op("pe", TR(bT[bk][:, csl], hn[hi][:, csl], ident[:]),
                     reads=[r_hn[hi], r_const], writes=[bres[bk]], inc=(c == 7))
            if b % 2 == 0:
                P.op("act", ACTV(hT[:, :, tsl], bT[bk].rearrange("p (c t) -> p c t", t=128), AF.Copy),
                     reads=[bres[bk]], writes=[r_hT[tb]])
            else:
                P.op("dve", CP(hT[:, :, tsl], bT[bk].rearrange("p (c t) -> p c t", t=128)),
                     reads=[bres[bk]], writes=[r_hT[tb]])

    nsb = PA_BLOCKS // BS
    for sb in range(nsb + 1):
        if sb < nsb:
            paA(sb)
        if sb >= 1:
            paB(sb - 1)
    fenceA = P.fence()
    es_pa.close()
    for r_ in r_qk + r_v:
        _merge(r_.w, fenceA)
    casts_pending = [do_ffn]

    if dbg == "hT":
        for c in range(8):
            out_tokens.append(P.dma("sp", DMA(dbg_d[:, c, :], hT[:, c, :]), reads=r_hT))

    def proj_fm(col0, dst_idx, scale, res_for_slice):
        for ts in range(8):
            bk = ts % 2
            tsl = slice(ts * 512, (ts + 1) * 512)
            for c in range(8):
                P.op("pe", MM(banks[bk], Wp[:, c, col0:col0 + 128], hT[:, c, tsl], c == 0, c == 7),
                     reads=[r_Wp] + r_hT[ts * 4:(ts + 1) * 4], writes=[bres[bk]], inc=(c == 7))
            wr = res_for_slice(ts)
            if ts % 2 == 0:
                P.op("act", ACTV(qkvT[:, dst_idx, tsl], banks[bk], AF.Copy, scale=scale),
                     reads=[bres[bk]], writes=wr)
            else:
                P.op("dve", TS(qkvT[:, dst_idx, tsl], banks[bk], scale, ALU.mult),
                     reads=[bres[bk]], writes=wr)

    def load_pair_weights(pi):
        P.dma("pool", DMA(Wp[:], wp_d[pi].rearrange("(c p) e -> p c e", p=128)), writes=[r_Wp])

    def normalize_group(gc):
        for ts in range(8):
            tsl = slice(ts * 512, (ts + 1) * 512)
            for c in range(4):
                if c < 3:
                    P.op("act", ACTV(sq[c][:], oT[:, gc + c, tsl], AF.Square),
                         reads=[r_oT[gc + c][ts]], writes=[r_sq[c]])
                else:
                    P.op("pool", TT(sq[c][:], oT[:, gc + c, tsl], oT[:, gc + c, tsl], ALU.mult),
                         reads=[r_oT[gc + c][ts]], writes=[r_sq[c]])
            bk = 2 + ts % 2
            for c in range(4):
                P.op("pe", MM(banks[bk], ones[:], sq[c][:], c == 0, c == 3),
                     reads=[r_sq[c], r_const], writes=[bres[bk]], inc=(c == 3))
            P.op("act", ACTV(rs[:], banks[bk], AF.Ln, scale=1.0 / 512, bias=EPS),
                 reads=[bres[bk]], writes=[r_rs])
            P.op("act", ACTV(rs[:], rs[:], AF.Exp, scale=-0.5), reads=[r_rs], writes=[r_rs])
            P.op("dve", TT(oT[:, gc:gc + 4, tsl], oT[:, gc:gc + 4, tsl],
                           rs[:].unsqueeze(1).to_broadcast([128, 4, 512]), ALU.mult),
                 reads=[r_rs], writes=[r_oT[gc + c][ts] for c in range(4)])

    LAG = 3
    NSL = 4
    pairsA = [p for p in pairs if p < 4]
    bank_rr = [0]

    def next_bank():
        b = bank_rr[0] % 4
        bank_rr[0] += 1
        return b

    if pairsA:
        es_ga = ExitStack()
        qkw = P.sb("qkw", [128, 256], F32, es_ga)
        r_qkw = Res(fenceA)
        P.dma("sp", DMA(qkw[:], qkw_d), writes=[r_qkw])
        P.op("dve", TS(qkw[:, 0:128], qkw[:, 0:128], 0.125, ALU.mult), writes=[r_qkw])
        fprev = fenceA
        for pi in pairsA:
            load_pair_weights(pi)
            es_p = ExitStack()
            NS = 4
            sa = [P.sb(f"sa{pi}_{i}", [128, 512], F32, es_p) for i in range(NS)]
            qn = [P.sb(f"qn{pi}_{i}", [128, 512], F32, es_p) for i in range(NS)]
            t2 = [P.sb(f"t2{pi}_{i}", [128, 512], F32, es_p) for i in range(NS)]
            qr = [P.sb(f"qr{pi}_{i}", [128, 512], BF16, es_p) for i in range(NS + 1)]
            st = [P.sb(f"st{pi}_{i}", [128, 24], F32, es_p) for i in range(NS)]
            ropeb = [P.sb(f"ropeb{pi}_{i}", [128, 2, 96], F32, es_p) for i in range(NS)]
            r_sa = [Res(fprev) for _ in range(NS)]
            r_qn = [Res(fprev) for _ in range(NS)]
            r_t2 = [Res(fprev) for _ in range(NS)]
            r_qr = [Res(fprev) for _ in range(NS + 1)]
            r_st = [Res(fprev) for _ in range(NS)]
            r_rope = [Res(fprev) for _ in range(NS)]
            w3 = lambda ap: ap.rearrange("p (s d) -> p s d", d=64)
            w4 = lambda ap: ap.rearrange("p (b s d) -> p b s d", b=2, d=32)
            w5 = lambda ap: ap.rearrange("p (b s h d) -> p b s h d", b=2, h=2, d=32)
            wb = lambda ap: ap.rearrange("p (b e) -> p b e", e=256)
            NBB = NB // 2

            def st0(bb):
                k = bb % NS
                tb0 = bb * 2
                bk = bb % 2
                P.dma("sp", DMA(ropeb[k][:], rope_d[tb0 * 128:(tb0 + 2) * 128, :].rearrange("(b p) d -> p b d", p=128)),
                      writes=[r_rope[k]])
                for b in range(2):
                    tsl = slice((tb0 + b) * 128, (tb0 + b + 1) * 128)
                    for c in range(8):
                        P.op("pe", MM(banks[bk][:, b * 256:(b + 1) * 256], hT[:, c, tsl], Wp[:, c, 0:256], c == 0, c == 7),
                             reads=[r_Wp, r_hT[tb0 + b]], writes=[bres[bk]], inc=(b == 1 and c == 7))

            def st1a(bb):
                k = bb % NS
                bk = bb % 2
                P.op("act", ACTV(sa[k][:], banks[bk], AF.Square), reads=[bres[bk]], writes=[r_sa[k]])
                P.op("dve", RSUM(st[k][:, 0:8], w3(sa[k][:])), reads=[r_sa[k]], writes=[r_st[k]])
                P.op("act", ACTV(st[k][:, 8:16], st[k][:, 0:8], AF.Sqrt, scale=1.0 / 64, bias=EPS),
                     reads=[r_st[k]], writes=[r_st[k]])

            def st1b(bb):
                k = bb % NS
                bk = bb % 2
                P.op("dve", RCP(st[k][:, 16:24], st[k][:, 8:16]), reads=[r_st[k]], writes=[r_st[k]])
                P.op("dve", TT(w3(qn[k][:]), w3(banks[bk]), st[k][:, 16:24].unsqueeze(2).to_broadcast([128, 8, 64]),
                               ALU.mult), reads=[bres[bk], r_st[k]], writes=[r_qn[k]])
                P.op("pool", TT(wb(qn[k][:]), wb(qn[k][:]), qkw[:].unsqueeze(1).to_broadcast([128, 2, 256]), ALU.mult),
                     reads=[r_qkw], writes=[r_qn[k]])

            def st2(bb):
                k = bb % NS
                cosb = ropeb[k][:, :, 0:32].unsqueeze(2).to_broadcast([128, 2, 8, 32])
                sinb = ropeb[k][:, :, 32:64].unsqueeze(2).to_broadcast([128, 2, 4, 32])
                nsinb = ropeb[k][:, :, 64:96].unsqueeze(2).to_broadcast([128, 2, 4, 32])
                P.op("dve", TT(w4(sa[k][:]), w4(qn[k][:]), cosb, ALU.mult),
                     reads=[r_qn[k], r_rope[k]], writes=[r_sa[k]])
                P.op("pool", TT(w5(t2[k][:])[:, :, :, 0, :], w5(qn[k][:])[:, :, :, 1, :], nsinb, ALU.mult),
                     reads=[r_qn[k], r_rope[k]], writes=[r_t2[k]])
                P.op("pool", TT(w5(t2[k][:])[:, :, :, 1, :], w5(qn[k][:])[:, :, :, 0, :], sinb, ALU.mult),
                     reads=[r_qn[k], r_rope[k]], writes=[r_t2[k]])

            def st3a(bb):
                k = bb % NS
                kq = bb % (NS + 1)
                P.op("dve", TT(qr[kq][:], sa[k][:], t2[k][:], ALU.add), reads=[r_sa[k], r_t2[k]], writes=[r_qr[kq]])

            def st3b(bb):
                kq = bb % (NS + 1)
                tb0 = bb * 2
                t2sl = slice(tb0 * 128, (tb0 + 2) * 128)
                bk2 = 2 + bb % 2
                for j in range(4):
                    jsl = slice(j * 128, (j + 1) * 128)
                    P.op("pe", TR(bT[bk2][:, jsl], qr[kq][:, jsl], ident[:]),
                         reads=[r_qr[kq], r_const], writes=[bres[bk2]], inc=(j == 3))
                P.op("act", ACTV(qkvT[:, 0:2, t2sl].rearrange("p j (b t) -> p j b t", t=128),
                                 bT[bk2][:, 0:512].rearrange("p (b j t) -> p j b t", j=2, t=128), AF.Copy),
                     reads=[bres[bk2]], writes=[r_qk[tb0], r_qk[tb0 + 1]])

            for it in range(NBB + 4):
                if it < NBB:
                    st0(it)
                if 0 <= it - 1 < NBB:
                    st1a(it - 1)
                if 0 <= it - 2 < NBB:
                    st2(it - 2)
                if 0 <= it - 1 < NBB:
                    st1b(it - 1)
                if 0 <= it - 3 < NBB:
                    st3a(it - 3)
                if 0 <= it - 4 < NBB:
                    st3b(it - 4)
            proj_fm(256, 2, 1.0, lambda ts: [r_v[ts]])
            fmid = P.fence()
            es_p.close()

            es_a = ExitStack()
            vaug = [P.sb(f"vaug{pi}_{i}", [128, NB, 128], BF16, es_a) for i in range(3)]
            r_vaug = [Res(fmid) for _ in range(3)]
            expS = [P.sb(f"expS{pi}_{i}", [128, 512], BF16, es_a) for i in range(NSL)]
            r_expS = [Res(fmid) for _ in range(NSL)]
            ptm = [P.sb(f"ptm{pi}_{i}", [128, 512], BF16, es_a) for i in range(NSL)]
            r_ptm = [Res(fmid) for _ in range(NSL)]
            rden = [P.sb(f"rden{pi}_{i}", [128, 512], F32, es_a) for i in range(1)]
            r_rden = [Res(fmid)]
            for i in range(3):
                P.op("pool", MSET(vaug[i][:, :, 64:128], 1.0), writes=[r_vaug[i]])
            if casts_pending[0]:
                casts_pending[0] = False
                for fc in range(NFC):
                    P.dma("pool", DMA(wgu_bf[fc], wgu_d[fc]), writes=[r_wgubf[fc]], deps=[fmid])
            VIDX = {1: 0, 4: 1, 16: 2}

            vi = 0
            gi = 0
            for hp in range(2):
                p0, p1 = hp * 64, hp * 64 + 64
                for R in range(2):
                    started = [False] * 4
                    groups = []
                    for r in (1, 4, 16):
                        nb = NB // r
                        if r == 16:
                            for c in range(0, 16, 2):
                                groups.append((r, [(c, R - 1, R), (c, R, R), (c + 1, R - 1, R), (c + 1, R, R)]))
                        else:
                            per = nb // 2
                            for c in range(r):
                                for n in range(R * per, (R + 1) * per, 2):
                                    groups.append((r, [(c, n - 1, n), (c, n, n), (c, n, n + 1), (c, n + 1, n + 1)]))

                    def build_vaug(r, vi_):
                        nb = NB // r
                        for g8 in range(4):
                            bk = next_bank()
                            for j in range(8):
                                blk = g8 * 8 + j
                                c, n = blk // nb, blk % nb
                                st_ = n * 128 * r + c
                                P.op("pe", TR(bT[bk][:, j * 64:(j + 1) * 64], qkvT[p0:p1, 2, st_:st_ + 127 * r + 1:r],
                                              ident[p0:p1, p0:p1]),
                                     reads=r_v + [r_const], writes=[bres[bk]], inc=(j == 7))
                            P.op("dve", CP(vaug[vi_][:, g8 * 8:(g8 + 1) * 8, 0:64],
                                           bT[bk][:, 0:512].rearrange("p (j d) -> p j d", d=64)),
                                 reads=[bres[bk]], writes=[r_vaug[vi_]])

                    def emit_pv(item):
                        r, slots, pslot, vi_ = item
                        nb = NB // r
                        mms = []
                        for si, (c, kb, qb) in enumerate(slots):
                            if kb < 0:
                                continue
                            vblk = c * nb + kb
                            if r == 16:
                                for u in range(4):
                                    stt = not started[u]
                                    started[u] = True
                                    q0 = si * 128 + u * 32
                                    mms.append((MM(banks[4 + u][:, c:512:16], vaug[vi_][:, vblk, :],
                                                   ptm[pslot][:, q0:q0 + 32], stt, True, skip_group_check=True), u))
                            else:
                                tok0 = qb * 128 * r + c - R * 2048
                                u = tok0 // 512
                                lo = tok0 - u * 512
                                stt = not started[u]
                                started[u] = True
                                mms.append((MM(banks[4 + u][:, lo:lo + 127 * r + 1:r], vaug[vi_][:, vblk, :],
                                               ptm[pslot][:, si * 128:(si + 1) * 128], stt, True,
                                               skip_group_check=True), u))
                        for kk, (fn, u) in enumerate(mms):
                            P.op("pe", fn, reads=[r_vaug[vi_], r_ptm[pslot]], writes=[bres[4 + u]],
                                 inc=(kk == len(mms) - 1))

                    pend = []
                    if R == 0:
                        for r_ in (1, 4, 16):
                            build_vaug(r_, VIDX[r_])
                    for (r, slots) in groups:
                        vi = VIDX[r]
                        bk = next_bank()
                        es_ = gi % NSL
                        for si, (c, kb, qb) in enumerate(slots):
                            ks = max(kb, 0) * 128 * r + c
                            qs = qb * 128 * r + c
                            P.op("pe", MM(banks[bk][:, si * 128:(si + 1) * 128],
                                          qkvT[p0:p1, 1, ks:ks + 127 * r + 1:r], qkvT[p0:p1, 0, qs:qs + 127 * r + 1:r],
                                          True, True),
                                 reads=r_qk, writes=[bres[bk]], inc=(si == 3))
                        P.op("act", ACTV(expS[es_][:], banks[bk], AF.Exp), reads=[bres[bk]], writes=[r_expS[es_]])
                        inv = [kb < 0 for (_, kb, _) in slots]
                        mk = dmz2 if (inv[0] and inv[2]) else (dmz if inv[0] else dm1)
                        P.op("dve", TT(ptm[es_][:], expS[es_][:], mk[:], ALU.mult),
                             reads=[r_expS[es_], r_const], writes=[r_ptm[es_]])
                        pend.append((r, slots, es_, vi))
                        if len(pend) > LAG:
                            emit_pv(pend.pop(0))
                        gi += 1
                    while pend:
                        emit_pv(pend.pop(0))
                    for u in range(4):
                        rd = 0
                        ts_ = R * 4 + u
                        P.op("act", ACTV(rden[rd][p0:p1, :], banks[4 + u][64:128, :], AF.Ln),
                             reads=[bres[4 + u]], writes=[r_rden[rd]])
                        P.op("act", ACTV(rden[rd][p0:p1, :], rden[rd][p0:p1, :], AF.Exp, scale=-1.0),
                             reads=[], writes=[r_rden[rd]])
                        P.op("dve", TT(oT[p0:p1, pi, ts_ * 512:(ts_ + 1) * 512], banks[4 + u][0:64, :],
                                       rden[rd][p0:p1, :], ALU.mult),
                             reads=[bres[4 + u], r_rden[rd]], writes=[r_oT[pi][ts_]])
            fprev = P.fence()
            es_a.close()
        if len(pairsA) == 4:
            normalize_group(0)
        fenceGA = P.fence()
        es_ga.close()
    else:
        fenceGA = fenceA

    pairsB = [p for p in pairs if p >= 4]
    if pairsB:
        es_gb = ExitStack()
        fB = fenceGA
        V2 = P.sb("V2", [128, NB, 128], BF16, es_gb)
        r_V2 = Res(fB)
        E2 = [P.sb(f"E2_{i}", [128, 1024], BF16, es_gb) for i in range(3)]
        r_E = [Res(fB) for _ in range(3)]
        sp2 = [P.sb(f"sp2_{i}", [128, 1024], BF16, es_gb) for i in range(3)]
        r_sp = [Res(fB) for _ in range(3)]
        X2 = [P.sb(f"X2_{i}", [128, 1024], BF16, es_gb) for i in range(2)]
        r_X = [Res(fB) for _ in range(2)]
        A2 = [P.sb(f"A2_{i}", [128, 1024], BF16, es_gb) for i in range(3)]
        r_A = [Res(fB) for _ in range(3)]

        for pi in pairsB:
            load_pair_weights(pi)
            proj_fm(0, 0, 0.125, lambda ts: r_qk[ts * 4:(ts + 1) * 4])
            proj_fm(128, 1, 1.0, lambda ts: r_qk[ts * 4:(ts + 1) * 4])
            proj_fm(256, 2, 1.0, lambda ts: [r_v[ts]])
            for g8 in range(4):
                bk = 2 + g8 % 2
                for j in range(8):
                    tb = g8 * 8 + j
                    P.op("pe", TR(bT[bk][:, j * 128:(j + 1) * 128], qkvT[:, 2, tb * 128:(tb + 1) * 128], ident[:]),
                         reads=r_v + [r_const], writes=[bres[bk]], inc=(j == 7))
                P.op("dve", CP(V2[:, g8 * 8:(g8 + 1) * 8, :], bT[bk].rearrange("p (j d) -> p j d", d=128)),
                     reads=[bres[bk]], writes=[r_V2])

            steps = [(G, J) for G in range(8) for J in range(4 * G + 3, -1, -1)]
            n = len(steps)

            def lo_of(i):
                G, J = steps[i]
                return max(0, J - 4 * G) * 128

            def hv(ap, lo):
                v = ap.rearrange("p (h t) -> p h t", h=2)
                return v if lo == 0 else v[:, :, lo:512]

            def pe_z(i):
                G, J = steps[i]
                k = i % 2
                lo = lo_of(i)
                for h in range(2):
                    hs = slice(h * 64, (h + 1) * 64)
                    P.op("pe", MM(banks[2 * k + h][:, lo:512], qkvT[hs, 1, J * 128:(J + 1) * 128],
                                  qkvT[hs, 0, G * 512 + lo:(G + 1) * 512], True, True),
                         reads=r_qk, writes=[bres[2 * k + h]], inc=(h == 1))

            def act_EL(i):
                G, J = steps[i]
                k = i % 2
                e_ = i % 3
                s_ = i % 3
                lo = lo_of(i)
                P.op("act", ACTV(hv(E2[e_][:], lo), hv(pb[k][:], lo), AF.Exp),
                     reads=[bres[2 * k], bres[2 * k + 1]], writes=[r_E[e_]])
                if J >= 4 * G:
                    ev = hv(E2[e_][:], lo)
                    P.op("dve", TT(ev, ev, sbm[J - 4 * G][:, lo:512].unsqueeze(1).to_broadcast([128, 2, 512 - lo]),
                                   ALU.mult), reads=[r_const], writes=[r_E[e_]])
                P.op("act", ACTV(hv(sp2[s_][:], lo), hv(E2[e_][:], lo), AF.Ln, bias=1.0, scale=1.0),
                     reads=[r_E[e_]], writes=[r_sp[s_]])

            def pe_C(i):
                G, J = steps[i]
                first = (J == 4 * G + 3)
                s_ = i % 3
                lo = lo_of(i)
                for h in range(2):
                    P.op("pe", MM(banks[4 + h][:, lo:512], tri[:], sp2[s_][:, h * 512 + lo:(h + 1) * 512], first, first,
                                  skip_group_check=True),
                         reads=[r_sp[s_], r_const], writes=[bres[4 + h]], inc=(first and h == 1))
                    if not first:
                        sp_prev = (i - 1) % 3
                        lop = lo_of(i - 1)
                        P.op("pe", MM(banks[4 + h][:, lop:512], sl[:], sp2[sp_prev][:, h * 512 + lop:(h + 1) * 512],
                                      False, True, skip_group_check=True),
                             reads=[r_sp[sp_prev], r_const], writes=[bres[4 + h]], inc=(h == 1))

            def act_X(i):
                e_ = i % 3
                x_ = i % 2
                a_ = i % 3
                lo = lo_of(i)
                P.op("act", ACTV(hv(X2[x_][:], lo), hv(pb[2][:], lo), AF.Exp, scale=-1.0),
                     reads=[bres[4], bres[5]], writes=[r_X[x_]])
                P.op("dve", TT(hv(A2[a_][:], lo), hv(E2[e_][:], lo), hv(X2[x_][:], lo), ALU.mult),
                     reads=[r_E[e_], r_X[x_]], writes=[r_A[a_]])

            def pe_pv(i):
                G, J = steps[i]
                first = (J == 4 * G + 3)
                last = (J == 0)
                ob = 6 + G % 2
                a_ = i % 3
                lo = lo_of(i)
                for h in range(2):
                    hs = slice(h * 64, (h + 1) * 64)
                    P.op("pe", MM(banks[ob][hs, lo:512], V2[:, J, hs], A2[a_][:, h * 512 + lo:(h + 1) * 512], first, last,
                                  skip_group_check=True),
                         reads=[r_V2, r_A[a_]], writes=[bres[ob]], inc=(h == 1))
                if last:
                    P.op("dve", CP(oT[:, pi, G * 512:(G + 1) * 512], banks[ob]),
                         reads=[bres[ob]], writes=[r_oT[pi][G]])

            for i in range(n + 3):
                if i < n:
                    pe_z(i)
                if 0 <= i - 3 < n:
                    pe_pv(i - 3)
                if 0 <= i - 1 < n:
                    act_EL(i - 1)
                if 0 <= i - 2 < n:
                    act_X(i - 2)
                if 0 <= i - 1 < n:
                    pe_C(i - 1)
        if len(pairsB) == 4:
            normalize_group(4)
        fenceGB = P.fence()
        es_gb.close()

    if dbg == "oT":
        for c in range(8):
            out_tokens.append(P.dma("sp", DMA(dbg_d[:, c, :], oT[:, c, :]), reads=r_oT[c]))
    fenceATT = P.fence()
    es_att.close()

    if do_ffn:
        fF = fenceATT
        es_f = ExitStack()
        Wd = P.sb("Wd", [128, NFC, D], BF16, es_f)
        r_Wd = Res(fF)
        Wo = P.sb("Wo", [128, 8, D], BF16, es_f)
        r_Wo = Res(fF)
        x1 = P.sb("x1", [128, 5, D], F32, es_f)
        r_x1 = [Res(fF) for _ in range(5)]
        h2T = P.sb("h2T", [128, 8, 512], BF16, es_f)
        r_h2T = [Res(fF) for _ in range(4)]
        actT = P.sb("actT", [128, NFC, 512], BF16, es_f)
        r_act = [Res(fF) for _ in range(NFC)]
        ring = [P.sb(f"ring{i}", [128, 8, 256], BF16, es_f) for i in range(3)]
        r_ring = [Res(fF) for _ in range(3)]
        fnw = P.sb("fnw", [128, D], F32, es_f)
        r_fnw = Res(fF)
        sg = [P.sb(f"sg{i}", [128, 512], BF16, es_f) for i in range(1)]
        r_sg = [Res(fF)]
        h2 = [P.sb(f"h2{i}", [128, D], BF16, es_f) for i in range(4)]
        r_h2 = [Res(fF) for _ in range(4)]
        r_ss2b = [Res(fF) for _ in range(4)]
        ss2 = P.sb("ss2", [128, 12], F32, es_f)
        r_ss2 = Res(fF)

        P.dma("pool", DMA(Wo[:], wout_d.rearrange("(c p) d -> p c d", p=128)), writes=[r_Wo])
        for c in range(8):
            P.op("dve", TS(Wo[:, c, :], Wo[:, c, :], wcol[:, c:c + 1], ALU.mult), reads=[r_const], writes=[r_Wo])
        for q4_ in range(2):
            P.dma("pool", DMA(Wd[:, q4_ * 11:(q4_ + 1) * 11, :],
                              wd_d[q4_ * 11 * 128:(q4_ + 1) * 11 * 128, :].rearrange("(f p) d -> p f d", p=128)),
                  writes=[r_Wd])
        P.dma("sp", DMA(fnw[:], fnw_d), writes=[r_fnw])

        ydr = [0]

        def yd_bank():
            b = ydr[0] % 4
            ydr[0] += 1
            return b

        def step1_block(tt, b4):
            tbk = tt * 4 + b4
            xsl = tbk % 5
            tsl = slice(tbk * 128, (tbk + 1) * 128)
            P.dma("sp", DMA(x1[:, xsl, :], x_d[tsl, :]), writes=[r_x1[xsl]])
            for half in range(2):
                bk = yd_bank()
                hsl = slice(half * 512, (half + 1) * 512)
                for c in range(8):
                    P.op("pe", MM(banks[bk], oT[:, c, tsl], Wo[:, c, hsl], c == 0, c == 7),
                         reads=[r_Wo] + [r_oT[cc][tt] for cc in range(8)], writes=[bres[bk]], inc=(c == 7))
                P.op("dve", TT(x1[:, xsl, hsl], banks[bk], x1[:, xsl, hsl], ALU.add),
                     reads=[bres[bk]], writes=[r_x1[xsl]])
            P.op("act", ACTV(h2[b4][:], x1[:, xsl, :], AF.Square, accum_out=ss2[:, b4:b4 + 1]),
                 reads=[r_x1[xsl]], writes=[r_h2[b4], r_ss2b[b4]])
            P.op("act", ACTV(ss2[:, 4 + b4:5 + b4], ss2[:, b4:b4 + 1], AF.Sqrt, scale=1.0 / D, bias=EPS),
                 reads=[r_ss2b[b4]], writes=[r_ss2b[b4]])
            P.op("dve", RCP(ss2[:, 8 + b4:9 + b4], ss2[:, 4 + b4:5 + b4]), reads=[r_ss2b[b4]], writes=[r_ss2b[b4]])
            P.op("dve", STT(h2[b4][:], x1[:, xsl, :], ss2[:, 8 + b4:9 + b4], fnw[:], ALU.mult, ALU.mult),
                 reads=[r_x1[xsl], r_ss2b[b4], r_fnw], writes=[r_h2[b4]])

        def step1_tr(b4):
            bk = yd_bank()
            for c in range(8):
                csl = slice(c * 128, (c + 1) * 128)
                P.op("pe", TR(bT[bk][:, csl], h2[b4][:, csl], ident[:]),
                     reads=[r_h2[b4], r_const], writes=[bres[bk]], inc=(c == 7))
            P.op("act", ACTV(h2T[:, :, b4 * 128:(b4 + 1) * 128], bT[bk].rearrange("p (c t) -> p c t", t=128), AF.Copy),
                 reads=[bres[bk]], writes=[r_h2T[b4]])

        def finish_norm(tt):
            pass

        gur = 0
        for b4 in range(4):
            step1_block(0, b4)
            if b4 >= 1:
                step1_tr(b4 - 1)
        step1_tr(3)
        for tt in range(8):
            finish_norm(tt)
            for fc in range(NFC):
                rg = fc % 3
                P.dma("pool", DMA(ring[rg][:], wgu_bf[fc].rearrange("p (c j) -> p c j", j=256)),
                      reads=[r_wgubf[fc]], writes=[r_ring[rg]])
                bg = 4 + (gur % 2) * 2
                bu = bg + 1
                gur += 1
                for (bk, off) in ((bg, 0), (bu, 128)):
                    for c in range(8):
                        P.op("pe", MM(banks[bk], ring[rg][:, c, off:off + 128], h2T[:, c, :], c == 0, c == 7),
                             reads=[r_ring[rg]] + r_h2T, writes=[bres[bk]], inc=(c == 7))
                s_ = 0
                P.op("act", ACTV(sg[s_][:], banks[bg], AF.Silu), reads=[bres[bg]], writes=[r_sg[s_]])
                P.op("dve", TT(actT[:, fc, :], banks[bu], sg[s_][:], ALU.mult),
                     reads=[bres[bu], r_sg[s_]], writes=[r_act[fc]])
            nxt = tt + 1 < 8
            for b4 in range(4):
                tbk = tt * 4 + b4
                xsl = tbk % 5
                if nxt:
                    step1_block(tt + 1, b4)
                for half in range(2):
                    bk = yd_bank()
                    hsl = slice(half * 512, (half + 1) * 512)
                    for fc in range(NFC):
                        P.op("pe", MM(banks[bk], actT[:, fc, b4 * 128:(b4 + 1) * 128], Wd[:, fc, hsl],
                                      fc == 0, fc == NFC - 1),
                             reads=[r_act[fc], r_Wd], writes=[bres[bk]], inc=(fc == NFC - 1))
                    P.op("dve", TT(x1[:, xsl, hsl], banks[bk], x1[:, xsl, hsl], ALU.add),
                         reads=[bres[bk]], writes=[r_x1[xsl]])
                out_tokens.append(P.dma("sp", DMA(out_d[tbk * 128:(tbk + 1) * 128, :], x1[:, xsl, :]),
                                        reads=[r_x1[xsl]]))
                if nxt and b4 >= 1:
                    step1_tr(b4 - 1)
            if nxt:
                step1_tr(3)
        es_f.close()

    P.wait("sp", out_tokens)
    P.emit()
    P.close()
    return nc


def make_inputs(x, attn_norm_w, w_in, q_norm_w, k_norm_w, dil_out_norm_w, sb_out_norm_w,
                w_out, ffn_norm_w, w_gate, w_up, w_down):
    f = np.float32
    w_in = np.asarray(w_in, f)[0]
    pairs = []
    for g in range(2):
        o0 = g * 1536
        for p in range(4):
            cols = [w_in[:, o0 + j * 512 + p * 128: o0 + j * 512 + (p + 1) * 128] for j in range(3)]
            pairs.append(np.concatenate(cols, axis=1))
    w_pairs = np.ascontiguousarray(np.stack(pairs, 0))
    wg = np.asarray(w_gate, f)[0].reshape(8, 128, NFC, 128)
    wu = np.asarray(w_up, f)[0].reshape(8, 128, NFC, 128)
    wgu = np.concatenate([wg, wu], axis=3)
    wgu = np.ascontiguousarray(wgu.transpose(2, 1, 0, 3)).reshape(NFC, 128, 2048)
    qw = np.asarray(q_norm_w, f)[0]
    kw = np.asarray(k_norm_w, f)[0]
    qkw = np.concatenate([qw, qw, kw, kw])[None, :]
    wcat = np.concatenate([np.asarray(dil_out_norm_w, f)[0], np.asarray(sb_out_norm_w, f)[0]])
    pos = np.arange(S, dtype=f)
    inv = (f(10000.0) ** (-np.arange(0, 64, 2, dtype=f) / f(64))).astype(f)
    ang = (pos[:, None] * inv[None, :]).astype(f)
    rope = np.concatenate([np.cos(ang), np.sin(ang), -np.sin(ang)], axis=1).astype(f)
    shared = {
        "anw_bc": np.ascontiguousarray(np.broadcast_to(np.asarray(attn_norm_w, f)[0][None, :], (128, D))),
        "fnw_bc": np.ascontiguousarray(np.broadcast_to(np.asarray(ffn_norm_w, f)[0][None, :], (128, D))),
        "qkw_bc": np.ascontiguousarray(np.broadcast_to(qkw, (128, 256))),
        "wcol": np.ascontiguousarray(wcat.reshape(8, 128).T),
        "rope": rope,
        "w_pairs": w_pairs,
        "w_out": np.ascontiguousarray(np.asarray(w_out, f)[0]),
        "wgu": wgu,
        "w_down": np.ascontiguousarray(np.asarray(w_down, f)[0]),
    }
    xs = np.asarray(x, f)
    return [dict(shared, x=np.ascontiguousarray(xs[b])) for b in range(xs.shape[0])]


_NC_CACHE = {}


def kernel(**inputs):
    in_maps = make_inputs(**inputs)
    if "nc" not in _NC_CACHE:
        _NC_CACHE["nc"] = build()
    nc = _NC_CACHE["nc"]
    res = run_bass_kernel_spmd(nc, in_maps, core_ids=list(range(8)))
    return np.stack([np.asarray(r["out"], np.float32) for r in res.results], axis=0)
```

```python
from contextlib import ExitStack
import numpy as np
import concourse.bass as bass
import concourse.mybir as mybir
from concourse.bass_utils import run_bass_kernel_spmd

F32 = mybir.dt.float32
BF16 = mybir.dt.bfloat16
AF = mybir.ActivationFunctionType
ALU = mybir.AluOpType
AX = mybir.AxisListType

S = 4096
D = 1024
NB = S // 128
DFF = 2816
NFC = DFF // 128
EPS = 1e-6
ENGS = ("pe", "act", "dve", "pool", "sp")
NDMASEM = 8


class Res:
    __slots__ = ("w", "r", "pend")

    def __init__(self, init=None):
        self.w = dict(init) if init else {}
        self.r = {}
        self.pend = None


def _merge(dst, src):
    for k, v in src.items():
        if dst.get(k, 0) < v:
            dst[k] = v


class Prog:
    def __init__(self, nc):
        self.nc = nc
        self.es = ExitStack()
        self.ops = {e: [] for e in ENGS}
        self.sem = {}
        self.cnt = {}
        self.seen = {e: {} for e in ENGS}
        self.pending = {e: [] for e in ENGS}
        for e in ENGS:
            self.newsem("E_" + e)
        self.dq = {}
        for q in ("sp", "pool"):
            self.dq[q] = {"keys": [self.newsem(f"D_{q}{i}") for i in range(NDMASEM)], "i": 0}

    def newsem(self, key):
        self.sem[key] = self.es.enter_context(self.nc.semaphore(key))
        self.cnt[key] = 0
        return key

    def sb(self, name, shape, dt, es=None):
        return (es or self.es).enter_context(self.nc.sbuf_tensor(name, list(shape), dt))

    def ps(self, name, shape, dt):
        return self.es.enter_context(self.nc.psum_tensor(name, list(shape), dt))

    def fence(self):
        f = {}
        for k, v in self.cnt.items():
            if v > 0:
                f[k] = v
        for e in ENGS:
            assert not self.pending[e], "fence with pending un-tokened ops"
        return f

    def _collect(self, eng, reads, writes, deps):
        need = {}
        for r in reads:
            assert r.pend is None or r.pend == eng, "resource pending on another engine"
            _merge(need, r.w)
        for r in writes:
            assert r.pend is None or r.pend == eng, "resource pending on another engine"
            _merge(need, r.w)
            _merge(need, r.r)
        for d in deps:
            if d:
                _merge(need, d)
        ws = []
        seen = self.seen[eng]
        for k, v in need.items():
            if k == "E_pe" and eng == "pe":
                continue
            if seen.get(k, 0) >= v:
                continue
            seen[k] = v
            ws.append((k, v))
        return ws

    def _commit(self, eng, tok, reads, writes):
        allp = self.pending[eng] + [(reads, writes)]
        self.pending[eng] = []
        for rs, wsx in allp:
            for r in rs:
                _merge(r.r, tok)
                r.pend = None
            for r in wsx:
                r.w = dict(tok)
                r.r = {}
                r.pend = None

    def op(self, eng, fn, reads=(), writes=(), inc=True, deps=()):
        ws = self._collect(eng, reads, writes, deps)
        if inc:
            key = "E_" + eng
            self.cnt[key] += 1
            tok = {key: self.cnt[key]}
            self.ops[eng].append((ws, fn, (key, 1)))
            self._commit(eng, tok, reads, writes)
            return tok
        self.ops[eng].append((ws, fn, None))
        self.pending[eng].append((reads, writes))
        for r in list(reads) + list(writes):
            r.pend = eng
        return None

    def dma(self, q, fn, reads=(), writes=(), deps=()):
        assert not self.pending[q]
        dq = self.dq[q]
        key = dq["keys"][dq["i"] % NDMASEM]
        dq["i"] += 1
        prev = {key: self.cnt[key]} if self.cnt[key] else None
        ws = self._collect(q, reads, writes, list(deps) + [prev])
        self.cnt[key] += 16
        tok = {key: self.cnt[key]}
        self.ops[q].append((ws, fn, (key, 16)))
        for r in reads:
            _merge(r.r, tok)
        for r in writes:
            r.w = dict(tok)
            r.r = {}
        return tok

    def wait(self, eng, deps):
        ws = self._collect(eng, (), (), deps)
        if ws:
            self.ops[eng].append((ws, None, None))

    def emit(self):
        prog = self
        for e in ENGS:
            assert not self.pending[e], f"pending ops on {e}"

        def run(name, e):
            for ws, fn, inc in prog.ops[name]:
                for key, val in ws:
                    e.wait_ge(prog.sem[key], val)
                if fn is None:
                    continue
                ins = fn(e)
                if inc is not None:
                    ins.then_inc(prog.sem[inc[0]], inc[1])

        with self.nc.Block() as block:
            @block.tensor
            def _(e):
                run("pe", e)

            @block.scalar
            def _(e):
                run("act", e)

            @block.vector
            def _(e):
                run("dve", e)

            @block.gpsimd
            def _(e):
                run("pool", e)

            @block.sync
            def _(e):
                run("sp", e)

    def close(self):
        self.es.close()


def MM(out, lhsT, rhs, start, stop, **kw):
    return lambda e: e.matmul(out, lhsT=lhsT, rhs=rhs, start=start, stop=stop, **kw)


def TR(out, in_, idn):
    return lambda e: e.transpose(out, in_, idn)


def ACTV(out, in_, func, **kw):
    return lambda e: e.activation(out=out, in_=in_, func=func, **kw)


def TT(out, in0, in1, op):
    return lambda e: e.tensor_tensor(out=out, in0=in0, in1=in1, op=op)


def TS(out, in0, s1, op0):
    return lambda e: e.tensor_scalar(out=out, in0=in0, scalar1=s1, scalar2=None, op0=op0)


def STT(out, in0, scalar, in1, op0, op1):
    return lambda e: e.scalar_tensor_tensor(out=out, in0=in0, scalar=scalar, in1=in1, op0=op0, op1=op1)


def CP(out, in_):
    return lambda e: e.tensor_copy(out=out, in_=in_)


def RCP(out, in_):
    return lambda e: e.reciprocal(out=out, in_=in_)


def RSUM(out, in_):
    return lambda e: e.reduce_sum(out=out, in_=in_, axis=AX.X)


def DMA(out, in_):
    return lambda e: e.dma_start(out=out, in_=in_)


def MSET(ap, v):
    return lambda e: e.memset(ap, v)

PA_BLOCKS = NB
KNOB = {}


def build(pairs=tuple(range(8)), do_ffn=True, dbg=None):
    nc = bass.Bass("TRN2", target_bir_lowering=False)

    def din(name, shape, dt=F32):
        return nc.dram_tensor(name, list(shape), dt, kind="ExternalInput").ap()

    x_d = din("x", [S, D])
    anw_d = din("anw_bc", [128, D])
    fnw_d = din("fnw_bc", [128, D])
    qkw_d = din("qkw_bc", [128, 256])
    wcol_d = din("wcol", [128, 8])
    rope_d = din("rope", [S, 96])
    wp_d = din("w_pairs", [8, D, 384])
    wout_d = din("w_out", [D, D])
    wgu_d = din("wgu", [NFC, 128, 2048])
    wd_d = din("w_down", [DFF, D])
    out_d = nc.dram_tensor("out", [S, D], F32, kind="ExternalOutput").ap()
    dbg_d = None
    if dbg in ("hT", "oT"):
        dbg_d = nc.dram_tensor("dbg", [128, 8, S], BF16, kind="ExternalOutput").ap()

    P = Prog(nc)
    out_tokens = []

    pb = [P.ps(f"pb{i}", [128, 1024], F32) for i in range(4)]
    banks = []
    for i in range(4):
        banks += [pb[i][:, 0:512], pb[i][:, 512:1024]]
    bres = [Res() for _ in range(8)]
    bT = [banks[i].bitcast(BF16) for i in range(8)]

    ident = P.sb("ident", [128, 128], BF16)
    tri = P.sb("tri", [128, 128], BF16)
    sl = P.sb("sl", [128, 128], BF16)
    ones = P.sb("ones", [128, 128], BF16)
    sbm = [P.sb(f"sbm{j}", [128, 512], BF16) for j in range(4)]
    dm1 = P.sb("dm1", [128, 512], BF16)
    dmz = P.sb("dmz", [128, 512], BF16)
    dmz2 = P.sb("dmz2", [128, 512], BF16)
    wcol = P.sb("wcol_sb", [128, 8], F32)
    oT = P.sb("oT", [128, 8, S], BF16)
    r_const = Res()
    r_oT = [[Res() for _ in range(8)] for _ in range(8)]

    def SEL(t_ap, pattern, base, cm, cmp):
        return lambda e: e.affine_select(out=t_ap, in_=t_ap, pattern=pattern, base=base,
                                         channel_multiplier=cm, compare_op=cmp, fill=0.0)

    for t in (ident, tri, sl, ones, dm1, dmz, dmz2, *sbm):
        P.op("pool", MSET(t[:], 1.0), writes=[r_const])
    P.op("pool", SEL(ident[:], [[-1, 128]], 0, 1, ALU.is_equal), writes=[r_const])
    P.op("pool", SEL(tri[:], [[-1, 128]], 0, 1, ALU.is_ge), writes=[r_const])
    P.op("pool", SEL(sl[:], [[1, 128]], 0, -1, ALU.is_gt), writes=[r_const])
    for j in range(4):
        P.op("pool", SEL(sbm[j][:], [[1, 512]], -128 * j, -1, ALU.is_gt), writes=[r_const])
    for m in (dm1, dmz, dmz2):
        for slot in range(4):
            ap = m[:, slot * 128:(slot + 1) * 128]
            if slot % 2 == 0:
                P.op("pool", SEL(ap, [[-1, 128]], 0, 1, ALU.is_ge), writes=[r_const])
            else:
                P.op("pool", SEL(ap, [[1, 128]], 0, -1, ALU.is_ge), writes=[r_const])
    P.op("pool", MSET(dmz[:, 0:128], 0.0), writes=[r_const])
    P.op("pool", MSET(dmz2[:, 0:128], 0.0), writes=[r_const])
    P.op("pool", MSET(dmz2[:, 256:384], 0.0), writes=[r_const])
    P.dma("sp", DMA(wcol[:], wcol_d), writes=[r_const])
    wgu_bf = nc.dram_tensor("wgu_bf", [NFC, 128, 2048], BF16, kind="Internal").ap()
    r_wgubf = [Res() for _ in range(NFC)]

    es_att = ExitStack()
    hT = P.sb("hT", [128, 8, S], BF16, es_att)
    r_hT = [Res() for _ in range(NB)]
    Wp = P.sb("Wp", [128, 8, 384], BF16, es_att)
    r_Wp = Res()
    qkvT = P.sb("qkvT", [128, 3, S], BF16, es_att)
    r_qk = [Res() for _ in range(NB)]
    r_v = [Res() for _ in range(8)]
    sq = [P.sb(f"sq{i}", [128, 512], BF16, es_att) for i in range(4)]
    r_sq = [Res() for _ in range(4)]
    rs = P.sb("rs", [128, 512], F32, es_att)
    r_rs = Res()

    es_pa = ExitStack()
    xs = [P.sb(f"xs{i}", [128, D], F32, es_pa)[:] for i in range(4)]
    for j in range(3):
        v32 = qkvT[:, j, :].bitcast(F32)
        xs += [v32[:, 0:D], v32[:, D:2 * D]]
    NX = len(xs)
    r_xs = [Res() for _ in range(NX)]
    hn = [P.sb(f"hn{i}", [128, D], BF16, es_pa) for i in range(4)]
    r_hn = [Res() for _ in range(4)]
    junk = P.sb("junkA", [128, D], BF16, es_pa)
    r_junk = Res()
    anw = P.sb("anw", [128, D], F32, es_pa)
    r_anw = Res()
    ssA = P.sb("ssA", [128, NB], F32, es_pa)
    sdA = P.sb("sdA", [128, NB], F32, es_pa)
    rsA = P.sb("rsA", [128, NB], F32, es_pa)
    r_ssA = [Res() for _ in range(NB)]
    P.dma("sp", DMA(anw[:], anw_d), writes=[r_anw])
    BS = 2

    def paA1(sb):
        for b in range(BS):
            tb = sb * BS + b
            xi = tb % NX
            tsl = slice(tb * 128, (tb + 1) * 128)
            P.dma("sp", DMA(xs[xi], x_d[tsl, :]), writes=[r_xs[xi]])
            P.op("act", ACTV(junk[:], xs[xi], AF.Square, accum_out=ssA[:, tb:tb + 1]),
                 reads=[r_xs[xi]], writes=[r_junk, r_ssA[sb]])
        c4 = slice(sb * BS, sb * BS + BS)
        P.op("act", ACTV(sdA[:, c4], ssA[:, c4], AF.Sqrt, scale=1.0 / D, bias=EPS), reads=[r_ssA[sb]], writes=[r_ssA[sb]])

    def paA2(sb):
        c4 = slice(sb * BS, sb * BS + BS)
        P.op("dve", RCP(rsA[:, c4], sdA[:, c4]), reads=[r_ssA[sb]], writes=[r_ssA[sb]])

    def paB1(sb):
        for b in range(BS):
            tb = sb * BS + b
            xi = tb % NX
            hi = tb % 4
            bk = tb % 4
            P.op("dve", STT(hn[hi][:], xs[xi], rsA[:, tb:tb + 1], anw[:], ALU.mult, ALU.mult),
                 reads=[r_xs[xi], r_ssA[sb], r_anw], writes=[r_hn[hi]])
            for c in range(8):
                csl = slice(c * 128, (c + 1) * 128)
                P.op("pe", TR(bT[bk][:, csl], hn[hi][:, csl], ident[:]),
                     reads=[r_hn[hi], r_const], writes=[bres[bk]], inc=(c == 7))

    def paB2(sb):
        for b in range(BS):
            tb = sb * BS + b
            bk = tb % 4
            tsl = slice(tb * 128, (tb + 1) * 128)
            if b % 2 == 0:
                P.op("act", ACTV(hT[:, :, tsl], bT[bk].rearrange("p (c t) -> p c t", t=128), AF.Copy),
                     reads=[bres[bk]], writes=[r_hT[tb]])
            else:
                P.op("dve", CP(hT[:, :, tsl], bT[bk].rearrange("p (c t) -> p c t", t=128)),
                     reads=[bres[bk]], writes=[r_hT[tb]])

    nsb = PA_BLOCKS // BS
    for sb in range(nsb + 1):
        if sb >= 1:
            paB1(sb - 1)
        if sb < nsb:
            paA1(sb)
        if sb >= 1:
            paB2(sb - 1)
        if sb < nsb:
            paA2(sb)
    fenceA = P.fence()
    es_pa.close()
    for r_ in r_qk + r_v:
        _merge(r_.w, fenceA)
    casts_pending = [do_ffn]

    if dbg == "hT":
        for c in range(8):
            out_tokens.append(P.dma("sp", DMA(dbg_d[:, c, :], hT[:, c, :]), reads=r_hT))

    def proj_fm(col0, dst_idx, scale, res_for_slice):
        for ts in range(8):
            bk = ts % 2
            tsl = slice(ts * 512, (ts + 1) * 512)
            for c in range(8):
                P.op("pe", MM(banks[bk], Wp[:, c, col0:col0 + 128], hT[:, c, tsl], c == 0, c == 7),
                     reads=[r_Wp] + r_hT[ts * 4:(ts + 1) * 4], writes=[bres[bk]], inc=(c == 7))
            wr = res_for_slice(ts)
            if ts % 2 == 0:
                P.op("act", ACTV(qkvT[:, dst_idx, tsl], banks[bk], AF.Copy, scale=scale),
                     reads=[bres[bk]], writes=wr)
            else:
                P.op("dve", TS(qkvT[:, dst_idx, tsl], banks[bk], scale, ALU.mult),
                     reads=[bres[bk]], writes=wr)

    def load_pair_weights(pi):
        P.dma("pool", DMA(Wp[:], wp_d[pi].rearrange("(c p) e -> p c e", p=128)), writes=[r_Wp])

    def normalize_group(gc):
        for ts in range(8):
            tsl = slice(ts * 512, (ts + 1) * 512)
            for c in range(4):
                if c < 3:
                    P.op("act", ACTV(sq[c][:], oT[:, gc + c, tsl], AF.Square),
                         reads=[r_oT[gc + c][ts]], writes=[r_sq[c]])
                else:
                    P.op("pool", TT(sq[c][:], oT[:, gc + c, tsl], oT[:, gc + c, tsl], ALU.mult),
                         reads=[r_oT[gc + c][ts]], writes=[r_sq[c]])
            bk = 2 + ts % 2
            for c in range(4):
                P.op("pe", MM(banks[bk], ones[:], sq[c][:], c == 0, c == 3),
                     reads=[r_sq[c], r_const], writes=[bres[bk]], inc=(c == 3))
            P.op("act", ACTV(rs[:], banks[bk], AF.Ln, scale=1.0 / 512, bias=EPS),
                 reads=[bres[bk]], writes=[r_rs])
            P.op("act", ACTV(rs[:], rs[:], AF.Exp, scale=-0.5), reads=[r_rs], writes=[r_rs])
            P.op("dve", TT(oT[:, gc:gc + 4, tsl], oT[:, gc:gc + 4, tsl],
                           rs[:].unsqueeze(1).to_broadcast([128, 4, 512]), ALU.mult),
                 reads=[r_rs], writes=[r_oT[gc + c][ts] for c in range(4)])

    LAG = 3
    NSL = 4
    pairsA = [p for p in pairs if p < 4]
    bank_rr = [0]

    def next_bank():
        b = bank_rr[0] % 4
        bank_rr[0] += 1
        return b

    if pairsA:
        es_ga = ExitStack()
        qkw = P.sb("qkw", [128, 256], F32, es_ga)
        r_qkw = Res(fenceA)
        P.dma("sp", DMA(qkw[:], qkw_d), writes=[r_qkw])
        P.op("dve", TS(qkw[:, 0:128], qkw[:, 0:128], 0.125, ALU.mult), writes=[r_qkw])
        fprev = fenceA
        for pi in pairsA:
            load_pair_weights(pi)
            es_p = ExitStack()
            NS = 4
            sa = [P.sb(f"sa{pi}_{i}", [128, 512], F32, es_p) for i in range(NS)]
            qn = [P.sb(f"qn{pi}_{i}", [128, 512], F32, es_p) for i in range(NS)]
            t2 = [P.sb(f"t2{pi}_{i}", [128, 512], F32, es_p) for i in range(NS)]
            qr = [P.sb(f"qr{pi}_{i}", [128, 512], BF16, es_p) for i in range(NS + 1)]
            st = [P.sb(f"st{pi}_{i}", [128, 24], F32, es_p) for i in range(NS)]
            ropeb = [P.sb(f"ropeb{pi}_{i}", [128, 2, 96], F32, es_p) for i in range(NS)]
            r_sa = [Res(fprev) for _ in range(NS)]
            r_qn = [Res(fprev) for _ in range(NS)]
            r_t2 = [Res(fprev) for _ in range(NS)]
            r_qr = [Res(fprev) for _ in range(NS + 1)]
            r_st = [Res(fprev) for _ in range(NS)]
            r_rope = [Res(fprev) for _ in range(NS)]
            w3 = lambda ap: ap.rearrange("p (s d) -> p s d", d=64)
            w4 = lambda ap: ap.rearrange("p (b s d) -> p b s d", b=2, d=32)
            w5 = lambda ap: ap.rearrange("p (b s h d) -> p b s h d", b=2, h=2, d=32)
            wb = lambda ap: ap.rearrange("p (b e) -> p b e", e=256)
            NBB = NB // 2

            def st0(bb):
                k = bb % NS
                tb0 = bb * 2
                bk = bb % 2
                P.dma("sp", DMA(ropeb[k][:], rope_d[tb0 * 128:(tb0 + 2) * 128, :].rearrange("(b p) d -> p b d", p=128)),
                      writes=[r_rope[k]])
                for b in range(2):
                    tsl = slice((tb0 + b) * 128, (tb0 + b + 1) * 128)
                    for c in range(8):
                        P.op("pe", MM(banks[bk][:, b * 256:(b + 1) * 256], hT[:, c, tsl], Wp[:, c, 0:256], c == 0, c == 7),
                             reads=[r_Wp, r_hT[tb0 + b]], writes=[bres[bk]], inc=(b == 1 and c == 7))

            def st1a(bb):
                k = bb % NS
                bk = bb % 2
                P.op("act", ACTV(sa[k][:], banks[bk], AF.Square), reads=[bres[bk]], writes=[r_sa[k]])
                P.op("dve", RSUM(st[k][:, 0:8], w3(sa[k][:])), reads=[r_sa[k]], writes=[r_st[k]])
                P.op("act", ACTV(st[k][:, 8:16], st[k][:, 0:8], AF.Sqrt, scale=1.0 / 64, bias=EPS),
                     reads=[r_st[k]], writes=[r_st[k]])

            def st1b(bb):
                k = bb % NS
                bk = bb % 2
                P.op("dve", RCP(st[k][:, 16:24], st[k][:, 8:16]), reads=[r_st[k]], writes=[r_st[k]])
                P.op("dve", TT(w3(qn[k][:]), w3(banks[bk]), st[k][:, 16:24].unsqueeze(2).to_broadcast([128, 8, 64]),
                               ALU.mult), reads=[bres[bk], r_st[k]], writes=[r_qn[k]])
                P.op("pool", TT(wb(qn[k][:]), wb(qn[k][:]), qkw[:].unsqueeze(1).to_broadcast([128, 2, 256]), ALU.mult),
                     reads=[r_qkw], writes=[r_qn[k]])

            def st2(bb):
                k = bb % NS
                cosb = ropeb[k][:, :, 0:32].unsqueeze(2).to_broadcast([128, 2, 8, 32])
                sinb = ropeb[k][:, :, 32:64].unsqueeze(2).to_broadcast([128, 2, 4, 32])
                nsinb = ropeb[k][:, :, 64:96].unsqueeze(2).to_broadcast([128, 2, 4, 32])
                P.op("dve", TT(w4(sa[k][:]), w4(qn[k][:]), cosb, ALU.mult),
                     reads=[r_qn[k], r_rope[k]], writes=[r_sa[k]])
                P.op("pool", TT(w5(t2[k][:])[:, :, :, 0, :], w5(qn[k][:])[:, :, :, 1, :], nsinb, ALU.mult),
                     reads=[r_qn[k], r_rope[k]], writes=[r_t2[k]])
                P.op("pool", TT(w5(t2[k][:])[:, :, :, 1, :], w5(qn[k][:])[:, :, :, 0, :], sinb, ALU.mult),
                     reads=[r_qn[k], r_rope[k]], writes=[r_t2[k]])

            def st3a(bb):
                k = bb % NS
                kq = bb % (NS + 1)
                P.op("dve", TT(qr[kq][:], sa[k][:], t2[k][:], ALU.add), reads=[r_sa[k], r_t2[k]], writes=[r_qr[kq]])

            def st3b(bb):
                kq = bb % (NS + 1)
                tb0 = bb * 2
                t2sl = slice(tb0 * 128, (tb0 + 2) * 128)
                bk2 = 2 + bb % 2
                for j in range(4):
                    jsl = slice(j * 128, (j + 1) * 128)
                    P.op("pe", TR(bT[bk2][:, jsl], qr[kq][:, jsl], ident[:]),
                         reads=[r_qr[kq], r_const], writes=[bres[bk2]], inc=(j == 3))
                P.op("act", ACTV(qkvT[:, 0:2, t2sl].rearrange("p j (b t) -> p j b t", t=128),
                                 bT[bk2][:, 0:512].rearrange("p (b j t) -> p j b t", j=2, t=128), AF.Copy),
                     reads=[bres[bk2]], writes=[r_qk[tb0], r_qk[tb0 + 1]])

            for it in range(NBB + 4):
                if it < NBB:
                    st0(it)
                if 0 <= it - 1 < NBB:
                    st1a(it - 1)
                if 0 <= it - 2 < NBB:
                    st2(it - 2)
                if 0 <= it - 1 < NBB:
                    st1b(it - 1)
                if 0 <= it - 3 < NBB:
                    st3a(it - 3)
                if 0 <= it - 4 < NBB:
                    st3b(it - 4)
            proj_fm(256, 2, 1.0, lambda ts: [r_v[ts]])
            fmid = P.fence()
            es_p.close()

            es_a = ExitStack()
            vaug = [P.sb(f"vaug{pi}_{i}", [128, NB, 128], BF16, es_a) for i in range(3)]
            r_vaug = [Res(fmid) for _ in range(3)]
            expS = [P.sb(f"expS{pi}_{i}", [128, 512], BF16, es_a) for i in range(NSL)]
            r_expS = [Res(fmid) for _ in range(NSL)]
            ptm = [P.sb(f"ptm{pi}_{i}", [128, 512], BF16, es_a) for i in range(NSL)]
            r_ptm = [Res(fmid) for _ in range(NSL)]
            rden = [P.sb(f"rden{pi}_{i}", [128, 512], F32, es_a) for i in range(1)]
            r_rden = [Res(fmid)]
            for i in range(3):
                P.op("pool", MSET(vaug[i][:, :, 64:128], 1.0), writes=[r_vaug[i]])
            if casts_pending[0]:
                casts_pending[0] = False
                for fc in range(NFC):
                    P.dma("pool", DMA(wgu_bf[fc], wgu_d[fc]), writes=[r_wgubf[fc]], deps=[fmid])
            VIDX = {1: 0, 4: 1, 16: 2}

            vi = 0
            gi = 0
            for hp in range(2):
                p0, p1 = hp * 64, hp * 64 + 64
                for R in range(2):
                    started = [False] * 4
                    groups = []
                    for r in (1, 4, 16):
                        nb = NB // r
                        if r == 16:
                            for c in range(0, 16, 2):
                                groups.append((r, [(c, R - 1, R), (c, R, R), (c + 1, R - 1, R), (c + 1, R, R)]))
                        else:
                            per = nb // 2
                            for c in range(r):
                                for n in range(R * per, (R + 1) * per, 2):
                                    groups.append((r, [(c, n - 1, n), (c, n, n), (c, n, n + 1), (c, n + 1, n + 1)]))

                    def build_vaug(r, vi_):
                        nb = NB // r
                        for g8 in range(4):
                            bk = next_bank()
                            for j in range(8):
                                blk = g8 * 8 + j
                                c, n = blk // nb, blk % nb
                                st_ = n * 128 * r + c
                                P.op("pe", TR(bT[bk][:, j * 64:(j + 1) * 64], qkvT[p0:p1, 2, st_:st_ + 127 * r + 1:r],
                                              ident[p0:p1, p0:p1]),
                                     reads=r_v + [r_const], writes=[bres[bk]], inc=(j == 7))
                            P.op("dve", CP(vaug[vi_][:, g8 * 8:(g8 + 1) * 8, 0:64],
                                           bT[bk][:, 0:512].rearrange("p (j d) -> p j d", d=64)),
                                 reads=[bres[bk]], writes=[r_vaug[vi_]])

                    def emit_pv(item):
                        r, slots, pslot, vi_ = item
                        nb = NB // r
                        mms = []
                        for si, (c, kb, qb) in enumerate(slots):
                            if kb < 0:
                                continue
                            vblk = c * nb + kb
                            if r == 16:
                                for u in range(4):
                                    stt = not started[u]
                                    started[u] = True
                                    q0 = si * 128 + u * 32
                                    mms.append((MM(banks[4 + u][:, c:512:16], vaug[vi_][:, vblk, :],
                                                   ptm[pslot][:, q0:q0 + 32], stt, True, skip_group_check=True), u))
                            else:
                                tok0 = qb * 128 * r + c - R * 2048
                                u = tok0 // 512
                                lo = tok0 - u * 512
                                stt = not started[u]
                                started[u] = True
                                mms.append((MM(banks[4 + u][:, lo:lo + 127 * r + 1:r], vaug[vi_][:, vblk, :],
                                               ptm[pslot][:, si * 128:(si + 1) * 128], stt, True,
                                               skip_group_check=True), u))
                        for kk, (fn, u) in enumerate(mms):
                            P.op("pe", fn, reads=[r_vaug[vi_], r_ptm[pslot]], writes=[bres[4 + u]],
                                 inc=(kk == len(mms) - 1))

                    pend = []
                    if R == 0:
                        for r_ in (1, 4, 16):
                            build_vaug(r_, VIDX[r_])
                    for (r, slots) in groups:
                        vi = VIDX[r]
                        bk = next_bank()
                        es_ = gi % NSL
                        for si, (c, kb, qb) in enumerate(slots):
                            ks = max(kb, 0) * 128 * r + c
                            qs = qb * 128 * r + c
                            P.op("pe", MM(banks[bk][:, si * 128:(si + 1) * 128],
                                          qkvT[p0:p1, 1, ks:ks + 127 * r + 1:r], qkvT[p0:p1, 0, qs:qs + 127 * r + 1:r],
                                          True, True),
                                 reads=r_qk, writes=[bres[bk]], inc=(si == 3))
                        P.op("act", ACTV(expS[es_][:], banks[bk], AF.Exp), reads=[bres[bk]], writes=[r_expS[es_]])
                        inv = [kb < 0 for (_, kb, _) in slots]
                        mk = dmz2 if (inv[0] and inv[2]) else (dmz if inv[0] else dm1)
                        P.op("dve", TT(ptm[es_][:], expS[es_][:], mk[:], ALU.mult),
                             reads=[r_expS[es_], r_const], writes=[r_ptm[es_]])
                        pend.append((r, slots, es_, vi))
                        if len(pend) > LAG:
                            emit_pv(pend.pop(0))
                        gi += 1
                    while pend:
                        emit_pv(pend.pop(0))
                    for u in range(4):
                        rd = 0
                        ts_ = R * 4 + u
                        P.op("act", ACTV(rden[rd][p0:p1, :], banks[4 + u][64:128, :], AF.Ln),
                             reads=[bres[4 + u]], writes=[r_rden[rd]])
                        P.op("act", ACTV(rden[rd][p0:p1, :], rden[rd][p0:p1, :], AF.Exp, scale=-1.0),
                             reads=[], writes=[r_rden[rd]])
                        P.op("dve", TT(oT[p0:p1, pi, ts_ * 512:(ts_ + 1) * 512], banks[4 + u][0:64, :],
                                       rden[rd][p0:p1, :], ALU.mult),
                             reads=[bres[4 + u], r_rden[rd]], writes=[r_oT[pi][ts_]])
            fprev = P.fence()
            es_a.close()
        if len(pairsA) == 4:
            normalize_group(0)
        fenceGA = P.fence()
        es_ga.close()
    else:
        fenceGA = fenceA

    pairsB = [p for p in pairs if p >= 4]
    if pairsB:
        es_gb = ExitStack()
        fB = fenceGA
        V2 = P.sb("V2", [128, NB, 128], BF16, es_gb)
        r_V2 = Res(fB)
        E2 = [P.sb(f"E2_{i}", [128, 1024], BF16, es_gb) for i in range(3)]
        r_E = [Res(fB) for _ in range(3)]
        sp2 = [P.sb(f"sp2_{i}", [128, 1024], BF16, es_gb) for i in range(3)]
        r_sp = [Res(fB) for _ in range(3)]
        X2 = [P.sb(f"X2_{i}", [128, 1024], BF16, es_gb) for i in range(2)]
        r_X = [Res(fB) for _ in range(2)]
        A2 = [P.sb(f"A2_{i}", [128, 1024], BF16, es_gb) for i in range(3)]
        r_A = [Res(fB) for _ in range(3)]

        for pi in pairsB:
            load_pair_weights(pi)
            proj_fm(0, 0, 0.125, lambda ts: r_qk[ts * 4:(ts + 1) * 4])
            proj_fm(128, 1, 1.0, lambda ts: r_qk[ts * 4:(ts + 1) * 4])
            proj_fm(256, 2, 1.0, lambda ts: [r_v[ts]])
            for g8 in range(4):
                bk = 2 + g8 % 2
                for j in range(8):
                    tb = g8 * 8 + j
                    P.op("pe", TR(bT[bk][:, j * 128:(j + 1) * 128], qkvT[:, 2, tb * 128:(tb + 1) * 128], ident[:]),
                         reads=r_v + [r_const], writes=[bres[bk]], inc=(j == 7))
                P.op("dve", CP(V2[:, g8 * 8:(g8 + 1) * 8, :], bT[bk].rearrange("p (j d) -> p j d", d=128)),
                     reads=[bres[bk]], writes=[r_V2])

            steps = [(G, J) for G in range(8) for J in range(4 * G + 3, -1, -1)]
            n = len(steps)

            def lo_of(i):
                G, J = steps[i]
                return max(0, J - 4 * G) * 128

            def hv(ap, lo):
                v = ap.rearrange("p (h t) -> p h t", h=2)
                return v if lo == 0 else v[:, :, lo:512]

            def pe_z(i):
                G, J = steps[i]
                k = i % 2
                lo = lo_of(i)
                for h in range(2):
                    hs = slice(h * 64, (h + 1) * 64)
                    P.op("pe", MM(banks[2 * k + h][:, lo:512], qkvT[hs, 1, J * 128:(J + 1) * 128],
                                  qkvT[hs, 0, G * 512 + lo:(G + 1) * 512], True, True),
                         reads=r_qk, writes=[bres[2 * k + h]], inc=(h == 1))

            def act_EL(i):
                G, J = steps[i]
                k = i % 2
                e_ = i % 3
                s_ = i % 3
                lo = lo_of(i)
                P.op("act", ACTV(hv(E2[e_][:], lo), hv(pb[k][:], lo), AF.Exp),
                     reads=[bres[2 * k], bres[2 * k + 1]], writes=[r_E[e_]])
                if J >= 4 * G:
                    ev = hv(E2[e_][:], lo)
                    P.op("dve", TT(ev, ev, sbm[J - 4 * G][:, lo:512].unsqueeze(1).to_broadcast([128, 2, 512 - lo]),
                                   ALU.mult), reads=[r_const], writes=[r_E[e_]])
                P.op("act", ACTV(hv(sp2[s_][:], lo), hv(E2[e_][:], lo), AF.Ln, bias=1.0, scale=1.0),
                     reads=[r_E[e_]], writes=[r_sp[s_]])

            def pe_C(i):
                G, J = steps[i]
                first = (J == 4 * G + 3)
                s_ = i % 3
                lo = lo_of(i)
                for h in range(2):
                    P.op("pe", MM(banks[4 + h][:, lo:512], tri[:], sp2[s_][:, h * 512 + lo:(h + 1) * 512], first, first,
                                  skip_group_check=True),
                         reads=[r_sp[s_], r_const], writes=[bres[4 + h]], inc=(first and h == 1))
                    if not first:
                        sp_prev = (i - 1) % 3
                        lop = lo_of(i - 1)
                        P.op("pe", MM(banks[4 + h][:, lop:512], sl[:], sp2[sp_prev][:, h * 512 + lop:(h + 1) * 512],
                                      False, True, skip_group_check=True),
                             reads=[r_sp[sp_prev], r_const], writes=[bres[4 + h]], inc=(h == 1))

            def act_X(i):
                e_ = i % 3
                x_ = i % 2
                a_ = i % 3
                lo = lo_of(i)
                P.op("act", ACTV(hv(X2[x_][:], lo), hv(pb[2][:], lo), AF.Exp, scale=-1.0),
                     reads=[bres[4], bres[5]], writes=[r_X[x_]])
                P.op("dve", TT(hv(A2[a_][:], lo), hv(E2[e_][:], lo), hv(X2[x_][:], lo), ALU.mult),
                     reads=[r_E[e_], r_X[x_]], writes=[r_A[a_]])

            def pe_pv(i):
                G, J = steps[i]
                first = (J == 4 * G + 3)
                last = (J == 0)
                ob = 6 + G % 2
                a_ = i % 3
                lo = lo_of(i)
                for h in range(2):
                    hs = slice(h * 64, (h + 1) * 64)
                    P.op("pe", MM(banks[ob][hs, lo:512], V2[:, J, hs], A2[a_][:, h * 512 + lo:(h + 1) * 512], first, last,
                                  skip_group_check=True),
                         reads=[r_V2, r_A[a_]], writes=[bres[ob]], inc=(h == 1))
                if last:
                    P.op("dve", CP(oT[:, pi, G * 512:(G + 1) * 512], banks[ob]),
                         reads=[bres[ob]], writes=[r_oT[pi][G]])

            for i in range(n + 3):
                if i < n:
                    pe_z(i)
                if 0 <= i - 3 < n:
                    pe_pv(i - 3)
                if 0 <= i - 1 < n:
                    act_EL(i - 1)
                if 0 <= i - 2 < n:
                    act_X(i - 2)
                if 0 <= i - 1 < n:
                    pe_C(i - 1)
        if len(pairsB) == 4:
            normalize_group(4)
        fenceGB = P.fence()
        es_gb.close()

    if dbg == "oT":
        for c in range(8):
            out_tokens.append(P.dma("sp", DMA(dbg_d[:, c, :], oT[:, c, :]), reads=r_oT[c]))
    fenceATT = P.fence()
    es_att.close()

    if do_ffn:
        fF = fenceATT
        es_f = ExitStack()
        Wd = P.sb("Wd", [128, NFC, D], BF16, es_f)
        r_Wd = Res(fF)
        Wo = P.sb("Wo", [128, 8, D], BF16, es_f)
        r_Wo = Res(fF)
        x1 = P.sb("x1", [128, 5, D], F32, es_f)
        r_x1 = [Res(fF) for _ in range(5)]
        h2T = P.sb("h2T", [128, 8, 512], BF16, es_f)
        r_h2T = [Res(fF) for _ in range(4)]
        actT = P.sb("actT", [128, NFC, 512], BF16, es_f)
        r_act = [Res(fF) for _ in range(NFC)]
        ring = [P.sb(f"ring{i}", [128, 8, 256], BF16, es_f) for i in range(3)]
        r_ring = [Res(fF) for _ in range(3)]
        fnw = P.sb("fnw", [128, D], F32, es_f)
        r_fnw = Res(fF)
        sg = [P.sb(f"sg{i}", [128, 512], BF16, es_f) for i in range(1)]
        r_sg = [Res(fF)]
        h2 = [P.sb(f"h2{i}", [128, D], BF16, es_f) for i in range(4)]
        r_h2 = [Res(fF) for _ in range(4)]
        r_ss2b = [Res(fF) for _ in range(4)]
        ss2 = P.sb("ss2", [128, 12], F32, es_f)
        r_ss2 = Res(fF)

        P.dma("pool", DMA(Wo[:], wout_d.rearrange("(c p) d -> p c d", p=128)), writes=[r_Wo])
        for c in range(8):
            P.op("dve", TS(Wo[:, c, :], Wo[:, c, :], wcol[:, c:c + 1], ALU.mult), reads=[r_const], writes=[r_Wo])
        for q4_ in range(2):
            P.dma("pool", DMA(Wd[:, q4_ * 11:(q4_ + 1) * 11, :],
                              wd_d[q4_ * 11 * 128:(q4_ + 1) * 11 * 128, :].rearrange("(f p) d -> p f d", p=128)),
                  writes=[r_Wd])
        P.dma("sp", DMA(fnw[:], fnw_d), writes=[r_fnw])

        ydr = [0]

        def yd_bank():
            b = ydr[0] % 4
            ydr[0] += 1
            return b

        def step1_block(tt, b4):
            tbk = tt * 4 + b4
            xsl = tbk % 5
            tsl = slice(tbk * 128, (tbk + 1) * 128)
            P.dma("sp", DMA(x1[:, xsl, :], x_d[tsl, :]), writes=[r_x1[xsl]])
            for half in range(2):
                bk = yd_bank()
                hsl = slice(half * 512, (half + 1) * 512)
                for c in range(8):
                    P.op("pe", MM(banks[bk], oT[:, c, tsl], Wo[:, c, hsl], c == 0, c == 7),
                         reads=[r_Wo] + [r_oT[cc][tt] for cc in range(8)], writes=[bres[bk]], inc=(c == 7))
                P.op("dve", TT(x1[:, xsl, hsl], banks[bk], x1[:, xsl, hsl], ALU.add),
                     reads=[bres[bk]], writes=[r_x1[xsl]])
            P.op("act", ACTV(h2[b4][:], x1[:, xsl, :], AF.Square, accum_out=ss2[:, b4:b4 + 1]),
                 reads=[r_x1[xsl]], writes=[r_h2[b4], r_ss2b[b4]])
            P.op("act", ACTV(ss2[:, 4 + b4:5 + b4], ss2[:, b4:b4 + 1], AF.Sqrt, scale=1.0 / D, bias=EPS),
                 reads=[r_ss2b[b4]], writes=[r_ss2b[b4]])
            P.op("dve", RCP(ss2[:, 8 + b4:9 + b4], ss2[:, 4 + b4:5 + b4]), reads=[r_ss2b[b4]], writes=[r_ss2b[b4]])
            P.op("dve", STT(h2[b4][:], x1[:, xsl, :], ss2[:, 8 + b4:9 + b4], fnw[:], ALU.mult, ALU.mult),
                 reads=[r_x1[xsl], r_ss2b[b4], r_fnw], writes=[r_h2[b4]])

        def step1_tr(b4):
            bk = yd_bank()
            for c in range(8):
                csl = slice(c * 128, (c + 1) * 128)
                P.op("pe", TR(bT[bk][:, csl], h2[b4][:, csl], ident[:]),
                     reads=[r_h2[b4], r_const], writes=[bres[bk]], inc=(c == 7))
            P.op("act", ACTV(h2T[:, :, b4 * 128:(b4 + 1) * 128], bT[bk].rearrange("p (c t) -> p c t", t=128), AF.Copy),
                 reads=[bres[bk]], writes=[r_h2T[b4]])

        def finish_norm(tt):
            pass

        gur = 0
        for b4 in range(4):
            step1_block(0, b4)
            if b4 >= 1:
                step1_tr(b4 - 1)
        step1_tr(3)
        for tt in range(8):
            finish_norm(tt)
            for fc in range(NFC):
                rg = fc % 3
                P.dma("pool", DMA(ring[rg][:], wgu_bf[fc].rearrange("p (c j) -> p c j", j=256)),
                      reads=[r_wgubf[fc]], writes=[r_ring[rg]])
                bg = 4 + (gur % 2) * 2
                bu = bg + 1
                gur += 1
                for (bk, off) in ((bg, 0), (bu, 128)):
                    for c in range(8):
                        P.op("pe", MM(banks[bk], ring[rg][:, c, off:off + 128], h2T[:, c, :], c == 0, c == 7),
                             reads=[r_ring[rg]] + r_h2T, writes=[bres[bk]], inc=(c == 7))
                s_ = 0
                P.op("act", ACTV(sg[s_][:], banks[bg], AF.Silu), reads=[bres[bg]], writes=[r_sg[s_]])
                P.op("dve", TT(actT[:, fc, :], banks[bu], sg[s_][:], ALU.mult),
                     reads=[bres[bu], r_sg[s_]], writes=[r_act[fc]])
            nxt = tt + 1 < 8
            for b4 in range(4):
                tbk = tt * 4 + b4
                xsl = tbk % 5
                if nxt:
                    step1_block(tt + 1, b4)
                for half in range(2):
                    bk = yd_bank()
                    hsl = slice(half * 512, (half + 1) * 512)
                    for fc in range(NFC):
                        P.op("pe", MM(banks[bk], actT[:, fc, b4 * 128:(b4 + 1) * 128], Wd[:, fc, hsl],
                                      fc == 0, fc == NFC - 1),
                             reads=[r_act[fc], r_Wd], writes=[bres[bk]], inc=(fc == NFC - 1))
                    P.op("dve", TT(x1[:, xsl, hsl], banks[bk], x1[:, xsl, hsl], ALU.add),
                         reads=[bres[bk]], writes=[r_x1[xsl]])
                out_tokens.append(P.dma("sp", DMA(out_d[tbk * 128:(tbk + 1) * 128, :], x1[:, xsl, :]),
                                        reads=[r_x1[xsl]]))
                if nxt and b4 >= 1:
                    step1_tr(b4 - 1)
            if nxt:
                step1_tr(3)
        es_f.close()

    P.wait("sp", out_tokens)
    P.emit()
    P.close()
    return nc


def make_inputs(x, attn_norm_w, w_in, q_norm_w, k_norm_w, dil_out_norm_w, sb_out_norm_w,
                w_out, ffn_norm_w, w_gate, w_up, w_down):
    f = np.float32
    w_in = np.asarray(w_in, f)[0]
    pairs = []
    for g in range(2):
        o0 = g * 1536
        for p in range(4):
            cols = [w_in[:, o0 + j * 512 + p * 128: o0 + j * 512 + (p + 1) * 128] for j in range(3)]
            pairs.append(np.concatenate(cols, axis=1))
    w_pairs = np.ascontiguousarray(np.stack(pairs, 0))
    wg = np.asarray(w_gate, f)[0].reshape(8, 128, NFC, 128)
    wu = np.asarray(w_up, f)[0].reshape(8, 128, NFC, 128)
    wgu = np.concatenate([wg, wu], axis=3)
    wgu = np.ascontiguousarray(wgu.transpose(2, 1, 0, 3)).reshape(NFC, 128, 2048)
    qw = np.asarray(q_norm_w, f)[0]
    kw = np.asarray(k_norm_w, f)[0]
    qkw = np.concatenate([qw, qw, kw, kw])[None, :]
    wcat = np.concatenate([np.asarray(dil_out_norm_w, f)[0], np.asarray(sb_out_norm_w, f)[0]])
    pos = np.arange(S, dtype=f)
    inv = (f(10000.0) ** (-np.arange(0, 64, 2, dtype=f) / f(64))).astype(f)
    ang = (pos[:, None] * inv[None, :]).astype(f)
    rope = np.concatenate([np.cos(ang), np.sin(ang), -np.sin(ang)], axis=1).astype(f)
    shared = {
        "anw_bc": np.ascontiguousarray(np.broadcast_to(np.asarray(attn_norm_w, f)[0][None, :], (128, D))),
        "fnw_bc": np.ascontiguousarray(np.broadcast_to(np.asarray(ffn_norm_w, f)[0][None, :], (128, D))),
        "qkw_bc": np.ascontiguousarray(np.broadcast_to(qkw, (128, 256))),
        "wcol": np.ascontiguousarray(wcat.reshape(8, 128).T),
        "rope": rope,
        "w_pairs": w_pairs,
        "w_out": np.ascontiguousarray(np.asarray(w_out, f)[0]),
        "wgu": wgu,
        "w_down": np.ascontiguousarray(np.asarray(w_down, f)[0]),
    }
    xs = np.asarray(x, f)
    return [dict(shared, x=np.ascontiguousarray(xs[b])) for b in range(xs.shape[0])]


_NC_CACHE = {}


def kernel(**inputs):
    in_maps = make_inputs(**inputs)
    if "nc" not in _NC_CACHE:
        _NC_CACHE["nc"] = build()
    nc = _NC_CACHE["nc"]
    res = run_bass_kernel_spmd(nc, in_maps, core_ids=list(range(8)))
    return np.stack([np.asarray(r["out"], np.float32) for r in res.results], axis=0)
```

```python
from contextlib import ExitStack
import numpy as np
import concourse.bass as bass
import concourse.mybir as mybir
from concourse.bass_utils import run_bass_kernel_spmd

F32 = mybir.dt.float32
BF16 = mybir.dt.bfloat16
AF = mybir.ActivationFunctionType
ALU = mybir.AluOpType
AX = mybir.AxisListType

S = 4096
D = 1024
NB = S // 128
DFF = 2816
NFC = DFF // 128
EPS = 1e-6
ENGS = ("pe", "act", "dve", "pool", "sp")
NDMASEM = 8


class Res:
    __slots__ = ("w", "r", "pend")

    def __init__(self, init=None):
        self.w = dict(init) if init else {}
        self.r = {}
        self.pend = None


def _merge(dst, src):
    for k, v in src.items():
        if dst.get(k, 0) < v:
            dst[k] = v


class Prog:
    def __init__(self, nc):
        self.nc = nc
        self.es = ExitStack()
        self.ops = {e: [] for e in ENGS}
        self.sem = {}
        self.cnt = {}
        self.seen = {e: {} for e in ENGS}
        self.pending = {e: [] for e in ENGS}
        for e in ENGS:
            self.newsem("E_" + e)
        self.dq = {}
        for q in ("sp", "pool"):
            self.dq[q] = {"keys": [self.newsem(f"D_{q}{i}") for i in range(NDMASEM)], "i": 0}

    def newsem(self, key):
        self.sem[key] = self.es.enter_context(self.nc.semaphore(key))
        self.cnt[key] = 0
        return key

    def sb(self, name, shape, dt, es=None):
        return (es or self.es).enter_context(self.nc.sbuf_tensor(name, list(shape), dt))

    def ps(self, name, shape, dt):
        return self.es.enter_context(self.nc.psum_tensor(name, list(shape), dt))

    def fence(self):
        f = {}
        for k, v in self.cnt.items():
            if v > 0:
                f[k] = v
        for e in ENGS:
            assert not self.pending[e], "fence with pending un-tokened ops"
        return f

    def _collect(self, eng, reads, writes, deps):
        need = {}
        for r in reads:
            assert r.pend is None or r.pend == eng, "resource pending on another engine"
            _merge(need, r.w)
        for r in writes:
            assert r.pend is None or r.pend == eng, "resource pending on another engine"
            _merge(need, r.w)
            _merge(need, r.r)
        for d in deps:
            if d:
                _merge(need, d)
        ws = []
        seen = self.seen[eng]
        for k, v in need.items():
            if k == "E_pe" and eng == "pe":
                continue
            if seen.get(k, 0) >= v:
                continue
            seen[k] = v
            ws.append((k, v))
        return ws

    def _commit(self, eng, tok, reads, writes):
        allp = self.pending[eng] + [(reads, writes)]
        self.pending[eng] = []
        for rs, wsx in allp:
            for r in rs:
                _merge(r.r, tok)
                r.pend = None
            for r in wsx:
                r.w = dict(tok)
                r.r = {}
                r.pend = None

    def op(self, eng, fn, reads=(), writes=(), inc=True, deps=()):
        ws = self._collect(eng, reads, writes, deps)
        if inc:
            key = "E_" + eng
            self.cnt[key] += 1
            tok = {key: self.cnt[key]}
            self.ops[eng].append((ws, fn, (key, 1)))
            self._commit(eng, tok, reads, writes)
            return tok
        self.ops[eng].append((ws, fn, None))
        self.pending[eng].append((reads, writes))
        for r in list(reads) + list(writes):
            r.pend = eng
        return None

    def dma(self, q, fn, reads=(), writes=(), deps=()):
        assert not self.pending[q]
        dq = self.dq[q]
        key = dq["keys"][dq["i"] % NDMASEM]
        dq["i"] += 1
        prev = {key: self.cnt[key]} if self.cnt[key] else None
        ws = self._collect(q, reads, writes, list(deps) + [prev])
        self.cnt[key] += 16
        tok = {key: self.cnt[key]}
        self.ops[q].append((ws, fn, (key, 16)))
        for r in reads:
            _merge(r.r, tok)
        for r in writes:
            r.w = dict(tok)
            r.r = {}
        return tok

    def wait(self, eng, deps):
        ws = self._collect(eng, (), (), deps)
        if ws:
            self.ops[eng].append((ws, None, None))

    def emit(self):
        prog = self
        for e in ENGS:
            assert not self.pending[e], f"pending ops on {e}"

        def run(name, e):
            for ws, fn, inc in prog.ops[name]:
                for key, val in ws:
                    e.wait_ge(prog.sem[key], val)
                if fn is None:
                    continue
                ins = fn(e)
                if inc is not None:
                    ins.then_inc(prog.sem[inc[0]], inc[1])

        with self.nc.Block() as block:
            @block.tensor
            def _(e):
                run("pe", e)

            @block.scalar
            def _(e):
                run("act", e)

            @block.vector
            def _(e):
                run("dve", e)

            @block.gpsimd
            def _(e):
                run("pool", e)

            @block.sync
            def _(e):
                run("sp", e)

    def close(self):
        self.es.close()


def MM(out, lhsT, rhs, start, stop, **kw):
    return lambda e: e.matmul(out, lhsT=lhsT, rhs=rhs, start=start, stop=stop, **kw)


def TR(out, in_, idn):
    return lambda e: e.transpose(out, in_, idn)


def ACTV(out, in_, func, **kw):
    return lambda e: e.activation(out=out, in_=in_, func=func, **kw)


def TT(out, in0, in1, op):
    return lambda e: e.tensor_tensor(out=out, in0=in0, in1=in1, op=op)


def TS(out, in0, s1, op0):
    return lambda e: e.tensor_scalar(out=out, in0=in0, scalar1=s1, scalar2=None, op0=op0)


def STT(out, in0, scalar, in1, op0, op1):
    return lambda e: e.scalar_tensor_tensor(out=out, in0=in0, scalar=scalar, in1=in1, op0=op0, op1=op1)


def CP(out, in_):
    return lambda e: e.tensor_copy(out=out, in_=in_)


def RCP(out, in_):
    return lambda e: e.reciprocal(out=out, in_=in_)


def RSUM(out, in_):
    return lambda e: e.reduce_sum(out=out, in_=in_, axis=AX.X)


def DMA(out, in_):
    return lambda e: e.dma_start(out=out, in_=in_)


def MSET(ap, v):
    return lambda e: e.memset(ap, v)

PA_BLOCKS = NB
KNOB = {}


def build(pairs=tuple(range(8)), do_ffn=True, dbg=None):
    nc = bass.Bass("TRN2", target_bir_lowering=False)

    def din(name, shape, dt=F32):
        return nc.dram_tensor(name, list(shape), dt, kind="ExternalInput").ap()

    x_d = din("x", [S, D])
    anw_d = din("anw_bc", [128, D])
    fnw_d = din("fnw_bc", [128, D])
    qkw_d = din("qkw_bc", [128, 256])
    wcol_d = din("wcol", [128, 8])
    rope_d = din("rope", [S, 96])
    wp_d = din("w_pairs", [8, D, 384])
    wout_d = din("w_out", [D, D])
    wgu_d = din("wgu", [NFC, 128, 2048])
    wd_d = din("w_down", [DFF, D])
    out_d = nc.dram_tensor("out", [S, D], F32, kind="ExternalOutput").ap()
    dbg_d = None
    if dbg in ("hT", "oT"):
        dbg_d = nc.dram_tensor("dbg", [128, 8, S], BF16, kind="ExternalOutput").ap()

    P = Prog(nc)
    out_tokens = []

    pb = [P.ps(f"pb{i}", [128, 1024], F32) for i in range(4)]
    banks = []
    for i in range(4):
        banks += [pb[i][:, 0:512], pb[i][:, 512:1024]]
    bres = [Res() for _ in range(8)]
    bT = [banks[i].bitcast(BF16) for i in range(8)]

    ident = P.sb("ident", [128, 128], BF16)
    tri = P.sb("tri", [128, 128], BF16)
    sl = P.sb("sl", [128, 128], BF16)
    ones = P.sb("ones", [128, 128], BF16)
    sbm = [P.sb(f"sbm{j}", [128, 512], BF16) for j in range(4)]
    dm1 = P.sb("dm1", [128, 512], BF16)
    dmz = P.sb("dmz", [128, 512], BF16)
    dmz2 = P.sb("dmz2", [128, 512], BF16)
    wcol = P.sb("wcol_sb", [128, 8], F32)
    oT = P.sb("oT", [128, 8, S], BF16)
    r_const = Res()
    r_oT = [[Res() for _ in range(8)] for _ in range(8)]

    def SEL(t_ap, pattern, base, cm, cmp):
        return lambda e: e.affine_select(out=t_ap, in_=t_ap, pattern=pattern, base=base,
                                         channel_multiplier=cm, compare_op=cmp, fill=0.0)

    for t in (ident, tri, sl, ones, dm1, dmz, dmz2, *sbm):
        P.op("pool", MSET(t[:], 1.0), writes=[r_const])
    P.op("pool", SEL(ident[:], [[-1, 128]], 0, 1, ALU.is_equal), writes=[r_const])
    P.op("pool", SEL(tri[:], [[-1, 128]], 0, 1, ALU.is_ge), writes=[r_const])
    P.op("pool", SEL(sl[:], [[1, 128]], 0, -1, ALU.is_gt), writes=[r_const])
    for j in range(4):
        P.op("pool", SEL(sbm[j][:], [[1, 512]], -128 * j, -1, ALU.is_gt), writes=[r_const])
    for m in (dm1, dmz, dmz2):
        for slot in range(4):
            ap = m[:, slot * 128:(slot + 1) * 128]
            if slot % 2 == 0:
                P.op("pool", SEL(ap, [[-1, 128]], 0, 1, ALU.is_ge), writes=[r_const])
            else:
                P.op("pool", SEL(ap, [[1, 128]], 0, -1, ALU.is_ge), writes=[r_const])
    P.op("pool", MSET(dmz[:, 0:128], 0.0), writes=[r_const])
    P.op("pool", MSET(dmz2[:, 0:128], 0.0), writes=[r_const])
    P.op("pool", MSET(dmz2[:, 256:384], 0.0), writes=[r_const])
    P.dma("sp", DMA(wcol[:], wcol_d), writes=[r_const])
    wgu_bf = nc.dram_tensor("wgu_bf", [NFC, 128, 2048], BF16, kind="Internal").ap()
    r_wgubf = [Res() for _ in range(NFC)]

    es_att = ExitStack()
    hT = P.sb("hT", [128, 8, S], BF16, es_att)
    r_hT = [Res() for _ in range(NB)]
    Wp = P.sb("Wp", [128, 8, 384], BF16, es_att)
    r_Wp = Res()
    qkvT = P.sb("qkvT", [128, 3, S], BF16, es_att)
    r_qk = [Res() for _ in range(NB)]
    r_v = [Res() for _ in range(8)]
    sq = [P.sb(f"sq{i}", [128, 512], BF16, es_att) for i in range(4)]
    r_sq = [Res() for _ in range(4)]
    rs = P.sb("rs", [128, 512], F32, es_att)
    r_rs = Res()

    es_pa = ExitStack()
    xs = [P.sb(f"xs{i}", [128, D], F32, es_pa)[:] for i in range(4)]
    for j in range(3):
        v32 = qkvT[:, j, :].bitcast(F32)
        xs += [v32[:, 0:D], v32[:, D:2 * D]]
    NX = len(xs)
    r_xs = [Res() for _ in range(NX)]
    hn = [P.sb(f"hn{i}", [128, D], BF16, es_pa) for i in range(4)]
    r_hn = [Res() for _ in range(4)]
    junk = P.sb("junkA", [128, D], BF16, es_pa)
    r_junk = Res()
    anw = P.sb("anw", [128, D], F32, es_pa)
    r_anw = Res()
    ssA = P.sb("ssA", [128, NB], F32, es_pa)
    sdA = P.sb("sdA", [128, NB], F32, es_pa)
    rsA = P.sb("rsA", [128, NB], F32, es_pa)
    r_ssA = [Res() for _ in range(NB)]
    P.dma("sp", DMA(anw[:], anw_d), writes=[r_anw])
    BS = 2

    def paA1(sb):
        for b in range(BS):
            tb = sb * BS + b
            xi = tb % NX
            tsl = slice(tb * 128, (tb + 1) * 128)
            P.dma("sp", DMA(xs[xi], x_d[tsl, :]), writes=[r_xs[xi]])
            P.op("act", ACTV(junk[:], xs[xi], AF.Square, accum_out=ssA[:, tb:tb + 1]),
                 reads=[r_xs[xi]], writes=[r_junk, r_ssA[sb]])
        c4 = slice(sb * BS, sb * BS + BS)
        P.op("act", ACTV(sdA[:, c4], ssA[:, c4], AF.Sqrt, scale=1.0 / D, bias=EPS), reads=[r_ssA[sb]], writes=[r_ssA[sb]])

    def paA2(sb):
        c4 = slice(sb * BS, sb * BS + BS)
        P.op("dve", RCP(rsA[:, c4], sdA[:, c4]), reads=[r_ssA[sb]], writes=[r_ssA[sb]])

    def paB1(sb):
        for b in range(BS):
            tb = sb * BS + b
            xi = tb % NX
            hi = tb % 4
            bk = tb % 4
            P.op("dve", STT(hn[hi][:], xs[xi], rsA[:, tb:tb + 1], anw[:], ALU.mult, ALU.mult),
                 reads=[r_xs[xi], r_ssA[sb], r_anw], writes=[r_hn[hi]])
            for c in range(8):
                csl = slice(c * 128, (c + 1) * 128)
                P.op("pe", TR(bT[bk][:, csl], hn[hi][:, csl], ident[:]),
                     reads=[r_hn[hi], r_const], writes=[bres[bk]], inc=(c == 7))

    def paB2(sb):
        for b in range(BS):
            tb = sb * BS + b
            bk = tb % 4
            tsl = slice(tb * 128, (tb + 1) * 128)
            if b % 2 == 0:
                P.op("act", ACTV(hT[:, :, tsl], bT[bk].rearrange("p (c t) -> p c t", t=128), AF.Copy),
                     reads=[bres[bk]], writes=[r_hT[tb]])
            else:
                P.op("dve", CP(hT[:, :, tsl], bT[bk].rearrange("p (c t) -> p c t", t=128)),
                     reads=[bres[bk]], writes=[r_hT[tb]])

    nsb = PA_BLOCKS // BS
    for sb in range(nsb + 1):
        if sb >= 1:
            paB1(sb - 1)
        if sb < nsb:
            paA1(sb)
        if sb >= 1:
            paB2(sb - 1)
        if sb < nsb:
            paA2(sb)
    fenceA = P.fence()
    es_pa.close()
    for r_ in r_qk + r_v:
        _merge(r_.w, fenceA)
    casts_pending = [do_ffn]

    if dbg == "hT":
        for c in range(8):
            out_tokens.append(P.dma("sp", DMA(dbg_d[:, c, :], hT[:, c, :]), reads=r_hT))

    def proj_fm(col0, dst_idx, scale, res_for_slice):
        for ts in range(8):
            bk = ts % 2
            tsl = slice(ts * 512, (ts + 1) * 512)
            for c in range(8):
                P.op("pe", MM(banks[bk], Wp[:, c, col0:col0 + 128], hT[:, c, tsl], c == 0, c == 7),
                     reads=[r_Wp] + r_hT[ts * 4:(ts + 1) * 4], writes=[bres[bk]], inc=(c == 7))
            wr = res_for_slice(ts)
            if ts % 2 == 0:
                P.op("act", ACTV(qkvT[:, dst_idx, tsl], banks[bk], AF.Copy, scale=scale),
                     reads=[bres[bk]], writes=wr)
            else:
                P.op("dve", TS(qkvT[:, dst_idx, tsl], banks[bk], scale, ALU.mult),
                     reads=[bres[bk]], writes=wr)

    def load_pair_weights(pi):
        P.dma("pool", DMA(Wp[:], wp_d[pi].rearrange("(c p) e -> p c e", p=128)), writes=[r_Wp])

    def normalize_group(gc):
        for ts in range(8):
            tsl = slice(ts * 512, (ts + 1) * 512)
            for c in range(4):
                if c < 3:
                    P.op("act", ACTV(sq[c][:], oT[:, gc + c, tsl], AF.Square),
                         reads=[r_oT[gc + c][ts]], writes=[r_sq[c]])
                else:
                    P.op("pool", TT(sq[c][:], oT[:, gc + c, tsl], oT[:, gc + c, tsl], ALU.mult),
                         reads=[r_oT[gc + c][ts]], writes=[r_sq[c]])
            bk = 2 + ts % 2
            for c in range(4):
                P.op("pe", MM(banks[bk], ones[:], sq[c][:], c == 0, c == 3),
                     reads=[r_sq[c], r_const], writes=[bres[bk]], inc=(c == 3))
            P.op("act", ACTV(rs[:], banks[bk], AF.Ln, scale=1.0 / 512, bias=EPS),
                 reads=[bres[bk]], writes=[r_rs])
            P.op("act", ACTV(rs[:], rs[:], AF.Exp, scale=-0.5), reads=[r_rs], writes=[r_rs])
            P.op("dve", TT(oT[:, gc:gc + 4, tsl], oT[:, gc:gc + 4, tsl],
                           rs[:].unsqueeze(1).to_broadcast([128, 4, 512]), ALU.mult),
                 reads=[r_rs], writes=[r_oT[gc + c][ts] for c in range(4)])

    LAG = 3
    NSL = 4
    pairsA = [p for p in pairs if p < 4]
    bank_rr = [0]

    def next_bank():
        b = bank_rr[0] % 4
        bank_rr[0] += 1
        return b

    if pairsA:
        es_ga = ExitStack()
        qkw = P.sb("qkw", [128, 256], F32, es_ga)
        r_qkw = Res(fenceA)
        P.dma("sp", DMA(qkw[:], qkw_d), writes=[r_qkw])
        P.op("dve", TS(qkw[:, 0:128], qkw[:, 0:128], 0.125, ALU.mult), writes=[r_qkw])
        fprev = fenceA
        for pi in pairsA:
            load_pair_weights(pi)
            es_p = ExitStack()
            NS = 4
            sa = [P.sb(f"sa{pi}_{i}", [128, 512], F32, es_p) for i in range(NS)]
            qn = [P.sb(f"qn{pi}_{i}", [128, 512], F32, es_p) for i in range(NS)]
            t2 = [P.sb(f"t2{pi}_{i}", [128, 512], F32, es_p) for i in range(NS)]
            qr = [P.sb(f"qr{pi}_{i}", [128, 512], BF16, es_p) for i in range(NS + 1)]
            st = [P.sb(f"st{pi}_{i}", [128, 24], F32, es_p) for i in range(NS)]
            ropeb = [P.sb(f"ropeb{pi}_{i}", [128, 2, 96], F32, es_p) for i in range(NS)]
            r_sa = [Res(fprev) for _ in range(NS)]
            r_qn = [Res(fprev) for _ in range(NS)]
            r_t2 = [Res(fprev) for _ in range(NS)]
            r_qr = [Res(fprev) for _ in range(NS + 1)]
            r_st = [Res(fprev) for _ in range(NS)]
            r_rope = [Res(fprev) for _ in range(NS)]
            w3 = lambda ap: ap.rearrange("p (s d) -> p s d", d=64)
            w4 = lambda ap: ap.rearrange("p (b s d) -> p b s d", b=2, d=32)
            w5 = lambda ap: ap.rearrange("p (b s h d) -> p b s h d", b=2, h=2, d=32)
            wb = lambda ap: ap.rearrange("p (b e) -> p b e", e=256)
            NBB = NB // 2

            def st0(bb):
                k = bb % NS
                tb0 = bb * 2
                bk = bb % 2
                P.dma("sp", DMA(ropeb[k][:], rope_d[tb0 * 128:(tb0 + 2) * 128, :].rearrange("(b p) d -> p b d", p=128)),
                      writes=[r_rope[k]])
                for b in range(2):
                    tsl = slice((tb0 + b) * 128, (tb0 + b + 1) * 128)
                    for c in range(8):
                        P.op("pe", MM(banks[bk][:, b * 256:(b + 1) * 256], hT[:, c, tsl], Wp[:, c, 0:256], c == 0, c == 7),
                             reads=[r_Wp, r_hT[tb0 + b]], writes=[bres[bk]], inc=(b == 1 and c == 7))

            def st1a(bb):
                k = bb % NS
                bk = bb % 2
                P.op("act", ACTV(sa[k][:], banks[bk], AF.Square), reads=[bres[bk]], writes=[r_sa[k]])
                P.op("dve", RSUM(st[k][:, 0:8], w3(sa[k][:])), reads=[r_sa[k]], writes=[r_st[k]])
                P.op("act", ACTV(st[k][:, 8:16], st[k][:, 0:8], AF.Sqrt, scale=1.0 / 64, bias=EPS),
                     reads=[r_st[k]], writes=[r_st[k]])

            def st1b(bb):
                k = bb % NS
                bk = bb % 2
                P.op("dve", RCP(st[k][:, 16:24], st[k][:, 8:16]), reads=[r_st[k]], writes=[r_st[k]])
                P.op("dve", TT(w3(qn[k][:]), w3(banks[bk]), st[k][:, 16:24].unsqueeze(2).to_broadcast([128, 8, 64]),
                               ALU.mult), reads=[bres[bk], r_st[k]], writes=[r_qn[k]])
                P.op("pool", TT(wb(qn[k][:]), wb(qn[k][:]), qkw[:].unsqueeze(1).to_broadcast([128, 2, 256]), ALU.mult),
                     reads=[r_qkw], writes=[r_qn[k]])

            def st2(bb):
                k = bb % NS
                cosb = ropeb[k][:, :, 0:32].unsqueeze(2).to_broadcast([128, 2, 8, 32])
                sinb = ropeb[k][:, :, 32:64].unsqueeze(2).to_broadcast([128, 2, 4, 32])
                nsinb = ropeb[k][:, :, 64:96].unsqueeze(2).to_broadcast([128, 2, 4, 32])
                P.op("dve", TT(w4(sa[k][:]), w4(qn[k][:]), cosb, ALU.mult),
                     reads=[r_qn[k], r_rope[k]], writes=[r_sa[k]])
                P.op("pool", TT(w5(t2[k][:])[:, :, :, 0, :], w5(qn[k][:])[:, :, :, 1, :], nsinb, ALU.mult),
                     reads=[r_qn[k], r_rope[k]], writes=[r_t2[k]])
                P.op("pool", TT(w5(t2[k][:])[:, :, :, 1, :], w5(qn[k][:])[:, :, :, 0, :], sinb, ALU.mult),
                     reads=[r_qn[k], r_rope[k]], writes=[r_t2[k]])

            def st3a(bb):
                k = bb % NS
                kq = bb % (NS + 1)
                P.op("dve", TT(qr[kq][:], sa[k][:], t2[k][:], ALU.add), reads=[r_sa[k], r_t2[k]], writes=[r_qr[kq]])

            def st3b(bb):
                kq = bb % (NS + 1)
                tb0 = bb * 2
                t2sl = slice(tb0 * 128, (tb0 + 2) * 128)
                bk2 = 2 + bb % 2
                for j in range(4):
                    jsl = slice(j * 128, (j + 1) * 128)
                    P.op("pe", TR(bT[bk2][:, jsl], qr[kq][:, jsl], ident[:]),
                         reads=[r_qr[kq], r_const], writes=[bres[bk2]], inc=(j == 3))
                P.op("act", ACTV(qkvT[:, 0:2, t2sl].rearrange("p j (b t) -> p j b t", t=128),
                                 bT[bk2][:, 0:512].rearrange("p (b j t) -> p j b t", j=2, t=128), AF.Copy),
                     reads=[bres[bk2]], writes=[r_qk[tb0], r_qk[tb0 + 1]])

            for it in range(NBB + 4):
                if it < NBB:
                    st0(it)
                if 0 <= it - 1 < NBB:
                    st1a(it - 1)
                if 0 <= it - 2 < NBB:
                    st2(it - 2)
                if 0 <= it - 1 < NBB:
                    st1b(it - 1)
                if 0 <= it - 3 < NBB:
                    st3a(it - 3)
                if 0 <= it - 4 < NBB:
                    st3b(it - 4)
            proj_fm(256, 2, 1.0, lambda ts: [r_v[ts]])
            fmid = P.fence()
            es_p.close()

            es_a = ExitStack()
            vaug = [P.sb(f"vaug{pi}_{i}", [128, NB, 128], BF16, es_a) for i in range(3)]
            r_vaug = [Res(fmid) for _ in range(3)]
            expS = [P.sb(f"expS{pi}_{i}", [128, 512], BF16, es_a) for i in range(NSL)]
            r_expS = [Res(fmid) for _ in range(NSL)]
            ptm = [P.sb(f"ptm{pi}_{i}", [128, 512], BF16, es_a) for i in range(NSL)]
            r_ptm = [Res(fmid) for _ in range(NSL)]
            rden = [P.sb(f"rden{pi}_{i}", [128, 512], F32, es_a) for i in range(1)]
            r_rden = [Res(fmid)]
            for i in range(3):
                P.op("pool", MSET(vaug[i][:, :, 64:128], 1.0), writes=[r_vaug[i]])
            if casts_pending[0]:
                casts_pending[0] = False
                for fc in range(NFC):
                    P.dma("pool", DMA(wgu_bf[fc], wgu_d[fc]), writes=[r_wgubf[fc]], deps=[fmid])
            VIDX = {1: 0, 4: 1, 16: 2}

            vi = 0
            gi = 0
            for hp in range(2):
                p0, p1 = hp * 64, hp * 64 + 64
                for R in range(2):
                    started = [False] * 4
                    groups = []
                    for r in (1, 4, 16):
                        nb = NB // r
                        if r == 16:
                            for c in range(0, 16, 2):
                                groups.append((r, [(c, R - 1, R), (c, R, R), (c + 1, R - 1, R), (c + 1, R, R)]))
                        else:
                            per = nb // 2
                            for c in range(r):
                                for n in range(R * per, (R + 1) * per, 2):
                                    groups.append((r, [(c, n - 1, n), (c, n, n), (c, n, n + 1), (c, n + 1, n + 1)]))

                    def build_vaug(r, vi_):
                        nb = NB // r
                        for g8 in range(4):
                            bk = next_bank()
                            for j in range(8):
                                blk = g8 * 8 + j
                                c, n = blk // nb, blk % nb
                                st_ = n * 128 * r + c
                                P.op("pe", TR(bT[bk][:, j * 64:(j + 1) * 64], qkvT[p0:p1, 2, st_:st_ + 127 * r + 1:r],
                                              ident[p0:p1, p0:p1]),
                                     reads=r_v + [r_const], writes=[bres[bk]], inc=(j == 7))
                            P.op("dve", CP(vaug[vi_][:, g8 * 8:(g8 + 1) * 8, 0:64],
                                           bT[bk][:, 0:512].rearrange("p (j d) -> p j d", d=64)),
                                 reads=[bres[bk]], writes=[r_vaug[vi_]])

                    def emit_pv(item):
                        r, slots, pslot, vi_ = item
                        nb = NB // r
                        mms = []
                        for si, (c, kb, qb) in enumerate(slots):
                            if kb < 0:
                                continue
                            vblk = c * nb + kb
                            if r == 16:
                                for u in range(4):
                                    stt = not started[u]
                                    started[u] = True
                                    q0 = si * 128 + u * 32
                                    mms.append((MM(banks[4 + u][:, c:512:16], vaug[vi_][:, vblk, :],
                                                   ptm[pslot][:, q0:q0 + 32], stt, True, skip_group_check=True), u))
                            else:
                                tok0 = qb * 128 * r + c - R * 2048
                                u = tok0 // 512
                                lo = tok0 - u * 512
                                stt = not started[u]
                                started[u] = True
                                mms.append((MM(banks[4 + u][:, lo:lo + 127 * r + 1:r], vaug[vi_][:, vblk, :],
                                               ptm[pslot][:, si * 128:(si + 1) * 128], stt, True,
                                               skip_group_check=True), u))
                        for kk, (fn, u) in enumerate(mms):
                            P.op("pe", fn, reads=[r_vaug[vi_], r_ptm[pslot]], writes=[bres[4 + u]],
                                 inc=(kk == len(mms) - 1))

                    pend = []
                    if R == 0:
                        for r_ in (1, 4, 16):
                            build_vaug(r_, VIDX[r_])
                    for (r, slots) in groups:
                        vi = VIDX[r]
                        bk = next_bank()
                        es_ = gi % NSL
                        for si, (c, kb, qb) in enumerate(slots):
                            ks = max(kb, 0) * 128 * r + c
                            qs = qb * 128 * r + c
                            P.op("pe", MM(banks[bk][:, si * 128:(si + 1) * 128],
                                          qkvT[p0:p1, 1, ks:ks + 127 * r + 1:r], qkvT[p0:p1, 0, qs:qs + 127 * r + 1:r],
                                          True, True),
                                 reads=r_qk, writes=[bres[bk]], inc=(si == 3))
                        P.op("act", ACTV(expS[es_][:], banks[bk], AF.Exp), reads=[bres[bk]], writes=[r_expS[es_]])
                        inv = [kb < 0 for (_, kb, _) in slots]
                        mk = dmz2 if (inv[0] and inv[2]) else (dmz if inv[0] else dm1)
                        P.op("dve", TT(ptm[es_][:], expS[es_][:], mk[:], ALU.mult),
                             reads=[r_expS[es_], r_const], writes=[r_ptm[es_]])
                        pend.append((r, slots, es_, vi))
                        if len(pend) > LAG:
                            emit_pv(pend.pop(0))
                        gi += 1
                    while pend:
                        emit_pv(pend.pop(0))
                    for u in range(4):
                        rd = 0
                        ts_ = R * 4 + u
                        P.op("act", ACTV(rden[rd][p0:p1, :], banks[4 + u][64:128, :], AF.Ln),
                             reads=[bres[4 + u]], writes=[r_rden[rd]])
                        P.op("act", ACTV(rden[rd][p0:p1, :], rden[rd][p0:p1, :], AF.Exp, scale=-1.0),
                             reads=[], writes=[r_rden[rd]])
                        P.op("dve", TT(oT[p0:p1, pi, ts_ * 512:(ts_ + 1) * 512], banks[4 + u][0:64, :],
                                       rden[rd][p0:p1, :], ALU.mult),
                             reads=[bres[4 + u], r_rden[rd]], writes=[r_oT[pi][ts_]])
            fprev = P.fence()
            es_a.close()
        if len(pairsA) == 4:
            normalize_group(0)
        fenceGA = P.fence()
        es_ga.close()
    else:
        fenceGA = fenceA

    pairsB = [p for p in pairs if p >= 4]
    if pairsB:
        es_gb = ExitStack()
        fB = fenceGA
        V2 = P.sb("V2", [128, NB, 128], BF16, es_gb)
        r_V2 = Res(fB)
        E2 = [P.sb(f"E2_{i}", [128, 1024], BF16, es_gb) for i in range(3)]
        r_E = [Res(fB) for _ in range(3)]
        sp2 = [P.sb(f"sp2_{i}", [128, 1024], BF16, es_gb) for i in range(3)]
        r_sp = [Res(fB) for _ in range(3)]
        X2 = [P.sb(f"X2_{i}", [128, 1024], BF16, es_gb) for i in range(2)]
        r_X = [Res(fB) for _ in range(2)]
        A2 = [P.sb(f"A2_{i}", [128, 1024], BF16, es_gb) for i in range(3)]
        r_A = [Res(fB) for _ in range(3)]

        for pi in pairsB:
            load_pair_weights(pi)
            proj_fm(0, 0, 0.125, lambda ts: r_qk[ts * 4:(ts + 1) * 4])
            proj_fm(128, 1, 1.0, lambda ts: r_qk[ts * 4:(ts + 1) * 4])
            proj_fm(256, 2, 1.0, lambda ts: [r_v[ts]])
            for g8 in range(4):
                bk = 2 + g8 % 2
                for j in range(8):
                    tb = g8 * 8 + j
                    P.op("pe", TR(bT[bk][:, j * 128:(j + 1) * 128], qkvT[:, 2, tb * 128:(tb + 1) * 128], ident[:]),
                         reads=r_v + [r_const], writes=[bres[bk]], inc=(j == 7))
                P.op("dve", CP(V2[:, g8 * 8:(g8 + 1) * 8, :], bT[bk].rearrange("p (j d) -> p j d", d=128)),
                     reads=[bres[bk]], writes=[r_V2])

            steps = [(G, J) for G in range(8) for J in range(4 * G + 3, -1, -1)]
            n = len(steps)

            def lo_of(i):
                G, J = steps[i]
                return max(0, J - 4 * G) * 128

            def hv(ap, lo):
                v = ap.rearrange("p (h t) -> p h t", h=2)
                return v if lo == 0 else v[:, :, lo:512]

            def pe_z(i):
                G, J = steps[i]
                k = i % 2
                lo = lo_of(i)
                for h in range(2):
                    hs = slice(h * 64, (h + 1) * 64)
                    P.op("pe", MM(banks[2 * k + h][:, lo:512], qkvT[hs, 1, J * 128:(J + 1) * 128],
                                  qkvT[hs, 0, G * 512 + lo:(G + 1) * 512], True, True),
                         reads=r_qk, writes=[bres[2 * k + h]], inc=(h == 1))

            def act_EL(i):
                G, J = steps[i]
                k = i % 2
                e_ = i % 3
                s_ = i % 3
                lo = lo_of(i)
                P.op("act", ACTV(hv(E2[e_][:], lo), hv(pb[k][:], lo), AF.Exp),
                     reads=[bres[2 * k], bres[2 * k + 1]], writes=[r_E[e_]])
                if J >= 4 * G:
                    ev = hv(E2[e_][:], lo)
                    P.op("dve", TT(ev, ev, sbm[J - 4 * G][:, lo:512].unsqueeze(1).to_broadcast([128, 2, 512 - lo]),
                                   ALU.mult), reads=[r_const], writes=[r_E[e_]])

            def act_L(i):
                e_ = i % 3
                s_ = i % 3
                lo = lo_of(i)
                P.op("act", ACTV(hv(sp2[s_][:], lo), hv(E2[e_][:], lo), AF.Ln, bias=1.0, scale=1.0),
                     reads=[r_E[e_]], writes=[r_sp[s_]])

            def pe_C(i):
                G, J = steps[i]
                first = (J == 4 * G + 3)
                s_ = i % 3
                lo = lo_of(i)
                for h in range(2):
                    P.op("pe", MM(banks[4 + h][:, lo:512], tri[:], sp2[s_][:, h * 512 + lo:(h + 1) * 512], first, first,
                                  skip_group_check=True),
                         reads=[r_sp[s_], r_const], writes=[bres[4 + h]], inc=(first and h == 1))
                    if not first:
                        sp_prev = (i - 1) % 3
                        lop = lo_of(i - 1)
                        P.op("pe", MM(banks[4 + h][:, lop:512], sl[:], sp2[sp_prev][:, h * 512 + lop:(h + 1) * 512],
                                      False, True, skip_group_check=True),
                             reads=[r_sp[sp_prev], r_const], writes=[bres[4 + h]], inc=(h == 1))

            def act_X(i):
                e_ = i % 3
                x_ = i % 2
                a_ = i % 3
                lo = lo_of(i)
                P.op("act", ACTV(hv(X2[x_][:], lo), hv(pb[2][:], lo), AF.Exp, scale=-1.0),
                     reads=[bres[4], bres[5]], writes=[r_X[x_]])
                P.op("dve", TT(hv(A2[a_][:], lo), hv(E2[e_][:], lo), hv(X2[x_][:], lo), ALU.mult),
                     reads=[r_E[e_], r_X[x_]], writes=[r_A[a_]])

            def pe_pv(i):
                G, J = steps[i]
                first = (J == 4 * G + 3)
                last = (J == 0)
                ob = 6 + G % 2
                a_ = i % 3
                lo = lo_of(i)
                for h in range(2):
                    hs = slice(h * 64, (h + 1) * 64)
                    P.op("pe", MM(banks[ob][hs, lo:512], V2[:, J, hs], A2[a_][:, h * 512 + lo:(h + 1) * 512], first, last,
                                  skip_group_check=True),
                         reads=[r_V2, r_A[a_]], writes=[bres[ob]], inc=(h == 1))
                if last:
                    P.op("dve", CP(oT[:, pi, G * 512:(G + 1) * 512], banks[ob]),
                         reads=[bres[ob]], writes=[r_oT[pi][G]])

            for i in range(n + 3):
                if i < n:
                    pe_z(i)
                if 0 <= i - 3 < n:
                    pe_pv(i - 3)
                diag = False
                if 0 <= i - 1 < n:
                    act_EL(i - 1)
                    G_, J_ = steps[i - 1]
                    diag = J_ >= 4 * G_
                    if not diag:
                        act_L(i - 1)
                if 0 <= i - 2 < n:
                    act_X(i - 2)
                if 0 <= i - 1 < n:
                    if diag:
                        act_L(i - 1)
                    pe_C(i - 1)
        if len(pairsB) == 4:
            normalize_group(4)
        fenceGB = P.fence()
        es_gb.close()

    if dbg == "oT":
        for c in range(8):
            out_tokens.append(P.dma("sp", DMA(dbg_d[:, c, :], oT[:, c, :]), reads=r_oT[c]))
    fenceATT = P.fence()
    es_att.close()

    if do_ffn:
        fF = fenceATT
        es_f = ExitStack()
        Wd = P.sb("Wd", [128, NFC, D], BF16, es_f)
        r_Wd = Res(fF)
        Wo = P.sb("Wo", [128, 8, D], BF16, es_f)
        r_Wo = Res(fF)
        x1 = P.sb("x1", [128, 5, D], F32, es_f)
        r_x1 = [Res(fF) for _ in range(5)]
        h2T = P.sb("h2T", [128, 8, 512], BF16, es_f)
        r_h2T = [Res(fF) for _ in range(4)]
        actT = P.sb("actT", [128, NFC, 512], BF16, es_f)
        r_act = [Res(fF) for _ in range(NFC)]
        ring = [P.sb(f"ring{i}", [128, 8, 256], BF16, es_f) for i in range(3)]
        r_ring = [Res(fF) for _ in range(3)]
        fnw = P.sb("fnw", [128, D], F32, es_f)
        r_fnw = Res(fF)
        sg = [P.sb(f"sg{i}", [128, 512], BF16, es_f) for i in range(1)]
        r_sg = [Res(fF)]
        h2 = [P.sb(f"h2{i}", [128, D], BF16, es_f) for i in range(4)]
        r_h2 = [Res(fF) for _ in range(4)]
        r_ss2b = [Res(fF) for _ in range(4)]
        ss2 = P.sb("ss2", [128, 12], F32, es_f)
        r_ss2 = Res(fF)

        P.dma("pool", DMA(Wo[:], wout_d.rearrange("(c p) d -> p c d", p=128)), writes=[r_Wo])
        for c in range(8):
            P.op("dve", TS(Wo[:, c, :], Wo[:, c, :], wcol[:, c:c + 1], ALU.mult), reads=[r_const], writes=[r_Wo])
        for q4_ in range(2):
            P.dma("pool", DMA(Wd[:, q4_ * 11:(q4_ + 1) * 11, :],
                              wd_d[q4_ * 11 * 128:(q4_ + 1) * 11 * 128, :].rearrange("(f p) d -> p f d", p=128)),
                  writes=[r_Wd])
        P.dma("sp", DMA(fnw[:], fnw_d), writes=[r_fnw])

        ydr = [0]

        def yd_bank():
            b = ydr[0] % 4
            ydr[0] += 1
            return b

        def step1_block(tt, b4):
            tbk = tt * 4 + b4
            xsl = tbk % 5
            tsl = slice(tbk * 128, (tbk + 1) * 128)
            P.dma("sp", DMA(x1[:, xsl, :], x_d[tsl, :]), writes=[r_x1[xsl]])
            for half in range(2):
                bk = yd_bank()
                hsl = slice(half * 512, (half + 1) * 512)
                for c in range(8):
                    P.op("pe", MM(banks[bk], oT[:, c, tsl], Wo[:, c, hsl], c == 0, c == 7),
                         reads=[r_Wo] + [r_oT[cc][tt] for cc in range(8)], writes=[bres[bk]], inc=(c == 7))
                P.op("dve", TT(x1[:, xsl, hsl], banks[bk], x1[:, xsl, hsl], ALU.add),
                     reads=[bres[bk]], writes=[r_x1[xsl]])
            P.op("act", ACTV(h2[b4][:], x1[:, xsl, :], AF.Square, accum_out=ss2[:, b4:b4 + 1]),
                 reads=[r_x1[xsl]], writes=[r_h2[b4], r_ss2b[b4]])
            P.op("act", ACTV(ss2[:, 4 + b4:5 + b4], ss2[:, b4:b4 + 1], AF.Sqrt, scale=1.0 / D, bias=EPS),
                 reads=[r_ss2b[b4]], writes=[r_ss2b[b4]])
            P.op("dve", RCP(ss2[:, 8 + b4:9 + b4], ss2[:, 4 + b4:5 + b4]), reads=[r_ss2b[b4]], writes=[r_ss2b[b4]])
            P.op("dve", STT(h2[b4][:], x1[:, xsl, :], ss2[:, 8 + b4:9 + b4], fnw[:], ALU.mult, ALU.mult),
                 reads=[r_x1[xsl], r_ss2b[b4], r_fnw], writes=[r_h2[b4]])

        def step1_tr(b4):
            bk = yd_bank()
            for c in range(8):
                csl = slice(c * 128, (c + 1) * 128)
                P.op("pe", TR(bT[bk][:, csl], h2[b4][:, csl], ident[:]),
                     reads=[r_h2[b4], r_const], writes=[bres[bk]], inc=(c == 7))
            P.op("act", ACTV(h2T[:, :, b4 * 128:(b4 + 1) * 128], bT[bk].rearrange("p (c t) -> p c t", t=128), AF.Copy),
                 reads=[bres[bk]], writes=[r_h2T[b4]])

        def finish_norm(tt):
            pass

        gur = 0
        for b4 in range(4):
            step1_block(0, b4)
            if b4 >= 1:
                step1_tr(b4 - 1)
        step1_tr(3)
        for tt in range(8):
            finish_norm(tt)
            for fc in range(NFC):
                rg = fc % 3
                P.dma("pool", DMA(ring[rg][:], wgu_bf[fc].rearrange("p (c j) -> p c j", j=256)),
                      reads=[r_wgubf[fc]], writes=[r_ring[rg]])
                bg = 4 + (gur % 2) * 2
                bu = bg + 1
                gur += 1
                for (bk, off) in ((bg, 0), (bu, 128)):
                    for c in range(8):
                        P.op("pe", MM(banks[bk], ring[rg][:, c, off:off + 128], h2T[:, c, :], c == 0, c == 7),
                             reads=[r_ring[rg]] + r_h2T, writes=[bres[bk]], inc=(c == 7))
                s_ = 0
                P.op("act", ACTV(sg[s_][:], banks[bg], AF.Silu), reads=[bres[bg]], writes=[r_sg[s_]])
                P.op("dve", TT(actT[:, fc, :], banks[bu], sg[s_][:], ALU.mult),
                     reads=[bres[bu], r_sg[s_]], writes=[r_act[fc]])
            nxt = tt + 1 < 8
            for b4 in range(4):
                tbk = tt * 4 + b4
                xsl = tbk % 5
                if nxt:
                    step1_block(tt + 1, b4)
                for half in range(2):
                    bk = yd_bank()
                    hsl = slice(half * 512, (half + 1) * 512)
                    for fc in range(NFC):
                        P.op("pe", MM(banks[bk], actT[:, fc, b4 * 128:(b4 + 1) * 128], Wd[:, fc, hsl],
                                      fc == 0, fc == NFC - 1),
                             reads=[r_act[fc], r_Wd], writes=[bres[bk]], inc=(fc == NFC - 1))
                    P.op("dve", TT(x1[:, xsl, hsl], banks[bk], x1[:, xsl, hsl], ALU.add),
                         reads=[bres[bk]], writes=[r_x1[xsl]])
                out_tokens.append(P.dma("sp", DMA(out_d[tbk * 128:(tbk + 1) * 128, :], x1[:, xsl, :]),
                                        reads=[r_x1[xsl]]))
                if nxt and b4 >= 1:
                    step1_tr(b4 - 1)
            if nxt:
                step1_tr(3)
        es_f.close()

    P.wait("sp", out_tokens)
    P.emit()
    P.close()
    return nc


def make_inputs(x, attn_norm_w, w_in, q_norm_w, k_norm_w, dil_out_norm_w, sb_out_norm_w,
                w_out, ffn_norm_w, w_gate, w_up, w_down):
    f = np.float32
    w_in = np.asarray(w_in, f)[0]
    pairs = []
    for g in range(2):
        o0 = g * 1536
        for p in range(4):
            cols = [w_in[:, o0 + j * 512 + p * 128: o0 + j * 512 + (p + 1) * 128] for j in range(3)]
            pairs.append(np.concatenate(cols, axis=1))
    w_pairs = np.ascontiguousarray(np.stack(pairs, 0))
    wg = np.asarray(w_gate, f)[0].reshape(8, 128, NFC, 128)
    wu = np.asarray(w_up, f)[0].reshape(8, 128, NFC, 128)
    wgu = np.concatenate([wg, wu], axis=3)
    wgu = np.ascontiguousarray(wgu.transpose(2, 1, 0, 3)).reshape(NFC, 128, 2048)
    qw = np.asarray(q_norm_w, f)[0]
    kw = np.asarray(k_norm_w, f)[0]
    qkw = np.concatenate([qw, qw, kw, kw])[None, :]
    wcat = np.concatenate([np.asarray(dil_out_norm_w, f)[0], np.asarray(sb_out_norm_w, f)[0]])
    pos = np.arange(S, dtype=f)
    inv = (f(10000.0) ** (-np.arange(0, 64, 2, dtype=f) / f(64))).astype(f)
    ang = (pos[:, None] * inv[None, :]).astype(f)
    rope = np.concatenate([np.cos(ang), np.sin(ang), -np.sin(ang)], axis=1).astype(f)
    shared = {
        "anw_bc": np.ascontiguousarray(np.broadcast_to(np.asarray(attn_norm_w, f)[0][None, :], (128, D))),
        "fnw_bc": np.ascontiguousarray(np.broadcast_to(np.asarray(ffn_norm_w, f)[0][None, :], (128, D))),
        "qkw_bc": np.ascontiguousarray(np.broadcast_to(qkw, (128, 256))),
        "wcol": np.ascontiguousarray(wcat.reshape(8, 128).T),
        "rope": rope,
        "w_pairs": w_pairs,
        "w_out": np.ascontiguousarray(np.asarray(w_out, f)[0]),
        "wgu": wgu,
        "w_down": np.ascontiguousarray(np.asarray(w_down, f)[0]),
    }
    xs = np.asarray(x, f)
    return [dict(shared, x=np.ascontiguousarray(xs[b])) for b in range(xs.shape[0])]


_NC_CACHE = {}


def kernel(**inputs):
    in_maps = make_inputs(**inputs)
    if "nc" not in _NC_CACHE:
        _NC_CACHE["nc"] = build()
    nc = _NC_CACHE["nc"]
    res = run_bass_kernel_spmd(nc, in_maps, core_ids=list(range(8)))
    return np.stack([np.asarray(r["out"], np.float32) for r in res.results], axis=0)
```

```python
from contextlib import ExitStack
import numpy as np
import concourse.bass as bass
import concourse.mybir as mybir
from concourse.bass_utils import run_bass_kernel_spmd

F32 = mybir.dt.float32
BF16 = mybir.dt.bfloat16
AF = mybir.ActivationFunctionType
ALU = mybir.AluOpType
AX = mybir.AxisListType

S = 4096
D = 1024
NB = S // 128
DFF = 2816
NFC = DFF // 128
EPS = 1e-6
ENGS = ("pe", "act", "dve", "pool", "sp")
NDMASEM = 8


class Res:
    __slots__ = ("w", "r", "pend")

    def __init__(self, init=None):
        self.w = dict(init) if init else {}
        self.r = {}
        self.pend = None


def _merge(dst, src):
    for k, v in src.items():
        if dst.get(k, 0) < v:
            dst[k] = v


class Prog:
    def __init__(self, nc):
        self.nc = nc
        self.es = ExitStack()
        self.ops = {e: [] for e in ENGS}
        self.sem = {}
        self.cnt = {}
        self.seen = {e: {} for e in ENGS}
        self.pending = {e: [] for e in ENGS}
        for e in ENGS:
            self.newsem("E_" + e)
        self.dq = {}
        for q in ("sp", "pool"):
            self.dq[q] = {"keys": [self.newsem(f"D_{q}{i}") for i in range(NDMASEM)], "i": 0}

    def newsem(self, key):
        self.sem[key] = self.es.enter_context(self.nc.semaphore(key))
        self.cnt[key] = 0
        return key

    def sb(self, name, shape, dt, es=None):
        return (es or self.es).enter_context(self.nc.sbuf_tensor(name, list(shape), dt))

    def ps(self, name, shape, dt):
        return self.es.enter_context(self.nc.psum_tensor(name, list(shape), dt))

    def fence(self):
        f = {}
        for k, v in self.cnt.items():
            if v > 0:
                f[k] = v
        for e in ENGS:
            assert not self.pending[e], "fence with pending un-tokened ops"
        return f

    def _collect(self, eng, reads, writes, deps):
        need = {}
        for r in reads:
            assert r.pend is None or r.pend == eng, "resource pending on another engine"
            _merge(need, r.w)
        for r in writes:
            assert r.pend is None or r.pend == eng, "resource pending on another engine"
            _merge(need, r.w)
            _merge(need, r.r)
        for d in deps:
            if d:
                _merge(need, d)
        ws = []
        seen = self.seen[eng]
        for k, v in need.items():
            if k == "E_pe" and eng == "pe":
                continue
            if seen.get(k, 0) >= v:
                continue
            seen[k] = v
            ws.append((k, v))
        return ws

    def _commit(self, eng, tok, reads, writes):
        allp = self.pending[eng] + [(reads, writes)]
        self.pending[eng] = []
        for rs, wsx in allp:
            for r in rs:
                _merge(r.r, tok)
                r.pend = None
            for r in wsx:
                r.w = dict(tok)
                r.r = {}
                r.pend = None

    def op(self, eng, fn, reads=(), writes=(), inc=True, deps=()):
        ws = self._collect(eng, reads, writes, deps)
        if inc:
            key = "E_" + eng
            self.cnt[key] += 1
            tok = {key: self.cnt[key]}
            self.ops[eng].append((ws, fn, (key, 1)))
            self._commit(eng, tok, reads, writes)
            return tok
        self.ops[eng].append((ws, fn, None))
        self.pending[eng].append((reads, writes))
        for r in list(reads) + list(writes):
            r.pend = eng
        return None

    def dma(self, q, fn, reads=(), writes=(), deps=()):
        assert not self.pending[q]
        dq = self.dq[q]
        key = dq["keys"][dq["i"] % NDMASEM]
        dq["i"] += 1
        prev = {key: self.cnt[key]} if self.cnt[key] else None
        ws = self._collect(q, reads, writes, list(deps) + [prev])
        self.cnt[key] += 16
        tok = {key: self.cnt[key]}
        self.ops[q].append((ws, fn, (key, 16)))
        for r in reads:
            _merge(r.r, tok)
        for r in writes:
            r.w = dict(tok)
            r.r = {}
        return tok

    def wait(self, eng, deps):
        ws = self._collect(eng, (), (), deps)
        if ws:
            self.ops[eng].append((ws, None, None))

    def emit(self):
        prog = self
        for e in ENGS:
            assert not self.pending[e], f"pending ops on {e}"

        def run(name, e):
            for ws, fn, inc in prog.ops[name]:
                for key, val in ws:
                    e.wait_ge(prog.sem[key], val)
                if fn is None:
                    continue
                ins = fn(e)
                if inc is not None:
                    ins.then_inc(prog.sem[inc[0]], inc[1])

        with self.nc.Block() as block:
            @block.tensor
            def _(e):
                run("pe", e)

            @block.scalar
            def _(e):
                run("act", e)

            @block.vector
            def _(e):
                run("dve", e)

            @block.gpsimd
            def _(e):
                run("pool", e)

            @block.sync
            def _(e):
                run("sp", e)

    def close(self):
        self.es.close()


def MM(out, lhsT, rhs, start, stop, **kw):
    return lambda e: e.matmul(out, lhsT=lhsT, rhs=rhs, start=start, stop=stop, **kw)


def TR(out, in_, idn):
    return lambda e: e.transpose(out, in_, idn)


def ACTV(out, in_, func, **kw):
    return lambda e: e.activation(out=out, in_=in_, func=func, **kw)


def TT(out, in0, in1, op):
    return lambda e: e.tensor_tensor(out=out, in0=in0, in1=in1, op=op)


def TS(out, in0, s1, op0):
    return lambda e: e.tensor_scalar(out=out, in0=in0, scalar1=s1, scalar2=None, op0=op0)


def STT(out, in0, scalar, in1, op0, op1):
    return lambda e: e.scalar_tensor_tensor(out=out, in0=in0, scalar=scalar, in1=in1, op0=op0, op1=op1)


def CP(out, in_):
    return lambda e: e.tensor_copy(out=out, in_=in_)


def RCP(out, in_):
    return lambda e: e.reciprocal(out=out, in_=in_)


def RSUM(out, in_):
    return lambda e: e.reduce_sum(out=out, in_=in_, axis=AX.X)


def DMA(out, in_):
    return lambda e: e.dma_start(out=out, in_=in_)


def MSET(ap, v):
    return lambda e: e.memset(ap, v)

PA_BLOCKS = NB
KNOB = {}


def build(pairs=tuple(range(8)), do_ffn=True, dbg=None):
    nc = bass.Bass("TRN2", target_bir_lowering=False)

    def din(name, shape, dt=F32):
        return nc.dram_tensor(name, list(shape), dt, kind="ExternalInput").ap()

    x_d = din("x", [S, D])
    anw_d = din("anw_bc", [128, D])
    fnw_d = din("fnw_bc", [128, D])
    qkw_d = din("qkw_bc", [128, 256])
    wcol_d = din("wcol", [128, 8])
    rope_d = din("rope", [S, 96])
    wp_d = din("w_pairs", [8, D, 384])
    wout_d = din("w_out", [D, D])
    wgu_d = din("wgu", [NFC, 128, 2048])
    wd_d = din("w_down", [DFF, D])
    out_d = nc.dram_tensor("out", [S, D], F32, kind="ExternalOutput").ap()
    dbg_d = None
    if dbg in ("hT", "oT"):
        dbg_d = nc.dram_tensor("dbg", [128, 8, S], BF16, kind="ExternalOutput").ap()

    P = Prog(nc)
    out_tokens = []

    pb = [P.ps(f"pb{i}", [128, 1024], F32) for i in range(4)]
    banks = []
    for i in range(4):
        banks += [pb[i][:, 0:512], pb[i][:, 512:1024]]
    bres = [Res() for _ in range(8)]
    bT = [banks[i].bitcast(BF16) for i in range(8)]

    ident = P.sb("ident", [128, 128], BF16)
    tri = P.sb("tri", [128, 128], BF16)
    sl = P.sb("sl", [128, 128], BF16)
    ones = P.sb("ones", [128, 128], BF16)
    sbm = [P.sb(f"sbm{j}", [128, 512], BF16) for j in range(4)]
    dm1 = P.sb("dm1", [128, 512], BF16)
    dmz = P.sb("dmz", [128, 512], BF16)
    dmz2 = P.sb("dmz2", [128, 512], BF16)
    wcol = P.sb("wcol_sb", [128, 8], F32)
    oT = P.sb("oT", [128, 8, S], BF16)
    r_const = Res()
    r_oT = [[Res() for _ in range(8)] for _ in range(8)]

    def SEL(t_ap, pattern, base, cm, cmp):
        return lambda e: e.affine_select(out=t_ap, in_=t_ap, pattern=pattern, base=base,
                                         channel_multiplier=cm, compare_op=cmp, fill=0.0)

    for t in (ident, tri, sl, ones, dm1, dmz, dmz2, *sbm):
        P.op("pool", MSET(t[:], 1.0), writes=[r_const])
    P.op("pool", SEL(ident[:], [[-1, 128]], 0, 1, ALU.is_equal), writes=[r_const])
    P.op("pool", SEL(tri[:], [[-1, 128]], 0, 1, ALU.is_ge), writes=[r_const])
    P.op("pool", SEL(sl[:], [[1, 128]], 0, -1, ALU.is_gt), writes=[r_const])
    for j in range(4):
        P.op("pool", SEL(sbm[j][:], [[1, 512]], -128 * j, -1, ALU.is_gt), writes=[r_const])
    for m in (dm1, dmz, dmz2):
        for slot in range(4):
            ap = m[:, slot * 128:(slot + 1) * 128]
            if slot % 2 == 0:
                P.op("pool", SEL(ap, [[-1, 128]], 0, 1, ALU.is_ge), writes=[r_const])
            else:
                P.op("pool", SEL(ap, [[1, 128]], 0, -1, ALU.is_ge), writes=[r_const])
    P.op("pool", MSET(dmz[:, 0:128], 0.0), writes=[r_const])
    P.op("pool", MSET(dmz2[:, 0:128], 0.0), writes=[r_const])
    P.op("pool", MSET(dmz2[:, 256:384], 0.0), writes=[r_const])
    P.dma("sp", DMA(wcol[:], wcol_d), writes=[r_const])
    wgu_bf = nc.dram_tensor("wgu_bf", [NFC, 128, 2048], BF16, kind="Internal").ap()
    r_wgubf = [Res() for _ in range(NFC)]

    es_att = ExitStack()
    hT = P.sb("hT", [128, 8, S], BF16, es_att)
    r_hT = [Res() for _ in range(NB)]
    Wp = P.sb("Wp", [128, 8, 384], BF16, es_att)
    r_Wp = Res()
    qkvT = P.sb("qkvT", [128, 3, S], BF16, es_att)
    r_qk = [Res() for _ in range(NB)]
    r_v = [Res() for _ in range(8)]
    sq = [P.sb(f"sq{i}", [128, 512], BF16, es_att) for i in range(4)]
    r_sq = [Res() for _ in range(4)]
    rs = P.sb("rs", [128, 512], F32, es_att)
    r_rs = Res()

    es_pa = ExitStack()
    xs = [P.sb(f"xs{i}", [128, D], F32, es_pa)[:] for i in range(4)]
    for j in range(3):
        v32 = qkvT[:, j, :].bitcast(F32)
        xs += [v32[:, 0:D], v32[:, D:2 * D]]
    NX = len(xs)
    r_xs = [Res() for _ in range(NX)]
    hn = [P.sb(f"hn{i}", [128, D], BF16, es_pa) for i in range(4)]
    r_hn = [Res() for _ in range(4)]
    junk = P.sb("junkA", [128, D], BF16, es_pa)
    r_junk = Res()
    anw = P.sb("anw", [128, D], F32, es_pa)
    r_anw = Res()
    ssA = P.sb("ssA", [128, NB], F32, es_pa)
    sdA = P.sb("sdA", [128, NB], F32, es_pa)
    rsA = P.sb("rsA", [128, NB], F32, es_pa)
    r_ssA = [Res() for _ in range(NB)]
    P.dma("sp", DMA(anw[:], anw_d), writes=[r_anw])
    BS = 2

    def paA1(sb):
        for b in range(BS):
            tb = sb * BS + b
            xi = tb % NX
            tsl = slice(tb * 128, (tb + 1) * 128)
            P.dma("sp", DMA(xs[xi], x_d[tsl, :]), writes=[r_xs[xi]])
            P.op("act", ACTV(junk[:], xs[xi], AF.Square, accum_out=ssA[:, tb:tb + 1]),
                 reads=[r_xs[xi]], writes=[r_junk, r_ssA[sb]])
        c4 = slice(sb * BS, sb * BS + BS)
        P.op("act", ACTV(sdA[:, c4], ssA[:, c4], AF.Sqrt, scale=1.0 / D, bias=EPS), reads=[r_ssA[sb]], writes=[r_ssA[sb]])

    def paA2(sb):
        c4 = slice(sb * BS, sb * BS + BS)
        P.op("dve", RCP(rsA[:, c4], sdA[:, c4]), reads=[r_ssA[sb]], writes=[r_ssA[sb]])

    def paB1(sb):
        for b in range(BS):
            tb = sb * BS + b
            xi = tb % NX
            hi = tb % 4
            bk = tb % 4
            P.op("dve", STT(hn[hi][:], xs[xi], rsA[:, tb:tb + 1], anw[:], ALU.mult, ALU.mult),
                 reads=[r_xs[xi], r_ssA[sb], r_anw], writes=[r_hn[hi]])
            for c in range(8):
                csl = slice(c * 128, (c + 1) * 128)
                P.op("pe", TR(bT[bk][:, csl], hn[hi][:, csl], ident[:]),
                     reads=[r_hn[hi], r_const], writes=[bres[bk]], inc=(c == 7))

    def paB2(sb):
        for b in range(BS):
            tb = sb * BS + b
            bk = tb % 4
            tsl = slice(tb * 128, (tb + 1) * 128)
            if b % 2 == 0:
                P.op("act", ACTV(hT[:, :, tsl], bT[bk].rearrange("p (c t) -> p c t", t=128), AF.Copy),
                     reads=[bres[bk]], writes=[r_hT[tb]])
            else:
                P.op("dve", CP(hT[:, :, tsl], bT[bk].rearrange("p (c t) -> p c t", t=128)),
                     reads=[bres[bk]], writes=[r_hT[tb]])

    nsb = PA_BLOCKS // BS
    for sb in range(nsb + 1):
        if sb >= 1:
            paB1(sb - 1)
        if sb < nsb:
            paA1(sb)
        if sb >= 1:
            paB2(sb - 1)
        if sb < nsb:
            paA2(sb)
    fenceA = P.fence()
    es_pa.close()
    for r_ in r_qk + r_v:
        _merge(r_.w, fenceA)
    casts_pending = [do_ffn]

    if dbg == "hT":
        for c in range(8):
            out_tokens.append(P.dma("sp", DMA(dbg_d[:, c, :], hT[:, c, :]), reads=r_hT))

    def proj_fm(col0, dst_idx, scale, res_for_slice):
        for ts in range(8):
            bk = ts % 2
            tsl = slice(ts * 512, (ts + 1) * 512)
            for c in range(8):
                P.op("pe", MM(banks[bk], Wp[:, c, col0:col0 + 128], hT[:, c, tsl], c == 0, c == 7),
                     reads=[r_Wp] + r_hT[ts * 4:(ts + 1) * 4], writes=[bres[bk]], inc=(c == 7))
            wr = res_for_slice(ts)
            if ts % 2 == 0:
                P.op("act", ACTV(qkvT[:, dst_idx, tsl], banks[bk], AF.Copy, scale=scale),
                     reads=[bres[bk]], writes=wr)
            else:
                P.op("dve", TS(qkvT[:, dst_idx, tsl], banks[bk], scale, ALU.mult),
                     reads=[bres[bk]], writes=wr)

    def load_pair_weights(pi):
        P.dma("pool", DMA(Wp[:], wp_d[pi].rearrange("(c p) e -> p c e", p=128)), writes=[r_Wp])

    def normalize_group(gc):
        for ts in range(8):
            tsl = slice(ts * 512, (ts + 1) * 512)
            for c in range(4):
                if c < 3:
                    P.op("act", ACTV(sq[c][:], oT[:, gc + c, tsl], AF.Square),
                         reads=[r_oT[gc + c][ts]], writes=[r_sq[c]])
                else:
                    P.op("pool", TT(sq[c][:], oT[:, gc + c, tsl], oT[:, gc + c, tsl], ALU.mult),
                         reads=[r_oT[gc + c][ts]], writes=[r_sq[c]])
            bk = 2 + ts % 2
            for c in range(4):
                P.op("pe", MM(banks[bk], ones[:], sq[c][:], c == 0, c == 3),
                     reads=[r_sq[c], r_const], writes=[bres[bk]], inc=(c == 3))
            P.op("act", ACTV(rs[:], banks[bk], AF.Ln, scale=1.0 / 512, bias=EPS),
                 reads=[bres[bk]], writes=[r_rs])
            P.op("act", ACTV(rs[:], rs[:], AF.Exp, scale=-0.5), reads=[r_rs], writes=[r_rs])
            P.op("dve", TT(oT[:, gc:gc + 4, tsl], oT[:, gc:gc + 4, tsl],
                           rs[:].unsqueeze(1).to_broadcast([128, 4, 512]), ALU.mult),
                 reads=[r_rs], writes=[r_oT[gc + c][ts] for c in range(4)])

    LAG = 3
    NSL = 4
    pairsA = [p for p in pairs if p < 4]
    bank_rr = [0]

    def next_bank():
        b = bank_rr[0] % 4
        bank_rr[0] += 1
        return b

    if pairsA:
        es_ga = ExitStack()
        qkw = P.sb("qkw", [128, 256], F32, es_ga)
        r_qkw = Res(fenceA)
        P.dma("sp", DMA(qkw[:], qkw_d), writes=[r_qkw])
        P.op("dve", TS(qkw[:, 0:128], qkw[:, 0:128], 0.125, ALU.mult), writes=[r_qkw])
        fprev = fenceA
        for pi in pairsA:
            load_pair_weights(pi)
            es_p = ExitStack()
            NS = 4
            sa = [P.sb(f"sa{pi}_{i}", [128, 512], F32, es_p) for i in range(NS)]
            qn = [P.sb(f"qn{pi}_{i}", [128, 512], F32, es_p) for i in range(NS)]
            t2 = [P.sb(f"t2{pi}_{i}", [128, 512], F32, es_p) for i in range(NS)]
            qr = [P.sb(f"qr{pi}_{i}", [128, 512], BF16, es_p) for i in range(NS + 1)]
            st = [P.sb(f"st{pi}_{i}", [128, 24], F32, es_p) for i in range(NS)]
            ropeb = [P.sb(f"ropeb{pi}_{i}", [128, 2, 96], F32, es_p) for i in range(NS)]
            r_sa = [Res(fprev) for _ in range(NS)]
            r_qn = [Res(fprev) for _ in range(NS)]
            r_t2 = [Res(fprev) for _ in range(NS)]
            r_qr = [Res(fprev) for _ in range(NS + 1)]
            r_st = [Res(fprev) for _ in range(NS)]
            r_rope = [Res(fprev) for _ in range(NS)]
            w3 = lambda ap: ap.rearrange("p (s d) -> p s d", d=64)
            w4 = lambda ap: ap.rearrange("p (b s d) -> p b s d", b=2, d=32)
            w5 = lambda ap: ap.rearrange("p (b s h d) -> p b s h d", b=2, h=2, d=32)
            wb = lambda ap: ap.rearrange("p (b e) -> p b e", e=256)
            NBB = NB // 2

            def st0(bb):
                k = bb % NS
                tb0 = bb * 2
                bk = bb % 2
                P.dma("sp", DMA(ropeb[k][:], rope_d[tb0 * 128:(tb0 + 2) * 128, :].rearrange("(b p) d -> p b d", p=128)),
                      writes=[r_rope[k]])
                for b in range(2):
                    tsl = slice((tb0 + b) * 128, (tb0 + b + 1) * 128)
                    for c in range(8):
                        P.op("pe", MM(banks[bk][:, b * 256:(b + 1) * 256], hT[:, c, tsl], Wp[:, c, 0:256], c == 0, c == 7),
                             reads=[r_Wp, r_hT[tb0 + b]], writes=[bres[bk]], inc=(b == 1 and c == 7))

            def st1a(bb):
                k = bb % NS
                bk = bb % 2
                P.op("act", ACTV(sa[k][:], banks[bk], AF.Square), reads=[bres[bk]], writes=[r_sa[k]])
                P.op("dve", RSUM(st[k][:, 0:8], w3(sa[k][:])), reads=[r_sa[k]], writes=[r_st[k]])
                P.op("act", ACTV(st[k][:, 8:16], st[k][:, 0:8], AF.Sqrt, scale=1.0 / 64, bias=EPS),
                     reads=[r_st[k]], writes=[r_st[k]])

            def st1b(bb):
                k = bb % NS
                bk = bb % 2
                P.op("dve", RCP(st[k][:, 16:24], st[k][:, 8:16]), reads=[r_st[k]], writes=[r_st[k]])
                P.op("dve", TT(w3(qn[k][:]), w3(banks[bk]), st[k][:, 16:24].unsqueeze(2).to_broadcast([128, 8, 64]),
                               ALU.mult), reads=[bres[bk], r_st[k]], writes=[r_qn[k]])
                P.op("pool", TT(wb(qn[k][:]), wb(qn[k][:]), qkw[:].unsqueeze(1).to_broadcast([128, 2, 256]), ALU.mult),
                     reads=[r_qkw], writes=[r_qn[k]])

            def st2(bb):
                k = bb % NS
                cosb = ropeb[k][:, :, 0:32].unsqueeze(2).to_broadcast([128, 2, 8, 32])
                sinb = ropeb[k][:, :, 32:64].unsqueeze(2).to_broadcast([128, 2, 4, 32])
                nsinb = ropeb[k][:, :, 64:96].unsqueeze(2).to_broadcast([128, 2, 4, 32])
                P.op("dve", TT(w4(sa[k][:]), w4(qn[k][:]), cosb, ALU.mult),
                     reads=[r_qn[k], r_rope[k]], writes=[r_sa[k]])
                P.op("pool", TT(w5(t2[k][:])[:, :, :, 0, :], w5(qn[k][:])[:, :, :, 1, :], nsinb, ALU.mult),
                     reads=[r_qn[k], r_rope[k]], writes=[r_t2[k]])
                P.op("pool", TT(w5(t2[k][:])[:, :, :, 1, :], w5(qn[k][:])[:, :, :, 0, :], sinb, ALU.mult),
                     reads=[r_qn[k], r_rope[k]], writes=[r_t2[k]])

            def st3a(bb):
                k = bb % NS
                kq = bb % (NS + 1)
                P.op("dve", TT(qr[kq][:], sa[k][:], t2[k][:], ALU.add), reads=[r_sa[k], r_t2[k]], writes=[r_qr[kq]])

            def st3b(bb):
                kq = bb % (NS + 1)
                tb0 = bb * 2
                t2sl = slice(tb0 * 128, (tb0 + 2) * 128)
                bk2 = 2 + bb % 2
                for j in range(4):
                    jsl = slice(j * 128, (j + 1) * 128)
                    P.op("pe", TR(bT[bk2][:, jsl], qr[kq][:, jsl], ident[:]),
                         reads=[r_qr[kq], r_const], writes=[bres[bk2]], inc=(j == 3))
                P.op("act", ACTV(qkvT[:, 0:2, t2sl].rearrange("p j (b t) -> p j b t", t=128),
                                 bT[bk2][:, 0:512].rearrange("p (b j t) -> p j b t", j=2, t=128), AF.Copy),
                     reads=[bres[bk2]], writes=[r_qk[tb0], r_qk[tb0 + 1]])

            for it in range(NBB + 4):
                if it < NBB:
                    st0(it)
                if 0 <= it - 1 < NBB:
                    st1a(it - 1)
                if 0 <= it - 2 < NBB:
                    st2(it - 2)
                if 0 <= it - 1 < NBB:
                    st1b(it - 1)
                if 0 <= it - 3 < NBB:
                    st3a(it - 3)
                if 0 <= it - 4 < NBB:
                    st3b(it - 4)
            proj_fm(256, 2, 1.0, lambda ts: [r_v[ts]])
            fmid = P.fence()
            es_p.close()

            es_a = ExitStack()
            vaug = [P.sb(f"vaug{pi}_{i}", [128, NB, 128], BF16, es_a) for i in range(3)]
            r_vaug = [Res(fmid) for _ in range(3)]
            expS = [P.sb(f"expS{pi}_{i}", [128, 512], BF16, es_a) for i in range(NSL)]
            r_expS = [Res(fmid) for _ in range(NSL)]
            ptm = [P.sb(f"ptm{pi}_{i}", [128, 512], BF16, es_a) for i in range(NSL)]
            r_ptm = [Res(fmid) for _ in range(NSL)]
            rden = [P.sb(f"rden{pi}_{i}", [128, 512], F32, es_a) for i in range(1)]
            r_rden = [Res(fmid), Res(fmid)]
            for i in range(3):
                P.op("pool", MSET(vaug[i][:, :, 64:128], 1.0), writes=[r_vaug[i]])
            if casts_pending[0]:
                casts_pending[0] = False
                for fc in range(NFC):
                    P.dma("pool", DMA(wgu_bf[fc], wgu_d[fc]), writes=[r_wgubf[fc]], deps=[fmid])
            VIDX = {1: 0, 4: 1, 16: 2}

            vi = 0
            gi = 0
            for hp in range(2):
                p0, p1 = hp * 64, hp * 64 + 64
                for R in range(2):
                    started = [False] * 4
                    groups = []
                    for r in (1, 4, 16):
                        nb = NB // r
                        if r == 16:
                            for c in range(0, 16, 2):
                                groups.append((r, [(c, R - 1, R), (c, R, R), (c + 1, R - 1, R), (c + 1, R, R)]))
                        else:
                            per = nb // 2
                            for c in range(r):
                                for n in range(R * per, (R + 1) * per, 2):
                                    groups.append((r, [(c, n - 1, n), (c, n, n), (c, n, n + 1), (c, n + 1, n + 1)]))

                    def build_vaug(r, vi_):
                        nb = NB // r
                        for g8 in range(4):
                            bk = next_bank()
                            for j in range(8):
                                blk = g8 * 8 + j
                                c, n = blk // nb, blk % nb
                                st_ = n * 128 * r + c
                                P.op("pe", TR(bT[bk][:, j * 64:(j + 1) * 64], qkvT[p0:p1, 2, st_:st_ + 127 * r + 1:r],
                                              ident[p0:p1, p0:p1]),
                                     reads=r_v + [r_const], writes=[bres[bk]], inc=(j == 7))
                            P.op("dve", CP(vaug[vi_][:, g8 * 8:(g8 + 1) * 8, 0:64],
                                           bT[bk][:, 0:512].rearrange("p (j d) -> p j d", d=64)),
                                 reads=[bres[bk]], writes=[r_vaug[vi_]])

                    def emit_pv(item):
                        r, slots, pslot, vi_ = item
                        nb = NB // r
                        mms = []
                        for si, (c, kb, qb) in enumerate(slots):
                            if kb < 0:
                                continue
                            vblk = c * nb + kb
                            if r == 16:
                                for u in range(4):
                                    stt = not started[u]
                                    started[u] = True
                                    q0 = si * 128 + u * 32
                                    mms.append((MM(banks[4 + u][:, c:512:16], vaug[vi_][:, vblk, :],
                                                   ptm[pslot][:, q0:q0 + 32], stt, True, skip_group_check=True), u))
                            else:
                                tok0 = qb * 128 * r + c - R * 2048
                                u = tok0 // 512
                                lo = tok0 - u * 512
                                stt = not started[u]
                                started[u] = True
                                mms.append((MM(banks[4 + u][:, lo:lo + 127 * r + 1:r], vaug[vi_][:, vblk, :],
                                               ptm[pslot][:, si * 128:(si + 1) * 128], stt, True,
                                               skip_group_check=True), u))
                        for kk, (fn, u) in enumerate(mms):
                            P.op("pe", fn, reads=[r_vaug[vi_], r_ptm[pslot]], writes=[bres[4 + u]],
                                 inc=(kk == len(mms) - 1))

                    pend = []
                    if R == 0:
                        for r_ in (1, 4, 16):
                            build_vaug(r_, VIDX[r_])
                    for (r, slots) in groups:
                        vi = VIDX[r]
                        bk = next_bank()
                        es_ = gi % NSL
                        for si, (c, kb, qb) in enumerate(slots):
                            ks = max(kb, 0) * 128 * r + c
                            qs = qb * 128 * r + c
                            P.op("pe", MM(banks[bk][:, si * 128:(si + 1) * 128],
                                          qkvT[p0:p1, 1, ks:ks + 127 * r + 1:r], qkvT[p0:p1, 0, qs:qs + 127 * r + 1:r],
                                          True, True),
                                 reads=r_qk, writes=[bres[bk]], inc=(si == 3))
                        P.op("act", ACTV(expS[es_][:], banks[bk], AF.Exp), reads=[bres[bk]], writes=[r_expS[es_]])
                        inv = [kb < 0 for (_, kb, _) in slots]
                        mk = dmz2 if (inv[0] and inv[2]) else (dmz if inv[0] else dm1)
                        P.op("dve", TT(ptm[es_][:], expS[es_][:], mk[:], ALU.mult),
                             reads=[r_expS[es_], r_const], writes=[r_ptm[es_]])
                        pend.append((r, slots, es_, vi))
                        if len(pend) > LAG:
                            emit_pv(pend.pop(0))
                        gi += 1
                    while pend:
                        emit_pv(pend.pop(0))
                    for u in range(4):
                        rd = u % 2
                        rsl = slice(rd * 64, rd * 64 + 64)
                        ts_ = R * 4 + u
                        P.op("act", ACTV(rden[0][rsl, :], banks[4 + u][64:128, :], AF.Ln),
                             reads=[bres[4 + u]], writes=[r_rden[rd]])
                        P.op("act", ACTV(rden[0][rsl, :], rden[0][rsl, :], AF.Exp, scale=-1.0),
                             reads=[], writes=[r_rden[rd]])
                        P.op("dve", TT(oT[p0:p1, pi, ts_ * 512:(ts_ + 1) * 512], banks[4 + u][0:64, :],
                                       rden[0][rsl, :], ALU.mult),
                             reads=[bres[4 + u], r_rden[rd]], writes=[r_oT[pi][ts_]])
            fprev = P.fence()
            es_a.close()
        if len(pairsA) == 4:
            normalize_group(0)
        fenceGA = P.fence()
        es_ga.close()
    else:
        fenceGA = fenceA

    pairsB = [p for p in pairs if p >= 4]
    if pairsB:
        es_gb = ExitStack()
        fB = fenceGA
        V2 = P.sb("V2", [128, NB, 128], BF16, es_gb)
        r_V2 = Res(fB)
        E2 = [P.sb(f"E2_{i}", [128, 1024], BF16, es_gb) for i in range(3)]
        r_E = [Res(fB) for _ in range(3)]
        sp2 = [P.sb(f"sp2_{i}", [128, 1024], BF16, es_gb) for i in range(3)]
        r_sp = [Res(fB) for _ in range(3)]
        X2 = [P.sb(f"X2_{i}", [128, 1024], BF16, es_gb) for i in range(2)]
        r_X = [Res(fB) for _ in range(2)]
        A2 = [P.sb(f"A2_{i}", [128, 1024], BF16, es_gb) for i in range(3)]
        r_A = [Res(fB) for _ in range(3)]

        for pi in pairsB:
            load_pair_weights(pi)
            proj_fm(0, 0, 0.125, lambda ts: r_qk[ts * 4:(ts + 1) * 4])
            proj_fm(128, 1, 1.0, lambda ts: r_qk[ts * 4:(ts + 1) * 4])
            proj_fm(256, 2, 1.0, lambda ts: [r_v[ts]])
            for g8 in range(4):
                bk = 2 + g8 % 2
                for j in range(8):
                    tb = g8 * 8 + j
                    P.op("pe", TR(bT[bk][:, j * 128:(j + 1) * 128], qkvT[:, 2, tb * 128:(tb + 1) * 128], ident[:]),
                         reads=r_v + [r_const], writes=[bres[bk]], inc=(j == 7))
                P.op("dve", CP(V2[:, g8 * 8:(g8 + 1) * 8, :], bT[bk].rearrange("p (j d) -> p j d", d=128)),
                     reads=[bres[bk]], writes=[r_V2])

            steps = [(G, J) for G in range(8) for J in range(4 * G + 3, -1, -1)]
            n = len(steps)

            def lo_of(i):
                G, J = steps[i]
                return max(0, J - 4 * G) * 128

            def hv(ap, lo):
                v = ap.rearrange("p (h t) -> p h t", h=2)
                return v if lo == 0 else v[:, :, lo:512]

            def pe_z(i):
                G, J = steps[i]
                k = i % 2
                lo = lo_of(i)
                for h in range(2):
                    hs = slice(h * 64, (h + 1) * 64)
                    P.op("pe", MM(banks[2 * k + h][:, lo:512], qkvT[hs, 1, J * 128:(J + 1) * 128],
                                  qkvT[hs, 0, G * 512 + lo:(G + 1) * 512], True, True),
                         reads=r_qk, writes=[bres[2 * k + h]], inc=(h == 1))

            def act_EL(i):
                G, J = steps[i]
                k = i % 2
                e_ = i % 3
                s_ = i % 3
                lo = lo_of(i)
                P.op("act", ACTV(hv(E2[e_][:], lo), hv(pb[k][:], lo), AF.Exp),
                     reads=[bres[2 * k], bres[2 * k + 1]], writes=[r_E[e_]])
                if J >= 4 * G:
                    ev = hv(E2[e_][:], lo)
                    P.op("dve", TT(ev, ev, sbm[J - 4 * G][:, lo:512].unsqueeze(1).to_broadcast([128, 2, 512 - lo]),
                                   ALU.mult), reads=[r_const], writes=[r_E[e_]])

            def act_L(i):
                e_ = i % 3
                s_ = i % 3
                lo = lo_of(i)
                P.op("act", ACTV(hv(sp2[s_][:], lo), hv(E2[e_][:], lo), AF.Ln, bias=1.0, scale=1.0),
                     reads=[r_E[e_]], writes=[r_sp[s_]])

            def pe_C(i):
                G, J = steps[i]
                first = (J == 4 * G + 3)
                s_ = i % 3
                lo = lo_of(i)
                for h in range(2):
                    P.op("pe", MM(banks[4 + h][:, lo:512], tri[:], sp2[s_][:, h * 512 + lo:(h + 1) * 512], first, first,
                                  skip_group_check=True),
                         reads=[r_sp[s_], r_const], writes=[bres[4 + h]], inc=(first and h == 1))
                    if not first:
                        sp_prev = (i - 1) % 3
                        lop = lo_of(i - 1)
                        P.op("pe", MM(banks[4 + h][:, lop:512], sl[:], sp2[sp_prev][:, h * 512 + lop:(h + 1) * 512],
                                      False, True, skip_group_check=True),
                             reads=[r_sp[sp_prev], r_const], writes=[bres[4 + h]], inc=(h == 1))

            def act_X(i):
                e_ = i % 3
                x_ = i % 2
                a_ = i % 3
                lo = lo_of(i)
                P.op("act", ACTV(hv(X2[x_][:], lo), hv(pb[2][:], lo), AF.Exp, scale=-1.0),
                     reads=[bres[4], bres[5]], writes=[r_X[x_]])
                P.op("dve", TT(hv(A2[a_][:], lo), hv(E2[e_][:], lo), hv(X2[x_][:], lo), ALU.mult),
                     reads=[r_E[e_], r_X[x_]], writes=[r_A[a_]])

            def pe_pv(i):
                G, J = steps[i]
                first = (J == 4 * G + 3)
                last = (J == 0)
                ob = 6 + G % 2
                a_ = i % 3
                lo = lo_of(i)
                for h in range(2):
                    hs = slice(h * 64, (h + 1) * 64)
                    P.op("pe", MM(banks[ob][hs, lo:512], V2[:, J, hs], A2[a_][:, h * 512 + lo:(h + 1) * 512], first, last,
                                  skip_group_check=True),
                         reads=[r_V2, r_A[a_]], writes=[bres[ob]], inc=(h == 1))
                if last:
                    P.op("dve", CP(oT[:, pi, G * 512:(G + 1) * 512], banks[ob]),
                         reads=[bres[ob]], writes=[r_oT[pi][G]])

            for i in range(n + 3):
                if i < n:
                    pe_z(i)
                if 0 <= i - 3 < n:
                    pe_pv(i - 3)
                diag = False
                if 0 <= i - 1 < n:
                    act_EL(i - 1)
                    G_, J_ = steps[i - 1]
                    diag = J_ >= 4 * G_
                    if not diag:
                        act_L(i - 1)
                if 0 <= i - 2 < n:
                    act_X(i - 2)
                if 0 <= i - 1 < n:
                    if diag:
                        act_L(i - 1)
                    pe_C(i - 1)
        if len(pairsB) == 4:
            normalize_group(4)
        fenceGB = P.fence()
        es_gb.close()

    if dbg == "oT":
        for c in range(8):
            out_tokens.append(P.dma("sp", DMA(dbg_d[:, c, :], oT[:, c, :]), reads=r_oT[c]))
    fenceATT = P.fence()
    es_att.close()

    if do_ffn:
        fF = fenceATT
        es_f = ExitStack()
        Wd = P.sb("Wd", [128, NFC, D], BF16, es_f)
        r_Wd = Res(fF)
        Wo = P.sb("Wo", [128, 8, D], BF16, es_f)
        r_Wo = Res(fF)
        x1 = P.sb("x1", [128, 5, D], F32, es_f)
        r_x1 = [Res(fF) for _ in range(5)]
        h2T = P.sb("h2T", [128, 8, 512], BF16, es_f)
        r_h2T = [Res(fF) for _ in range(4)]
        actT = P.sb("actT", [128, NFC, 512], BF16, es_f)
        r_act = [Res(fF) for _ in range(NFC)]
        ring = [P.sb(f"ring{i}", [128, 8, 256], BF16, es_f) for i in range(3)]
        r_ring = [Res(fF) for _ in range(3)]
        fnw = P.sb("fnw", [128, D], F32, es_f)
        r_fnw = Res(fF)
        sg = [P.sb(f"sg{i}", [128, 512], BF16, es_f) for i in range(1)]
        r_sg = [Res(fF)]
        h2 = [P.sb(f"h2{i}", [128, D], BF16, es_f) for i in range(4)]
        r_h2 = [Res(fF) for _ in range(4)]
        r_ss2b = [Res(fF) for _ in range(4)]
        ss2 = P.sb("ss2", [128, 12], F32, es_f)
        r_ss2 = Res(fF)

        P.dma("pool", DMA(Wo[:], wout_d.rearrange("(c p) d -> p c d", p=128)), writes=[r_Wo])
        for c in range(8):
            P.op("dve", TS(Wo[:, c, :], Wo[:, c, :], wcol[:, c:c + 1], ALU.mult), reads=[r_const], writes=[r_Wo])
        for q4_ in range(2):
            P.dma("pool", DMA(Wd[:, q4_ * 11:(q4_ + 1) * 11, :],
                              wd_d[q4_ * 11 * 128:(q4_ + 1) * 11 * 128, :].rearrange("(f p) d -> p f d", p=128)),
                  writes=[r_Wd])
        P.dma("sp", DMA(fnw[:], fnw_d), writes=[r_fnw])

        ydr = [0]

        def yd_bank():
            b = ydr[0] % 4
            ydr[0] += 1
            return b

        def step1_block(tt, b4):
            tbk = tt * 4 + b4
            xsl = tbk % 5
            tsl = slice(tbk * 128, (tbk + 1) * 128)
            P.dma("sp", DMA(x1[:, xsl, :], x_d[tsl, :]), writes=[r_x1[xsl]])
            for half in range(2):
                bk = yd_bank()
                hsl = slice(half * 512, (half + 1) * 512)
                for c in range(8):
                    P.op("pe", MM(banks[bk], oT[:, c, tsl], Wo[:, c, hsl], c == 0, c == 7),
                         reads=[r_Wo] + [r_oT[cc][tt] for cc in range(8)], writes=[bres[bk]], inc=(c == 7))
                P.op("dve", TT(x1[:, xsl, hsl], banks[bk], x1[:, xsl, hsl], ALU.add),
                     reads=[bres[bk]], writes=[r_x1[xsl]])
            P.op("act", ACTV(h2[b4][:], x1[:, xsl, :], AF.Square, accum_out=ss2[:, b4:b4 + 1]),
                 reads=[r_x1[xsl]], writes=[r_h2[b4], r_ss2b[b4]])
            P.op("act", ACTV(ss2[:, 4 + b4:5 + b4], ss2[:, b4:b4 + 1], AF.Sqrt, scale=1.0 / D, bias=EPS),
                 reads=[r_ss2b[b4]], writes=[r_ss2b[b4]])
            P.op("dve", RCP(ss2[:, 8 + b4:9 + b4], ss2[:, 4 + b4:5 + b4]), reads=[r_ss2b[b4]], writes=[r_ss2b[b4]])
            P.op("dve", STT(h2[b4][:], x1[:, xsl, :], ss2[:, 8 + b4:9 + b4], fnw[:], ALU.mult, ALU.mult),
                 reads=[r_x1[xsl], r_ss2b[b4], r_fnw], writes=[r_h2[b4]])

        def step1_tr(b4):
            bk = yd_bank()
            for c in range(8):
                csl = slice(c * 128, (c + 1) * 128)
                P.op("pe", TR(bT[bk][:, csl], h2[b4][:, csl], ident[:]),
                     reads=[r_h2[b4], r_const], writes=[bres[bk]], inc=(c == 7))
            P.op("act", ACTV(h2T[:, :, b4 * 128:(b4 + 1) * 128], bT[bk].rearrange("p (c t) -> p c t", t=128), AF.Copy),
                 reads=[bres[bk]], writes=[r_h2T[b4]])

        def finish_norm(tt):
            pass

        gur = 0
        for b4 in range(4):
            step1_block(0, b4)
            if b4 >= 1:
                step1_tr(b4 - 1)
        step1_tr(3)
        for tt in range(8):
            finish_norm(tt)
            for fc in range(NFC):
                rg = fc % 3
                P.dma("pool", DMA(ring[rg][:], wgu_bf[fc].rearrange("p (c j) -> p c j", j=256)),
                      reads=[r_wgubf[fc]], writes=[r_ring[rg]])
                bg = 4 + (gur % 2) * 2
                bu = bg + 1
                gur += 1
                for (bk, off) in ((bg, 0), (bu, 128)):
                    for c in range(8):
                        P.op("pe", MM(banks[bk], ring[rg][:, c, off:off + 128], h2T[:, c, :], c == 0, c == 7),
                             reads=[r_ring[rg]] + r_h2T, writes=[bres[bk]], inc=(c == 7))
                s_ = 0
                P.op("act", ACTV(sg[s_][:], banks[bg], AF.Silu), reads=[bres[bg]], writes=[r_sg[s_]])
                P.op("dve", TT(actT[:, fc, :], banks[bu], sg[s_][:], ALU.mult),
                     reads=[bres[bu], r_sg[s_]], writes=[r_act[fc]])
            nxt = tt + 1 < 8
            for b4 in range(4):
                tbk = tt * 4 + b4
                xsl = tbk % 5
                if nxt:
                    step1_block(tt + 1, b4)
                for half in range(2):
                    bk = yd_bank()
                    hsl = slice(half * 512, (half + 1) * 512)
                    for fc in range(NFC):
                        P.op("pe", MM(banks[bk], actT[:, fc, b4 * 128:(b4 + 1) * 128], Wd[:, fc, hsl],
                                      fc == 0, fc == NFC - 1),
                             reads=[r_act[fc], r_Wd], writes=[bres[bk]], inc=(fc == NFC - 1))
                    P.op("dve", TT(x1[:, xsl, hsl], banks[bk], x1[:, xsl, hsl], ALU.add),
                         reads=[bres[bk]], writes=[r_x1[xsl]])
                out_tokens.append(P.dma("sp", DMA(out_d[tbk * 128:(tbk + 1) * 128, :], x1[:, xsl, :]),
                                        reads=[r_x1[xsl]]))
                if nxt and b4 >= 1:
                    step1_tr(b4 - 1)
            if nxt:
                step1_tr(3)
        es_f.close()

    P.wait("sp", out_tokens)
    P.emit()
    P.close()
    return nc


def make_inputs(x, attn_norm_w, w_in, q_norm_w, k_norm_w, dil_out_norm_w, sb_out_norm_w,
                w_out, ffn_norm_w, w_gate, w_up, w_down):
    f = np.float32
    w_in = np.asarray(w_in, f)[0]
    pairs = []
    for g in range(2):
        o0 = g * 1536
        for p in range(4):
            cols = [w_in[:, o0 + j * 512 + p * 128: o0 + j * 512 + (p + 1) * 128] for j in range(3)]
            pairs.append(np.concatenate(cols, axis=1))
    w_pairs = np.ascontiguousarray(np.stack(pairs, 0))
    wg = np.asarray(w_gate, f)[0].reshape(8, 128, NFC, 128)
    wu = np.asarray(w_up, f)[0].reshape(8, 128, NFC, 128)
    wgu = np.concatenate([wg, wu], axis=3)
    wgu = np.ascontiguousarray(wgu.transpose(2, 1, 0, 3)).reshape(NFC, 128, 2048)
    qw = np.asarray(q_norm_w, f)[0]
    kw = np.asarray(k_norm_w, f)[0]
    qkw = np.concatenate([qw, qw, kw, kw])[None, :]
    wcat = np.concatenate([np.asarray(dil_out_norm_w, f)[0], np.asarray(sb_out_norm_w, f)[0]])
    pos = np.arange(S, dtype=f)
    inv = (f(10000.0) ** (-np.arange(0, 64, 2, dtype=f) / f(64))).astype(f)
    ang = (pos[:, None] * inv[None, :]).astype(f)
    rope = np.concatenate([np.cos(ang), np.sin(ang), -np.sin(ang)], axis=1).astype(f)
    shared = {
        "anw_bc": np.ascontiguousarray(np.broadcast_to(np.asarray(attn_norm_w, f)[0][None, :], (128, D))),
        "fnw_bc": np.ascontiguousarray(np.broadcast_to(np.asarray(ffn_norm_w, f)[0][None, :], (128, D))),
        "qkw_bc": np.ascontiguousarray(np.broadcast_to(qkw, (128, 256))),
        "wcol": np.ascontiguousarray(wcat.reshape(8, 128).T),
        "rope": rope,
        "w_pairs": w_pairs,
        "w_out": np.ascontiguousarray(np.asarray(w_out, f)[0]),
        "wgu": wgu,
        "w_down": np.ascontiguousarray(np.asarray(w_down, f)[0]),
    }
    xs = np.asarray(x, f)
    return [dict(shared, x=np.ascontiguousarray(xs[b])) for b in range(xs.shape[0])]


_NC_CACHE = {}


def kernel(**inputs):
    in_maps = make_inputs(**inputs)
    if "nc" not in _NC_CACHE:
        _NC_CACHE["nc"] = build()
    nc = _NC_CACHE["nc"]
    res = run_bass_kernel_spmd(nc, in_maps, core_ids=list(range(8)))
    return np.stack([np.asarray(r["out"], np.float32) for r in res.results], axis=0)
```

```python
from contextlib import ExitStack
import numpy as np
import concourse.bass as bass
import concourse.mybir as mybir
from concourse.bass_utils import run_bass_kernel_spmd

F32 = mybir.dt.float32
BF16 = mybir.dt.bfloat16
AF = mybir.ActivationFunctionType
ALU = mybir.AluOpType
AX = mybir.AxisListType

S = 4096
D = 1024
NB = S // 128
DFF = 2816
NFC = DFF // 128
EPS = 1e-6
ENGS = ("pe", "act", "dve", "pool", "sp")
NDMASEM = 8


class Res:
    __slots__ = ("w", "r", "pend")

    def __init__(self, init=None):
        self.w = dict(init) if init else {}
        self.r = {}
        self.pend = None


def _merge(dst, src):
    for k, v in src.items():
        if dst.get(k, 0) < v:
            dst[k] = v


class Prog:
    def __init__(self, nc):
        self.nc = nc
        self.es = ExitStack()
        self.ops = {e: [] for e in ENGS}
        self.sem = {}
        self.cnt = {}
        self.seen = {e: {} for e in ENGS}
        self.pending = {e: [] for e in ENGS}
        for e in ENGS:
            self.newsem("E_" + e)
        self.dq = {}
        for q in ("sp", "pool"):
            self.dq[q] = {"keys": [self.newsem(f"D_{q}{i}") for i in range(NDMASEM)], "i": 0}

    def newsem(self, key):
        self.sem[key] = self.es.enter_context(self.nc.semaphore(key))
        self.cnt[key] = 0
        return key

    def sb(self, name, shape, dt, es=None):
        return (es or self.es).enter_context(self.nc.sbuf_tensor(name, list(shape), dt))

    def ps(self, name, shape, dt):
        return self.es.enter_context(self.nc.psum_tensor(name, list(shape), dt))

    def fence(self):
        f = {}
        for k, v in self.cnt.items():
            if v > 0:
                f[k] = v
        for e in ENGS:
            assert not self.pending[e], "fence with pending un-tokened ops"
        return f

    def _collect(self, eng, reads, writes, deps):
        need = {}
        for r in reads:
            assert r.pend is None or r.pend == eng, "resource pending on another engine"
            _merge(need, r.w)
        for r in writes:
            assert r.pend is None or r.pend == eng, "resource pending on another engine"
            _merge(need, r.w)
            _merge(need, r.r)
        for d in deps:
            if d:
                _merge(need, d)
        ws = []
        seen = self.seen[eng]
        for k, v in need.items():
            if k == "E_pe" and eng == "pe":
                continue
            if seen.get(k, 0) >= v:
                continue
            seen[k] = v
            ws.append((k, v))
        return ws

    def _commit(self, eng, tok, reads, writes):
        allp = self.pending[eng] + [(reads, writes)]
        self.pending[eng] = []
        for rs, wsx in allp:
            for r in rs:
                _merge(r.r, tok)
                r.pend = None
            for r in wsx:
                r.w = dict(tok)
                r.r = {}
                r.pend = None

    def op(self, eng, fn, reads=(), writes=(), inc=True, deps=()):
        ws = self._collect(eng, reads, writes, deps)
        if inc:
            key = "E_" + eng
            self.cnt[key] += 1
            tok = {key: self.cnt[key]}
            self.ops[eng].append((ws, fn, (key, 1)))
            self._commit(eng, tok, reads, writes)
            return tok
        self.ops[eng].append((ws, fn, None))
        self.pending[eng].append((reads, writes))
        for r in list(reads) + list(writes):
            r.pend = eng
        return None

    def dma(self, q, fn, reads=(), writes=(), deps=()):
        assert not self.pending[q]
        dq = self.dq[q]
        key = dq["keys"][dq["i"] % NDMASEM]
        dq["i"] += 1
        prev = {key: self.cnt[key]} if self.cnt[key] else None
        ws = self._collect(q, reads, writes, list(deps) + [prev])
        self.cnt[key] += 16
        tok = {key: self.cnt[key]}
        self.ops[q].append((ws, fn, (key, 16)))
        for r in reads:
            _merge(r.r, tok)
        for r in writes:
            r.w = dict(tok)
            r.r = {}
        return tok

    def wait(self, eng, deps):
        ws = self._collect(eng, (), (), deps)
        if ws:
            self.ops[eng].append((ws, None, None))

    def emit(self):
        prog = self
        for e in ENGS:
            assert not self.pending[e], f"pending ops on {e}"

        def run(name, e):
            for ws, fn, inc in prog.ops[name]:
                for key, val in ws:
                    e.wait_ge(prog.sem[key], val)
                if fn is None:
                    continue
                ins = fn(e)
                if inc is not None:
                    ins.then_inc(prog.sem[inc[0]], inc[1])

        with self.nc.Block() as block:
            @block.tensor
            def _(e):
                run("pe", e)

            @block.scalar
            def _(e):
                run("act", e)

            @block.vector
            def _(e):
                run("dve", e)

            @block.gpsimd
            def _(e):
                run("pool", e)

            @block.sync
            def _(e):
                run("sp", e)

    def close(self):
        self.es.close()


def MM(out, lhsT, rhs, start, stop, **kw):
    return lambda e: e.matmul(out, lhsT=lhsT, rhs=rhs, start=start, stop=stop, **kw)


def TR(out, in_, idn):
    return lambda e: e.transpose(out, in_, idn)


def ACTV(out, in_, func, **kw):
    return lambda e: e.activation(out=out, in_=in_, func=func, **kw)


def TT(out, in0, in1, op):
    return lambda e: e.tensor_tensor(out=out, in0=in0, in1=in1, op=op)


def TS(out, in0, s1, op0):
    return lambda e: e.tensor_scalar(out=out, in0=in0, scalar1=s1, scalar2=None, op0=op0)


def STT(out, in0, scalar, in1, op0, op1):
    return lambda e: e.scalar_tensor_tensor(out=out, in0=in0, scalar=scalar, in1=in1, op0=op0, op1=op1)


def CP(out, in_):
    return lambda e: e.tensor_copy(out=out, in_=in_)


def RCP(out, in_):
    return lambda e: e.reciprocal(out=out, in_=in_)


def RSUM(out, in_):
    return lambda e: e.reduce_sum(out=out, in_=in_, axis=AX.X)


def DMA(out, in_):
    return lambda e: e.dma_start(out=out, in_=in_)


def MSET(ap, v):
    return lambda e: e.memset(ap, v)

PA_BLOCKS = NB
KNOB = {}


def build(pairs=tuple(range(8)), do_ffn=True, dbg=None):
    nc = bass.Bass("TRN2", target_bir_lowering=False)

    def din(name, shape, dt=F32):
        return nc.dram_tensor(name, list(shape), dt, kind="ExternalInput").ap()

    x_d = din("x", [S, D])
    anw_d = din("anw_bc", [128, D])
    fnw_d = din("fnw_bc", [128, D])
    qkw_d = din("qkw_bc", [128, 256])
    wcol_d = din("wcol", [128, 8])
    rope_d = din("rope", [S, 96])
    wp_d = din("w_pairs", [8, D, 384])
    wout_d = din("w_out", [D, D])
    wgu_d = din("wgu", [NFC, 128, 2048])
    wd_d = din("w_down", [DFF, D])
    out_d = nc.dram_tensor("out", [S, D], F32, kind="ExternalOutput").ap()
    dbg_d = None
    if dbg in ("hT", "oT"):
        dbg_d = nc.dram_tensor("dbg", [128, 8, S], BF16, kind="ExternalOutput").ap()

    P = Prog(nc)
    out_tokens = []

    pb = [P.ps(f"pb{i}", [128, 1024], F32) for i in range(4)]
    banks = []
    for i in range(4):
        banks += [pb[i][:, 0:512], pb[i][:, 512:1024]]
    bres = [Res() for _ in range(8)]
    bT = [banks[i].bitcast(BF16) for i in range(8)]

    ident = P.sb("ident", [128, 128], BF16)
    tri = P.sb("tri", [128, 128], BF16)
    sl = P.sb("sl", [128, 128], BF16)
    ones = P.sb("ones", [128, 128], BF16)
    sbm = [P.sb(f"sbm{j}", [128, 512], BF16) for j in range(4)]
    dm1 = P.sb("dm1", [128, 512], BF16)
    dmz = P.sb("dmz", [128, 512], BF16)
    dmz2 = P.sb("dmz2", [128, 512], BF16)
    wcol = P.sb("wcol_sb", [128, 8], F32)
    oT = P.sb("oT", [128, 8, S], BF16)
    r_const = Res()
    r_oT = [[Res() for _ in range(8)] for _ in range(8)]

    def SEL(t_ap, pattern, base, cm, cmp):
        return lambda e: e.affine_select(out=t_ap, in_=t_ap, pattern=pattern, base=base,
                                         channel_multiplier=cm, compare_op=cmp, fill=0.0)

    for t in (ident, tri, sl, ones, dm1, dmz, dmz2, *sbm):
        P.op("pool", MSET(t[:], 1.0), writes=[r_const])
    P.op("pool", SEL(ident[:], [[-1, 128]], 0, 1, ALU.is_equal), writes=[r_const])
    P.op("pool", SEL(tri[:], [[-1, 128]], 0, 1, ALU.is_ge), writes=[r_const])
    P.op("pool", SEL(sl[:], [[1, 128]], 0, -1, ALU.is_gt), writes=[r_const])
    for j in range(4):
        P.op("pool", SEL(sbm[j][:], [[1, 512]], -128 * j, -1, ALU.is_gt), writes=[r_const])
    for m in (dm1, dmz, dmz2):
        for slot in range(4):
            ap = m[:, slot * 128:(slot + 1) * 128]
            if slot % 2 == 0:
                P.op("pool", SEL(ap, [[-1, 128]], 0, 1, ALU.is_ge), writes=[r_const])
            else:
                P.op("pool", SEL(ap, [[1, 128]], 0, -1, ALU.is_ge), writes=[r_const])
    P.op("pool", MSET(dmz[:, 0:128], 0.0), writes=[r_const])
    P.op("pool", MSET(dmz2[:, 0:128], 0.0), writes=[r_const])
    P.op("pool", MSET(dmz2[:, 256:384], 0.0), writes=[r_const])
    P.dma("sp", DMA(wcol[:], wcol_d), writes=[r_const])
    wgu_bf = nc.dram_tensor("wgu_bf", [NFC, 128, 2048], BF16, kind="Internal").ap()
    r_wgubf = [Res() for _ in range(NFC)]

    es_att = ExitStack()
    hT = P.sb("hT", [128, 8, S], BF16, es_att)
    r_hT = [Res() for _ in range(NB)]
    Wp = P.sb("Wp", [128, 8, 384], BF16, es_att)
    r_Wp = Res()
    qkvT = P.sb("qkvT", [128, 3, S], BF16, es_att)
    r_qk = [Res() for _ in range(NB)]
    r_v = [Res() for _ in range(8)]
    sq = [P.sb(f"sq{i}", [128, 512], BF16, es_att) for i in range(4)]
    r_sq = [Res() for _ in range(4)]
    rs = P.sb("rs", [128, 512], F32, es_att)
    r_rs = Res()

    es_pa = ExitStack()
    xs = [P.sb(f"xs{i}", [128, D], F32, es_pa)[:] for i in range(4)]
    for j in range(3):
        v32 = qkvT[:, j, :].bitcast(F32)
        xs += [v32[:, 0:D], v32[:, D:2 * D]]
    NX = len(xs)
    r_xs = [Res() for _ in range(NX)]
    hn = [P.sb(f"hn{i}", [128, D], BF16, es_pa) for i in range(4)]
    r_hn = [Res() for _ in range(4)]
    junk = P.sb("junkA", [128, D], BF16, es_pa)
    r_junk = Res()
    anw = P.sb("anw", [128, D], F32, es_pa)
    r_anw = Res()
    ssA = P.sb("ssA", [128, NB], F32, es_pa)
    sdA = P.sb("sdA", [128, NB], F32, es_pa)
    rsA = P.sb("rsA", [128, NB], F32, es_pa)
    r_ssA = [Res() for _ in range(NB)]
    P.dma("sp", DMA(anw[:], anw_d), writes=[r_anw])
    BS = 2

    def paA1(sb):
        for b in range(BS):
            tb = sb * BS + b
            xi = tb % NX
            tsl = slice(tb * 128, (tb + 1) * 128)
            P.dma("sp", DMA(xs[xi], x_d[tsl, :]), writes=[r_xs[xi]])
            P.op("act", ACTV(junk[:], xs[xi], AF.Square, accum_out=ssA[:, tb:tb + 1]),
                 reads=[r_xs[xi]], writes=[r_junk, r_ssA[sb]])
        c4 = slice(sb * BS, sb * BS + BS)
        P.op("act", ACTV(sdA[:, c4], ssA[:, c4], AF.Sqrt, scale=1.0 / D, bias=EPS), reads=[r_ssA[sb]], writes=[r_ssA[sb]])

    def paA2(sb):
        c4 = slice(sb * BS, sb * BS + BS)
        P.op("dve", RCP(rsA[:, c4], sdA[:, c4]), reads=[r_ssA[sb]], writes=[r_ssA[sb]])

    def paB1(sb):
        for b in range(BS):
            tb = sb * BS + b
            xi = tb % NX
            hi = tb % 4
            bk = tb % 4
            P.op("dve", STT(hn[hi][:], xs[xi], rsA[:, tb:tb + 1], anw[:], ALU.mult, ALU.mult),
                 reads=[r_xs[xi], r_ssA[sb], r_anw], writes=[r_hn[hi]])
            for c in range(8):
                csl = slice(c * 128, (c + 1) * 128)
                P.op("pe", TR(bT[bk][:, csl], hn[hi][:, csl], ident[:]),
                     reads=[r_hn[hi], r_const], writes=[bres[bk]], inc=(c == 7))

    def paB2(sb):
        for b in range(BS):
            tb = sb * BS + b
            bk = tb % 4
            tsl = slice(tb * 128, (tb + 1) * 128)
            if b % 2 == 0:
                P.op("act", ACTV(hT[:, :, tsl], bT[bk].rearrange("p (c t) -> p c t", t=128), AF.Copy),
                     reads=[bres[bk]], writes=[r_hT[tb]])
            else:
                P.op("dve", CP(hT[:, :, tsl], bT[bk].rearrange("p (c t) -> p c t", t=128)),
                     reads=[bres[bk]], writes=[r_hT[tb]])

    nsb = PA_BLOCKS // BS
    for sb in range(nsb + 1):
        if sb >= 1:
            paB1(sb - 1)
        if sb < nsb:
            paA1(sb)
        if sb >= 1:
            paB2(sb - 1)
        if sb < nsb:
            paA2(sb)
    fenceA = P.fence()
    es_pa.close()
    for r_ in r_qk + r_v:
        _merge(r_.w, fenceA)
    casts_pending = [do_ffn]

    if dbg == "hT":
        for c in range(8):
            out_tokens.append(P.dma("sp", DMA(dbg_d[:, c, :], hT[:, c, :]), reads=r_hT))

    def proj_fm(col0, dst_idx, scale, res_for_slice):
        for ts in range(8):
            bk = ts % 2
            tsl = slice(ts * 512, (ts + 1) * 512)
            for c in range(8):
                P.op("pe", MM(banks[bk], Wp[:, c, col0:col0 + 128], hT[:, c, tsl], c == 0, c == 7),
                     reads=[r_Wp] + r_hT[ts * 4:(ts + 1) * 4], writes=[bres[bk]], inc=(c == 7))
            wr = res_for_slice(ts)
            if ts % 2 == 0:
                P.op("act", ACTV(qkvT[:, dst_idx, tsl], banks[bk], AF.Copy, scale=scale),
                     reads=[bres[bk]], writes=wr)
            else:
                P.op("dve", TS(qkvT[:, dst_idx, tsl], banks[bk], scale, ALU.mult),
                     reads=[bres[bk]], writes=wr)

    def load_pair_weights(pi):
        P.dma("pool", DMA(Wp[:], wp_d[pi].rearrange("(c p) e -> p c e", p=128)), writes=[r_Wp])

    def normalize_group(gc):
        for ts in range(8):
            tsl = slice(ts * 512, (ts + 1) * 512)
            for c in range(4):
                if c < 3:
                    P.op("act", ACTV(sq[c][:], oT[:, gc + c, tsl], AF.Square),
                         reads=[r_oT[gc + c][ts]], writes=[r_sq[c]])
                else:
                    P.op("pool", TT(sq[c][:], oT[:, gc + c, tsl], oT[:, gc + c, tsl], ALU.mult),
                         reads=[r_oT[gc + c][ts]], writes=[r_sq[c]])
            bk = 2 + ts % 2
            for c in range(4):
                P.op("pe", MM(banks[bk], ones[:], sq[c][:], c == 0, c == 3),
                     reads=[r_sq[c], r_const], writes=[bres[bk]], inc=(c == 3))
            P.op("act", ACTV(rs[:], banks[bk], AF.Ln, scale=1.0 / 512, bias=EPS),
                 reads=[bres[bk]], writes=[r_rs])
            P.op("act", ACTV(rs[:], rs[:], AF.Exp, scale=-0.5), reads=[r_rs], writes=[r_rs])
            P.op("dve", TT(oT[:, gc:gc + 4, tsl], oT[:, gc:gc + 4, tsl],
                           rs[:].unsqueeze(1).to_broadcast([128, 4, 512]), ALU.mult),
                 reads=[r_rs], writes=[r_oT[gc + c][ts] for c in range(4)])

    LAG = 6
    NSL = 8
    pairsA = [p for p in pairs if p < 4]
    bank_rr = [0]

    def next_bank():
        b = bank_rr[0] % 4
        bank_rr[0] += 1
        return b

    if pairsA:
        es_ga = ExitStack()
        qkw = P.sb("qkw", [128, 256], F32, es_ga)
        r_qkw = Res(fenceA)
        P.dma("sp", DMA(qkw[:], qkw_d), writes=[r_qkw])
        P.op("dve", TS(qkw[:, 0:128], qkw[:, 0:128], 0.125, ALU.mult), writes=[r_qkw])
        fprev = fenceA
        for pi in pairsA:
            load_pair_weights(pi)
            es_p = ExitStack()
            NS = 4
            sa = [P.sb(f"sa{pi}_{i}", [128, 512], F32, es_p) for i in range(NS)]
            qn = [P.sb(f"qn{pi}_{i}", [128, 512], F32, es_p) for i in range(NS)]
            t2 = [P.sb(f"t2{pi}_{i}", [128, 512], F32, es_p) for i in range(NS)]
            qr = [P.sb(f"qr{pi}_{i}", [128, 512], BF16, es_p) for i in range(NS + 1)]
            st = [P.sb(f"st{pi}_{i}", [128, 24], F32, es_p) for i in range(NS)]
            ropeb = [P.sb(f"ropeb{pi}_{i}", [128, 2, 96], F32, es_p) for i in range(NS)]
            r_sa = [Res(fprev) for _ in range(NS)]
            r_qn = [Res(fprev) for _ in range(NS)]
            r_t2 = [Res(fprev) for _ in range(NS)]
            r_qr = [Res(fprev) for _ in range(NS + 1)]
            r_st = [Res(fprev) for _ in range(NS)]
            r_rope = [Res(fprev) for _ in range(NS)]
            w3 = lambda ap: ap.rearrange("p (s d) -> p s d", d=64)
            w4 = lambda ap: ap.rearrange("p (b s d) -> p b s d", b=2, d=32)
            w5 = lambda ap: ap.rearrange("p (b s h d) -> p b s h d", b=2, h=2, d=32)
            wb = lambda ap: ap.rearrange("p (b e) -> p b e", e=256)
            NBB = NB // 2

            def st0(bb):
                k = bb % NS
                tb0 = bb * 2
                bk = bb % 2
                P.dma("sp", DMA(ropeb[k][:], rope_d[tb0 * 128:(tb0 + 2) * 128, :].rearrange("(b p) d -> p b d", p=128)),
                      writes=[r_rope[k]])
                for b in range(2):
                    tsl = slice((tb0 + b) * 128, (tb0 + b + 1) * 128)
                    for c in range(8):
                        P.op("pe", MM(banks[bk][:, b * 256:(b + 1) * 256], hT[:, c, tsl], Wp[:, c, 0:256], c == 0, c == 7),
                             reads=[r_Wp, r_hT[tb0 + b]], writes=[bres[bk]], inc=(b == 1 and c == 7))

            def st1a(bb):
                k = bb % NS
                bk = bb % 2
                P.op("act", ACTV(sa[k][:], banks[bk], AF.Square), reads=[bres[bk]], writes=[r_sa[k]])
                P.op("dve", RSUM(st[k][:, 0:8], w3(sa[k][:])), reads=[r_sa[k]], writes=[r_st[k]])
                P.op("act", ACTV(st[k][:, 8:16], st[k][:, 0:8], AF.Sqrt, scale=1.0 / 64, bias=EPS),
                     reads=[r_st[k]], writes=[r_st[k]])

            def st1b(bb):
                k = bb % NS
                bk = bb % 2
                P.op("dve", RCP(st[k][:, 16:24], st[k][:, 8:16]), reads=[r_st[k]], writes=[r_st[k]])
                P.op("dve", TT(w3(qn[k][:]), w3(banks[bk]), st[k][:, 16:24].unsqueeze(2).to_broadcast([128, 8, 64]),
                               ALU.mult), reads=[bres[bk], r_st[k]], writes=[r_qn[k]])
                P.op("pool", TT(wb(qn[k][:]), wb(qn[k][:]), qkw[:].unsqueeze(1).to_broadcast([128, 2, 256]), ALU.mult),
                     reads=[r_qkw], writes=[r_qn[k]])

            def st2(bb):
                k = bb % NS
                cosb = ropeb[k][:, :, 0:32].unsqueeze(2).to_broadcast([128, 2, 8, 32])
                sinb = ropeb[k][:, :, 32:64].unsqueeze(2).to_broadcast([128, 2, 4, 32])
                nsinb = ropeb[k][:, :, 64:96].unsqueeze(2).to_broadcast([128, 2, 4, 32])
                P.op("dve", TT(w4(sa[k][:]), w4(qn[k][:]), cosb, ALU.mult),
                     reads=[r_qn[k], r_rope[k]], writes=[r_sa[k]])
                P.op("pool", TT(w5(t2[k][:])[:, :, :, 0, :], w5(qn[k][:])[:, :, :, 1, :], nsinb, ALU.mult),
                     reads=[r_qn[k], r_rope[k]], writes=[r_t2[k]])
                P.op("pool", TT(w5(t2[k][:])[:, :, :, 1, :], w5(qn[k][:])[:, :, :, 0, :], sinb, ALU.mult),
                     reads=[r_qn[k], r_rope[k]], writes=[r_t2[k]])

            def st3a(bb):
                k = bb % NS
                kq = bb % (NS + 1)
                P.op("dve", TT(qr[kq][:], sa[k][:], t2[k][:], ALU.add), reads=[r_sa[k], r_t2[k]], writes=[r_qr[kq]])

            def st3b(bb):
                kq = bb % (NS + 1)
                tb0 = bb * 2
                t2sl = slice(tb0 * 128, (tb0 + 2) * 128)
                bk2 = 2 + bb % 2
                for j in range(4):
                    jsl = slice(j * 128, (j + 1) * 128)
                    P.op("pe", TR(bT[bk2][:, jsl], qr[kq][:, jsl], ident[:]),
                         reads=[r_qr[kq], r_const], writes=[bres[bk2]], inc=(j == 3))
                P.op("act", ACTV(qkvT[:, 0:2, t2sl].rearrange("p j (b t) -> p j b t", t=128),
                                 bT[bk2][:, 0:512].rearrange("p (b j t) -> p j b t", j=2, t=128), AF.Copy),
                     reads=[bres[bk2]], writes=[r_qk[tb0], r_qk[tb0 + 1]])

            for it in range(NBB + 4):
                if it < NBB:
                    st0(it)
                if 0 <= it - 1 < NBB:
                    st1a(it - 1)
                if 0 <= it - 2 < NBB:
                    st2(it - 2)
                if 0 <= it - 1 < NBB:
                    st1b(it - 1)
                if 0 <= it - 3 < NBB:
                    st3a(it - 3)
                if 0 <= it - 4 < NBB:
                    st3b(it - 4)
            proj_fm(256, 2, 1.0, lambda ts: [r_v[ts]])
            fmid = P.fence()
            es_p.close()

            es_a = ExitStack()
            vaug = [P.sb(f"vaug{pi}_{i}", [128, NB, 128], BF16, es_a) for i in range(3)]
            r_vaug = [Res(fmid) for _ in range(3)]
            expS = [P.sb(f"expS{pi}_{i}", [128, 512], BF16, es_a) for i in range(NSL)]
            r_expS = [Res(fmid) for _ in range(NSL)]
            ptm = expS
            r_ptm = r_expS
            rden = [P.sb(f"rden{pi}_{i}", [128, 512], F32, es_a) for i in range(1)]
            r_rden = [Res(fmid), Res(fmid)]
            for i in range(3):
                P.op("pool", MSET(vaug[i][:, :, 64:128], 1.0), writes=[r_vaug[i]])
            if casts_pending[0]:
                casts_pending[0] = False
                for fc in range(NFC):
                    P.dma("pool", DMA(wgu_bf[fc], wgu_d[fc]), writes=[r_wgubf[fc]], deps=[fmid])
            VIDX = {1: 0, 4: 1, 16: 2}

            vi = 0
            gi = 0
            for hp in range(2):
                p0, p1 = hp * 64, hp * 64 + 64
                for R in range(2):
                    started = [False] * 4
                    groups = []
                    for r in (1, 4, 16):
                        nb = NB // r
                        if r == 16:
                            for c in range(0, 16, 2):
                                groups.append((r, [(c, R - 1, R), (c, R, R), (c + 1, R - 1, R), (c + 1, R, R)]))
                        else:
                            per = nb // 2
                            for c in range(r):
                                for n in range(R * per, (R + 1) * per, 2):
                                    groups.append((r, [(c, n - 1, n), (c, n, n), (c, n, n + 1), (c, n + 1, n + 1)]))

                    def build_vaug(r, vi_):
                        nb = NB // r
                        for g8 in range(4):
                            bk = next_bank()
                            for j in range(8):
                                blk = g8 * 8 + j
                                c, n = blk // nb, blk % nb
                                st_ = n * 128 * r + c
                                P.op("pe", TR(bT[bk][:, j * 64:(j + 1) * 64], qkvT[p0:p1, 2, st_:st_ + 127 * r + 1:r],
                                              ident[p0:p1, p0:p1]),
                                     reads=r_v + [r_const], writes=[bres[bk]], inc=(j == 7))
                            P.op("dve", CP(vaug[vi_][:, g8 * 8:(g8 + 1) * 8, 0:64],
                                           bT[bk][:, 0:512].rearrange("p (j d) -> p j d", d=64)),
                                 reads=[bres[bk]], writes=[r_vaug[vi_]])

                    def emit_pv(item):
                        r, slots, pslot, vi_ = item
                        nb = NB // r
                        mms = []
                        for si, (c, kb, qb) in enumerate(slots):
                            if kb < 0:
                                continue
                            vblk = c * nb + kb
                            if r == 16:
                                for u in range(4):
                                    stt = not started[u]
                                    started[u] = True
                                    q0 = si * 128 + u * 32
                                    mms.append((MM(banks[4 + u][:, c:512:16], vaug[vi_][:, vblk, :],
                                                   ptm[pslot][:, q0:q0 + 32], stt, True, skip_group_check=True), u))
                            else:
                                tok0 = qb * 128 * r + c - R * 2048
                                u = tok0 // 512
                                lo = tok0 - u * 512
                                stt = not started[u]
                                started[u] = True
                                mms.append((MM(banks[4 + u][:, lo:lo + 127 * r + 1:r], vaug[vi_][:, vblk, :],
                                               ptm[pslot][:, si * 128:(si + 1) * 128], stt, True,
                                               skip_group_check=True), u))
                        for kk, (fn, u) in enumerate(mms):
                            P.op("pe", fn, reads=[r_vaug[vi_], r_ptm[pslot]], writes=[bres[4 + u]],
                                 inc=(kk == len(mms) - 1))

                    pend = []
                    if R == 0:
                        for r_ in (1, 4, 16):
                            build_vaug(r_, VIDX[r_])
                    for (r, slots) in groups:
                        vi = VIDX[r]
                        bk = next_bank()
                        es_ = gi % NSL
                        for si, (c, kb, qb) in enumerate(slots):
                            ks = max(kb, 0) * 128 * r + c
                            qs = qb * 128 * r + c
                            P.op("pe", MM(banks[bk][:, si * 128:(si + 1) * 128],
                                          qkvT[p0:p1, 1, ks:ks + 127 * r + 1:r], qkvT[p0:p1, 0, qs:qs + 127 * r + 1:r],
                                          True, True),
                                 reads=r_qk, writes=[bres[bk]], inc=(si == 3))
                        P.op("act", ACTV(expS[es_][:], banks[bk], AF.Exp), reads=[bres[bk]], writes=[r_expS[es_]])
                        inv = [kb < 0 for (_, kb, _) in slots]
                        mk = dmz2 if (inv[0] and inv[2]) else (dmz if inv[0] else dm1)
                        P.op("dve", TT(expS[es_][:], expS[es_][:], mk[:], ALU.mult),
                             reads=[r_const], writes=[r_expS[es_]])
                        pend.append((r, slots, es_, vi))
                        if len(pend) > LAG:
                            emit_pv(pend.pop(0))
                        gi += 1
                    while pend:
                        emit_pv(pend.pop(0))
                    for u in range(4):
                        rd = u % 2
                        rsl = slice(rd * 64, rd * 64 + 64)
                        ts_ = R * 4 + u
                        P.op("act", ACTV(rden[0][rsl, :], banks[4 + u][64:128, :], AF.Ln),
                             reads=[bres[4 + u]], writes=[r_rden[rd]])
                        P.op("act", ACTV(rden[0][rsl, :], rden[0][rsl, :], AF.Exp, scale=-1.0),
                             reads=[], writes=[r_rden[rd]])
                        P.op("dve", TT(oT[p0:p1, pi, ts_ * 512:(ts_ + 1) * 512], banks[4 + u][0:64, :],
                                       rden[0][rsl, :], ALU.mult),
                             reads=[bres[4 + u], r_rden[rd]], writes=[r_oT[pi][ts_]])
            fprev = P.fence()
            es_a.close()
        if len(pairsA) == 4:
            normalize_group(0)
        fenceGA = P.fence()
        es_ga.close()
    else:
        fenceGA = fenceA

    pairsB = [p for p in pairs if p >= 4]
    if pairsB:
        es_gb = ExitStack()
        fB = fenceGA
        V2 = P.sb("V2", [128, NB, 128], BF16, es_gb)
        r_V2 = Res(fB)
        E2 = [P.sb(f"E2_{i}", [128, 1024], BF16, es_gb) for i in range(3)]
        r_E = [Res(fB) for _ in range(3)]
        sp2 = [P.sb(f"sp2_{i}", [128, 1024], BF16, es_gb) for i in range(3)]
        r_sp = [Res(fB) for _ in range(3)]
        X2 = [P.sb(f"X2_{i}", [128, 1024], BF16, es_gb) for i in range(2)]
        r_X = [Res(fB) for _ in range(2)]
        A2 = [P.sb(f"A2_{i}", [128, 1024], BF16, es_gb) for i in range(3)]
        r_A = [Res(fB) for _ in range(3)]

        for pi in pairsB:
            load_pair_weights(pi)
            proj_fm(0, 0, 0.125, lambda ts: r_qk[ts * 4:(ts + 1) * 4])
            proj_fm(128, 1, 1.0, lambda ts: r_qk[ts * 4:(ts + 1) * 4])
            proj_fm(256, 2, 1.0, lambda ts: [r_v[ts]])
            for g8 in range(4):
                bk = 2 + g8 % 2
                for j in range(8):
                    tb = g8 * 8 + j
                    P.op("pe", TR(bT[bk][:, j * 128:(j + 1) * 128], qkvT[:, 2, tb * 128:(tb + 1) * 128], ident[:]),
                         reads=r_v + [r_const], writes=[bres[bk]], inc=(j == 7))
                P.op("dve", CP(V2[:, g8 * 8:(g8 + 1) * 8, :], bT[bk].rearrange("p (j d) -> p j d", d=128)),
                     reads=[bres[bk]], writes=[r_V2])

            steps = [(G, J) for G in range(8) for J in range(4 * G + 3, -1, -1)]
            n = len(steps)

            def lo_of(i):
                G, J = steps[i]
                return max(0, J - 4 * G) * 128

            def hv(ap, lo):
                v = ap.rearrange("p (h t) -> p h t", h=2)
                return v if lo == 0 else v[:, :, lo:512]

            def pe_z(i):
                G, J = steps[i]
                k = i % 2
                lo = lo_of(i)
                for h in range(2):
                    hs = slice(h * 64, (h + 1) * 64)
                    P.op("pe", MM(banks[2 * k + h][:, lo:512], qkvT[hs, 1, J * 128:(J + 1) * 128],
                                  qkvT[hs, 0, G * 512 + lo:(G + 1) * 512], True, True),
                         reads=r_qk, writes=[bres[2 * k + h]], inc=(h == 1))

            def act_EL(i):
                G, J = steps[i]
                k = i % 2
                e_ = i % 3
                s_ = i % 3
                lo = lo_of(i)
                P.op("act", ACTV(hv(E2[e_][:], lo), hv(pb[k][:], lo), AF.Exp),
                     reads=[bres[2 * k], bres[2 * k + 1]], writes=[r_E[e_]])
                if J >= 4 * G:
                    ev = hv(E2[e_][:], lo)
                    P.op("dve", TT(ev, ev, sbm[J - 4 * G][:, lo:512].unsqueeze(1).to_broadcast([128, 2, 512 - lo]),
                                   ALU.mult), reads=[r_const], writes=[r_E[e_]])

            def act_L(i):
                e_ = i % 3
                s_ = i % 3
                lo = lo_of(i)
                P.op("act", ACTV(hv(sp2[s_][:], lo), hv(E2[e_][:], lo), AF.Ln, bias=1.0, scale=1.0),
                     reads=[r_E[e_]], writes=[r_sp[s_]])

            def pe_C(i):
                G, J = steps[i]
                first = (J == 4 * G + 3)
                s_ = i % 3
                lo = lo_of(i)
                for h in range(2):
                    P.op("pe", MM(banks[4 + h][:, lo:512], tri[:], sp2[s_][:, h * 512 + lo:(h + 1) * 512], first, first,
                                  skip_group_check=True),
                         reads=[r_sp[s_], r_const], writes=[bres[4 + h]], inc=(first and h == 1))
                    if not first:
                        sp_prev = (i - 1) % 3
                        lop = lo_of(i - 1)
                        P.op("pe", MM(banks[4 + h][:, lop:512], sl[:], sp2[sp_prev][:, h * 512 + lop:(h + 1) * 512],
                                      False, True, skip_group_check=True),
                             reads=[r_sp[sp_prev], r_const], writes=[bres[4 + h]], inc=(h == 1))

            def act_X(i):
                e_ = i % 3
                x_ = i % 2
                a_ = i % 3
                lo = lo_of(i)
                P.op("act", ACTV(hv(X2[x_][:], lo), hv(pb[2][:], lo), AF.Exp, scale=-1.0),
                     reads=[bres[4], bres[5]], writes=[r_X[x_]])
                P.op("dve", TT(hv(A2[a_][:], lo), hv(E2[e_][:], lo), hv(X2[x_][:], lo), ALU.mult),
                     reads=[r_E[e_], r_X[x_]], writes=[r_A[a_]])

            def pe_pv(i):
                G, J = steps[i]
                first = (J == 4 * G + 3)
                last = (J == 0)
                ob = 6 + G % 2
                a_ = i % 3
                lo = lo_of(i)
                for h in range(2):
                    hs = slice(h * 64, (h + 1) * 64)
                    P.op("pe", MM(banks[ob][hs, lo:512], V2[:, J, hs], A2[a_][:, h * 512 + lo:(h + 1) * 512], first, last,
                                  skip_group_check=True),
                         reads=[r_V2, r_A[a_]], writes=[bres[ob]], inc=(h == 1))
                if last:
                    P.op("dve", CP(oT[:, pi, G * 512:(G + 1) * 512], banks[ob]),
                         reads=[bres[ob]], writes=[r_oT[pi][G]])

            for i in range(n + 3):
                if i < n:
                    pe_z(i)
                if 0 <= i - 3 < n:
                    pe_pv(i - 3)
                diag = False
                if 0 <= i - 1 < n:
                    act_EL(i - 1)
                    G_, J_ = steps[i - 1]
                    diag = J_ >= 4 * G_
                    if not diag:
                        act_L(i - 1)
                if 0 <= i - 2 < n:
                    act_X(i - 2)
                if 0 <= i - 1 < n:
                    if diag:
                        act_L(i - 1)
                    pe_C(i - 1)
        if len(pairsB) == 4:
            normalize_group(4)
        fenceGB = P.fence()
        es_gb.close()

    if dbg == "oT":
        for c in range(8):
            out_tokens.append(P.dma("sp", DMA(dbg_d[:, c, :], oT[:, c, :]), reads=r_oT[c]))
    fenceATT = P.fence()
    es_att.close()

    if do_ffn:
        fF = fenceATT
        es_f = ExitStack()
        Wd = P.sb("Wd", [128, NFC, D], BF16, es_f)
        r_Wd = Res(fF)
        Wo = P.sb("Wo", [128, 8, D], BF16, es_f)
        r_Wo = Res(fF)
        x1 = P.sb("x1", [128, 5, D], F32, es_f)
        r_x1 = [Res(fF) for _ in range(5)]
        h2T = P.sb("h2T", [128, 8, 512], BF16, es_f)
        r_h2T = [Res(fF) for _ in range(4)]
        actT = P.sb("actT", [128, NFC, 512], BF16, es_f)
        r_act = [Res(fF) for _ in range(NFC)]
        ring = [P.sb(f"ring{i}", [128, 8, 256], BF16, es_f) for i in range(3)]
        r_ring = [Res(fF) for _ in range(3)]
        fnw = P.sb("fnw", [128, D], F32, es_f)
        r_fnw = Res(fF)
        sg = [P.sb(f"sg{i}", [128, 512], BF16, es_f) for i in range(1)]
        r_sg = [Res(fF)]
        h2 = [P.sb(f"h2{i}", [128, D], BF16, es_f) for i in range(4)]
        r_h2 = [Res(fF) for _ in range(4)]
        r_ss2b = [Res(fF) for _ in range(4)]
        ss2 = P.sb("ss2", [128, 12], F32, es_f)
        r_ss2 = Res(fF)

        P.dma("pool", DMA(Wo[:], wout_d.rearrange("(c p) d -> p c d", p=128)), writes=[r_Wo])
        for c in range(8):
            P.op("dve", TS(Wo[:, c, :], Wo[:, c, :], wcol[:, c:c + 1], ALU.mult), reads=[r_const], writes=[r_Wo])
        for q4_ in range(2):
            P.dma("pool", DMA(Wd[:, q4_ * 11:(q4_ + 1) * 11, :],
                              wd_d[q4_ * 11 * 128:(q4_ + 1) * 11 * 128, :].rearrange("(f p) d -> p f d", p=128)),
                  writes=[r_Wd])
        P.dma("sp", DMA(fnw[:], fnw_d), writes=[r_fnw])

        ydr = [0]

        def yd_bank():
            b = ydr[0] % 4
            ydr[0] += 1
            return b

        def step1_block(tt, b4):
            tbk = tt * 4 + b4
            xsl = tbk % 5
            tsl = slice(tbk * 128, (tbk + 1) * 128)
            P.dma("sp", DMA(x1[:, xsl, :], x_d[tsl, :]), writes=[r_x1[xsl]])
            for half in range(2):
                bk = yd_bank()
                hsl = slice(half * 512, (half + 1) * 512)
                for c in range(8):
                    P.op("pe", MM(banks[bk], oT[:, c, tsl], Wo[:, c, hsl], c == 0, c == 7),
                         reads=[r_Wo] + [r_oT[cc][tt] for cc in range(8)], writes=[bres[bk]], inc=(c == 7))
                P.op("dve", TT(x1[:, xsl, hsl], banks[bk], x1[:, xsl, hsl], ALU.add),
                     reads=[bres[bk]], writes=[r_x1[xsl]])
            P.op("act", ACTV(h2[b4][:], x1[:, xsl, :], AF.Square, accum_out=ss2[:, b4:b4 + 1]),
                 reads=[r_x1[xsl]], writes=[r_h2[b4], r_ss2b[b4]])
            P.op("act", ACTV(ss2[:, 4 + b4:5 + b4], ss2[:, b4:b4 + 1], AF.Sqrt, scale=1.0 / D, bias=EPS),
                 reads=[r_ss2b[b4]], writes=[r_ss2b[b4]])
            P.op("dve", RCP(ss2[:, 8 + b4:9 + b4], ss2[:, 4 + b4:5 + b4]), reads=[r_ss2b[b4]], writes=[r_ss2b[b4]])
            P.op("dve", STT(h2[b4][:], x1[:, xsl, :], ss2[:, 8 + b4:9 + b4], fnw[:], ALU.mult, ALU.mult),
                 reads=[r_x1[xsl], r_ss2b[b4], r_fnw], writes=[r_h2[b4]])

        def step1_tr(b4):
            bk = yd_bank()
            for c in range(8):
                csl = slice(c * 128, (c + 1) * 128)
                P.op("pe", TR(bT[bk][:, csl], h2[b4][:, csl], ident[:]),
                     reads=[r_h2[b4], r_const], writes=[bres[bk]], inc=(c == 7))
            P.op("act", ACTV(h2T[:, :, b4 * 128:(b4 + 1) * 128], bT[bk].rearrange("p (c t) -> p c t", t=128), AF.Copy),
                 reads=[bres[bk]], writes=[r_h2T[b4]])

        def finish_norm(tt):
            pass

        gur = 0
        for b4 in range(4):
            step1_block(0, b4)
            if b4 >= 1:
                step1_tr(b4 - 1)
        step1_tr(3)
        for tt in range(8):
            finish_norm(tt)
            for fc in range(NFC):
                rg = fc % 3
                P.dma("pool", DMA(ring[rg][:], wgu_bf[fc].rearrange("p (c j) -> p c j", j=256)),
                      reads=[r_wgubf[fc]], writes=[r_ring[rg]])
                bg = 4 + (gur % 2) * 2
                bu = bg + 1
                gur += 1
                for (bk, off) in ((bg, 0), (bu, 128)):
                    for c in range(8):
                        P.op("pe", MM(banks[bk], ring[rg][:, c, off:off + 128], h2T[:, c, :], c == 0, c == 7),
                             reads=[r_ring[rg]] + r_h2T, writes=[bres[bk]], inc=(c == 7))
                s_ = 0
                P.op("act", ACTV(sg[s_][:], banks[bg], AF.Silu), reads=[bres[bg]], writes=[r_sg[s_]])
                P.op("dve", TT(actT[:, fc, :], banks[bu], sg[s_][:], ALU.mult),
                     reads=[bres[bu], r_sg[s_]], writes=[r_act[fc]])
            nxt = tt + 1 < 8
            for b4 in range(4):
                tbk = tt * 4 + b4
                xsl = tbk % 5
                if nxt:
                    step1_block(tt + 1, b4)
                for half in range(2):
                    bk = yd_bank()
                    hsl = slice(half * 512, (half + 1) * 512)
                    for fc in range(NFC):
                        P.op("pe", MM(banks[bk], actT[:, fc, b4 * 128:(b4 + 1) * 128], Wd[:, fc, hsl],
                                      fc == 0, fc == NFC - 1),
                             reads=[r_act[fc], r_Wd], writes=[bres[bk]], inc=(fc == NFC - 1))
                    P.op("dve", TT(x1[:, xsl, hsl], banks[bk], x1[:, xsl, hsl], ALU.add),
                         reads=[bres[bk]], writes=[r_x1[xsl]])
                out_tokens.append(P.dma("sp", DMA(out_d[tbk * 128:(tbk + 1) * 128, :], x1[:, xsl, :]),
                                        reads=[r_x1[xsl]]))
                if nxt and b4 >= 1:
                    step1_tr(b4 - 1)
            if nxt:
                step1_tr(3)
        es_f.close()

    P.wait("sp", out_tokens)
    P.emit()
    P.close()
    return nc


def make_inputs(x, attn_norm_w, w_in, q_norm_w, k_norm_w, dil_out_norm_w, sb_out_norm_w,
                w_out, ffn_norm_w, w_gate, w_up, w_down):
    f = np.float32
    w_in = np.asarray(w_in, f)[0]
    pairs = []
    for g in range(2):
        o0 = g * 1536
        for p in range(4):
            cols = [w_in[:, o0 + j * 512 + p * 128: o0 + j * 512 + (p + 1) * 128] for j in range(3)]
            pairs.append(np.concatenate(cols, axis=1))
    w_pairs = np.ascontiguousarray(np.stack(pairs, 0))
    wg = np.asarray(w_gate, f)[0].reshape(8, 128, NFC, 128)
    wu = np.asarray(w_up, f)[0].reshape(8, 128, NFC, 128)
    wgu = np.concatenate([wg, wu], axis=3)
    wgu = np.ascontiguousarray(wgu.transpose(2, 1, 0, 3)).reshape(NFC, 128, 2048)
    qw = np.asarray(q_norm_w, f)[0]
    kw = np.asarray(k_norm_w, f)[0]
    qkw = np.concatenate([qw, qw, kw, kw])[None, :]
    wcat = np.concatenate([np.asarray(dil_out_norm_w, f)[0], np.asarray(sb_out_norm_w, f)[0]])
    pos = np.arange(S, dtype=f)
    inv = (f(10000.0) ** (-np.arange(0, 64, 2, dtype=f) / f(64))).astype(f)
    ang = (pos[:, None] * inv[None, :]).astype(f)
    rope = np.concatenate([np.cos(ang), np.sin(ang), -np.sin(ang)], axis=1).astype(f)
    shared = {
        "anw_bc": np.ascontiguousarray(np.broadcast_to(np.asarray(attn_norm_w, f)[0][None, :], (128, D))),
        "fnw_bc": np.ascontiguousarray(np.broadcast_to(np.asarray(ffn_norm_w, f)[0][None, :], (128, D))),
        "qkw_bc": np.ascontiguousarray(np.broadcast_to(qkw, (128, 256))),
        "wcol": np.ascontiguousarray(wcat.reshape(8, 128).T),
        "rope": rope,
        "w_pairs": w_pairs,
        "w_out": np.ascontiguousarray(np.asarray(w_out, f)[0]),
        "wgu": wgu,
        "w_down": np.ascontiguousarray(np.asarray(w_down, f)[0]),
    }
    xs = np.asarray(x, f)
    return [dict(shared, x=np.ascontiguousarray(xs[b])) for b in range(xs.shape[0])]


_NC_CACHE = {}


def kernel(**inputs):
    in_maps = make_inputs(**inputs)
    if "nc" not in _NC_CACHE:
        _NC_CACHE["nc"] = build()
    nc = _NC_CACHE["nc"]
    res = run_bass_kernel_spmd(nc, in_maps, core_ids=list(range(8)))
    return np.stack([np.asarray(r["out"], np.float32) for r in res.results], axis=0)
```

```python
from contextlib import ExitStack
import numpy as np
import concourse.bass as bass
import concourse.mybir as mybir
from concourse.bass_utils import run_bass_kernel_spmd

F32 = mybir.dt.float32
BF16 = mybir.dt.bfloat16
AF = mybir.ActivationFunctionType
ALU = mybir.AluOpType
AX = mybir.AxisListType

S = 4096
D = 1024
NB = S // 128
DFF = 2816
NFC = DFF // 128
EPS = 1e-6
ENGS = ("pe", "act", "dve", "pool", "sp")
NDMASEM = 8


class Res:
    __slots__ = ("w", "r", "pend")

    def __init__(self, init=None):
        self.w = dict(init) if init else {}
        self.r = {}
        self.pend = None


def _merge(dst, src):
    for k, v in src.items():
        if dst.get(k, 0) < v:
            dst[k] = v


class Prog:
    def __init__(self, nc):
        self.nc = nc
        self.es = ExitStack()
        self.ops = {e: [] for e in ENGS}
        self.sem = {}
        self.cnt = {}
        self.seen = {e: {} for e in ENGS}
        self.pending = {e: [] for e in ENGS}
        for e in ENGS:
            self.newsem("E_" + e)
        self.dq = {}
        for q in ("sp", "pool"):
            self.dq[q] = {"keys": [self.newsem(f"D_{q}{i}") for i in range(NDMASEM)], "i": 0}

    def newsem(self, key):
        self.sem[key] = self.es.enter_context(self.nc.semaphore(key))
        self.cnt[key] = 0
        return key

    def sb(self, name, shape, dt, es=None):
        return (es or self.es).enter_context(self.nc.sbuf_tensor(name, list(shape), dt))

    def ps(self, name, shape, dt):
        return self.es.enter_context(self.nc.psum_tensor(name, list(shape), dt))

    def fence(self):
        f = {}
        for k, v in self.cnt.items():
            if v > 0:
                f[k] = v
        for e in ENGS:
            assert not self.pending[e], "fence with pending un-tokened ops"
        return f

    def _collect(self, eng, reads, writes, deps):
        need = {}
        for r in reads:
            assert r.pend is None or r.pend == eng, "resource pending on another engine"
            _merge(need, r.w)
        for r in writes:
            assert r.pend is None or r.pend == eng, "resource pending on another engine"
            _merge(need, r.w)
            _merge(need, r.r)
        for d in deps:
            if d:
                _merge(need, d)
        ws = []
        seen = self.seen[eng]
        for k, v in need.items():
            if k == "E_pe" and eng == "pe":
                continue
            if seen.get(k, 0) >= v:
                continue
            seen[k] = v
            ws.append((k, v))
        return ws

    def _commit(self, eng, tok, reads, writes):
        allp = self.pending[eng] + [(reads, writes)]
        self.pending[eng] = []
        for rs, wsx in allp:
            for r in rs:
                _merge(r.r, tok)
                r.pend = None
            for r in wsx:
                r.w = dict(tok)
                r.r = {}
                r.pend = None

    def op(self, eng, fn, reads=(), writes=(), inc=True, deps=()):
        ws = self._collect(eng, reads, writes, deps)
        if inc:
            key = "E_" + eng
            self.cnt[key] += 1
            tok = {key: self.cnt[key]}
            self.ops[eng].append((ws, fn, (key, 1)))
            self._commit(eng, tok, reads, writes)
            return tok
        self.ops[eng].append((ws, fn, None))
        self.pending[eng].append((reads, writes))
        for r in list(reads) + list(writes):
            r.pend = eng
        return None

    def dma(self, q, fn, reads=(), writes=(), deps=()):
        assert not self.pending[q]
        dq = self.dq[q]
        key = dq["keys"][dq["i"] % NDMASEM]
        dq["i"] += 1
        prev = {key: self.cnt[key]} if self.cnt[key] else None
        ws = self._collect(q, reads, writes, list(deps) + [prev])
        self.cnt[key] += 16
        tok = {key: self.cnt[key]}
        self.ops[q].append((ws, fn, (key, 16)))
        for r in reads:
            _merge(r.r, tok)
        for r in writes:
            r.w = dict(tok)
            r.r = {}
        return tok

    def wait(self, eng, deps):
        ws = self._collect(eng, (), (), deps)
        if ws:
            self.ops[eng].append((ws, None, None))

    def emit(self):
        prog = self
        for e in ENGS:
            assert not self.pending[e], f"pending ops on {e}"

        def run(name, e):
            for ws, fn, inc in prog.ops[name]:
                for key, val in ws:
                    e.wait_ge(prog.sem[key], val)
                if fn is None:
                    continue
                ins = fn(e)
                if inc is not None:
                    ins.then_inc(prog.sem[inc[0]], inc[1])

        with self.nc.Block() as block:
            @block.tensor
            def _(e):
                run("pe", e)

            @block.scalar
            def _(e):
                run("act", e)

            @block.vector
            def _(e):
                run("dve", e)

            @block.gpsimd
            def _(e):
                run("pool", e)

            @block.sync
            def _(e):
                run("sp", e)

    def close(self):
        self.es.close()


def MM(out, lhsT, rhs, start, stop, **kw):
    return lambda e: e.matmul(out, lhsT=lhsT, rhs=rhs, start=start, stop=stop, **kw)


def TR(out, in_, idn):
    return lambda e: e.transpose(out, in_, idn)


def ACTV(out, in_, func, **kw):
    return lambda e: e.activation(out=out, in_=in_, func=func, **kw)


def TT(out, in0, in1, op):
    return lambda e: e.tensor_tensor(out=out, in0=in0, in1=in1, op=op)


def TS(out, in0, s1, op0):
    return lambda e: e.tensor_scalar(out=out, in0=in0, scalar1=s1, scalar2=None, op0=op0)


def STT(out, in0, scalar, in1, op0, op1):
    return lambda e: e.scalar_tensor_tensor(out=out, in0=in0, scalar=scalar, in1=in1, op0=op0, op1=op1)


def CP(out, in_):
    return lambda e: e.tensor_copy(out=out, in_=in_)


def RCP(out, in_):
    return lambda e: e.reciprocal(out=out, in_=in_)


def RSUM(out, in_):
    return lambda e: e.reduce_sum(out=out, in_=in_, axis=AX.X)


def DMA(out, in_):
    return lambda e: e.dma_start(out=out, in_=in_)


def MSET(ap, v):
    return lambda e: e.memset(ap, v)

PA_BLOCKS = NB
KNOB = {}


def build(pairs=tuple(range(8)), do_ffn=True, dbg=None):
    nc = bass.Bass("TRN2", target_bir_lowering=False)

    def din(name, shape, dt=F32):
        return nc.dram_tensor(name, list(shape), dt, kind="ExternalInput").ap()

    x_d = din("x", [S, D])
    anw_d = din("anw_bc", [128, D])
    fnw_d = din("fnw_bc", [128, D])
    qkw_d = din("qkw_bc", [128, 256])
    wcol_d = din("wcol", [128, 8])
    rope_d = din("rope", [S, 96])
    wp_d = din("w_pairs", [8, D, 384])
    wout_d = din("w_out", [D, D])
    wgu_d = din("wgu", [NFC, 128, 2048])
    wd_d = din("w_down", [DFF, D])
    out_d = nc.dram_tensor("out", [S, D], F32, kind="ExternalOutput").ap()
    dbg_d = None
    if dbg in ("hT", "oT"):
        dbg_d = nc.dram_tensor("dbg", [128, 8, S], BF16, kind="ExternalOutput").ap()

    P = Prog(nc)
    out_tokens = []

    pb = [P.ps(f"pb{i}", [128, 1024], F32) for i in range(4)]
    banks = []
    for i in range(4):
        banks += [pb[i][:, 0:512], pb[i][:, 512:1024]]
    bres = [Res() for _ in range(8)]
    bT = [banks[i].bitcast(BF16) for i in range(8)]

    ident = P.sb("ident", [128, 128], BF16)
    tri = P.sb("tri", [128, 128], BF16)
    sl = P.sb("sl", [128, 128], BF16)
    ones = P.sb("ones", [128, 128], BF16)
    sbm = [P.sb(f"sbm{j}", [128, 512], BF16) for j in range(4)]
    dm1 = P.sb("dm1", [128, 512], BF16)
    dmz = P.sb("dmz", [128, 512], BF16)
    dmz2 = P.sb("dmz2", [128, 512], BF16)
    wcol = P.sb("wcol_sb", [128, 8], F32)
    oT = P.sb("oT", [128, 8, S], BF16)
    r_const = Res()
    r_oT = [[Res() for _ in range(8)] for _ in range(8)]

    def SEL(t_ap, pattern, base, cm, cmp):
        return lambda e: e.affine_select(out=t_ap, in_=t_ap, pattern=pattern, base=base,
                                         channel_multiplier=cm, compare_op=cmp, fill=0.0)

    for t in (ident, tri, sl, ones, dm1, dmz, dmz2, *sbm):
        P.op("pool", MSET(t[:], 1.0), writes=[r_const])
    P.op("pool", SEL(ident[:], [[-1, 128]], 0, 1, ALU.is_equal), writes=[r_const])
    P.op("pool", SEL(tri[:], [[-1, 128]], 0, 1, ALU.is_ge), writes=[r_const])
    P.op("pool", SEL(sl[:], [[1, 128]], 0, -1, ALU.is_gt), writes=[r_const])
    for j in range(4):
        P.op("pool", SEL(sbm[j][:], [[1, 512]], -128 * j, -1, ALU.is_gt), writes=[r_const])
    for m in (dm1, dmz, dmz2):
        for slot in range(4):
            ap = m[:, slot * 128:(slot + 1) * 128]
            if slot % 2 == 0:
                P.op("pool", SEL(ap, [[-1, 128]], 0, 1, ALU.is_ge), writes=[r_const])
            else:
                P.op("pool", SEL(ap, [[1, 128]], 0, -1, ALU.is_ge), writes=[r_const])
    P.op("pool", MSET(dmz[:, 0:128], 0.0), writes=[r_const])
    P.op("pool", MSET(dmz2[:, 0:128], 0.0), writes=[r_const])
    P.op("pool", MSET(dmz2[:, 256:384], 0.0), writes=[r_const])
    P.dma("sp", DMA(wcol[:], wcol_d), writes=[r_const])
    wgu_bf = nc.dram_tensor("wgu_bf", [NFC, 128, 2048], BF16, kind="Internal").ap()
    r_wgubf = [Res() for _ in range(NFC)]

    es_att = ExitStack()
    hT = P.sb("hT", [128, 8, S], BF16, es_att)
    r_hT = [Res() for _ in range(NB)]
    Wp = P.sb("Wp", [128, 8, 384], BF16, es_att)
    r_Wp = Res()
    qkvT = P.sb("qkvT", [128, 3, S], BF16, es_att)
    r_qk = [Res() for _ in range(NB)]
    r_v = [Res() for _ in range(8)]
    sq = [P.sb(f"sq{i}", [128, 512], BF16, es_att) for i in range(4)]
    r_sq = [Res() for _ in range(4)]
    rs = P.sb("rs", [128, 512], F32, es_att)
    r_rs = Res()

    es_pa = ExitStack()
    xs = [P.sb(f"xs{i}", [128, D], F32, es_pa)[:] for i in range(4)]
    for j in range(3):
        v32 = qkvT[:, j, :].bitcast(F32)
        xs += [v32[:, 0:D], v32[:, D:2 * D]]
    NX = len(xs)
    r_xs = [Res() for _ in range(NX)]
    hn = [P.sb(f"hn{i}", [128, D], BF16, es_pa) for i in range(4)]
    r_hn = [Res() for _ in range(4)]
    junk = P.sb("junkA", [128, D], BF16, es_pa)
    r_junk = Res()
    anw = P.sb("anw", [128, D], F32, es_pa)
    r_anw = Res()
    ssA = P.sb("ssA", [128, NB], F32, es_pa)
    sdA = P.sb("sdA", [128, NB], F32, es_pa)
    rsA = P.sb("rsA", [128, NB], F32, es_pa)
    r_ssA = [Res() for _ in range(NB)]
    P.dma("sp", DMA(anw[:], anw_d), writes=[r_anw])
    BS = 2

    def paA1(sb):
        for b in range(BS):
            tb = sb * BS + b
            xi = tb % NX
            tsl = slice(tb * 128, (tb + 1) * 128)
            P.dma("sp", DMA(xs[xi], x_d[tsl, :]), writes=[r_xs[xi]])
            P.op("act", ACTV(junk[:], xs[xi], AF.Square, accum_out=ssA[:, tb:tb + 1]),
                 reads=[r_xs[xi]], writes=[r_junk, r_ssA[sb]])
        c4 = slice(sb * BS, sb * BS + BS)
        P.op("act", ACTV(sdA[:, c4], ssA[:, c4], AF.Sqrt, scale=1.0 / D, bias=EPS), reads=[r_ssA[sb]], writes=[r_ssA[sb]])

    def paA2(sb):
        c4 = slice(sb * BS, sb * BS + BS)
        P.op("dve", RCP(rsA[:, c4], sdA[:, c4]), reads=[r_ssA[sb]], writes=[r_ssA[sb]])

    def paB1(sb):
        for b in range(BS):
            tb = sb * BS + b
            xi = tb % NX
            hi = tb % 4
            bk = tb % 4
            P.op("dve", STT(hn[hi][:], xs[xi], rsA[:, tb:tb + 1], anw[:], ALU.mult, ALU.mult),
                 reads=[r_xs[xi], r_ssA[sb], r_anw], writes=[r_hn[hi]])
            for c in range(8):
                csl = slice(c * 128, (c + 1) * 128)
                P.op("pe", TR(bT[bk][:, csl], hn[hi][:, csl], ident[:]),
                     reads=[r_hn[hi], r_const], writes=[bres[bk]], inc=(c == 7))

    def paB2(sb):
        for b in range(BS):
            tb = sb * BS + b
            bk = tb % 4
            tsl = slice(tb * 128, (tb + 1) * 128)
            if b % 2 == 0:
                P.op("act", ACTV(hT[:, :, tsl], bT[bk].rearrange("p (c t) -> p c t", t=128), AF.Copy),
                     reads=[bres[bk]], writes=[r_hT[tb]])
            else:
                P.op("dve", CP(hT[:, :, tsl], bT[bk].rearrange("p (c t) -> p c t", t=128)),
                     reads=[bres[bk]], writes=[r_hT[tb]])

    nsb = PA_BLOCKS // BS
    for sb in range(nsb + 1):
        if sb >= 1:
            paB1(sb - 1)
        if sb < nsb:
            paA1(sb)
        if sb >= 1:
            paB2(sb - 1)
        if sb < nsb:
            paA2(sb)
    fenceA = P.fence()
    es_pa.close()
    for r_ in r_qk + r_v:
        _merge(r_.w, fenceA)
    casts_pending = [do_ffn]

    if dbg == "hT":
        for c in range(8):
            out_tokens.append(P.dma("sp", DMA(dbg_d[:, c, :], hT[:, c, :]), reads=r_hT))

    def proj_fm(col0, dst_idx, scale, res_for_slice):
        for ts in range(8):
            bk = ts % 2
            tsl = slice(ts * 512, (ts + 1) * 512)
            for c in range(8):
                P.op("pe", MM(banks[bk], Wp[:, c, col0:col0 + 128], hT[:, c, tsl], c == 0, c == 7),
                     reads=[r_Wp] + r_hT[ts * 4:(ts + 1) * 4], writes=[bres[bk]], inc=(c == 7))
            wr = res_for_slice(ts)
            if ts % 2 == 0:
                P.op("act", ACTV(qkvT[:, dst_idx, tsl], banks[bk], AF.Copy, scale=scale),
                     reads=[bres[bk]], writes=wr)
            else:
                P.op("dve", TS(qkvT[:, dst_idx, tsl], banks[bk], scale, ALU.mult),
                     reads=[bres[bk]], writes=wr)

    def load_pair_weights(pi):
        P.dma("pool", DMA(Wp[:], wp_d[pi].rearrange("(c p) e -> p c e", p=128)), writes=[r_Wp])

    def normalize_group(gc):
        for ts in range(8):
            tsl = slice(ts * 512, (ts + 1) * 512)
            for c in range(4):
                if c < 3:
                    P.op("act", ACTV(sq[c][:], oT[:, gc + c, tsl], AF.Square),
                         reads=[r_oT[gc + c][ts]], writes=[r_sq[c]])
                else:
                    P.op("pool", TT(sq[c][:], oT[:, gc + c, tsl], oT[:, gc + c, tsl], ALU.mult),
                         reads=[r_oT[gc + c][ts]], writes=[r_sq[c]])
            bk = 2 + ts % 2
            for c in range(4):
                P.op("pe", MM(banks[bk], ones[:], sq[c][:], c == 0, c == 3),
                     reads=[r_sq[c], r_const], writes=[bres[bk]], inc=(c == 3))
            P.op("act", ACTV(rs[:], banks[bk], AF.Ln, scale=1.0 / 512, bias=EPS),
                 reads=[bres[bk]], writes=[r_rs])
            P.op("act", ACTV(rs[:], rs[:], AF.Exp, scale=-0.5), reads=[r_rs], writes=[r_rs])
            P.op("dve", TT(oT[:, gc:gc + 4, tsl], oT[:, gc:gc + 4, tsl],
                           rs[:].unsqueeze(1).to_broadcast([128, 4, 512]), ALU.mult),
                 reads=[r_rs], writes=[r_oT[gc + c][ts] for c in range(4)])

    LAG = 7
    NSL = 8
    pairsA = [p for p in pairs if p < 4]
    bank_rr = [0]

    def next_bank():
        b = bank_rr[0] % 4
        bank_rr[0] += 1
        return b

    if pairsA:
        es_ga = ExitStack()
        qkw = P.sb("qkw", [128, 256], F32, es_ga)
        r_qkw = Res(fenceA)
        P.dma("sp", DMA(qkw[:], qkw_d), writes=[r_qkw])
        P.op("dve", TS(qkw[:, 0:128], qkw[:, 0:128], 0.125, ALU.mult), writes=[r_qkw])
        fprev = fenceA
        for pi in pairsA:
            load_pair_weights(pi)
            es_p = ExitStack()
            NS = 4
            sa = [P.sb(f"sa{pi}_{i}", [128, 512], F32, es_p) for i in range(NS)]
            qn = [P.sb(f"qn{pi}_{i}", [128, 512], F32, es_p) for i in range(NS)]
            t2 = [P.sb(f"t2{pi}_{i}", [128, 512], F32, es_p) for i in range(NS)]
            qr = [P.sb(f"qr{pi}_{i}", [128, 512], BF16, es_p) for i in range(NS + 1)]
            st = [P.sb(f"st{pi}_{i}", [128, 24], F32, es_p) for i in range(NS)]
            ropeb = [P.sb(f"ropeb{pi}_{i}", [128, 2, 96], F32, es_p) for i in range(NS)]
            r_sa = [Res(fprev) for _ in range(NS)]
            r_qn = [Res(fprev) for _ in range(NS)]
            r_t2 = [Res(fprev) for _ in range(NS)]
            r_qr = [Res(fprev) for _ in range(NS + 1)]
            r_st = [Res(fprev) for _ in range(NS)]
            r_rope = [Res(fprev) for _ in range(NS)]
            w3 = lambda ap: ap.rearrange("p (s d) -> p s d", d=64)
            w4 = lambda ap: ap.rearrange("p (b s d) -> p b s d", b=2, d=32)
            w5 = lambda ap: ap.rearrange("p (b s h d) -> p b s h d", b=2, h=2, d=32)
            wb = lambda ap: ap.rearrange("p (b e) -> p b e", e=256)
            NBB = NB // 2

            def st0(bb):
                k = bb % NS
                tb0 = bb * 2
                bk = bb % 2
                P.dma("sp", DMA(ropeb[k][:], rope_d[tb0 * 128:(tb0 + 2) * 128, :].rearrange("(b p) d -> p b d", p=128)),
                      writes=[r_rope[k]])
                for b in range(2):
                    tsl = slice((tb0 + b) * 128, (tb0 + b + 1) * 128)
                    for c in range(8):
                        P.op("pe", MM(banks[bk][:, b * 256:(b + 1) * 256], hT[:, c, tsl], Wp[:, c, 0:256], c == 0, c == 7),
                             reads=[r_Wp, r_hT[tb0 + b]], writes=[bres[bk]], inc=(b == 1 and c == 7))

            def st1a(bb):
                k = bb % NS
                bk = bb % 2
                P.op("act", ACTV(sa[k][:], banks[bk], AF.Square), reads=[bres[bk]], writes=[r_sa[k]])
                P.op("dve", RSUM(st[k][:, 0:8], w3(sa[k][:])), reads=[r_sa[k]], writes=[r_st[k]])
                P.op("act", ACTV(st[k][:, 8:16], st[k][:, 0:8], AF.Sqrt, scale=1.0 / 64, bias=EPS),
                     reads=[r_st[k]], writes=[r_st[k]])

            def st1b(bb):
                k = bb % NS
                bk = bb % 2
                P.op("dve", RCP(st[k][:, 16:24], st[k][:, 8:16]), reads=[r_st[k]], writes=[r_st[k]])
                P.op("dve", TT(w3(qn[k][:]), w3(banks[bk]), st[k][:, 16:24].unsqueeze(2).to_broadcast([128, 8, 64]),
                               ALU.mult), reads=[bres[bk], r_st[k]], writes=[r_qn[k]])
                P.op("pool", TT(wb(qn[k][:]), wb(qn[k][:]), qkw[:].unsqueeze(1).to_broadcast([128, 2, 256]), ALU.mult),
                     reads=[r_qkw], writes=[r_qn[k]])

            def st2(bb):
                k = bb % NS
                cosb = ropeb[k][:, :, 0:32].unsqueeze(2).to_broadcast([128, 2, 8, 32])
                sinb = ropeb[k][:, :, 32:64].unsqueeze(2).to_broadcast([128, 2, 4, 32])
                nsinb = ropeb[k][:, :, 64:96].unsqueeze(2).to_broadcast([128, 2, 4, 32])
                P.op("dve", TT(w4(sa[k][:]), w4(qn[k][:]), cosb, ALU.mult),
                     reads=[r_qn[k], r_rope[k]], writes=[r_sa[k]])
                P.op("pool", TT(w5(t2[k][:])[:, :, :, 0, :], w5(qn[k][:])[:, :, :, 1, :], nsinb, ALU.mult),
                     reads=[r_qn[k], r_rope[k]], writes=[r_t2[k]])
                P.op("pool", TT(w5(t2[k][:])[:, :, :, 1, :], w5(qn[k][:])[:, :, :, 0, :], sinb, ALU.mult),
                     reads=[r_qn[k], r_rope[k]], writes=[r_t2[k]])

            def st3a(bb):
                k = bb % NS
                kq = bb % (NS + 1)
                P.op("dve", TT(qr[kq][:], sa[k][:], t2[k][:], ALU.add), reads=[r_sa[k], r_t2[k]], writes=[r_qr[kq]])

            def st3b(bb):
                kq = bb % (NS + 1)
                tb0 = bb * 2
                t2sl = slice(tb0 * 128, (tb0 + 2) * 128)
                bk2 = 2 + bb % 2
                for j in range(4):
                    jsl = slice(j * 128, (j + 1) * 128)
                    P.op("pe", TR(bT[bk2][:, jsl], qr[kq][:, jsl], ident[:]),
                         reads=[r_qr[kq], r_const], writes=[bres[bk2]], inc=(j == 3))
                P.op("act", ACTV(qkvT[:, 0:2, t2sl].rearrange("p j (b t) -> p j b t", t=128),
                                 bT[bk2][:, 0:512].rearrange("p (b j t) -> p j b t", j=2, t=128), AF.Copy),
                     reads=[bres[bk2]], writes=[r_qk[tb0], r_qk[tb0 + 1]])

            for it in range(NBB + 4):
                if it < NBB:
                    st0(it)
                if 0 <= it - 1 < NBB:
                    st1a(it - 1)
                if 0 <= it - 2 < NBB:
                    st2(it - 2)
                if 0 <= it - 1 < NBB:
                    st1b(it - 1)
                if 0 <= it - 3 < NBB:
                    st3a(it - 3)
                if 0 <= it - 4 < NBB:
                    st3b(it - 4)
            proj_fm(256, 2, 1.0, lambda ts: [r_v[ts]])
            fmid = P.fence()
            es_p.close()

            es_a = ExitStack()
            vaug = [P.sb(f"vaug{pi}_{i}", [128, NB, 128], BF16, es_a) for i in range(3)]
            r_vaug = [Res(fmid) for _ in range(3)]
            expS = [P.sb(f"expS{pi}_{i}", [128, 512], BF16, es_a) for i in range(NSL)]
            r_expS = [Res(fmid) for _ in range(NSL)]
            ptm = expS
            r_ptm = r_expS
            rden = [P.sb(f"rden{pi}_{i}", [128, 512], F32, es_a) for i in range(1)]
            r_rden = [Res(fmid), Res(fmid)]
            for i in range(3):
                P.op("pool", MSET(vaug[i][:, :, 64:128], 1.0), writes=[r_vaug[i]])
            if casts_pending[0]:
                casts_pending[0] = False
                for fc in range(NFC):
                    P.dma("pool", DMA(wgu_bf[fc], wgu_d[fc]), writes=[r_wgubf[fc]], deps=[fmid])
            VIDX = {1: 0, 4: 1, 16: 2}

            vi = 0
            gi = 0
            for hp in range(2):
                p0, p1 = hp * 64, hp * 64 + 64
                for R in range(2):
                    started = [False] * 4
                    groups = []
                    for r in (1, 4, 16):
                        nb = NB // r
                        if r == 16:
                            for c in range(0, 16, 2):
                                groups.append((r, [(c, R - 1, R), (c, R, R), (c + 1, R - 1, R), (c + 1, R, R)]))
                        else:
                            per = nb // 2
                            for c in range(r):
                                for n in range(R * per, (R + 1) * per, 2):
                                    groups.append((r, [(c, n - 1, n), (c, n, n), (c, n, n + 1), (c, n + 1, n + 1)]))

                    def build_vaug(r, vi_):
                        nb = NB // r
                        for g8 in range(4):
                            bk = next_bank()
                            for j in range(8):
                                blk = g8 * 8 + j
                                c, n = blk // nb, blk % nb
                                st_ = n * 128 * r + c
                                P.op("pe", TR(bT[bk][:, j * 64:(j + 1) * 64], qkvT[p0:p1, 2, st_:st_ + 127 * r + 1:r],
                                              ident[p0:p1, p0:p1]),
                                     reads=r_v + [r_const], writes=[bres[bk]], inc=(j == 7))
                            P.op("dve", CP(vaug[vi_][:, g8 * 8:(g8 + 1) * 8, 0:64],
                                           bT[bk][:, 0:512].rearrange("p (j d) -> p j d", d=64)),
                                 reads=[bres[bk]], writes=[r_vaug[vi_]])

                    def emit_pv(item):
                        r, slots, pslot, vi_ = item
                        nb = NB // r
                        mms = []
                        for si, (c, kb, qb) in enumerate(slots):
                            if kb < 0:
                                continue
                            vblk = c * nb + kb
                            if r == 16:
                                for u in range(4):
                                    stt = not started[u]
                                    started[u] = True
                                    q0 = si * 128 + u * 32
                                    mms.append((MM(banks[4 + u][:, c:512:16], vaug[vi_][:, vblk, :],
                                                   ptm[pslot][:, q0:q0 + 32], stt, True, skip_group_check=True), u))
                            else:
                                tok0 = qb * 128 * r + c - R * 2048
                                u = tok0 // 512
                                lo = tok0 - u * 512
                                stt = not started[u]
                                started[u] = True
                                mms.append((MM(banks[4 + u][:, lo:lo + 127 * r + 1:r], vaug[vi_][:, vblk, :],
                                               ptm[pslot][:, si * 128:(si + 1) * 128], stt, True,
                                               skip_group_check=True), u))
                        for kk, (fn, u) in enumerate(mms):
                            P.op("pe", fn, reads=[r_vaug[vi_], r_ptm[pslot]], writes=[bres[4 + u]],
                                 inc=(kk == len(mms) - 1))

                    pend = []
                    if R == 0:
                        for r_ in (1, 4, 16):
                            build_vaug(r_, VIDX[r_])
                    for (r, slots) in groups:
                        vi = VIDX[r]
                        bk = next_bank()
                        es_ = gi % NSL
                        for si, (c, kb, qb) in enumerate(slots):
                            ks = max(kb, 0) * 128 * r + c
                            qs = qb * 128 * r + c
                            P.op("pe", MM(banks[bk][:, si * 128:(si + 1) * 128],
                                          qkvT[p0:p1, 1, ks:ks + 127 * r + 1:r], qkvT[p0:p1, 0, qs:qs + 127 * r + 1:r],
                                          True, True),
                                 reads=r_qk, writes=[bres[bk]], inc=(si == 3))
                        P.op("act", ACTV(expS[es_][:], banks[bk], AF.Exp), reads=[bres[bk]], writes=[r_expS[es_]])
                        inv = [kb < 0 for (_, kb, _) in slots]
                        mk = dmz2 if (inv[0] and inv[2]) else (dmz if inv[0] else dm1)
                        P.op("dve", TT(expS[es_][:], expS[es_][:], mk[:], ALU.mult),
                             reads=[r_const], writes=[r_expS[es_]])
                        pend.append((r, slots, es_, vi))
                        if len(pend) > LAG:
                            emit_pv(pend.pop(0))
                        gi += 1
                    while pend:
                        emit_pv(pend.pop(0))
                    for u in range(4):
                        rd = u % 2
                        rsl = slice(rd * 64, rd * 64 + 64)
                        ts_ = R * 4 + u
                        P.op("act", ACTV(rden[0][rsl, :], banks[4 + u][64:128, :], AF.Ln),
                             reads=[bres[4 + u]], writes=[r_rden[rd]])
                        P.op("act", ACTV(rden[0][rsl, :], rden[0][rsl, :], AF.Exp, scale=-1.0),
                             reads=[], writes=[r_rden[rd]])
                        P.op("dve", TT(oT[p0:p1, pi, ts_ * 512:(ts_ + 1) * 512], banks[4 + u][0:64, :],
                                       rden[0][rsl, :], ALU.mult),
                             reads=[bres[4 + u], r_rden[rd]], writes=[r_oT[pi][ts_]])
            fprev = P.fence()
            es_a.close()
        if len(pairsA) == 4:
            normalize_group(0)
        fenceGA = P.fence()
        es_ga.close()
    else:
        fenceGA = fenceA

    pairsB = [p for p in pairs if p >= 4]
    if pairsB:
        es_gb = ExitStack()
        fB = fenceGA
        V2 = P.sb("V2", [128, NB, 128], BF16, es_gb)
        r_V2 = Res(fB)
        E2 = [P.sb(f"E2_{i}", [128, 1024], BF16, es_gb) for i in range(3)]
        r_E = [Res(fB) for _ in range(3)]
        sp2 = [P.sb(f"sp2_{i}", [128, 1024], BF16, es_gb) for i in range(3)]
        r_sp = [Res(fB) for _ in range(3)]
        X2 = [P.sb(f"X2_{i}", [128, 1024], BF16, es_gb) for i in range(2)]
        r_X = [Res(fB) for _ in range(2)]
        A2 = [P.sb(f"A2_{i}", [128, 1024], BF16, es_gb) for i in range(3)]
        r_A = [Res(fB) for _ in range(3)]

        for pi in pairsB:
            load_pair_weights(pi)
            proj_fm(0, 0, 0.125, lambda ts: r_qk[ts * 4:(ts + 1) * 4])
            proj_fm(128, 1, 1.0, lambda ts: r_qk[ts * 4:(ts + 1) * 4])
            proj_fm(256, 2, 1.0, lambda ts: [r_v[ts]])
            for g8 in range(4):
                bk = 2 + g8 % 2
                for j in range(8):
                    tb = g8 * 8 + j
                    P.op("pe", TR(bT[bk][:, j * 128:(j + 1) * 128], qkvT[:, 2, tb * 128:(tb + 1) * 128], ident[:]),
                         reads=r_v + [r_const], writes=[bres[bk]], inc=(j == 7))
                P.op("dve", CP(V2[:, g8 * 8:(g8 + 1) * 8, :], bT[bk].rearrange("p (j d) -> p j d", d=128)),
                     reads=[bres[bk]], writes=[r_V2])

            steps = [(G, J) for G in range(8) for J in range(4 * G + 3, -1, -1)]
            n = len(steps)

            def lo_of(i):
                G, J = steps[i]
                return max(0, J - 4 * G) * 128

            def hv(ap, lo):
                v = ap.rearrange("p (h t) -> p h t", h=2)
                return v if lo == 0 else v[:, :, lo:512]

            def pe_z(i):
                G, J = steps[i]
                k = i % 2
                lo = lo_of(i)
                for h in range(2):
                    hs = slice(h * 64, (h + 1) * 64)
                    P.op("pe", MM(banks[2 * k + h][:, lo:512], qkvT[hs, 1, J * 128:(J + 1) * 128],
                                  qkvT[hs, 0, G * 512 + lo:(G + 1) * 512], True, True),
                         reads=r_qk, writes=[bres[2 * k + h]], inc=(h == 1))

            def act_EL(i):
                G, J = steps[i]
                k = i % 2
                e_ = i % 3
                s_ = i % 3
                lo = lo_of(i)
                P.op("act", ACTV(hv(E2[e_][:], lo), hv(pb[k][:], lo), AF.Exp),
                     reads=[bres[2 * k], bres[2 * k + 1]], writes=[r_E[e_]])
                if J >= 4 * G:
                    ev = hv(E2[e_][:], lo)
                    P.op("dve", TT(ev, ev, sbm[J - 4 * G][:, lo:512].unsqueeze(1).to_broadcast([128, 2, 512 - lo]),
                                   ALU.mult), reads=[r_const], writes=[r_E[e_]])

            def act_L(i):
                e_ = i % 3
                s_ = i % 3
                lo = lo_of(i)
                P.op("act", ACTV(hv(sp2[s_][:], lo), hv(E2[e_][:], lo), AF.Ln, bias=1.0, scale=1.0),
                     reads=[r_E[e_]], writes=[r_sp[s_]])

            def pe_C(i):
                G, J = steps[i]
                first = (J == 4 * G + 3)
                s_ = i % 3
                lo = lo_of(i)
                for h in range(2):
                    P.op("pe", MM(banks[4 + h][:, lo:512], tri[:], sp2[s_][:, h * 512 + lo:(h + 1) * 512], first, first,
                                  skip_group_check=True),
                         reads=[r_sp[s_], r_const], writes=[bres[4 + h]], inc=(first and h == 1))
                    if not first:
                        sp_prev = (i - 1) % 3
                        lop = lo_of(i - 1)
                        P.op("pe", MM(banks[4 + h][:, lop:512], sl[:], sp2[sp_prev][:, h * 512 + lop:(h + 1) * 512],
                                      False, True, skip_group_check=True),
                             reads=[r_sp[sp_prev], r_const], writes=[bres[4 + h]], inc=(h == 1))

            def act_X(i):
                e_ = i % 3
                x_ = i % 2
                a_ = i % 3
                lo = lo_of(i)
                P.op("act", ACTV(hv(X2[x_][:], lo), hv(pb[2][:], lo), AF.Exp, scale=-1.0),
                     reads=[bres[4], bres[5]], writes=[r_X[x_]])
                P.op("dve", TT(hv(A2[a_][:], lo), hv(E2[e_][:], lo), hv(X2[x_][:], lo), ALU.mult),
                     reads=[r_E[e_], r_X[x_]], writes=[r_A[a_]])

            def pe_pv(i):
                G, J = steps[i]
                first = (J == 4 * G + 3)
                last = (J == 0)
                ob = 6 + G % 2
                a_ = i % 3
                lo = lo_of(i)
                for h in range(2):
                    hs = slice(h * 64, (h + 1) * 64)
                    P.op("pe", MM(banks[ob][hs, lo:512], V2[:, J, hs], A2[a_][:, h * 512 + lo:(h + 1) * 512], first, last,
                                  skip_group_check=True),
                         reads=[r_V2, r_A[a_]], writes=[bres[ob]], inc=(h == 1))
                if last:
                    P.op("dve", CP(oT[:, pi, G * 512:(G + 1) * 512], banks[ob]),
                         reads=[bres[ob]], writes=[r_oT[pi][G]])

            for i in range(n + 3):
                if i < n:
                    pe_z(i)
                if 0 <= i - 3 < n:
                    pe_pv(i - 3)
                diag = False
                if 0 <= i - 1 < n:
                    act_EL(i - 1)
                    G_, J_ = steps[i - 1]
                    diag = J_ >= 4 * G_
                    if not diag:
                        act_L(i - 1)
                if 0 <= i - 2 < n:
                    act_X(i - 2)
                if 0 <= i - 1 < n:
                    if diag:
                        act_L(i - 1)
                    pe_C(i - 1)
        if len(pairsB) == 4:
            normalize_group(4)
        fenceGB = P.fence()
        es_gb.close()

    if dbg == "oT":
        for c in range(8):
            out_tokens.append(P.dma("sp", DMA(dbg_d[:, c, :], oT[:, c, :]), reads=r_oT[c]))
    fenceATT = P.fence()
    es_att.close()

    if do_ffn:
        fF = fenceATT
        es_f = ExitStack()
        Wd = P.sb("Wd", [128, NFC, D], BF16, es_f)
        r_Wd = Res(fF)
        Wo = P.sb("Wo", [128, 8, D], BF16, es_f)
        r_Wo = Res(fF)
        x1 = P.sb("x1", [128, 5, D], F32, es_f)
        r_x1 = [Res(fF) for _ in range(5)]
        h2T = P.sb("h2T", [128, 8, 512], BF16, es_f)
        r_h2T = [Res(fF) for _ in range(4)]
        actT = P.sb("actT", [128, NFC, 512], BF16, es_f)
        r_act = [Res(fF) for _ in range(NFC)]
        ring = [P.sb(f"ring{i}", [128, 8, 256], BF16, es_f) for i in range(3)]
        r_ring = [Res(fF) for _ in range(3)]
        fnw = P.sb("fnw", [128, D], F32, es_f)
        r_fnw = Res(fF)
        sg = [P.sb(f"sg{i}", [128, 512], BF16, es_f) for i in range(1)]
        r_sg = [Res(fF)]
        h2 = [P.sb(f"h2{i}", [128, D], BF16, es_f) for i in range(4)]
        r_h2 = [Res(fF) for _ in range(4)]
        r_ss2b = [Res(fF) for _ in range(4)]
        ss2 = P.sb("ss2", [128, 12], F32, es_f)
        r_ss2 = Res(fF)

        P.dma("pool", DMA(Wo[:], wout_d.rearrange("(c p) d -> p c d", p=128)), writes=[r_Wo])
        for c in range(8):
            P.op("dve", TS(Wo[:, c, :], Wo[:, c, :], wcol[:, c:c + 1], ALU.mult), reads=[r_const], writes=[r_Wo])
        for q4_ in range(2):
            P.dma("pool", DMA(Wd[:, q4_ * 11:(q4_ + 1) * 11, :],
                              wd_d[q4_ * 11 * 128:(q4_ + 1) * 11 * 128, :].rearrange("(f p) d -> p f d", p=128)),
                  writes=[r_Wd])
        P.dma("sp", DMA(fnw[:], fnw_d), writes=[r_fnw])

        ydr = [0]

        def yd_bank():
            b = ydr[0] % 4
            ydr[0] += 1
            return b

        def step1_block(tt, b4):
            tbk = tt * 4 + b4
            xsl = tbk % 5
            tsl = slice(tbk * 128, (tbk + 1) * 128)
            P.dma("sp", DMA(x1[:, xsl, :], x_d[tsl, :]), writes=[r_x1[xsl]])
            for half in range(2):
                bk = yd_bank()
                hsl = slice(half * 512, (half + 1) * 512)
                for c in range(8):
                    P.op("pe", MM(banks[bk], oT[:, c, tsl], Wo[:, c, hsl], c == 0, c == 7),
                         reads=[r_Wo] + [r_oT[cc][tt] for cc in range(8)], writes=[bres[bk]], inc=(c == 7))
                P.op("dve", TT(x1[:, xsl, hsl], banks[bk], x1[:, xsl, hsl], ALU.add),
                     reads=[bres[bk]], writes=[r_x1[xsl]])
            P.op("act", ACTV(h2[b4][:], x1[:, xsl, :], AF.Square, accum_out=ss2[:, b4:b4 + 1]),
                 reads=[r_x1[xsl]], writes=[r_h2[b4], r_ss2b[b4]])
            P.op("act", ACTV(ss2[:, 4 + b4:5 + b4], ss2[:, b4:b4 + 1], AF.Sqrt, scale=1.0 / D, bias=EPS),
                 reads=[r_ss2b[b4]], writes=[r_ss2b[b4]])
            P.op("dve", RCP(ss2[:, 8 + b4:9 + b4], ss2[:, 4 + b4:5 + b4]), reads=[r_ss2b[b4]], writes=[r_ss2b[b4]])
            P.op("dve", STT(h2[b4][:], x1[:, xsl, :], ss2[:, 8 + b4:9 + b4], fnw[:], ALU.mult, ALU.mult),
                 reads=[r_x1[xsl], r_ss2b[b4], r_fnw], writes=[r_h2[b4]])

        def step1_tr(b4):
            bk = yd_bank()
            for c in range(8):
                csl = slice(c * 128, (c + 1) * 128)
                P.op("pe", TR(bT[bk][:, csl], h2[b4][:, csl], ident[:]),
                     reads=[r_h2[b4], r_const], writes=[bres[bk]], inc=(c == 7))
            P.op("act", ACTV(h2T[:, :, b4 * 128:(b4 + 1) * 128], bT[bk].rearrange("p (c t) -> p c t", t=128), AF.Copy),
                 reads=[bres[bk]], writes=[r_h2T[b4]])

        def finish_norm(tt):
            pass

        gur = 0
        for b4 in range(4):
            step1_block(0, b4)
            if b4 >= 1:
                step1_tr(b4 - 1)
        step1_tr(3)
        for tt in range(8):
            finish_norm(tt)
            for fc in range(NFC):
                rg = fc % 3
                P.dma("pool", DMA(ring[rg][:], wgu_bf[fc].rearrange("p (c j) -> p c j", j=256)),
                      reads=[r_wgubf[fc]], writes=[r_ring[rg]])
                bg = 4 + (gur % 2) * 2
                bu = bg + 1
                gur += 1
                for (bk, off) in ((bg, 0), (bu, 128)):
                    for c in range(8):
                        P.op("pe", MM(banks[bk], ring[rg][:, c, off:off + 128], h2T[:, c, :], c == 0, c == 7),
                             reads=[r_ring[rg]] + r_h2T, writes=[bres[bk]], inc=(c == 7))
                s_ = 0
                P.op("act", ACTV(sg[s_][:], banks[bg], AF.Silu), reads=[bres[bg]], writes=[r_sg[s_]])
                P.op("dve", TT(actT[:, fc, :], banks[bu], sg[s_][:], ALU.mult),
                     reads=[bres[bu], r_sg[s_]], writes=[r_act[fc]])
            nxt = tt + 1 < 8
            for b4 in range(4):
                tbk = tt * 4 + b4
                xsl = tbk % 5
                if nxt:
                    step1_block(tt + 1, b4)
                for half in range(2):
                    bk = yd_bank()
                    hsl = slice(half * 512, (half + 1) * 512)
                    for fc in range(NFC):
                        P.op("pe", MM(banks[bk], actT[:, fc, b4 * 128:(b4 + 1) * 128], Wd[:, fc, hsl],
                                      fc == 0, fc == NFC - 1),
                             reads=[r_act[fc], r_Wd], writes=[bres[bk]], inc=(fc == NFC - 1))
                    P.op("dve", TT(x1[:, xsl, hsl], banks[bk], x1[:, xsl, hsl], ALU.add),
                         reads=[bres[bk]], writes=[r_x1[xsl]])
                out_tokens.append(P.dma("sp", DMA(out_d[tbk * 128:(tbk + 1) * 128, :], x1[:, xsl, :]),
                                        reads=[r_x1[xsl]]))
                if nxt and b4 >= 1:
                    step1_tr(b4 - 1)
            if nxt:
                step1_tr(3)
        es_f.close()

    P.wait("sp", out_tokens)
    P.emit()
    P.close()
    return nc


def make_inputs(x, attn_norm_w, w_in, q_norm_w, k_norm_w, dil_out_norm_w, sb_out_norm_w,
                w_out, ffn_norm_w, w_gate, w_up, w_down):
    f = np.float32
    w_in = np.asarray(w_in, f)[0]
    pairs = []
    for g in range(2):
        o0 = g * 1536
        for p in range(4):
            cols = [w_in[:, o0 + j * 512 + p * 128: o0 + j * 512 + (p + 1) * 128] for j in range(3)]
            pairs.append(np.concatenate(cols, axis=1))
    w_pairs = np.ascontiguousarray(np.stack(pairs, 0))
    wg = np.asarray(w_gate, f)[0].reshape(8, 128, NFC, 128)
    wu = np.asarray(w_up, f)[0].reshape(8, 128, NFC, 128)
    wgu = np.concatenate([wg, wu], axis=3)
    wgu = np.ascontiguousarray(wgu.transpose(2, 1, 0, 3)).reshape(NFC, 128, 2048)
    qw = np.asarray(q_norm_w, f)[0]
    kw = np.asarray(k_norm_w, f)[0]
    qkw = np.concatenate([qw, qw, kw, kw])[None, :]
    wcat = np.concatenate([np.asarray(dil_out_norm_w, f)[0], np.asarray(sb_out_norm_w, f)[0]])
    pos = np.arange(S, dtype=f)
    inv = (f(10000.0) ** (-np.arange(0, 64, 2, dtype=f) / f(64))).astype(f)
    ang = (pos[:, None] * inv[None, :]).astype(f)
    rope = np.concatenate([np.cos(ang), np.sin(ang), -np.sin(ang)], axis=1).astype(f)
    shared = {
        "anw_bc": np.ascontiguousarray(np.broadcast_to(np.asarray(attn_norm_w, f)[0][None, :], (128, D))),
        "fnw_bc": np.ascontiguousarray(np.broadcast_to(np.asarray(ffn_norm_w, f)[0][None, :], (128, D))),
        "qkw_bc": np.ascontiguousarray(np.broadcast_to(qkw, (128, 256))),
        "wcol": np.ascontiguousarray(wcat.reshape(8, 128).T),
        "rope": rope,
        "w_pairs": w_pairs,
        "w_out": np.ascontiguousarray(np.asarray(w_out, f)[0]),
        "wgu": wgu,
        "w_down": np.ascontiguousarray(np.asarray(w_down, f)[0]),
    }
    xs = np.asarray(x, f)
    return [dict(shared, x=np.ascontiguousarray(xs[b])) for b in range(xs.shape[0])]


_NC_CACHE = {}


def kernel(**inputs):
    in_maps = make_inputs(**inputs)
    if "nc" not in _NC_CACHE:
        _NC_CACHE["nc"] = build()
    nc = _NC_CACHE["nc"]
    res = run_bass_kernel_spmd(nc, in_maps, core_ids=list(range(8)))
    return np.stack([np.asarray(r["out"], np.float32) for r in res.results], axis=0)
```
